# Optimizing a Trainium2 kernel written in Bass

```python
import math
import jax, jax.numpy as jnp
from jax import lax
import numpy as np

D_MODEL = 1024
BATCH = 8
SEQ = 2048
DEPTH = 2

N_MEM = 256
NORM_EPS = 1e-6
NEG_INF = -1e30
ATTN_BLOCK = 128

LRU_WIDTH = 1024
LRU_BLOCKS = 16
LRU_BLOCK_DIM = LRU_WIDTH // LRU_BLOCKS
CONV_WIDTH = 4
LRU_C = 8.0
LRU_A_MIN = 0.9
LRU_A_MAX = 0.999

DIL_GROUPS = ((128, 1), (512, 4), (2048, 16))
DIL_HEADS = 4
DIL_HEAD_DIM = 128
DIL_WIDTH = DIL_HEADS * DIL_HEAD_DIM
DIL_QKV = len(DIL_GROUPS) * DIL_WIDTH

MEM_HEADS = 4
MEM_HEAD_DIM = 64
MEM_WIDTH = MEM_HEADS * MEM_HEAD_DIM

NSA_HEADS = 16
NSA_KV_GROUPS = 2
NSA_HEADS_PER_GROUP = NSA_HEADS // NSA_KV_GROUPS
NSA_HEAD_DIM = 64
NSA_WIDTH = NSA_HEADS * NSA_HEAD_DIM
NSA_KV = NSA_KV_GROUPS * NSA_HEAD_DIM
CMP_BLOCK = 32
CMP_STRIDE = 16
SLC_BLOCK = 64
SLC_TOP_N = 8
WIN_SIZE = 512
PHI_HIDDEN = 256
SEL_FORCE = 1e6

N_EVEN = (DEPTH + 1) // 2
N_ODD = DEPTH // 2

HAWK_IN_SPLITS = (LRU_WIDTH, LRU_WIDTH, DIL_QKV, DIL_QKV, DIL_QKV, DIL_WIDTH, MEM_WIDTH, MEM_WIDTH)
HAWK_IN = sum(HAWK_IN_SPLITS)
HAWK_OUT = LRU_WIDTH + DIL_WIDTH + MEM_WIDTH
NSA_IN_SPLITS = (NSA_WIDTH, 6 * NSA_KV, 3 * NSA_HEADS, NSA_WIDTH, MEM_WIDTH, MEM_WIDTH)
NSA_IN = sum(NSA_IN_SPLITS)
NSA_OUT = NSA_WIDTH + MEM_WIDTH

kernel_name = "hybrid_rglru_dilated_nsa_memory"


def rms_norm(x, g):
    xf = x.astype(jnp.float32)
    y = xf * lax.rsqrt(jnp.mean(xf * xf, axis=-1, keepdims=True) + NORM_EPS)
    return (y * g.astype(jnp.float32)).astype(x.dtype)


def split_cols(h, sizes):
    return jnp.split(h, np.cumsum(sizes)[:-1].tolist(), axis=-1)


def alibi_slopes(n):
    return np.exp2(-8.0 * np.arange(1, n + 1) / n).astype(np.float32)


def banded_attention(q, k, v, slopes, max_dist, pos_scale):
    n, g, r, length, hd = q.shape
    blk = ATTN_BLOCK
    nb = -(-length // blk)
    pad_end = nb * blk - length
    p = -(-max_dist // blk)
    width = (p + 1) * blk
    q = jnp.pad(q, ((0, 0), (0, 0), (0, 0), (0, pad_end), (0, 0)))
    k = jnp.pad(k, ((0, 0), (0, 0), (p * blk, pad_end), (0, 0))).reshape(n, g, nb + p, blk, hd)
    v = jnp.pad(v, ((0, 0), (0, 0), (p * blk, pad_end), (0, 0))).reshape(n, g, nb + p, blk, hd)
    k_win = jnp.concatenate([k[:, :, i:i + nb] for i in range(p + 1)], axis=3).transpose(2, 0, 1, 3, 4)
    v_win = jnp.concatenate([v[:, :, i:i + nb] for i in range(p + 1)], axis=3).transpose(2, 0, 1, 3, 4)
    q_blk = q.reshape(n, g, r, nb, blk, hd).transpose(3, 0, 1, 2, 4, 5)
    dist = p * blk + np.arange(blk)[:, None] - np.arange(width)[None, :]
    key_idx = (np.arange(nb)[:, None] - p) * blk + np.arange(width)[None, :]
    valid = (dist >= 0)[None] & (dist <= max_dist)[None] & (key_idx >= 0)[:, None, :]
    bias = -(slopes[:, :, None, None] * (dist * pos_scale).astype(np.float32)[None, None])
    scale = hd ** -0.5

    def one_block(args):
        qb, kb, vb, ok = args
        s = jnp.einsum('ngrqd,ngkd->ngrqk', qb, kb, preferred_element_type=jnp.float32) * scale + bias
        s = jnp.where(ok, s, NEG_INF)
        m = jnp.max(s, axis=-1, keepdims=True)
        e = jnp.exp(s - m)
        den = jnp.sum(e, axis=-1, keepdims=True)
        o = jnp.einsum('ngrqk,ngkd->ngrqd', (e / den).astype(vb.dtype), vb,
                       preferred_element_type=jnp.float32)
        return o.astype(qb.dtype), (m + jnp.log(den))[..., 0]

    o, lse = lax.map(one_block, (q_blk, k_win, v_win, jnp.asarray(valid)))
    o = o.transpose(1, 2, 3, 0, 4, 5).reshape(n, g, r, nb * blk, hd)[:, :, :, :length]
    lse = lse.transpose(1, 2, 3, 0, 4).reshape(n, g, r, nb * blk)[..., :length]
    return o, lse


def dilated_attention(q, k, v):
    b, s, _, hd = q.shape
    slopes_all = alibi_slopes(len(DIL_GROUPS) * DIL_HEADS).reshape(len(DIL_GROUPS), DIL_HEADS)
    outs, lses = [], []
    for gi, (window, dil) in enumerate(DIL_GROUPS):
        sub = s // dil

        def to_sub(t):
            t = t[:, :, gi * DIL_HEADS:(gi + 1) * DIL_HEADS]
            return t.reshape(b, sub, dil, DIL_HEADS, hd).transpose(0, 2, 3, 1, 4).reshape(
                b * dil, DIL_HEADS, sub, hd)

        o, lse = banded_attention(to_sub(q)[:, :, None], to_sub(k), to_sub(v),
                                  slopes_all[gi][:, None], window // dil, dil)
        outs.append(o[:, :, 0].reshape(b, dil, DIL_HEADS, sub, hd).transpose(0, 3, 1, 2, 4).reshape(
            b, s, DIL_HEADS, hd))
        lses.append(lse[:, :, 0].reshape(b, dil, DIL_HEADS, sub).transpose(0, 3, 1, 2).reshape(
            b, s, DIL_HEADS))
    w = jax.nn.softmax(jnp.stack(lses, axis=0), axis=0)
    o = jnp.einsum('gbsh,gbshd->bshd', w, jnp.stack(outs, axis=0).astype(jnp.float32))
    return o.astype(q.dtype)


def causal_depthwise_conv(x, w, bias):
    y = lax.conv_general_dilated(x, w[:, None, :], window_strides=(1,),
                                 padding=((CONV_WIDTH - 1, 0),),
                                 dimension_numbers=('NWC', 'WIO', 'NWC'),
                                 feature_group_count=x.shape[-1])
    return y + bias


def rg_lru(x, gate_a_w, gate_a_b, gate_x_w, gate_x_b, lam):
    b, s, w = x.shape
    xb = x.reshape(b, s, LRU_BLOCKS, LRU_BLOCK_DIM)
    r = jax.nn.sigmoid((jnp.einsum('bshi,hij->bshj', xb, gate_a_w) + gate_a_b).astype(jnp.float32)).reshape(b, s, w)
    i = jax.nn.sigmoid((jnp.einsum('bshi,hij->bshj', xb, gate_x_w) + gate_x_b).astype(jnp.float32)).reshape(b, s, w)
    log_a = -LRU_C * r * jax.nn.softplus(-lam.astype(jnp.float32))
    a = jnp.exp(log_a)
    mult = jnp.sqrt(-jnp.expm1(2.0 * log_a))
    mult = jnp.where((jnp.arange(s) == 0)[None, :, None], 1.0, mult)
    u = x.astype(jnp.float32) * i * mult

    def combine(c1, c2):
        a1, b1 = c1
        a2, b2 = c2
        return a1 * a2, a2 * b1 + b2

    _, h = lax.associative_scan(combine, (a, u), axis=1)
    return h.astype(x.dtype)


def memory_attention(q, mem_n, w_mem_kv):
    b, s, _ = q.shape
    k, v = jnp.split(mem_n @ w_mem_kv, 2, axis=-1)
    q = q.reshape(b, s, MEM_HEADS, MEM_HEAD_DIM)
    k = k.reshape(b, -1, MEM_HEADS, MEM_HEAD_DIM)
    v = v.reshape(b, -1, MEM_HEADS, MEM_HEAD_DIM)
    sc = jnp.einsum('bshd,bmhd->bhsm', q, k, preferred_element_type=jnp.float32) * MEM_HEAD_DIM ** -0.5
    p = jax.nn.softmax(sc, axis=-1)
    o = jnp.einsum('bhsm,bmhd->bshd', p.astype(v.dtype), v)
    return o.reshape(b, s, MEM_WIDTH)


def hawk_dilated_layer(x, mem, norm_g, w_in, conv_w, conv_b, ga_w, ga_b, gx_w, gx_b, lam,
                       mem_norm_g, w_mem_kv, w_out):
    b, s, _ = x.shape
    h = rms_norm(x, norm_g) @ w_in
    xa, za, q, k, v, zb, qm, zm = split_cols(h, HAWK_IN_SPLITS)
    ya = rg_lru(causal_depthwise_conv(xa, conv_w, conv_b), ga_w, ga_b, gx_w, gx_b, lam)
    nh = len(DIL_GROUPS) * DIL_HEADS
    shp = (b, s, nh, DIL_HEAD_DIM)
    yb = dilated_attention(q.reshape(shp), k.reshape(shp), v.reshape(shp)).reshape(b, s, DIL_WIDTH)
    ym = memory_attention(qm, rms_norm(mem, mem_norm_g), w_mem_kv)
    y = jnp.concatenate([ya * jax.nn.silu(za), yb * jax.nn.silu(zb), ym * jax.nn.silu(zm)], axis=-1)
    return x + (y.astype(x.dtype) @ w_out)


def compress_tokens(t, pe, w1, w2):
    s, hd = t.shape[2], t.shape[3]
    n_cmp = (s - CMP_BLOCK) // CMP_STRIDE + 1
    idx = np.arange(n_cmp)[:, None] * CMP_STRIDE + np.arange(CMP_BLOCK)[None, :]
    blocks = t[:, :, idx] + pe
    flat = blocks.reshape(blocks.shape[0], blocks.shape[1], n_cmp, CMP_BLOCK * hd)
    return jax.nn.silu(flat @ w1) @ w2


def compressed_attention(q, k, v, slopes):
    s, hd = q.shape[3], q.shape[4]
    n_cmp = k.shape[2]
    block_end = np.arange(n_cmp) * CMP_STRIDE + CMP_BLOCK - 1
    dist = (np.arange(s)[:, None] - block_end[None, :]).astype(np.float32)
    visible = dist >= 0
    bias = -jnp.asarray(slopes)[:, :, None, None] * jnp.asarray(dist)
    sc = jnp.einsum('bgrsd,bgnd->bgrsn', q, k, preferred_element_type=jnp.float32) * hd ** -0.5 + bias
    sc = jnp.where(visible, sc, NEG_INF)
    p = jax.nn.softmax(sc, axis=-1) * visible.any(axis=-1, keepdims=True).astype(np.float32)
    o = jnp.einsum('bgrsn,bgnd->bgrsd', p.astype(v.dtype), v)
    return o, p


def selected_attention(q, k, v, p_cmp, slopes):
    b, g, r, s, hd = q.shape
    n_slc = s // SLC_BLOCK
    top_n = min(SLC_TOP_N, n_slc)
    n_cmp = p_cmp.shape[-1]
    c_start = np.arange(n_cmp)[:, None] * CMP_STRIDE
    s_start = np.arange(n_slc)[None, :] * SLC_BLOCK
    overlap = ((c_start < s_start + SLC_BLOCK) & (c_start + CMP_BLOCK > s_start)).astype(np.float32)
    imp = jnp.einsum('bgrsn,nj->bgsj', p_cmp, overlap)
    cur = (np.arange(s) // SLC_BLOCK)[:, None]
    j = np.arange(n_slc)[None, :]
    forced = (j == 0) | (j == cur) | (j == cur - 1)
    imp = jnp.where(forced, SEL_FORCE, jnp.where(j > cur, -SEL_FORCE, imp))
    _, sel = lax.top_k(imp, top_n)
    nqb = s // ATTN_BLOCK
    k_blocks = k.reshape(b, g, n_slc, SLC_BLOCK, hd)
    v_blocks = v.reshape(b, g, n_slc, SLC_BLOCK, hd)
    q_blk = q.reshape(b, g, r, nqb, ATTN_BLOCK, hd).transpose(3, 0, 1, 2, 4, 5)
    sel_blk = sel.reshape(b, g, nqb, ATTN_BLOCK, top_n).transpose(2, 0, 1, 3, 4)
    qpos_blk = jnp.arange(s, dtype=jnp.int32).reshape(nqb, ATTN_BLOCK)
    bi = jnp.arange(b)[:, None, None]
    gi = jnp.arange(g)[None, :, None]
    slope_b = jnp.asarray(slopes)[None, :, :, None, None]
    n_keys = top_n * SLC_BLOCK

    def one_block(args):
        qb, ib, qpos = args
        flat = ib.reshape(b, g, ATTN_BLOCK * top_n)
        kg = k_blocks[bi, gi, flat].reshape(b, g, ATTN_BLOCK, n_keys, hd)
        vg = v_blocks[bi, gi, flat].reshape(b, g, ATTN_BLOCK, n_keys, hd)
        kpos = (ib[..., None] * SLC_BLOCK + jnp.arange(SLC_BLOCK)).reshape(b, g, ATTN_BLOCK, n_keys)
        dist = (qpos[None, None, :, None] - kpos)[:, :, None]
        sc = jnp.einsum('bgrqd,bgqkd->bgrqk', qb, kg, preferred_element_type=jnp.float32) * hd ** -0.5
        sc = jnp.where(dist >= 0, sc - slope_b * dist.astype(jnp.float32), NEG_INF)
        p = jax.nn.softmax(sc, axis=-1)
        return jnp.einsum('bgrqk,bgqkd->bgrqd', p.astype(vg.dtype), vg)

    o = lax.map(one_block, (q_blk, sel_blk, qpos_blk))
    return o.transpose(1, 2, 3, 0, 4, 5).reshape(b, g, r, s, hd)


def nsa_layer(x, mem, norm_g, w_in, pe_k, pe_v, phik_w1, phik_w2, phiv_w1, phiv_w2,
              mem_norm_g, w_mem_kv, w_out):
    b, s, _ = x.shape
    g, r, hd = NSA_KV_GROUPS, NSA_HEADS_PER_GROUP, NSA_HEAD_DIM
    h = rms_norm(x, norm_g) @ w_in
    q, kv, gate_logits, z, qm, zm = split_cols(h, NSA_IN_SPLITS)
    q = q.reshape(b, s, g, r, hd).transpose(0, 2, 3, 1, 4)
    kc, vc, ks, vs, kw, vw = [t.reshape(b, s, g, hd).transpose(0, 2, 1, 3)
                              for t in jnp.split(kv, 6, axis=-1)]
    slopes = alibi_slopes(NSA_HEADS).reshape(g, r)
    k_cmp = compress_tokens(kc, pe_k, phik_w1, phik_w2)
    v_cmp = compress_tokens(vc, pe_v, phiv_w1, phiv_w2)
    o_cmp, p_cmp = compressed_attention(q, k_cmp, v_cmp, slopes)
    o_slc = selected_attention(q, ks, vs, p_cmp, slopes)
    o_win, _ = banded_attention(q, kw, vw, slopes, WIN_SIZE - 1, 1)
    gates = jax.nn.sigmoid(gate_logits.astype(jnp.float32)).reshape(b, s, g, r, 3).transpose(0, 2, 3, 1, 4)
    o = gates[..., 0:1] * o_cmp + gates[..., 1:2] * o_slc + gates[..., 2:3] * o_win
    o = o.transpose(0, 3, 1, 2, 4).reshape(b, s, NSA_WIDTH).astype(x.dtype)
    ym = memory_attention(qm, rms_norm(mem, mem_norm_g), w_mem_kv)
    y = jnp.concatenate([o * jax.nn.silu(z), ym * jax.nn.silu(zm)], axis=-1)
    return x + (y.astype(x.dtype) @ w_out)


def setup_inputs(seed: int = 0) -> dict:
    key = jax.random.key(seed)
    keys = iter(jax.random.split(key, 40))

    def nrm(shape, scale):
        return jax.random.normal(next(keys), shape, jnp.float32) * scale

    def gain(shape):
        return 1.0 + 0.02 * jax.random.normal(next(keys), shape, jnp.float32)

    x = nrm((BATCH, SEQ, D_MODEL), 1.0)
    mem = nrm((BATCH, N_MEM, D_MODEL), 1.0)
    u = jax.random.uniform(next(keys), (N_EVEN, LRU_WIDTH), jnp.float32, LRU_A_MIN, LRU_A_MAX)
    a_base = u ** (1.0 / LRU_C)
    hawk_lambda = jnp.log(a_base) - jnp.log1p(-a_base)
    phi_in = CMP_BLOCK * NSA_HEAD_DIM
    return {
        "x": x,
        "mem": mem,
        "hawk_norm": gain((N_EVEN, D_MODEL)),
        "hawk_w_in": nrm((N_EVEN, D_MODEL, HAWK_IN), D_MODEL ** -0.5),
        "hawk_conv_w": nrm((N_EVEN, CONV_WIDTH, LRU_WIDTH), CONV_WIDTH ** -0.5),
        "hawk_conv_b": nrm((N_EVEN, LRU_WIDTH), 0.02),
        "hawk_gate_a_w": nrm((N_EVEN, LRU_BLOCKS, LRU_BLOCK_DIM, LRU_BLOCK_DIM), LRU_BLOCK_DIM ** -0.5),
        "hawk_gate_a_b": nrm((N_EVEN, LRU_BLOCKS, LRU_BLOCK_DIM), 0.02),
        "hawk_gate_x_w": nrm((N_EVEN, LRU_BLOCKS, LRU_BLOCK_DIM, LRU_BLOCK_DIM), LRU_BLOCK_DIM ** -0.5),
        "hawk_gate_x_b": nrm((N_EVEN, LRU_BLOCKS, LRU_BLOCK_DIM), 0.02),
        "hawk_lambda": hawk_lambda,
        "hawk_mem_norm": gain((N_EVEN, D_MODEL)),
        "hawk_w_mem_kv": nrm((N_EVEN, D_MODEL, 2 * MEM_WIDTH), D_MODEL ** -0.5),
        "hawk_w_out": nrm((N_EVEN, HAWK_OUT, D_MODEL), HAWK_OUT ** -0.5),
        "nsa_norm": gain((N_ODD, D_MODEL)),
        "nsa_w_in": nrm((N_ODD, D_MODEL, NSA_IN), D_MODEL ** -0.5),
        "nsa_pe_k": nrm((N_ODD, CMP_BLOCK, NSA_HEAD_DIM), 0.1),
        "nsa_pe_v": nrm((N_ODD, CMP_BLOCK, NSA_HEAD_DIM), 0.1),
        "nsa_phi_k_w1": nrm((N_ODD, phi_in, PHI_HIDDEN), phi_in ** -0.5),
        "nsa_phi_k_w2": nrm((N_ODD, PHI_HIDDEN, NSA_HEAD_DIM), PHI_HIDDEN ** -0.5),
        "nsa_phi_v_w1": nrm((N_ODD, phi_in, PHI_HIDDEN), phi_in ** -0.5),
        "nsa_phi_v_w2": nrm((N_ODD, PHI_HIDDEN, NSA_HEAD_DIM), PHI_HIDDEN ** -0.5),
        "nsa_mem_norm": gain((N_ODD, D_MODEL)),
        "nsa_w_mem_kv": nrm((N_ODD, D_MODEL, 2 * MEM_WIDTH), D_MODEL ** -0.5),
        "nsa_w_out": nrm((N_ODD, NSA_OUT, D_MODEL), NSA_OUT ** -0.5),
        "final_norm": gain((D_MODEL,)),
    }


def reference(x, mem, hawk_norm, hawk_w_in, hawk_conv_w, hawk_conv_b, hawk_gate_a_w, hawk_gate_a_b,
              hawk_gate_x_w, hawk_gate_x_b, hawk_lambda, hawk_mem_norm, hawk_w_mem_kv, hawk_w_out,
              nsa_norm, nsa_w_in, nsa_pe_k, nsa_pe_v, nsa_phi_k_w1, nsa_phi_k_w2, nsa_phi_v_w1,
              nsa_phi_v_w2, nsa_mem_norm, nsa_w_mem_kv, nsa_w_out, final_norm):
    for layer in range(DEPTH):
        i = layer // 2
        if layer % 2 == 0:
            x = hawk_dilated_layer(x, mem, hawk_norm[i], hawk_w_in[i], hawk_conv_w[i], hawk_conv_b[i],
                                   hawk_gate_a_w[i], hawk_gate_a_b[i], hawk_gate_x_w[i], hawk_gate_x_b[i],
                                   hawk_lambda[i], hawk_mem_norm[i], hawk_w_mem_kv[i], hawk_w_out[i])
        else:
            x = nsa_layer(x, mem, nsa_norm[i], nsa_w_in[i], nsa_pe_k[i], nsa_pe_v[i], nsa_phi_k_w1[i],
                          nsa_phi_k_w2[i], nsa_phi_v_w1[i], nsa_phi_v_w2[i], nsa_mem_norm[i],
                          nsa_w_mem_kv[i], nsa_w_out[i])
    return rms_norm(x, final_norm)
```

```python
import math
from contextlib import ExitStack

import numpy as np
import concourse.bass as bass
import concourse.mybir as mybir
from concourse.bass_utils import run_bass_kernel_spmd

F32 = mybir.dt.float32
BF16 = mybir.dt.bfloat16
AF = mybir.ActivationFunctionType
ALU = mybir.AluOpType
AX = mybir.AxisListType

S_LEN = 2048
D = 1024
NT_ = 16
EPS = 1e-6
DIL_GROUPS = ((128, 1), (512, 4), (2048, 16))

SEM_LIMIT = 30000
N_DMA_SEMS = 24
SAME_ENGINE_SYNC = True


class Buf:
    def __init__(self, name, t, excl=False):
        self.name = name
        self.t = t
        self.excl = excl
        self.st = {}

    def __getitem__(self, idx):
        return self.t[idx]


class Sync:
    def __init__(self, nc, stack):
        self.nc = nc
        self.stack = stack
        self.engs = ["pe", "act", "dve", "pool", "sp"]
        self.ops = {e: [] for e in self.engs}
        self.cur_sem = {}
        self.cnt = {}
        self.nsem = 0
        for e in self.engs:
            self._new_sem(e)
        self.dma_sems = {}
        self.dma_val = {}
        self.dma_rr = {}
        for e in ["sp", "pool", "act"]:
            self.dma_sems[e] = [self._alloc_sem(f"d{e}{i}") for i in range(N_DMA_SEMS)]
            self.dma_val[e] = [0] * N_DMA_SEMS
            self.dma_rr[e] = 0
        self.seen = {e: {} for e in self.engs}
        self.all_ticks = {}
        self.nops = 0

    def _alloc_sem(self, name):
        self.nsem += 1
        return self.stack.enter_context(self.nc.semaphore(f"s_{name}_{self.nsem}"))

    def _new_sem(self, e):
        self.cur_sem[e] = self._alloc_sem(e)
        self.cnt[e] = 0

    def _states(self, buf, key, create):
        if key is None:
            if create and None not in buf.st:
                buf.st[None] = [None, {}]
            return list(buf.st.values())
        out = []
        if None in buf.st:
            out.append(buf.st[None])
        if key not in buf.st and create:
            buf.st[key] = [None, {}]
        if key in buf.st:
            out.append(buf.st[key])
        return out

    @staticmethod
    def _norm(lst):
        out = []
        for r in lst or []:
            out.append(r if isinstance(r, tuple) else (r, None))
        return out

    def op(self, eng, fn, reads=None, writes=None, dma=False):
        reads = self._norm(reads)
        writes = self._norm(writes)
        ex = [(b, None) for (b, k) in reads + writes if b.excl]
        if ex:
            reads = [(b, k) for (b, k) in reads if not b.excl]
            writes = [(b, k) for (b, k) in writes if not b.excl]
            for bk in ex:
                if bk not in writes:
                    writes.append(bk)
        need = []
        for buf, key in reads:
            for st in self._states(buf, key, False):
                if st[0] is not None:
                    need.append(st[0])
        for buf, key in writes:
            for st in self._states(buf, key, False):
                if st[0] is not None:
                    need.append(st[0])
                need.extend(st[1].values())
        if dma:
            i = self.dma_rr[eng]
            self.dma_rr[eng] = (i + 1) % N_DMA_SEMS
            sem = self.dma_sems[eng][i]
            prev = self.dma_val[eng][i]
            if prev > 0:
                need.append((sem, prev, "dma"))
            if prev + 16 > SEM_LIMIT:
                sem = self._alloc_sem(f"d{eng}{i}")
                self.dma_sems[eng][i] = sem
                prev = 0
            val = prev + 16
            self.dma_val[eng][i] = val
            inc = 16
            tick = (sem, val, "dma")
        else:
            if self.cnt[eng] + 1 > SEM_LIMIT:
                self._new_sem(eng)
            self.cnt[eng] += 1
            sem = self.cur_sem[eng]
            val = self.cnt[eng]
            inc = 1
            tick = (sem, val, eng)
        waits = {}
        seen = self.seen[eng]
        for (s, v, src) in need:
            if src == eng and (eng == "pe" or not SAME_ENGINE_SYNC):
                continue
            sid = id(s)
            if seen.get(sid, 0) >= v:
                continue
            if sid not in waits or waits[sid][1] < v:
                waits[sid] = (s, v)
        for sid, (s, v) in waits.items():
            seen[sid] = v
        self.ops[eng].append((list(waits.values()), fn, sem, inc))
        self.all_ticks[id(sem)] = (sem, val)
        self.nops += 1
        wset = set((id(b), k) for b, k in writes)
        for buf, key in reads:
            if (id(buf), key) in wset:
                continue
            self._states(buf, key, True)
            buf.st[key][1][eng if not dma else ("dma", id(sem))] = tick
        for buf, key in writes:
            if key is None:
                buf.st = {None: [tick, {}]}
            else:
                buf.st[key] = [tick, {}]
        return tick

    def barrier(self):
        ticks = list(self.all_ticks.values())
        for e in self.engs:
            wl = []
            for (s, v) in ticks:
                if self.seen[e].get(id(s), 0) < v:
                    wl.append((s, v))
                    self.seen[e][id(s)] = v
            if wl:
                self.ops[e].append((wl, None, None, 0))

    def emit(self, block):
        S = self

        def run(engname, e):
            for (wl, fn, sem, inc) in S.ops[engname]:
                for (s, v) in wl:
                    e.wait_ge(s, v)
                if fn is not None:
                    fn(e).then_inc(sem, inc)

        @block.sync
        def _(e):
            run("sp", e)

        @block.tensor
        def _(e):
            run("pe", e)

        @block.scalar
        def _(e):
            run("act", e)

        @block.vector
        def _(e):
            run("dve", e)

        @block.gpsimd
        def _(e):
            run("pool", e)


class KB:
    def __init__(self, nc, stack):
        self.nc = nc
        self.gst = stack
        self.S = Sync(nc, stack)
        self.banks = [Buf(f"bank{i}", stack.enter_context(nc.psum_tensor(f"bank{i}", [128, 512], F32)), excl=True)
                      for i in range(8)]
        self.bank_rr = 0
        self.uid = 0

    def sb(self, name, shape, dt, stack=None):
        self.uid += 1
        t = (stack or self.gst).enter_context(self.nc.sbuf_tensor(f"{name}_{self.uid}", shape, dt))
        return Buf(name, t)

    def bank(self):
        b = self.banks[self.bank_rr]
        self.bank_rr = (self.bank_rr + 1) % 8
        return b

    def mm(self, out, lhsT, rhs, start, stop, r, w):
        self.S.op("pe", lambda e: e.matmul(out, lhsT=lhsT, rhs=rhs, start=start, stop=stop), reads=r, writes=w)

    def tr(self, out, in_, ident, r, w):
        self.S.op("pe", lambda e: e.transpose(out, in_, ident), reads=r, writes=w)

    def act(self, out, in_, func, r, w, **kw):
        self.S.op("act", lambda e: e.activation(out=out, in_=in_, func=func, **kw), reads=r, writes=w)

    def tt(self, eng, out, in0, in1, op, r, w):
        self.S.op(eng, lambda e: e.tensor_tensor(out=out, in0=in0, in1=in1, op=op), reads=r, writes=w)

    def ts(self, eng, out, in0, s1, s2, op0, op1, r, w, **kw):
        if op1 is None:
            self.S.op(eng, lambda e: e.tensor_scalar(out=out, in0=in0, scalar1=s1, scalar2=None, op0=op0, **kw), reads=r, writes=w)
        else:
            self.S.op(eng, lambda e: e.tensor_scalar(out=out, in0=in0, scalar1=s1, scalar2=s2, op0=op0, op1=op1, **kw), reads=r, writes=w)

    def stt(self, out, in0, scalar, in1, op0, op1, r, w, **kw):
        self.S.op("dve", lambda e: e.scalar_tensor_tensor(out=out, in0=in0, scalar=scalar, in1=in1, op0=op0, op1=op1, **kw), reads=r, writes=w)

    def cp(self, eng, out, in_, r, w):
        if eng == "act":
            self.S.op("act", lambda e: e.activation(out=out, in_=in_, func=AF.Copy), reads=r, writes=w)
        else:
            self.S.op(eng, lambda e: e.tensor_copy(out=out, in_=in_), reads=r, writes=w)

    def memset(self, eng, ap, val, w):
        self.S.op(eng, lambda e: e.memset(ap, val), writes=w)

    def recip(self, out, in_, r, w):
        self.S.op("dve", lambda e: e.reciprocal(out=out, in_=in_), reads=r, writes=w)

    def dma(self, q, out, in_, r, w):
        self.S.op(q, lambda e: e.dma_start(out=out, in_=in_), reads=r, writes=w, dma=True)


def alibi_slopes(n):
    return np.exp2(-8.0 * np.arange(1, n + 1) / n).astype(np.float32)


def host_constants():
    c = {}
    c["identf"] = np.eye(128, dtype=np.float32)
    sl = alibi_slopes(12)
    ik = np.arange(128)[:, None].astype(np.float64)
    iq = np.arange(128)[None, :].astype(np.float64)
    E = np.zeros((128, 12, 256), np.float32)
    for g, (win, dil) in enumerate(DIL_GROUPS):
        for hs in range(4):
            hh = g * 4 + hs
            s = float(sl[hh]) * dil
            dist_prev = 128 + iq - ik
            ok_prev = (dist_prev <= 128)
            E[:, hh, 0:128] = np.where(ok_prev, np.exp(-s * dist_prev), 0.0)
            dist_cur = iq - ik
            ok_cur = dist_cur >= 0
            E[:, hh, 128:256] = np.where(ok_cur, np.exp(-s * dist_cur), 0.0)
    c["edil"] = E
    return c


def expand_gain(g):
    return np.ascontiguousarray(np.broadcast_to(g.reshape(8, 128).T[:, :, None], (128, 8, 128))).astype(np.float32)


def vec_fm(v):
    return np.ascontiguousarray(v.reshape(8, 128).T).astype(np.float32)


def block_diag(gw):
    out = np.zeros((128, 8, 128), np.float32)
    for c in range(8):
        out[0:64, c, 0:64] = gw[2 * c]
        out[64:128, c, 64:128] = gw[2 * c + 1]
    return out


NEGB = 8192.0


def _bf16_split3(a):
    import ml_dtypes
    a = a.astype(np.float32)
    hi = a.astype(ml_dtypes.bfloat16).astype(np.float32)
    r1 = (a - hi).astype(np.float32)
    mid = r1.astype(ml_dtypes.bfloat16).astype(np.float32)
    r2 = (r1 - mid).astype(np.float32)
    lo = r2.astype(ml_dtypes.bfloat16).astype(np.float32)
    return hi, mid, lo


def host_constants_nsa():
    c = {}
    sl = alibi_slopes(16)
    i = np.arange(128)[:, None].astype(np.float64)
    m = np.arange(247)[None, :].astype(np.float64)
    dist = i - 16.0 * (m - 120.0) - 31.0
    E = np.zeros((128, 16, 247), np.float32)
    for h in range(16):
        E[:, h, :] = np.where(dist >= 0, np.exp(-float(sl[h]) * np.maximum(dist, 0.0)), 0.0)
    c["ecmp"] = E
    ii = np.arange(128)[:, None]
    rel = np.arange(62)[None, :] - 30
    cur = (ii >= 64).astype(np.int64)
    forced = (rel == cur) | (rel == cur - 1)
    future = rel > cur
    m1 = np.where(forced | future, 0.0, 1.0).astype(np.float32)
    m2 = np.where(forced, 1e6, np.where(future, -1e6, 0.0)).astype(np.float32)
    c["m12"] = np.ascontiguousarray(np.stack([m1, m2], axis=1))
    ik = np.arange(128)[:, None]
    iq = np.arange(128)[None, :]
    diag = np.where(ik > iq, -NEGB, 0.0).astype(np.float32)
    far = np.where(ik <= iq, -NEGB, 0.0).astype(np.float32)
    c["tri"] = np.ascontiguousarray(np.stack([np.tile(diag, (1, 4)), np.tile(far, (1, 4))], axis=1))
    k = np.arange(2048)
    kp = k - 1024
    hi = (np.floor(kp / 128.0) * 128.0).astype(np.float32)
    lo = (kp - hi).astype(np.float32)
    ka = np.zeros((2, 41, 2048), np.float32)
    for j in range(32):
        ka[0, j, :] = (k // 64 == j).astype(np.float32)
    for v in range(2):
        ka[v, 32:35, :] = 1.0
        ka[v, 35:38, :] = lo[None, :]
        ka[v, 38:41, :] = hi[None, :]
    c["kaug"] = ka
    qa = np.zeros((16, 9, 2048), np.float32)
    qp = (np.arange(2048) - 1024).astype(np.float32)
    for h in range(16):
        s8 = np.float32(8.0) * np.float32(sl[h])
        a = (-s8 * qp).astype(np.float32)
        ah, am, al = _bf16_split3(a)
        sh, sm, sl_ = _bf16_split3(np.full((2048,), s8, np.float32))
        qa[h, 0], qa[h, 1], qa[h, 2] = ah, am, al
        qa[h, 3], qa[h, 4], qa[h, 5] = sh, sm, sl_
        qa[h, 6], qa[h, 7], qa[h, 8] = sh, sm, sl_
    c["qal"] = qa
    return c


def build_program(stop_after=None):
    nc = bass.Bass("TRN2", target_bir_lowering=False)

    def din(name, shape):
        return nc.dram_tensor(name, list(shape), F32, kind="ExternalInput").ap()

    x_d = din("x", [S_LEN, D])
    mem_d = din("mem", [256, D])
    hawk_w_in = din("hawk_w_in", [D, 7680])
    hawk_w_out = din("hawk_w_out", [1792, D])
    hawk_w_mem_kv = din("hawk_w_mem_kv", [D, 512])
    g_hawk = din("g_hawk", [128, 8, 128])
    g_hawk_mem = din("g_hawk_mem", [128, 8, 128])
    lru_vec = din("lru_vec", [128, 8, 8])
    bd_a = din("bd_a", [128, 8, 128])
    bd_x = din("bd_x", [128, 8, 128])
    identf_d = din("identf", [128, 128])
    edil_d = din("edil", [128, 12, 256])
    final_g = din("final_norm", [D])
    nsa_w_in = din("nsa_w_in", [D, 3376])
    nsa_w_out = din("nsa_w_out", [1280, D])
    nsa_w_mem_kv = din("nsa_w_mem_kv", [D, 512])
    g_nsa = din("g_nsa", [128, 8, 128])
    g_nsa_mem = din("g_nsa_mem", [128, 8, 128])
    w1k_d = din("w1k", [128, 32, 256])
    w1v_d = din("w1v", [128, 32, 256])
    w2k_d = din("w2k", [256, 64])
    w2v_d = din("w2v", [256, 64])
    peT_d = din("peT", [64, 2, 32])
    ecmp_d = din("ecmp", [128, 16, 247])
    m12_d = din("m12", [128, 2, 62])
    tri_d = din("tri", [128, 2, 512])
    kaug_d = din("kaug", [2, 41, 2048])
    qal_d = din("qal", [16, 9, 2048])
    out_d = nc.dram_tensor("out", [S_LEN, D], F32, kind="ExternalOutput").ap()
    x1_scr = nc.dram_tensor("x1_scr", [S_LEN, D], F32, kind="Internal").ap()

    with ExitStack() as gst:
        kb = KB(nc, gst)
        S = kb.S
        DX = Buf("x_dram", None)
        DX1 = Buf("x1_dram", None)
        DOUT = Buf("out_dram", None)

        xnT = kb.sb("xnT", [128, 8, S_LEN], BF16)
        memnT = kb.sb("memnT", [128, 8, 256], BF16)
        identf = kb.sb("identf", [128, 128], F32)
        identb = kb.sb("identb", [128, 128], BF16)
        onesb = kb.sb("onesb", [128, 128], BF16)
        wstage = [kb.sb(f"wstage{i}", [128, 1024], F32) for i in range(3)]
        ws_rr = [0]
        stat = kb.sb("stat", [128, 64], F32)
        stat_rr = [0]

        kb.dma("sp", identf[:], identf_d, [], [identf])
        kb.cp("dve", identb[:], identf[:], [identf], [identb])
        kb.memset("dve", onesb[:], 1.0, [onesb])

        def next_ws():
            b = wstage[ws_rr[0]]
            ws_rr[0] = (ws_rr[0] + 1) % len(wstage)
            return b

        def load_w(dst, dst_ap3, src_ap3, n, gain=None, key=None, q="sp", part=128, eng="pool"):
            dcs = dst_ap3.shape[1]
            assert dcs * n <= 1024
            stg = next_ws()
            sv = stg[0:part, 0:dcs * n].rearrange("p (c n) -> p c n", c=dcs)
            kb.dma(q, sv, src_ap3, [], [stg])
            if gain is not None:
                kb.tt(eng, dst_ap3, sv, gain[0:part, 0:dcs, 0:n], ALU.mult, [stg, gain], [(dst, key)])
            else:
                kb.cp(eng, dst_ap3, sv, [stg], [(dst, key)])

        def win_cols(w_dram, c0, n):
            return w_dram.rearrange("(dc p) n -> p dc n", p=128)[:, :, c0:c0 + n]

        class NormCtx:
            def __init__(self, stack):
                self.xstage = [kb.sb(f"xstage{i}", [128, 1024], F32, stack) for i in range(2)]
                self.xnb = [kb.sb(f"xnb{i}", [128, 1024], BF16, stack) for i in range(2)]
                self.junk = kb.sb("junk", [128, 1024], BF16, stack)

        def tile_rstd(ncx, xbuf, xap):
            i = stat_rr[0]
            stat_rr[0] = (stat_rr[0] + 1) % 32
            ss = stat[:, 2 * i:2 * i + 1]
            rs = stat[:, 2 * i + 1:2 * i + 2]
            kb.stt(ncx.junk[:], xap, 1.0, xap, ALU.mult, ALU.mult, [xbuf], [ncx.junk, (stat, i)], accum_out=ss)
            kb.ts("dve", ss, ss, 1.0 / D, EPS, ALU.mult, ALU.add, [(stat, i)], [(stat, i)])
            kb.act(ss, ss, AF.Sqrt, [(stat, i)], [(stat, i)])
            kb.recip(rs, ss, [(stat, i)], [(stat, i)])
            return rs, i

        def norm_to_T(ncx, xbuf, xap, dstT, t, ntok_off):
            rs, i = tile_rstd(ncx, xbuf, xap)
            nb = ncx.xnb[t % 2]
            kb.ts("dve", nb[:], xap, rs, None, ALU.mult, None, [xbuf, (stat, i)], [nb])
            bk = kb.bank()
            bv = bk[:].bitcast(BF16)
            for c in range(8):
                kb.tr(bv[:, c * 128:(c + 1) * 128], nb[:, c * 128:(c + 1) * 128], identb[:], [nb, identb], [bk])
            kb.cp("act", dstT[:, :, ntok_off:ntok_off + 128], bv.rearrange("p (c n) -> p c n", c=8), [bk], [(dstT, t)])

        with ExitStack() as pa:
            ncx = NormCtx(pa)
            for t in range(NT_):
                xs = ncx.xstage[t % 2]
                kb.dma("sp", xs[:], x_d[t * 128:(t + 1) * 128, :], [DX], [xs])
                norm_to_T(ncx, xs, xs[:], xnT, t, t * 128)
            for t in range(2):
                xs = ncx.xstage[t % 2]
                kb.dma("sp", xs[:], mem_d[t * 128:(t + 1) * 128, :], [], [xs])
                norm_to_T(ncx, xs, xs[:], memnT, t, t * 128)
            S.barrier()

        def make_loader(w_in_d, gain, nslots, stack):
            wslots = [kb.sb(f"wslot{i}", [128, 8, 128], BF16, stack) for i in range(nslots)]
            rr = [0]

            def load_win(c0, n=128, q="sp", into=None, off=0):
                if into is None:
                    wsl = wslots[rr[0]]
                    rr[0] = (rr[0] + 1) % nslots
                else:
                    wsl = into
                load_w(wsl, wsl[:, :, off:off + n], win_cols(w_in_d, c0, n), n, gain=gain, q=q, key=off)
                return wsl
            return load_win

        def proj_fm(wsl, n, evac, woff=0):
            for tc in range(4):
                bk = kb.bank()
                for dc in range(8):
                    kb.mm(bk[0:n, :], wsl[:, dc, woff:woff + n], xnT[:, dc, tc * 512:(tc + 1) * 512], dc == 0, dc == 7,
                          [wsl, xnT], [bk])
                evac(bk, bk[0:n, :], tc)

        def mem_kv(w_kv_d, gain, kmT, vm, stack):
            wkv = kb.sb("wkv", [128, 8, 512], BF16, stack)
            for j in range(4):
                load_w(wkv, wkv[:, :, j * 128:(j + 1) * 128], win_cols(w_kv_d, j * 128, 128), 128, gain=gain, key=j)
            for h in range(4):
                bk = kb.bank()
                for dc in range(8):
                    kb.mm(bk[0:64, 0:256], wkv[:, dc, h * 64:(h + 1) * 64], memnT[:, dc, :], dc == 0, dc == 7,
                          [wkv, memnT], [bk])
                kb.cp("act", kmT[0:64, h, :], bk[0:64, 0:256], [bk], [(kmT, h)])
            for mt in range(2):
                bk = kb.bank()
                for dc in range(8):
                    kb.mm(bk[:, 0:256], memnT[:, dc, mt * 128:(mt + 1) * 128], wkv[:, dc, 256:512], dc == 0, dc == 7,
                          [wkv, memnT], [bk])
                kb.cp("act", vm[:, mt, :], bk[:, 0:256], [bk], [(vm, mt)])

        def mem_attn(load_win, colq, colz, kmT, vm, ymT, stack):
            qmT = kb.sb("qmT", [64, S_LEN], BF16, stack)
            szm = kb.sb("szm", [64, S_LEN], BF16, stack)
            PTm = [kb.sb(f"PTm{i}", [128, 512], BF16, stack) for i in range(4)]
            rdm = [kb.sb(f"rdm{i}", [64, 512], F32, stack) for i in range(2)]
            ptr = 0
            for h in range(4):
                wq = load_win(colq + h * 64, 64)
                proj_fm(wq, 64, lambda bk, ap, tc: kb.cp("act", qmT[0:64, tc * 512:(tc + 1) * 512], ap, [bk], [(qmT, tc)]))
                wz = load_win(colz + h * 64, 64)
                proj_fm(wz, 64, lambda bk, ap, tc: kb.act(szm[0:64, tc * 512:(tc + 1) * 512], ap, AF.Silu, [bk], [(szm, tc)]))
                for tc in range(4):
                    pts = []
                    for mt in range(2):
                        bk = kb.bank()
                        kb.mm(bk[:, :], kmT[0:64, h, mt * 128:(mt + 1) * 128], qmT[0:64, tc * 512:(tc + 1) * 512],
                              True, True, [kmT, (qmT, tc)], [bk])
                        pt = PTm[ptr % 4]
                        ptr += 1
                        kb.act(pt[:], bk[:], AF.Exp, [bk], [pt], scale=0.125)
                        pts.append(pt)
                    nb = kb.bank()
                    db = kb.bank()
                    for mt in range(2):
                        kb.mm(nb[0:64, :], vm[:, mt, h * 64:(h + 1) * 64], pts[mt][:], mt == 0, mt == 1, [vm, pts[mt]], [nb])
                    for mt in range(2):
                        kb.mm(db[0:64, :], onesb[:, 0:64], pts[mt][:], mt == 0, mt == 1, [onesb, pts[mt]], [db])
                    rd = rdm[tc % 2]
                    kb.recip(rd[:], db[0:64, :], [db], [rd])
                    kb.tt("dve", rd[:], nb[0:64, :], rd[:], ALU.mult, [nb, rd], [rd])
                    kb.tt("dve", ymT[0:64, h, tc * 512:(tc + 1) * 512], rd[:], szm[0:64, tc * 512:(tc + 1) * 512], ALU.mult,
                          [rd, (szm, tc)], [(ymT, (h, tc))])

        def out_proj(w_out_d, nch, yT, ymT, resid_d, resid_buf, final, stack, dbg=False):
            ncx = NormCtx(stack)
            WO = kb.sb("WO", [128, nch, 1024], BF16, stack)
            WOm = kb.sb("WOm", [64, 4, 1024], BF16, stack)
            wo_v = w_out_d[0:nch * 128, :].rearrange("(c p) n -> p c n", p=128)
            for c in range(nch):
                load_w(WO, WO[:, c:c + 1, :], wo_v[:, c:c + 1, :], 1024, key=c)
            wom_v = w_out_d[nch * 128:nch * 128 + 256, :].rearrange("(h p) n -> p h n", p=64)
            for h in range(4):
                load_w(WOm, WOm[0:64, h:h + 1, :], wom_v[:, h:h + 1, :], 1024, key=h, part=64)
            x1t = [kb.sb(f"x1t{i}", [128, 1024], F32, stack) for i in range(2)]
            if final:
                gF = kb.sb("gF", [128, 1024], F32, stack)
                kb.dma("sp", gF[:], final_g.partition_broadcast(128), [], [gF])
                ot = [kb.sb(f"ot{i}", [128, 1024], F32, stack) for i in range(2)]
            for t in range(NT_):
                xs = ncx.xstage[t % 2]
                kb.dma("sp", xs[:], resid_d[t * 128:(t + 1) * 128, :], [resid_buf], [xs])
                x1 = x1t[t % 2]
                for half in range(2):
                    bk = kb.bank()
                    for c in range(nch):
                        kb.mm(bk[:, :], yT[:, c, t * 128:(t + 1) * 128], WO[:, c, half * 512:(half + 1) * 512], c == 0, False,
                              [yT, WO], [bk])
                    for h in range(4):
                        kb.mm(bk[:, :], ymT[0:64, h, t * 128:(t + 1) * 128], WOm[0:64, h, half * 512:(half + 1) * 512], False, h == 3,
                              [ymT, WOm], [bk])
                    kb.tt("dve", x1[:, half * 512:(half + 1) * 512], xs[:, half * 512:(half + 1) * 512], bk[:], ALU.add,
                          [xs, bk], [(x1, half)])
                if not final:
                    kb.dma("sp", x1_scr[t * 128:(t + 1) * 128, :], x1[:], [x1], [DX1])
                    norm_to_T(ncx, x1, x1[:], xnT, t, t * 128)
                    if dbg:
                        kb.dma("sp", out_d[t * 128:(t + 1) * 128, :], x1[:], [x1], [DOUT])
                else:
                    rs, i = tile_rstd(ncx, x1, x1[:])
                    o = ot[t % 2]
                    kb.stt(o[:], x1[:], rs, gF[:], ALU.mult, ALU.mult, [x1, (stat, i), gF], [o])
                    kb.dma("sp", out_d[t * 128:(t + 1) * 128, :], o[:], [o], [DOUT])

        with ExitStack() as l0:
            yT = kb.sb("yT", [128, 12, S_LEN], BF16, l0)
            ymT = kb.sb("ymT", [64, 4, S_LEN], BF16, l0)
            gH = kb.sb("gH", [128, 8, 128], F32, l0)
            kb.dma("sp", gH[:], g_hawk, [], [gH])
            load_win = make_loader(hawk_w_in, gH, 6, l0)
            kmT = kb.sb("kmT", [64, 4, 256], BF16, l0)
            vm = kb.sb("vm", [128, 2, 256], BF16, l0)
            with ExitStack() as pm:
                gHm = kb.sb("gHm", [128, 8, 128], F32, pm)
                kb.dma("sp", gHm[:], g_hawk_mem, [], [gHm])
                mem_kv(hawk_w_mem_kv, gHm, kmT, vm, pm)
                S.barrier()
            with ExitStack() as pd:
                mem_attn(load_win, 7168, 7424, kmT, vm, ymT, pd)
                S.barrier()

            with ExitStack() as pb:
                lv = kb.sb("lv", [128, 8, 8], F32, pb)
                cvec = kb.sb("cvec", [128, 8, 2], F32, pb)
                bda = kb.sb("bda", [128, 8, 128], BF16, pb)
                bdx = kb.sb("bdx", [128, 8, 128], BF16, pb)
                kb.dma("sp", lv[:], lru_vec, [], [lv])
                load_w(bda, bda[:], bd_a, 128)
                load_w(bdx, bdx[:], bd_x, 128)
                kb.act(cvec[:, :, 0], lv[:, :, 7], AF.Exp, [lv], [cvec], scale=-1.0)
                kb.act(cvec[:, :, 0], cvec[:, :, 0], AF.Ln, [cvec], [cvec], bias=1.0)
                kb.ts("dve", cvec[:, :, 1], cvec[:, :, 0], -16.0, None, ALU.mult, None, [cvec], [cvec])
                kb.ts("dve", cvec[:, :, 0], cvec[:, :, 0], -8.0, None, ALU.mult, None, [cvec], [cvec])
                B1 = kb.sb("B1", [128, S_LEN + 4], F32, pb)
                B2 = kb.sb("B2", [128, S_LEN], F32, pb)
                B3 = kb.sb("B3", [128, S_LEN], F32, pb)
                B4 = kb.sb("B4", [128, S_LEN], F32, pb)
                xcb = kb.sb("xcb", [128, S_LEN], BF16, pb)
                sz = kb.sb("sz", [128, S_LEN], BF16, pb)
                for c in range(8):
                    wxa = load_win(c * 128)
                    wza = load_win(1024 + c * 128)
                    kb.memset("dve", B1[:, 0:3], 0.0, [(B1, "pad")])
                    proj_fm(wxa, 128, lambda bk, ap, tc: kb.cp("act", B1[:, 3 + tc * 512:3 + (tc + 1) * 512], ap, [bk], [(B1, tc)]))
                    proj_fm(wza, 128, lambda bk, ap, tc: kb.act(sz[:, tc * 512:(tc + 1) * 512], ap, AF.Silu, [bk], [(sz, tc)]))
                    kb.ts("dve", B2[:], B1[:, 0:S_LEN], lv[:, c, 0:1], lv[:, c, 4:5], ALU.mult, ALU.add, [B1, lv], [B2])
                    for k in range(1, 4):
                        kb.stt(B2[:], B1[:, k:k + S_LEN], lv[:, c, k:k + 1], B2[:], ALU.mult, ALU.add, [B1, lv, B2], [B2])
                    kb.cp("pool", xcb[:], B2[:], [B2], [xcb])
                    for tc in range(4):
                        bk = kb.bank()
                        kb.mm(bk[:, :], bda[:, c, :], xcb[:, tc * 512:(tc + 1) * 512], True, True, [bda, xcb], [bk])
                        kb.act(B1[:, tc * 512:(tc + 1) * 512], bk[:], AF.Sigmoid, [bk, lv], [(B1, tc)], bias=lv[:, c, 5:6])
                    for tc in range(4):
                        bk = kb.bank()
                        kb.mm(bk[:, :], bdx[:, c, :], xcb[:, tc * 512:(tc + 1) * 512], True, True, [bdx, xcb], [bk])
                        kb.act(B4[:, tc * 512:(tc + 1) * 512], bk[:], AF.Sigmoid, [bk, lv], [(B4, tc)], bias=lv[:, c, 6:7])
                    r_ap = B1[:, 0:S_LEN]
                    kb.act(B3[:], r_ap, AF.Exp, [B1, cvec], [B3], scale=cvec[:, c, 0:1])
                    kb.act(r_ap, r_ap, AF.Exp, [B1, cvec], [B1], scale=cvec[:, c, 1:2])
                    kb.ts("dve", r_ap, r_ap, -1.0, 1.0, ALU.mult, ALU.add, [B1], [B1])
                    kb.ts("dve", r_ap, r_ap, 0.0, None, ALU.max, None, [B1], [B1])
                    kb.act(r_ap, r_ap, AF.Sqrt, [B1], [B1])
                    kb.memset("dve", B1[:, 0:1], 1.0, [B1])
                    kb.tt("dve", B2[:], B2[:], B4[:], ALU.mult, [B2, B4], [B2])
                    kb.tt("dve", B2[:], B2[:], r_ap, ALU.mult, [B2, B1], [B2])
                    S.op("dve", lambda e: e.tensor_tensor_scan(out=B4[:], data0=B3[:], data1=B2[:], initial=0.0,
                                                               op0=ALU.mult, op1=ALU.add), reads=[B3, B2], writes=[B4])
                    kb.tt("dve", yT[:, c, :], B4[:], sz[:], ALU.mult, [B4, sz], [(yT, c)])
                S.barrier()

            with ExitStack() as pc:
                edil = kb.sb("edil", [128, 12, 256], F32, pc)
                kb.dma("sp", edil[:], edil_d, [], [edil])
                qT = kb.sb("qT", [128, S_LEN], BF16, pc)
                kT = kb.sb("kT", [128, S_LEN], BF16, pc)
                Vp = kb.sb("Vp", [128, 16, 128], BF16, pc)
                szb = kb.sb("szb", [128, S_LEN], BF16, pc)
                NTa = kb.sb("NTa", [128, S_LEN], F32, pc)
                DBa = kb.sb("DBa", [128, S_LEN], F32, pc)
                Pf = [kb.sb(f"Pf{i}", [128, 256], F32, pc) for i in range(2)]
                PT = [kb.sb(f"PT{i}", [128, 256], BF16, pc) for i in range(2)]
                sc_d = 128.0 ** -0.5
                for hs in range(4):
                    wz = load_win(2048 + 4608 + hs * 128)
                    proj_fm(wz, 128, lambda bk, ap, tc: kb.act(szb[:, tc * 512:(tc + 1) * 512], ap, AF.Silu, [bk], [(szb, tc)]))
                    for g, (win, d) in enumerate(DIL_GROUPS):
                        hh = g * 4 + hs
                        sub = S_LEN // d
                        nqb = sub // 128
                        wq = load_win(2048 + hh * 128)
                        wk = load_win(2048 + 1536 + hh * 128)
                        wv = load_win(2048 + 3072 + hh * 128)
                        proj_fm(wq, 128, lambda bk, ap, tc: kb.cp("act", qT[:, tc * 512:(tc + 1) * 512], ap, [bk], [(qT, tc)]))
                        proj_fm(wk, 128, lambda bk, ap, tc: kb.cp("act", kT[:, tc * 512:(tc + 1) * 512], ap, [bk], [(kT, tc)]))

                        def toks(r, b):
                            t0 = r + d * 128 * b
                            return slice(t0, t0 + d * 127 + 1, d)

                        for ti in range(16):
                            r, b = divmod(ti, nqb)
                            bk = kb.bank()
                            for dc in range(8):
                                kb.mm(bk[:, 0:128], xnT[:, dc, toks(r, b)], wv[:, dc, :], dc == 0, dc == 7, [xnT, wv], [bk])
                            kb.cp("act", Vp[:, ti, :], bk[:, 0:128], [bk], [(Vp, ti)])
                        for ti in range(16):
                            r, qb = divmod(ti, nqb)
                            qs = toks(r, qb)
                            kbs = [qb - 1, qb] if qb > 0 else [qb]
                            sbk = kb.bank()
                            for kbi in kbs:
                                typ = 0 if kbi < qb else 1
                                kb.mm(sbk[:, typ * 128:(typ + 1) * 128], kT[:, toks(r, kbi)], qT[:, qs], True, True, [kT, qT], [sbk])
                            lo = 0 if qb > 0 else 128
                            pf = Pf[ti % 2]
                            pt = PT[ti % 2]
                            kb.act(pf[:, lo:256], sbk[:, lo:256], AF.Exp, [sbk], [pf], scale=sc_d)
                            kb.tt("dve", pt[:, lo:256], pf[:, lo:256], edil[:, hh, lo:256], ALU.mult, [pf, edil], [pt])
                            nb = kb.bank()
                            db = kb.bank()
                            for j, kbi in enumerate(kbs):
                                typ = 0 if kbi < qb else 1
                                kb.mm(nb[:, 0:128], Vp[:, r * nqb + kbi, :], pt[:, typ * 128:(typ + 1) * 128], j == 0, j == len(kbs) - 1,
                                      [Vp, pt], [nb])
                            for j, kbi in enumerate(kbs):
                                typ = 0 if kbi < qb else 1
                                kb.mm(db[:, 0:128], onesb[:], pt[:, typ * 128:(typ + 1) * 128], j == 0, j == len(kbs) - 1,
                                      [onesb, pt], [db])
                            if g == 0:
                                kb.cp("dve", NTa[:, qs], nb[:, 0:128], [nb], [NTa])
                                kb.cp("dve", DBa[:, qs], db[:, 0:128], [db], [DBa])
                            else:
                                kb.tt("dve", NTa[:, qs], NTa[:, qs], nb[:, 0:128], ALU.add, [nb, NTa], [NTa])
                                kb.tt("dve", DBa[:, qs], DBa[:, qs], db[:, 0:128], ALU.add, [db, DBa], [DBa])
                    kb.recip(DBa[:], DBa[:], [DBa], [DBa])
                    kb.tt("dve", NTa[:], NTa[:], DBa[:], ALU.mult, [NTa, DBa], [NTa])
                    kb.tt("dve", yT[:, 8 + hs, :], NTa[:], szb[:], ALU.mult, [NTa, szb], [(yT, 8 + hs)])
                S.barrier()

            with ExitStack() as pe_:
                out_proj(hawk_w_out, 12, yT, ymT, x_d, DX, False, pe_, dbg=(stop_after == "l0"))
                S.barrier()

        if stop_after != "l0":
          with ExitStack() as l1:
            yT1 = kb.sb("yT1", [128, 8, S_LEN], BF16, l1)
            ymT1 = kb.sb("ymT1", [64, 4, S_LEN], BF16, l1)
            gN = kb.sb("gN", [128, 8, 128], F32, l1)
            kb.dma("sp", gN[:], g_nsa, [], [gN])
            load_win = make_loader(nsa_w_in, gN, 3, l1)
            kcmpT = kb.sb("kcmpT", [64, 2, 128], BF16, l1)
            vcmp = kb.sb("vcmp", [128, 2, 64], BF16, l1)
            gates = kb.sb("gates", [128, 16, 48], F32, l1)
            with ExitStack() as pmm:
                kmT = kb.sb("kmT1", [64, 4, 256], BF16, pmm)
                vm = kb.sb("vm1", [128, 2, 256], BF16, pmm)
                with ExitStack() as pm:
                    gNm = kb.sb("gNm", [128, 8, 128], F32, pm)
                    kb.dma("sp", gNm[:], g_nsa_mem, [], [gNm])
                    mem_kv(nsa_w_mem_kv, gNm, kmT, vm, pm)
                    S.barrier()
                with ExitStack() as pd:
                    mem_attn(load_win, 2864, 3120, kmT, vm, ymT1, pd)
                    S.barrier()

            with ExitStack() as pq:
                wg = load_win(1792, 48)
                for t in range(NT_):
                    bk = kb.bank()
                    for dc in range(8):
                        kb.mm(bk[:, 0:48], xnT[:, dc, t * 128:(t + 1) * 128], wg[:, dc, 0:48], dc == 0, dc == 7, [xnT, wg], [bk])
                    kb.act(gates[:, t, :], bk[:, 0:48], AF.Sigmoid, [bk], [(gates, t)])
                kcT = kb.sb("kcT", [128, S_LEN], BF16, pq)
                vcT = kb.sb("vcT", [128, S_LEN], BF16, pq)
                wkc = load_win(1024)
                proj_fm(wkc, 128, lambda bk, ap, tc: kb.cp("act", kcT[:, tc * 512:(tc + 1) * 512], ap, [bk], [(kcT, tc)]))
                wvc = load_win(1024 + 128)
                proj_fm(wvc, 128, lambda bk, ap, tc: kb.cp("act", vcT[:, tc * 512:(tc + 1) * 512], ap, [bk], [(vcT, tc)]))
                W1 = kb.sb("W1", [128, 32, 256], BF16, pq)
                w2 = kb.sb("w2", [128, 2, 64], BF16, pq)
                peS = kb.sb("peS", [64, 2, 32], F32, pq)
                peb = kb.sb("peb", [64, 2, 32], BF16, pq)
                hidT = kb.sb("hidT", [128, 2, 128], BF16, pq)
                cb = kb.sb("cb", [128, 2], F32, pq)
                kb.dma("sp", peS[:], peT_d, [], [peS])
                kb.cp("dve", peb[:], peS[:], [peS], [peb])
                for kv in range(2):
                    w1d = w1k_d if kv == 0 else w1v_d
                    w2d = w2k_d if kv == 0 else w2v_d
                    srcT = kcT if kv == 0 else vcT
                    for p4 in range(8):
                        load_w(W1, W1[:, p4 * 4:(p4 + 1) * 4, :], w1d[:, p4 * 4:(p4 + 1) * 4, :], 256, key=p4)
                    load_w(w2, w2[:, :, :], w2d.rearrange("(hc p) d -> p hc d", p=128), 64)
                    for hc in range(2):
                        bk = kb.bank()
                        for p in range(32):
                            kb.mm(bk[:, 0:1], W1[0:64, p, hc * 128:(hc + 1) * 128], peb[0:64, kv, p:p + 1], p == 0, p == 31,
                                  [W1, peb], [bk])
                        kb.cp("dve", cb[:, hc:hc + 1], bk[:, 0:1], [bk], [(cb, hc)])
                    for g in range(2):
                        for hc in range(2):
                            bk = kb.bank()
                            for p in range(32):
                                kb.mm(bk[:, 0:127], W1[g * 64:(g + 1) * 64, p, hc * 128:(hc + 1) * 128],
                                      srcT[g * 64:(g + 1) * 64, p:p + 16 * 126 + 1:16], p == 0, p == 31, [W1, srcT], [bk])
                            kb.act(hidT[:, hc, 0:127], bk[:, 0:127], AF.Silu, [bk, cb], [(hidT, hc)], bias=cb[:, hc:hc + 1])
                        bk = kb.bank()
                        if kv == 0:
                            for hc in range(2):
                                kb.mm(bk[0:64, 0:127], w2[:, hc, :], hidT[:, hc, 0:127], hc == 0, hc == 1, [w2, hidT], [bk])
                            kb.cp("dve", kcmpT[0:64, g, 0:127], bk[0:64, 0:127], [bk], [(kcmpT, g)])
                        else:
                            for hc in range(2):
                                kb.mm(bk[0:127, 0:64], hidT[:, hc, 0:127], w2[:, hc, :], hc == 0, hc == 1, [w2, hidT], [bk])
                            kb.cp("dve", vcmp[0:127, g, :], bk[0:127, 0:64], [bk], [(vcmp, g)])
                S.barrier()

            with ExitStack() as pg:
                QAg = kb.sb("QAg", [105, 8, S_LEN], BF16, pg)
                KAs = kb.sb("KAs", [105, S_LEN], BF16, pg)
                KAw = kb.sb("KAw", [105, S_LEN], BF16, pg)
                VAs = kb.sb("VAs", [128, 16, 128], BF16, pg)
                VAw = kb.sb("VAw", [128, 16, 128], BF16, pg)
                ecmp = kb.sb("ecmp", [128, 8, 247], F32, pg)
                Wz = kb.sb("Wz", [128, 8, 512], BF16, pg)
                m12 = kb.sb("m12", [128, 2, 62], F32, pg)
                trib = kb.sb("trib", [128, 2, 512], BF16, pg)
                kb.dma("sp", m12[:], m12_d, [], [m12])
                for v in range(2):
                    stg = next_ws()
                    kb.dma("sp", stg[:, 0:512], tri_d[:, v, :], [], [stg])
                    kb.cp("pool", trib[:, v, :], stg[:, 0:512], [stg], [(trib, v)])
                for v, KA in enumerate((KAs, KAw)):
                    for hf in range(2):
                        stg = next_ws()
                        kb.dma("sp", stg[64:105, 0:1024], kaug_d[v][:, hf * 1024:(hf + 1) * 1024], [], [stg])
                        kb.cp("pool", KA[64:105, hf * 1024:(hf + 1) * 1024], stg[64:105, 0:1024], [stg], [(KA, ("aug", hf))])
                kb.memset("pool", VAs[:, :, 64:128], 1.0, [(VAs, "ones")])
                kb.memset("pool", VAw[:, :, 64:128], 1.0, [(VAw, "ones")])
                Pc = [kb.sb(f"Pc{i}", [128, 4, 128], F32, pg) for i in range(2)]
                Pu = [kb.sb(f"Pu{i}", [128, 4, 128], F32, pg) for i in range(2)]
                Pub = [kb.sb(f"Pub{i}", [128, 4, 128], BF16, pg) for i in range(2)]
                pT = [kb.sb(f"pT{i}", [128, 4, 128], BF16, pg) for i in range(2)]
                for i in range(2):
                    kb.memset("pool", Pu[i][:], 0.0, [Pu[i]])
                psg = kb.sb("psg", [128, 128], F32, pg)
                den8 = kb.sb("den8", [128, 8], F32, pg)
                cg8 = kb.sb("cg8", [128, 8], F32, pg)
                imp = kb.sb("imp", [128, 32], F32, pg)
                impm = kb.sb("impm", [128, 32], F32, pg)
                m8 = kb.sb("m8", [128, 8], F32, pg)
                negp = kb.sb("negp", [128, 96], F32, pg)
                negS = kb.sb("negS", [96, 128], BF16, pg)
                kb.memset("pool", negp[:], 0.0, [negp])
                PTs = [kb.sb(f"PTs{i}", [128, 512], BF16, pg) for i in range(3)]
                pts_rr = [0]
                accs = [kb.sb(f"accs{i}", [128, 512], F32, pg) for i in range(2)]
                rd4 = kb.sb("rd4", [128, 4], F32, pg)
                cg4 = kb.sb("cg4", [128, 4], F32, pg)
                Oa = [kb.sb(f"Oa{i}", [128, 512], F32, pg) for i in range(2)]
                szt = kb.sb("szt", [128, 512], F32, pg)
                Ob = kb.sb("Ob", [128, 512], BF16, pg)
                accbanks = [kb.banks[0], kb.banks[1]]
                rot = [2]

                def rbank():
                    b = kb.banks[rot[0]]
                    rot[0] = rot[0] + 1 if rot[0] < 7 else 2
                    return b

                for g in range(2):
                    kb.dma("sp", ecmp[:], ecmp_d[:, g * 8:(g + 1) * 8, :], [], [ecmp])
                    for j in range(4):
                        load_w(Wz, Wz[:, :, j * 128:(j + 1) * 128], win_cols(nsa_w_in, 1840 + g * 512 + j * 128, 128), 128,
                               gain=gN, key=j)
                    for r in range(8):
                        hq = g * 8 + r
                        wq = load_win(hq * 64, 64)
                        for tc in range(4):
                            bk = rbank()
                            for dc in range(8):
                                kb.mm(bk[0:64, :], wq[:, dc, 0:64], xnT[:, dc, tc * 512:(tc + 1) * 512], dc == 0, dc == 7, [wq, xnT], [bk])
                            kb.cp("act", QAg[0:64, r, tc * 512:(tc + 1) * 512], bk[0:64, :], [bk], [(QAg, ("q", r, tc))])
                        for hf in range(2):
                            stg = next_ws()
                            kb.dma("sp", stg[96:105, 0:1024], qal_d[hq][:, hf * 1024:(hf + 1) * 1024], [], [stg])
                            kb.cp("pool", QAg[96:105, r, hf * 1024:(hf + 1) * 1024], stg[96:105, 0:1024], [stg], [(QAg, ("al", r, hf))])
                    for KA, col in ((KAs, 1024 + 2 * 128 + g * 64), (KAw, 1024 + 4 * 128 + g * 64)):
                        wk = load_win(col, 64)
                        for tc in range(4):
                            bk = rbank()
                            for dc in range(8):
                                kb.mm(bk[0:64, :], wk[:, dc, 0:64], xnT[:, dc, tc * 512:(tc + 1) * 512], dc == 0, dc == 7, [wk, xnT], [bk])
                            kb.cp("act", KA[0:64, tc * 512:(tc + 1) * 512], bk[0:64, :], [bk], [(KA, tc)])
                    wv = load_win(1024 + 3 * 128 + g * 64, 64)
                    load_win(1024 + 5 * 128 + g * 64, 64, into=wv, off=64)
                    for t in range(NT_):
                        bk = rbank()
                        for dc in range(8):
                            kb.mm(bk[:, 0:128], xnT[:, dc, t * 128:(t + 1) * 128], wv[:, dc, 0:128], dc == 0, dc == 7, [xnT, wv], [bk])
                        kb.cp("act", VAs[:, t, 0:64], bk[:, 0:64], [bk], [(VAs, t)])
                        kb.cp("dve", VAw[:, t, 0:64], bk[:, 64:128], [bk], [(VAw, t)])

                    for qt in range(NT_):
                        qc = slice(qt * 128, (qt + 1) * 128)
                        O = Oa[qt % 2]
                        gq = gates[:, qt, :]
                        eoff = 120 - 8 * qt
                        ocb = rbank()
                        for b4 in range(2):
                            sbk = rbank()
                            for hl in range(4):
                                r = b4 * 4 + hl
                                kb.mm(sbk[:, hl * 128:hl * 128 + 127], QAg[0:64, r, qc], kcmpT[0:64, g, 0:127], True, True,
                                      [QAg, kcmpT], [sbk])
                            pc = Pc[b4]
                            pu = Pu[b4]
                            s3 = sbk[:].rearrange("p (h n) -> p h n", h=4)
                            kb.act(pc[:, :, 0:127], s3[:, :, 0:127], AF.Exp, [sbk], [pc], scale=0.125)
                            kb.tt("dve", pu[:, :, 0:127], pc[:, :, 0:127], ecmp[:, b4 * 4:(b4 + 1) * 4, eoff:eoff + 127], ALU.mult,
                                  [pc, ecmp], [pu])
                            S.op("dve", lambda e, pu=pu, b4=b4: e.tensor_reduce(out=den8[:, b4 * 4:(b4 + 1) * 4], in_=pu[:, :, 0:127],
                                                                                axis=AX.X, op=ALU.add),
                                 reads=[pu], writes=[(den8, b4)])
                            kb.ts("dve", den8[:, b4 * 4:(b4 + 1) * 4], den8[:, b4 * 4:(b4 + 1) * 4], 1e-30, None, ALU.max, None,
                                  [(den8, b4)], [(den8, b4)])
                            kb.recip(den8[:, b4 * 4:(b4 + 1) * 4], den8[:, b4 * 4:(b4 + 1) * 4], [(den8, b4)], [(den8, b4)])
                            for hl in range(4):
                                r = b4 * 4 + hl
                                if r == 0:
                                    kb.ts("dve", psg[:, :], pu[:, hl, :], den8[:, r:r + 1], None, ALU.mult, None, [pu, (den8, b4)], [psg])
                                else:
                                    kb.stt(psg[:, :], pu[:, hl, :], den8[:, r:r + 1], psg[:, :], ALU.mult, ALU.add,
                                           [pu, (den8, b4), psg], [psg])
                            pub = Pub[b4]
                            kb.cp("pool", pub[:], pu[:], [pu], [pub])
                            tbk = rbank()
                            tv = tbk[:].bitcast(BF16)
                            for hl in range(4):
                                kb.tr(tv[0:127, hl * 128:(hl + 1) * 128], pub[:, hl, 0:127], identb[:], [pub, identb], [tbk])
                            ptt = pT[b4]
                            kb.cp("act", ptt[0:127, :, :], tv[0:127, 0:512].rearrange("p (h n) -> p h n", h=4), [tbk], [ptt])
                            for hl in range(4):
                                r = b4 * 4 + hl
                                kb.mm(ocb[:, r * 64:(r + 1) * 64], ptt[0:127, hl, :], vcmp[0:127, g, :], True, True, [ptt, vcmp], [ocb])
                        kb.tt("dve", cg8[:], den8[:], gq[:, g * 24:g * 24 + 24:3], ALU.mult, [den8, gates], [cg8])
                        kb.tt("dve", O[:].rearrange("p (h d) -> p h d", h=8), ocb[:].rearrange("p (h d) -> p h d", h=8),
                              cg8[:, 0:8].unsqueeze(2).to_broadcast([128, 8, 64]), ALU.mult, [ocb, cg8], [O])
                        S.op("dve", lambda e: e.tensor_reduce(out=imp[:, :], in_=psg[:].rearrange("p (j a) -> p j a", a=4),
                                                              axis=AX.X, op=ALU.add), reads=[psg], writes=[imp])
                        kb.tt("dve", imp[:, 1:32], imp[:, 1:32], psg[:, 3:127:4], ALU.add, [imp, psg], [imp])
                        moff = 30 - 2 * qt
                        kb.tt("dve", impm[:], imp[:], m12[:, 0, moff:moff + 32], ALU.mult, [imp, m12], [impm])
                        kb.tt("dve", impm[:], impm[:], m12[:, 1, moff:moff + 32], ALU.add, [impm, m12], [impm])
                        kb.memset("dve", impm[:, 0:1], 1e6, [impm])
                        S.op("dve", lambda e: e.max(out=m8[:], in_=impm[:]), reads=[impm], writes=[m8])
                        kb.ts("dve", negp[:, 64:96], impm[:], m8[:, 7:8], 1.0, ALU.is_ge, ALU.subtract, [impm, m8], [negp])
                        kb.ts("dve", negp[:, 64:96], negp[:, 64:96], NEGB, None, ALU.mult, None, [negp], [negp])
                        tbk = rbank()
                        kb.tr(tbk[0:96, 0:128], negp[:, 0:96], identf[:], [negp, identf], [tbk])
                        kb.cp("dve", negS[64:96, :], tbk[64:96, 0:128], [tbk], [negS])
                        for r in range(8):
                            kb.cp("pool", QAg[64:96, r, qc], negS[64:96, :], [negS], [(QAg, ("sel", r, qt))])

                        for br, KA, VA in ((2, KAw, VAw), (1, KAs, VAs)):
                            if br == 2:
                                kbs = list(range(max(0, qt - 4), qt + 1))
                            else:
                                kbs = list(range(0, qt + 1))
                            for b4 in range(2):
                                accb = accbanks[b4]
                                for idx, kbi in enumerate(kbs):
                                    sbk = rbank()
                                    masks = []
                                    if kbi == qt:
                                        masks.append(0)
                                    if br == 2 and kbi == qt - 4:
                                        masks.append(1)
                                    kb.mm(sbk[:, :], KA[0:105, kbi * 128:(kbi + 1) * 128], QAg[0:105, b4 * 4:(b4 + 1) * 4, qc],
                                          True, len(masks) == 0, [KA, QAg], [sbk])
                                    for mi, mv in enumerate(masks):
                                        kb.mm(sbk[:, :], identb[:], trib[:, mv, :], False, mi == len(masks) - 1, [identb, trib], [sbk])
                                    pt = PTs[pts_rr[0]]
                                    pts_rr[0] = (pts_rr[0] + 1) % 3
                                    kb.act(pt[:], sbk[:], AF.Exp, [sbk], [pt], scale=0.125)
                                    kb.mm(accb[:, :], VA[:, kbi, :], pt[:], idx == 0, idx == len(kbs) - 1, [VA, pt], [accb])
                                ac = accs[b4]
                                kb.cp("act", ac[:], accb[:], [accb], [ac])
                                tbk = rbank()
                                for hl in range(4):
                                    kb.tr(tbk[:, hl * 128:(hl + 1) * 128], ac[:, hl * 128:(hl + 1) * 128], identf[:], [ac, identf], [tbk])
                                t3 = tbk[:].rearrange("p (h n) -> p h n", h=4)
                                kb.recip(rd4[:], t3[:, :, 64], [tbk], [rd4])
                                h0 = (g * 8 + b4 * 4) * 3 + br
                                kb.tt("dve", cg4[:], rd4[:], gq[:, h0:h0 + 10:3], ALU.mult, [rd4, gates], [cg4])
                                for hl in range(4):
                                    r = b4 * 4 + hl
                                    kb.stt(O[:, r * 64:(r + 1) * 64], tbk[:, hl * 128:hl * 128 + 64], cg4[:, hl:hl + 1], O[:, r * 64:(r + 1) * 64],
                                           ALU.mult, ALU.add, [tbk, cg4, O], [O])
                        zb = rbank()
                        for dc in range(8):
                            kb.mm(zb[:, :], xnT[:, dc, qc], Wz[:, dc, :], dc == 0, dc == 7, [xnT, Wz], [zb])
                        kb.act(szt[:], zb[:], AF.Silu, [zb], [szt])
                        kb.tt("dve", Ob[:], O[:], szt[:], ALU.mult, [O, szt], [Ob])
                        tbk = rbank()
                        tv = tbk[:].bitcast(BF16)
                        for c4 in range(4):
                            kb.tr(tv[:, c4 * 128:(c4 + 1) * 128], Ob[:, c4 * 128:(c4 + 1) * 128], identb[:], [Ob, identb], [tbk])
                        kb.cp("act", yT1[:, g * 4:(g + 1) * 4, qc], tv[:, 0:512].rearrange("p (c n) -> p c n", c=4), [tbk], [(yT1, (g, qt))])
                S.barrier()

            with ExitStack() as pe_:
                out_proj(nsa_w_out, 8, yT1, ymT1, x1_scr, DX1, True, pe_)
                S.barrier()

        S.barrier()
        with nc.Block() as block:
            S.emit(block)
        print("program ops:", S.nops, "sems:", S.nsem)
    return nc


_CONST = None


def prep_inputs(inp):
    global _CONST
    if _CONST is None:
        _CONST = host_constants()
        _CONST.update(host_constants_nsa())
    f = lambda a: np.ascontiguousarray(np.asarray(a, dtype=np.float32))
    shared = {
        "hawk_w_in": f(inp["hawk_w_in"][0]),
        "hawk_w_out": f(inp["hawk_w_out"][0]),
        "hawk_w_mem_kv": f(inp["hawk_w_mem_kv"][0]),
        "g_hawk": expand_gain(f(inp["hawk_norm"][0])),
        "g_hawk_mem": expand_gain(f(inp["hawk_mem_norm"][0])),
        "bd_a": block_diag(f(inp["hawk_gate_a_w"][0])),
        "bd_x": block_diag(f(inp["hawk_gate_x_w"][0])),
        "final_norm": f(inp["final_norm"]),
        "nsa_w_in": f(inp["nsa_w_in"][0]),
        "nsa_w_out": f(inp["nsa_w_out"][0]),
        "nsa_w_mem_kv": f(inp["nsa_w_mem_kv"][0]),
        "g_nsa": expand_gain(f(inp["nsa_norm"][0])),
        "g_nsa_mem": expand_gain(f(inp["nsa_mem_norm"][0])),
        "w2k": f(inp["nsa_phi_k_w2"][0]),
        "w2v": f(inp["nsa_phi_v_w2"][0]),
    }
    for k in ("identf", "edil", "ecmp", "m12", "tri", "kaug", "qal"):
        shared[k] = _CONST[k]

    def w1_layout(w1):
        a = w1.reshape(32, 64, 256).transpose(1, 0, 2)
        return np.ascontiguousarray(np.concatenate([a, a], axis=0))
    shared["w1k"] = w1_layout(f(inp["nsa_phi_k_w1"][0]))
    shared["w1v"] = w1_layout(f(inp["nsa_phi_v_w1"][0]))
    shared["peT"] = np.ascontiguousarray(np.stack([f(inp["nsa_pe_k"][0]).T, f(inp["nsa_pe_v"][0]).T], axis=1))
    lv = np.zeros((128, 8, 8), np.float32)
    cw = f(inp["hawk_conv_w"][0])
    for k in range(4):
        lv[:, :, k] = vec_fm(cw[k])
    lv[:, :, 4] = vec_fm(f(inp["hawk_conv_b"][0]))
    lv[:, :, 5] = vec_fm(f(inp["hawk_gate_a_b"][0]).reshape(-1))
    lv[:, :, 6] = vec_fm(f(inp["hawk_gate_x_b"][0]).reshape(-1))
    lv[:, :, 7] = vec_fm(f(inp["hawk_lambda"][0]))
    shared["lru_vec"] = lv
    x = f(inp["x"])
    mem = f(inp["mem"])
    maps = []
    for b in range(x.shape[0]):
        m = dict(shared)
        m["x"] = x[b]
        m["mem"] = mem[b]
        maps.append(m)
    return maps


def kernel(**inputs):
    maps = prep_inputs(inputs)
    nc = build_program()
    res = run_bass_kernel_spmd(nc, maps, core_ids=list(range(len(maps))))
    out = np.stack([np.asarray(r["out"], dtype=np.float32) for r in res.results], axis=0)
    return out
```

```python
import math
from contextlib import ExitStack

import numpy as np
import concourse.bass as bass
import concourse.mybir as mybir
from concourse.bass_utils import run_bass_kernel_spmd

F32 = mybir.dt.float32
BF16 = mybir.dt.bfloat16
AF = mybir.ActivationFunctionType
ALU = mybir.AluOpType
AX = mybir.AxisListType

S_LEN = 2048
D = 1024
NT_ = 16
EPS = 1e-6
DIL_GROUPS = ((128, 1), (512, 4), (2048, 16))

SEM_LIMIT = 30000
N_DMA_SEMS = 24
SAME_ENGINE_SYNC = True


class Buf:
    def __init__(self, name, t, excl=False):
        self.name = name
        self.t = t
        self.excl = excl
        self.st = {}

    def __getitem__(self, idx):
        return self.t[idx]


class Sync:
    def __init__(self, nc, stack):
        self.nc = nc
        self.stack = stack
        self.engs = ["pe", "act", "dve", "pool", "sp"]
        self.ops = {e: [] for e in self.engs}
        self.cur_sem = {}
        self.cnt = {}
        self.nsem = 0
        for e in self.engs:
            self._new_sem(e)
        self.dma_sems = {}
        self.dma_val = {}
        self.dma_rr = {}
        for e in ["sp", "pool", "act"]:
            self.dma_sems[e] = [self._alloc_sem(f"d{e}{i}") for i in range(N_DMA_SEMS)]
            self.dma_val[e] = [0] * N_DMA_SEMS
            self.dma_rr[e] = 0
        self.seen = {e: {} for e in self.engs}
        self.all_ticks = {}
        self.nops = 0
        self.dead = False

    def _alloc_sem(self, name):
        self.nsem += 1
        return self.stack.enter_context(self.nc.semaphore(f"s_{name}_{self.nsem}"))

    def _new_sem(self, e):
        self.cur_sem[e] = self._alloc_sem(e)
        self.cnt[e] = 0

    def _states(self, buf, key, create):
        if key is None:
            if create and None not in buf.st:
                buf.st[None] = [None, {}]
            return list(buf.st.values())
        out = []
        if None in buf.st:
            out.append(buf.st[None])
        if key not in buf.st and create:
            buf.st[key] = [None, {}]
        if key in buf.st:
            out.append(buf.st[key])
        return out

    @staticmethod
    def _norm(lst):
        out = []
        for r in lst or []:
            out.append(r if isinstance(r, tuple) else (r, None))
        return out

    def op(self, eng, fn, reads=None, writes=None, dma=False):
        if self.dead:
            return None
        reads = self._norm(reads)
        writes = self._norm(writes)
        ex = [(b, None) for (b, k) in reads + writes if b.excl]
        if ex:
            reads = [(b, k) for (b, k) in reads if not b.excl]
            writes = [(b, k) for (b, k) in writes if not b.excl]
            for bk in ex:
                if bk not in writes:
                    writes.append(bk)
        need = []
        for buf, key in reads:
            for st in self._states(buf, key, False):
                if st[0] is not None:
                    need.append(st[0])
        for buf, key in writes:
            for st in self._states(buf, key, False):
                if st[0] is not None:
                    need.append(st[0])
                need.extend(st[1].values())
        if dma:
            i = self.dma_rr[eng]
            self.dma_rr[eng] = (i + 1) % N_DMA_SEMS
            sem = self.dma_sems[eng][i]
            prev = self.dma_val[eng][i]
            if prev > 0:
                need.append((sem, prev, "dma"))
            if prev + 16 > SEM_LIMIT:
                sem = self._alloc_sem(f"d{eng}{i}")
                self.dma_sems[eng][i] = sem
                prev = 0
            val = prev + 16
            self.dma_val[eng][i] = val
            inc = 16
            tick = (sem, val, "dma")
        else:
            if self.cnt[eng] + 1 > SEM_LIMIT:
                self._new_sem(eng)
            self.cnt[eng] += 1
            sem = self.cur_sem[eng]
            val = self.cnt[eng]
            inc = 1
            tick = (sem, val, eng)
        waits = {}
        seen = self.seen[eng]
        for (s, v, src) in need:
            if src == eng and (eng == "pe" or not SAME_ENGINE_SYNC):
                continue
            sid = id(s)
            if seen.get(sid, 0) >= v:
                continue
            if sid not in waits or waits[sid][1] < v:
                waits[sid] = (s, v)
        for sid, (s, v) in waits.items():
            seen[sid] = v
        self.ops[eng].append((list(waits.values()), fn, sem, inc))
        self.all_ticks[id(sem)] = (sem, val)
        self.nops += 1
        wset = set((id(b), k) for b, k in writes)
        for buf, key in reads:
            if (id(buf), key) in wset:
                continue
            self._states(buf, key, True)
            buf.st[key][1][eng if not dma else ("dma", id(sem))] = tick
        for buf, key in writes:
            if key is None:
                buf.st = {None: [tick, {}]}
            else:
                buf.st[key] = [tick, {}]
        return tick

    def barrier(self):
        if self.dead:
            return
        ticks = list(self.all_ticks.values())
        for e in self.engs:
            wl = []
            for (s, v) in ticks:
                if self.seen[e].get(id(s), 0) < v:
                    wl.append((s, v))
                    self.seen[e][id(s)] = v
            if wl:
                self.ops[e].append((wl, None, None, 0))

    def emit(self, block):
        S = self

        def run(engname, e):
            for (wl, fn, sem, inc) in S.ops[engname]:
                for (s, v) in wl:
                    e.wait_ge(s, v)
                if fn is not None:
                    fn(e).then_inc(sem, inc)

        @block.sync
        def _(e):
            run("sp", e)

        @block.tensor
        def _(e):
            run("pe", e)

        @block.scalar
        def _(e):
            run("act", e)

        @block.vector
        def _(e):
            run("dve", e)

        @block.gpsimd
        def _(e):
            run("pool", e)


class KB:
    def __init__(self, nc, stack):
        self.nc = nc
        self.gst = stack
        self.S = Sync(nc, stack)
        self.banks = [Buf(f"bank{i}", stack.enter_context(nc.psum_tensor(f"bank{i}", [128, 512], F32)), excl=True)
                      for i in range(8)]
        self.bank_rr = 0
        self.uid = 0

    def sb(self, name, shape, dt, stack=None):
        self.uid += 1
        t = (stack or self.gst).enter_context(self.nc.sbuf_tensor(f"{name}_{self.uid}", shape, dt))
        return Buf(name, t)

    def bank(self):
        b = self.banks[self.bank_rr]
        self.bank_rr = (self.bank_rr + 1) % 8
        return b

    def mm(self, out, lhsT, rhs, start, stop, r, w):
        self.S.op("pe", lambda e: e.matmul(out, lhsT=lhsT, rhs=rhs, start=start, stop=stop), reads=r, writes=w)

    def tr(self, out, in_, ident, r, w):
        self.S.op("pe", lambda e: e.transpose(out, in_, ident), reads=r, writes=w)

    def act(self, out, in_, func, r, w, **kw):
        self.S.op("act", lambda e: e.activation(out=out, in_=in_, func=func, **kw), reads=r, writes=w)

    def tt(self, eng, out, in0, in1, op, r, w):
        self.S.op(eng, lambda e: e.tensor_tensor(out=out, in0=in0, in1=in1, op=op), reads=r, writes=w)

    def ts(self, eng, out, in0, s1, s2, op0, op1, r, w, **kw):
        if op1 is None:
            self.S.op(eng, lambda e: e.tensor_scalar(out=out, in0=in0, scalar1=s1, scalar2=None, op0=op0, **kw), reads=r, writes=w)
        else:
            self.S.op(eng, lambda e: e.tensor_scalar(out=out, in0=in0, scalar1=s1, scalar2=s2, op0=op0, op1=op1, **kw), reads=r, writes=w)

    def stt(self, out, in0, scalar, in1, op0, op1, r, w, **kw):
        self.S.op("dve", lambda e: e.scalar_tensor_tensor(out=out, in0=in0, scalar=scalar, in1=in1, op0=op0, op1=op1, **kw), reads=r, writes=w)

    def cp(self, eng, out, in_, r, w):
        if eng == "act":
            self.S.op("act", lambda e: e.activation(out=out, in_=in_, func=AF.Copy), reads=r, writes=w)
        else:
            self.S.op(eng, lambda e: e.tensor_copy(out=out, in_=in_), reads=r, writes=w)

    def memset(self, eng, ap, val, w):
        self.S.op(eng, lambda e: e.memset(ap, val), writes=w)

    def recip(self, out, in_, r, w):
        self.S.op("dve", lambda e: e.reciprocal(out=out, in_=in_), reads=r, writes=w)

    def dma(self, q, out, in_, r, w):
        self.S.op(q, lambda e: e.dma_start(out=out, in_=in_), reads=r, writes=w, dma=True)


def alibi_slopes(n):
    return np.exp2(-8.0 * np.arange(1, n + 1) / n).astype(np.float32)


def host_constants():
    c = {}
    c["identf"] = np.eye(128, dtype=np.float32)
    sl = alibi_slopes(12)
    ik = np.arange(128)[:, None].astype(np.float64)
    iq = np.arange(128)[None, :].astype(np.float64)
    E = np.zeros((128, 12, 256), np.float32)
    for g, (win, dil) in enumerate(DIL_GROUPS):
        for hs in range(4):
            hh = g * 4 + hs
            s = float(sl[hh]) * dil
            dist_prev = 128 + iq - ik
            ok_prev = (dist_prev <= 128)
            E[:, hh, 0:128] = np.where(ok_prev, np.exp(-s * dist_prev), 0.0)
            dist_cur = iq - ik
            ok_cur = dist_cur >= 0
            E[:, hh, 128:256] = np.where(ok_cur, np.exp(-s * dist_cur), 0.0)
    c["edil"] = E
    return c


def expand_gain(g):
    return np.ascontiguousarray(np.broadcast_to(g.reshape(8, 128).T[:, :, None], (128, 8, 128))).astype(np.float32)


def vec_fm(v):
    return np.ascontiguousarray(v.reshape(8, 128).T).astype(np.float32)


def block_diag(gw):
    out = np.zeros((128, 8, 128), np.float32)
    for c in range(8):
        out[0:64, c, 0:64] = gw[2 * c]
        out[64:128, c, 64:128] = gw[2 * c + 1]
    return out


NEGB = 8192.0


def _bf16_split3(a):
    import ml_dtypes
    a = a.astype(np.float32)
    hi = a.astype(ml_dtypes.bfloat16).astype(np.float32)
    r1 = (a - hi).astype(np.float32)
    mid = r1.astype(ml_dtypes.bfloat16).astype(np.float32)
    r2 = (r1 - mid).astype(np.float32)
    lo = r2.astype(ml_dtypes.bfloat16).astype(np.float32)
    return hi, mid, lo


def host_constants_nsa():
    c = {}
    sl = alibi_slopes(16)
    i = np.arange(128)[:, None].astype(np.float64)
    m = np.arange(247)[None, :].astype(np.float64)
    dist = i - 16.0 * (m - 120.0) - 31.0
    E = np.zeros((128, 16, 247), np.float32)
    for h in range(16):
        E[:, h, :] = np.where(dist >= 0, np.exp(-float(sl[h]) * np.maximum(dist, 0.0)), 0.0)
    c["ecmp"] = E
    ii = np.arange(128)[:, None]
    rel = np.arange(62)[None, :] - 30
    cur = (ii >= 64).astype(np.int64)
    forced = (rel == cur) | (rel == cur - 1)
    future = rel > cur
    m1 = np.where(forced | future, 0.0, 1.0).astype(np.float32)
    m2 = np.where(forced, 1e6, np.where(future, -1e6, 0.0)).astype(np.float32)
    c["m12"] = np.ascontiguousarray(np.stack([m1, m2], axis=1))
    ik = np.arange(128)[:, None]
    iq = np.arange(128)[None, :]
    diag = np.where(ik > iq, -NEGB, 0.0).astype(np.float32)
    far = np.where(ik <= iq, -NEGB, 0.0).astype(np.float32)
    c["tri"] = np.ascontiguousarray(np.stack([np.tile(diag, (1, 4)), np.tile(far, (1, 4))], axis=1))
    k = np.arange(2048)
    kp = k - 1024
    hi = (np.floor(kp / 128.0) * 128.0).astype(np.float32)
    lo = (kp - hi).astype(np.float32)
    ka = np.zeros((2, 41, 2048), np.float32)
    for j in range(32):
        ka[0, j, :] = (k // 64 == j).astype(np.float32)
    for v in range(2):
        ka[v, 32:35, :] = 1.0
        ka[v, 35:38, :] = lo[None, :]
        ka[v, 38:41, :] = hi[None, :]
    c["kaug"] = ka
    qa = np.zeros((16, 9, 2048), np.float32)
    qp = (np.arange(2048) - 1024).astype(np.float32)
    for h in range(16):
        s8 = np.float32(8.0) * np.float32(sl[h])
        a = (-s8 * qp).astype(np.float32)
        ah, am, al = _bf16_split3(a)
        sh, sm, sl_ = _bf16_split3(np.full((2048,), s8, np.float32))
        qa[h, 0], qa[h, 1], qa[h, 2] = ah, am, al
        qa[h, 3], qa[h, 4], qa[h, 5] = sh, sm, sl_
        qa[h, 6], qa[h, 7], qa[h, 8] = sh, sm, sl_
    c["qal"] = qa
    return c


def chain(*gens):
    for g_ in gens:
        yield from g_


def run_lanes(lanes):
    active = list(lanes)
    while active:
        for l in list(active):
            try:
                next(l)
            except StopIteration:
                active.remove(l)


def build_program(stop_after=None):
    nc = bass.Bass("TRN2", target_bir_lowering=False)

    ckstate = {}

    def ck(name):
        if stop_after == name:
            ckstate["S"].dead = True

    def din(name, shape):
        return nc.dram_tensor(name, list(shape), F32, kind="ExternalInput").ap()

    x_d = din("x", [S_LEN, D])
    mem_d = din("mem", [256, D])
    hawk_w_in = din("hawk_w_in", [D, 7680])
    hawk_w_out = din("hawk_w_out", [1792, D])
    hawk_w_mem_kv = din("hawk_w_mem_kv", [D, 512])
    g_hawk = din("g_hawk", [128, 8, 128])
    g_hawk_mem = din("g_hawk_mem", [128, 8, 128])
    lru_vec = din("lru_vec", [128, 8, 8])
    bd_a = din("bd_a", [128, 8, 128])
    bd_x = din("bd_x", [128, 8, 128])
    identf_d = din("identf", [128, 128])
    edil_d = din("edil", [128, 12, 256])
    final_g = din("final_norm", [D])
    nsa_w_in = din("nsa_w_in", [D, 3376])
    nsa_w_out = din("nsa_w_out", [1280, D])
    nsa_w_mem_kv = din("nsa_w_mem_kv", [D, 512])
    g_nsa = din("g_nsa", [128, 8, 128])
    g_nsa_mem = din("g_nsa_mem", [128, 8, 128])
    w1k_d = din("w1k", [128, 32, 256])
    w1v_d = din("w1v", [128, 32, 256])
    w2k_d = din("w2k", [256, 64])
    w2v_d = din("w2v", [256, 64])
    peT_d = din("peT", [64, 2, 32])
    ecmp_d = din("ecmp", [128, 16, 247])
    m12_d = din("m12", [128, 2, 62])
    tri_d = din("tri", [128, 2, 512])
    kaug_d = din("kaug", [2, 41, 2048])
    qal_d = din("qal", [16, 9, 2048])
    out_d = nc.dram_tensor("out", [S_LEN, D], F32, kind="ExternalOutput").ap()
    x1_scr = nc.dram_tensor("x1_scr", [S_LEN, D], F32, kind="Internal").ap()

    with ExitStack() as gst:
        kb = KB(nc, gst)
        S = kb.S
        ckstate["S"] = S
        DX = Buf("x_dram", None)
        DX1 = Buf("x1_dram", None)
        DOUT = Buf("out_dram", None)

        xnT = kb.sb("xnT", [128, 8, S_LEN], BF16)
        memnT = kb.sb("memnT", [128, 8, 256], BF16)
        identf = kb.sb("identf", [128, 128], F32)
        identb = kb.sb("identb", [128, 128], BF16)
        onesb = kb.sb("onesb", [128, 128], BF16)
        wstage = [kb.sb(f"wstage{i}", [128, 1024], F32) for i in range(3)]
        ws_rr = [0]
        stat = kb.sb("stat", [128, 64], F32)
        stat_rr = [0]

        kb.dma("sp", identf[:], identf_d, [], [identf])
        kb.cp("dve", identb[:], identf[:], [identf], [identb])
        kb.memset("dve", onesb[:], 1.0, [onesb])

        def next_ws():
            b = wstage[ws_rr[0]]
            ws_rr[0] = (ws_rr[0] + 1) % len(wstage)
            return b

        def load_w(dst, dst_ap3, src_ap3, n, gain=None, key=None, q="sp", part=128, eng="pool"):
            dcs = dst_ap3.shape[1]
            assert dcs * n <= 1024
            stg = next_ws()
            sv = stg[0:part, 0:dcs * n].rearrange("p (c n) -> p c n", c=dcs)
            kb.dma(q, sv, src_ap3, [], [stg])
            if gain is not None:
                kb.tt(eng, dst_ap3, sv, gain[0:part, 0:dcs, 0:n], ALU.mult, [stg, gain], [(dst, key)])
            else:
                kb.cp(eng, dst_ap3, sv, [stg], [(dst, key)])

        def win_cols(w_dram, c0, n):
            return w_dram.rearrange("(dc p) n -> p dc n", p=128)[:, :, c0:c0 + n]

        class NormCtx:
            def __init__(self, stack):
                self.xstage = [kb.sb(f"xstage{i}", [128, 1024], F32, stack) for i in range(2)]
                self.xnb = [kb.sb(f"xnb{i}", [128, 1024], BF16, stack) for i in range(2)]
                self.junk = kb.sb("junk", [128, 1024], BF16, stack)

        def tile_rstd(ncx, xbuf, xap):
            i = stat_rr[0]
            stat_rr[0] = (stat_rr[0] + 1) % 32
            ss = stat[:, 2 * i:2 * i + 1]
            rs = stat[:, 2 * i + 1:2 * i + 2]
            kb.stt(ncx.junk[:], xap, 1.0, xap, ALU.mult, ALU.mult, [xbuf], [ncx.junk, (stat, i)], accum_out=ss)
            kb.ts("dve", ss, ss, 1.0 / D, EPS, ALU.mult, ALU.add, [(stat, i)], [(stat, i)])
            kb.act(ss, ss, AF.Sqrt, [(stat, i)], [(stat, i)])
            kb.recip(rs, ss, [(stat, i)], [(stat, i)])
            return rs, i

        def norm_to_T(ncx, xbuf, xap, dstT, t, ntok_off):
            rs, i = tile_rstd(ncx, xbuf, xap)
            nb = ncx.xnb[t % 2]
            kb.ts("dve", nb[:], xap, rs, None, ALU.mult, None, [xbuf, (stat, i)], [nb])
            bk = kb.bank()
            bv = bk[:].bitcast(BF16)
            for c in range(8):
                kb.tr(bv[:, c * 128:(c + 1) * 128], nb[:, c * 128:(c + 1) * 128], identb[:], [nb, identb], [bk])
            kb.cp("act", dstT[:, :, ntok_off:ntok_off + 128], bv.rearrange("p (c n) -> p c n", c=8), [bk], [(dstT, t)])

        with ExitStack() as pa:
            ncx = NormCtx(pa)
            for t in range(NT_):
                xs = ncx.xstage[t % 2]
                kb.dma("sp", xs[:], x_d[t * 128:(t + 1) * 128, :], [DX], [xs])
                norm_to_T(ncx, xs, xs[:], xnT, t, t * 128)
            for t in range(2):
                xs = ncx.xstage[t % 2]
                kb.dma("sp", xs[:], mem_d[t * 128:(t + 1) * 128, :], [], [xs])
                norm_to_T(ncx, xs, xs[:], memnT, t, t * 128)
            S.barrier()
            ck("A")

        def make_loader(w_in_d, gain, nslots, stack):
            wslots = [kb.sb(f"wslot{i}", [128, 8, 128], BF16, stack) for i in range(nslots)]
            rr = [0]

            def load_win(c0, n=128, q="sp", into=None, off=0):
                if into is None:
                    wsl = wslots[rr[0]]
                    rr[0] = (rr[0] + 1) % nslots
                else:
                    wsl = into
                load_w(wsl, wsl[:, :, off:off + n], win_cols(w_in_d, c0, n), n, gain=gain, q=q, key=off)
                return wsl
            return load_win

        def proj_fm(wsl, n, evac, woff=0):
            for tc in range(4):
                bk = kb.bank()
                for dc in range(8):
                    kb.mm(bk[0:n, :], wsl[:, dc, woff:woff + n], xnT[:, dc, tc * 512:(tc + 1) * 512], dc == 0, dc == 7,
                          [wsl, xnT], [bk])
                evac(bk, bk[0:n, :], tc)

        def mem_kv(w_kv_d, gain, kmT, vm, stack):
            wkv = kb.sb("wkv", [128, 8, 512], BF16, stack)
            for j in range(4):
                load_w(wkv, wkv[:, :, j * 128:(j + 1) * 128], win_cols(w_kv_d, j * 128, 128), 128, gain=gain, key=j)
            for h in range(4):
                bk = kb.bank()
                for dc in range(8):
                    kb.mm(bk[0:64, 0:256], wkv[:, dc, h * 64:(h + 1) * 64], memnT[:, dc, :], dc == 0, dc == 7,
                          [wkv, memnT], [bk])
                kb.cp("act", kmT[0:64, h, :], bk[0:64, 0:256], [bk], [(kmT, h)])
            for mt in range(2):
                bk = kb.bank()
                for dc in range(8):
                    kb.mm(bk[:, 0:256], memnT[:, dc, mt * 128:(mt + 1) * 128], wkv[:, dc, 256:512], dc == 0, dc == 7,
                          [wkv, memnT], [bk])
                kb.cp("act", vm[:, mt, :], bk[:, 0:256], [bk], [(vm, mt)])

        def mem_attn(load_win, colq, colz, kmT, vm, ymT, stack):
            qmT = kb.sb("qmT", [64, S_LEN], BF16, stack)
            szm = kb.sb("szm", [64, S_LEN], BF16, stack)
            PTm = [kb.sb(f"PTm{i}", [128, 512], BF16, stack) for i in range(4)]
            rdm = [kb.sb(f"rdm{i}", [64, 512], F32, stack) for i in range(2)]
            ptr = 0
            for h in range(4):
                wq = load_win(colq + h * 64, 64)
                proj_fm(wq, 64, lambda bk, ap, tc: kb.cp("act", qmT[0:64, tc * 512:(tc + 1) * 512], ap, [bk], [(qmT, tc)]))
                wz = load_win(colz + h * 64, 64)
                proj_fm(wz, 64, lambda bk, ap, tc: kb.act(szm[0:64, tc * 512:(tc + 1) * 512], ap, AF.Silu, [bk], [(szm, tc)]))
                for tc in range(4):
                    pts = []
                    for mt in range(2):
                        bk = kb.bank()
                        kb.mm(bk[:, :], kmT[0:64, h, mt * 128:(mt + 1) * 128], qmT[0:64, tc * 512:(tc + 1) * 512],
                              True, True, [kmT, (qmT, tc)], [bk])
                        pt = PTm[ptr % 4]
                        ptr += 1
                        kb.act(pt[:], bk[:], AF.Exp, [bk], [pt], scale=0.125)
                        pts.append(pt)
                    nb = kb.bank()
                    db = kb.bank()
                    for mt in range(2):
                        kb.mm(nb[0:64, :], vm[:, mt, h * 64:(h + 1) * 64], pts[mt][:], mt == 0, mt == 1, [vm, pts[mt]], [nb])
                    for mt in range(2):
                        kb.mm(db[0:64, :], onesb[:, 0:64], pts[mt][:], mt == 0, mt == 1, [onesb, pts[mt]], [db])
                    rd = rdm[tc % 2]
                    kb.recip(rd[:], db[0:64, :], [db], [rd])
                    kb.tt("dve", rd[:], nb[0:64, :], rd[:], ALU.mult, [nb, rd], [rd])
                    kb.tt("dve", ymT[0:64, h, tc * 512:(tc + 1) * 512], rd[:], szm[0:64, tc * 512:(tc + 1) * 512], ALU.mult,
                          [rd, (szm, tc)], [(ymT, (h, tc))])

        def out_proj(w_out_d, nch, yT, ymT, resid_d, resid_buf, final, stack, dbg=False):
            ncx = NormCtx(stack)
            WO = kb.sb("WO", [128, nch, 1024], BF16, stack)
            WOm = kb.sb("WOm", [64, 4, 1024], BF16, stack)
            wo_v = w_out_d[0:nch * 128, :].rearrange("(c p) n -> p c n", p=128)
            for c in range(nch):
                load_w(WO, WO[:, c:c + 1, :], wo_v[:, c:c + 1, :], 1024, key=c)
            wom_v = w_out_d[nch * 128:nch * 128 + 256, :].rearrange("(h p) n -> p h n", p=64)
            for h in range(4):
                load_w(WOm, WOm[0:64, h:h + 1, :], wom_v[:, h:h + 1, :], 1024, key=h, part=64)
            x1t = [kb.sb(f"x1t{i}", [128, 1024], F32, stack) for i in range(2)]
            if final:
                gF = kb.sb("gF", [128, 1024], F32, stack)
                kb.dma("sp", gF[:], final_g.partition_broadcast(128), [], [gF])
                ot = [kb.sb(f"ot{i}", [128, 1024], F32, stack) for i in range(2)]
            for t in range(NT_):
                xs = ncx.xstage[t % 2]
                kb.dma("sp", xs[:], resid_d[t * 128:(t + 1) * 128, :], [resid_buf], [xs])
                x1 = x1t[t % 2]
                for half in range(2):
                    bk = kb.bank()
                    for c in range(nch):
                        kb.mm(bk[:, :], yT[:, c, t * 128:(t + 1) * 128], WO[:, c, half * 512:(half + 1) * 512], c == 0, False,
                              [yT, WO], [bk])
                    for h in range(4):
                        kb.mm(bk[:, :], ymT[0:64, h, t * 128:(t + 1) * 128], WOm[0:64, h, half * 512:(half + 1) * 512], False, h == 3,
                              [ymT, WOm], [bk])
                    kb.tt("dve", x1[:, half * 512:(half + 1) * 512], xs[:, half * 512:(half + 1) * 512], bk[:], ALU.add,
                          [xs, bk], [(x1, half)])
                if not final:
                    kb.dma("sp", x1_scr[t * 128:(t + 1) * 128, :], x1[:], [x1], [DX1])
                    norm_to_T(ncx, x1, x1[:], xnT, t, t * 128)
                    if dbg:
                        kb.dma("sp", out_d[t * 128:(t + 1) * 128, :], x1[:], [x1], [DOUT])
                else:
                    rs, i = tile_rstd(ncx, x1, x1[:])
                    o = ot[t % 2]
                    kb.stt(o[:], x1[:], rs, gF[:], ALU.mult, ALU.mult, [x1, (stat, i), gF], [o])
                    kb.dma("sp", out_d[t * 128:(t + 1) * 128, :], o[:], [o], [DOUT])

        with ExitStack() as l0:
            yT = kb.sb("yT", [128, 12, S_LEN], BF16, l0)
            ymT = kb.sb("ymT", [64, 4, S_LEN], BF16, l0)
            gH = kb.sb("gH", [128, 8, 128], F32, l0)
            kb.dma("sp", gH[:], g_hawk, [], [gH])
            load_win = make_loader(hawk_w_in, gH, 6, l0)
            kmT = kb.sb("kmT", [64, 4, 256], BF16, l0)
            vm = kb.sb("vm", [128, 2, 256], BF16, l0)
            with ExitStack() as pm:
                gHm = kb.sb("gHm", [128, 8, 128], F32, pm)
                kb.dma("sp", gHm[:], g_hawk_mem, [], [gHm])
                mem_kv(hawk_w_mem_kv, gHm, kmT, vm, pm)
                S.barrier()
                ck("memkv0")
            with ExitStack() as pd:
                mem_attn(load_win, 7168, 7424, kmT, vm, ymT, pd)
                S.barrier()
                ck("mem0")

            with ExitStack() as pb:
                lv = kb.sb("lv", [128, 8, 8], F32, pb)
                cvec = kb.sb("cvec", [128, 8, 2], F32, pb)
                bda = kb.sb("bda", [128, 8, 128], BF16, pb)
                bdx = kb.sb("bdx", [128, 8, 128], BF16, pb)
                kb.dma("sp", lv[:], lru_vec, [], [lv])
                load_w(bda, bda[:], bd_a, 128)
                load_w(bdx, bdx[:], bd_x, 128)
                kb.act(cvec[:, :, 0], lv[:, :, 7], AF.Exp, [lv], [cvec], scale=-1.0)
                kb.act(cvec[:, :, 0], cvec[:, :, 0], AF.Ln, [cvec], [cvec], bias=1.0)
                kb.ts("dve", cvec[:, :, 1], cvec[:, :, 0], -16.0, None, ALU.mult, None, [cvec], [cvec])
                kb.ts("dve", cvec[:, :, 0], cvec[:, :, 0], -8.0, None, ALU.mult, None, [cvec], [cvec])
                B1 = kb.sb("B1", [128, S_LEN + 4], F32, pb)
                B2 = kb.sb("B2", [128, S_LEN], F32, pb)
                B3 = kb.sb("B3", [128, S_LEN], F32, pb)
                B4 = kb.sb("B4", [128, S_LEN], F32, pb)
                xcb = kb.sb("xcb", [128, S_LEN], BF16, pb)
                sz = kb.sb("sz", [128, S_LEN], BF16, pb)
                for c in range(8):
                    wxa = load_win(c * 128)
                    wza = load_win(1024 + c * 128)
                    kb.memset("dve", B1[:, 0:3], 0.0, [(B1, "pad")])
                    proj_fm(wxa, 128, lambda bk, ap, tc: kb.cp("act", B1[:, 3 + tc * 512:3 + (tc + 1) * 512], ap, [bk], [(B1, tc)]))
                    proj_fm(wza, 128, lambda bk, ap, tc: kb.act(sz[:, tc * 512:(tc + 1) * 512], ap, AF.Silu, [bk], [(sz, tc)]))
                    kb.ts("dve", B2[:], B1[:, 0:S_LEN], lv[:, c, 0:1], lv[:, c, 4:5], ALU.mult, ALU.add, [B1, lv], [B2])
                    for k in range(1, 4):
                        kb.stt(B2[:], B1[:, k:k + S_LEN], lv[:, c, k:k + 1], B2[:], ALU.mult, ALU.add, [B1, lv, B2], [B2])
                    kb.cp("pool", xcb[:], B2[:], [B2], [xcb])
                    for tc in range(4):
                        bk = kb.bank()
                        kb.mm(bk[:, :], bda[:, c, :], xcb[:, tc * 512:(tc + 1) * 512], True, True, [bda, xcb], [bk])
                        kb.act(B1[:, tc * 512:(tc + 1) * 512], bk[:], AF.Sigmoid, [bk, lv], [(B1, tc)], bias=lv[:, c, 5:6])
                    for tc in range(4):
                        bk = kb.bank()
                        kb.mm(bk[:, :], bdx[:, c, :], xcb[:, tc * 512:(tc + 1) * 512], True, True, [bdx, xcb], [bk])
                        kb.act(B4[:, tc * 512:(tc + 1) * 512], bk[:], AF.Sigmoid, [bk, lv], [(B4, tc)], bias=lv[:, c, 6:7])
                    r_ap = B1[:, 0:S_LEN]
                    kb.act(B3[:], r_ap, AF.Exp, [B1, cvec], [B3], scale=cvec[:, c, 0:1])
                    kb.act(r_ap, r_ap, AF.Exp, [B1, cvec], [B1], scale=cvec[:, c, 1:2])
                    kb.ts("dve", r_ap, r_ap, -1.0, 1.0, ALU.mult, ALU.add, [B1], [B1])
                    kb.ts("dve", r_ap, r_ap, 0.0, None, ALU.max, None, [B1], [B1])
                    kb.act(r_ap, r_ap, AF.Sqrt, [B1], [B1])
                    kb.memset("dve", B1[:, 0:1], 1.0, [B1])
                    kb.tt("dve", B2[:], B2[:], B4[:], ALU.mult, [B2, B4], [B2])
                    kb.tt("dve", B2[:], B2[:], r_ap, ALU.mult, [B2, B1], [B2])
                    S.op("dve", lambda e: e.tensor_tensor_scan(out=B4[:], data0=B3[:], data1=B2[:], initial=0.0,
                                                               op0=ALU.mult, op1=ALU.add), reads=[B3, B2], writes=[B4])
                    kb.tt("dve", yT[:, c, :], B4[:], sz[:], ALU.mult, [B4, sz], [(yT, c)])
                S.barrier()
                ck("lru")

            with ExitStack() as pc:
                edil = kb.sb("edil", [128, 12, 256], F32, pc)
                kb.dma("sp", edil[:], edil_d, [], [edil])
                qT = kb.sb("qT", [128, S_LEN], BF16, pc)
                kT = kb.sb("kT", [128, S_LEN], BF16, pc)
                Vp = kb.sb("Vp", [128, 16, 128], BF16, pc)
                szb = kb.sb("szb", [128, S_LEN], BF16, pc)
                NTa = kb.sb("NTa", [128, S_LEN], F32, pc)
                DBa = kb.sb("DBa", [128, S_LEN], F32, pc)
                Pf = [kb.sb(f"Pf{i}", [128, 256], F32, pc) for i in range(2)]
                PT = [kb.sb(f"PT{i}", [128, 256], BF16, pc) for i in range(2)]
                sc_d = 128.0 ** -0.5
                for hs in range(4):
                    wz = load_win(2048 + 4608 + hs * 128)
                    proj_fm(wz, 128, lambda bk, ap, tc: kb.act(szb[:, tc * 512:(tc + 1) * 512], ap, AF.Silu, [bk], [(szb, tc)]))
                    for g, (win, d) in enumerate(DIL_GROUPS):
                        hh = g * 4 + hs
                        sub = S_LEN // d
                        nqb = sub // 128
                        wq = load_win(2048 + hh * 128)
                        wk = load_win(2048 + 1536 + hh * 128)
                        wv = load_win(2048 + 3072 + hh * 128)
                        proj_fm(wq, 128, lambda bk, ap, tc: kb.cp("act", qT[:, tc * 512:(tc + 1) * 512], ap, [bk], [(qT, tc)]))
                        proj_fm(wk, 128, lambda bk, ap, tc: kb.cp("act", kT[:, tc * 512:(tc + 1) * 512], ap, [bk], [(kT, tc)]))

                        def toks(r, b):
                            t0 = r + d * 128 * b
                            return slice(t0, t0 + d * 127 + 1, d)

                        for ti in range(16):
                            r, b = divmod(ti, nqb)
                            bk = kb.bank()
                            for dc in range(8):
                                kb.mm(bk[:, 0:128], xnT[:, dc, toks(r, b)], wv[:, dc, :], dc == 0, dc == 7, [xnT, wv], [bk])
                            kb.cp("act", Vp[:, ti, :], bk[:, 0:128], [bk], [(Vp, ti)])
                        for ti in range(16):
                            r, qb = divmod(ti, nqb)
                            qs = toks(r, qb)
                            kbs = [qb - 1, qb] if qb > 0 else [qb]
                            sbk = kb.bank()
                            for kbi in kbs:
                                typ = 0 if kbi < qb else 1
                                kb.mm(sbk[:, typ * 128:(typ + 1) * 128], kT[:, toks(r, kbi)], qT[:, qs], True, True, [kT, qT], [sbk])
                            lo = 0 if qb > 0 else 128
                            pf = Pf[ti % 2]
                            pt = PT[ti % 2]
                            kb.act(pf[:, lo:256], sbk[:, lo:256], AF.Exp, [sbk], [pf], scale=sc_d)
                            kb.tt("dve", pt[:, lo:256], pf[:, lo:256], edil[:, hh, lo:256], ALU.mult, [pf, edil], [pt])
                            nb = kb.bank()
                            db = kb.bank()
                            for j, kbi in enumerate(kbs):
                                typ = 0 if kbi < qb else 1
                                kb.mm(nb[:, 0:128], Vp[:, r * nqb + kbi, :], pt[:, typ * 128:(typ + 1) * 128], j == 0, j == len(kbs) - 1,
                                      [Vp, pt], [nb])
                            for j, kbi in enumerate(kbs):
                                typ = 0 if kbi < qb else 1
                                kb.mm(db[:, 0:128], onesb[:], pt[:, typ * 128:(typ + 1) * 128], j == 0, j == len(kbs) - 1,
                                      [onesb, pt], [db])
                            if g == 0:
                                kb.cp("dve", NTa[:, qs], nb[:, 0:128], [nb], [NTa])
                                kb.cp("dve", DBa[:, qs], db[:, 0:128], [db], [DBa])
                            else:
                                kb.tt("dve", NTa[:, qs], NTa[:, qs], nb[:, 0:128], ALU.add, [nb, NTa], [NTa])
                                kb.tt("dve", DBa[:, qs], DBa[:, qs], db[:, 0:128], ALU.add, [db, DBa], [DBa])
                    kb.recip(DBa[:], DBa[:], [DBa], [DBa])
                    kb.tt("dve", NTa[:], NTa[:], DBa[:], ALU.mult, [NTa, DBa], [NTa])
                    kb.tt("dve", yT[:, 8 + hs, :], NTa[:], szb[:], ALU.mult, [NTa, szb], [(yT, 8 + hs)])
                S.barrier()
                ck("dil")

            with ExitStack() as pe_:
                out_proj(hawk_w_out, 12, yT, ymT, x_d, DX, False, pe_, dbg=(stop_after == "l0"))
                S.barrier()
                ck("l0end")

        if stop_after != "l0":
          with ExitStack() as l1:
            yT1 = kb.sb("yT1", [128, 8, S_LEN], BF16, l1)
            ymT1 = kb.sb("ymT1", [64, 4, S_LEN], BF16, l1)
            gN = kb.sb("gN", [128, 8, 128], F32, l1)
            kb.dma("sp", gN[:], g_nsa, [], [gN])
            load_win = make_loader(nsa_w_in, gN, 3, l1)
            kcmpT = kb.sb("kcmpT", [64, 2, 128], BF16, l1)
            vcmp = kb.sb("vcmp", [128, 2, 64], BF16, l1)
            gates = kb.sb("gates", [128, 16, 48], F32, l1)
            with ExitStack() as pmm:
                kmT = kb.sb("kmT1", [64, 4, 256], BF16, pmm)
                vm = kb.sb("vm1", [128, 2, 256], BF16, pmm)
                with ExitStack() as pm:
                    gNm = kb.sb("gNm", [128, 8, 128], F32, pm)
                    kb.dma("sp", gNm[:], g_nsa_mem, [], [gNm])
                    mem_kv(nsa_w_mem_kv, gNm, kmT, vm, pm)
                    S.barrier()
                    ck("memkv1")
                with ExitStack() as pd:
                    mem_attn(load_win, 2864, 3120, kmT, vm, ymT1, pd)
                    S.barrier()
                    ck("mem1")

            with ExitStack() as pq:
                wg = load_win(1792, 48)
                for t in range(NT_):
                    bk = kb.bank()
                    for dc in range(8):
                        kb.mm(bk[:, 0:48], xnT[:, dc, t * 128:(t + 1) * 128], wg[:, dc, 0:48], dc == 0, dc == 7, [xnT, wg], [bk])
                    kb.act(gates[:, t, :], bk[:, 0:48], AF.Sigmoid, [bk], [(gates, t)])
                kcT = kb.sb("kcT", [128, S_LEN], BF16, pq)
                vcT = kb.sb("vcT", [128, S_LEN], BF16, pq)
                wkc = load_win(1024)
                proj_fm(wkc, 128, lambda bk, ap, tc: kb.cp("act", kcT[:, tc * 512:(tc + 1) * 512], ap, [bk], [(kcT, tc)]))
                wvc = load_win(1024 + 128)
                proj_fm(wvc, 128, lambda bk, ap, tc: kb.cp("act", vcT[:, tc * 512:(tc + 1) * 512], ap, [bk], [(vcT, tc)]))
                W1 = kb.sb("W1", [128, 32, 256], BF16, pq)
                w2 = kb.sb("w2", [128, 2, 64], BF16, pq)
                peS = kb.sb("peS", [64, 2, 32], F32, pq)
                peb = kb.sb("peb", [64, 2, 32], BF16, pq)
                hidT = kb.sb("hidT", [128, 2, 128], BF16, pq)
                cb = kb.sb("cb", [128, 2], F32, pq)
                kb.dma("sp", peS[:], peT_d, [], [peS])
                kb.cp("dve", peb[:], peS[:], [peS], [peb])
                for kv in range(2):
                    w1d = w1k_d if kv == 0 else w1v_d
                    w2d = w2k_d if kv == 0 else w2v_d
                    srcT = kcT if kv == 0 else vcT
                    for p4 in range(8):
                        load_w(W1, W1[:, p4 * 4:(p4 + 1) * 4, :], w1d[:, p4 * 4:(p4 + 1) * 4, :], 256, key=p4)
                    load_w(w2, w2[:, :, :], w2d.rearrange("(hc p) d -> p hc d", p=128), 64)
                    for hc in range(2):
                        bk = kb.bank()
                        for p in range(32):
                            kb.mm(bk[:, 0:1], W1[0:64, p, hc * 128:(hc + 1) * 128], peb[0:64, kv, p:p + 1], p == 0, p == 31,
                                  [W1, peb], [bk])
                        kb.cp("dve", cb[:, hc:hc + 1], bk[:, 0:1], [bk], [(cb, hc)])
                    for g in range(2):
                        for hc in range(2):
                            bk = kb.bank()
                            for p in range(32):
                                kb.mm(bk[:, 0:127], W1[g * 64:(g + 1) * 64, p, hc * 128:(hc + 1) * 128],
                                      srcT[g * 64:(g + 1) * 64, p:p + 16 * 126 + 1:16], p == 0, p == 31, [W1, srcT], [bk])
                            kb.act(hidT[:, hc, 0:127], bk[:, 0:127], AF.Silu, [bk, cb], [(hidT, hc)], bias=cb[:, hc:hc + 1])
                        bk = kb.bank()
                        if kv == 0:
                            for hc in range(2):
                                kb.mm(bk[0:64, 0:127], w2[:, hc, :], hidT[:, hc, 0:127], hc == 0, hc == 1, [w2, hidT], [bk])
                            kb.cp("dve", kcmpT[0:64, g, 0:127], bk[0:64, 0:127], [bk], [(kcmpT, g)])
                        else:
                            for hc in range(2):
                                kb.mm(bk[0:127, 0:64], hidT[:, hc, 0:127], w2[:, hc, :], hc == 0, hc == 1, [w2, hidT], [bk])
                            kb.cp("dve", vcmp[0:127, g, :], bk[0:127, 0:64], [bk], [(vcmp, g)])
                S.barrier()
                ck("cmpkv")

            with ExitStack() as pg:
                QAg = kb.sb("QAg", [105, 8, S_LEN], BF16, pg)
                KAs = kb.sb("KAs", [105, S_LEN], BF16, pg)
                KAw = kb.sb("KAw", [105, S_LEN], BF16, pg)
                VAs = kb.sb("VAs", [128, 16, 128], BF16, pg)
                VAw = kb.sb("VAw", [128, 16, 128], BF16, pg)
                ecmp = kb.sb("ecmp", [128, 8, 247], F32, pg)
                Wz = kb.sb("Wz", [128, 8, 512], BF16, pg)
                m12 = kb.sb("m12", [128, 2, 62], F32, pg)
                trib = kb.sb("trib", [128, 2, 512], BF16, pg)
                kb.dma("sp", m12[:], m12_d, [], [m12])
                for v in range(2):
                    stg = next_ws()
                    kb.dma("sp", stg[:, 0:512], tri_d[:, v, :], [], [stg])
                    kb.cp("pool", trib[:, v, :], stg[:, 0:512], [stg], [(trib, v)])
                for v, KA in enumerate((KAs, KAw)):
                    for hf in range(2):
                        stg = next_ws()
                        kb.dma("sp", stg[64:105, 0:1024], kaug_d[v][:, hf * 1024:(hf + 1) * 1024], [], [stg])
                        kb.cp("pool", KA[64:105, hf * 1024:(hf + 1) * 1024], stg[64:105, 0:1024], [stg], [(KA, ("aug", hf))])
                kb.memset("pool", VAs[:, :, 64:128], 1.0, [(VAs, "ones")])
                kb.memset("pool", VAw[:, :, 64:128], 1.0, [(VAw, "ones")])
                Pc = [kb.sb(f"Pc{i}", [128, 4, 128], F32, pg) for i in range(2)]
                Pu = [kb.sb(f"Pu{i}", [128, 4, 128], F32, pg) for i in range(2)]
                Pub = [kb.sb(f"Pub{i}", [128, 4, 128], BF16, pg) for i in range(2)]
                pT = [kb.sb(f"pT{i}", [128, 4, 128], BF16, pg) for i in range(2)]
                for i in range(2):
                    kb.memset("pool", Pu[i][:], 0.0, [Pu[i]])
                psg = kb.sb("psg", [128, 128], F32, pg)
                den8 = kb.sb("den8", [128, 8], F32, pg)
                cg8 = kb.sb("cg8", [128, 8], F32, pg)
                imp = kb.sb("imp", [128, 32], F32, pg)
                impm = kb.sb("impm", [128, 32], F32, pg)
                m8 = kb.sb("m8", [128, 8], F32, pg)
                negp = kb.sb("negp", [128, 96], F32, pg)
                negS = kb.sb("negS", [96, 128], BF16, pg)
                kb.memset("pool", negp[:], 0.0, [negp])
                PTs = [kb.sb(f"PTs{i}", [128, 512], BF16, pg) for i in range(6)]
                pts_rr = [0]
                accs = [kb.sb(f"accs{i}", [128, 512], F32, pg) for i in range(2)]
                rd4 = [kb.sb(f"rd4{i}", [128, 4], F32, pg) for i in range(2)]
                cg4 = [kb.sb(f"cg4{i}", [128, 4], F32, pg) for i in range(2)]
                Oa = [kb.sb(f"Oa{i}", [128, 512], F32, pg) for i in range(2)]
                szt = kb.sb("szt", [128, 512], F32, pg)
                Ob = kb.sb("Ob", [128, 512], BF16, pg)
                accbanks = [kb.banks[0], kb.banks[1]]
                rot = [2]

                def rbank():
                    b = kb.banks[rot[0]]
                    rot[0] = rot[0] + 1 if rot[0] < 7 else 2
                    return b

                strot = [0, 0]

                def stbank(lane):
                    b = kb.banks[2 + 2 * lane + strot[lane]]
                    strot[lane] ^= 1
                    return b

                def mbank():
                    return kb.banks[6]

                ptrot = [0, 0]

                def next_pt(lane):
                    p = PTs[3 * lane + ptrot[lane]]
                    ptrot[lane] = (ptrot[lane] + 1) % 3
                    return p

                for g in range(2):
                    kb.dma("sp", ecmp[:], ecmp_d[:, g * 8:(g + 1) * 8, :], [], [ecmp])
                    for j in range(4):
                        load_w(Wz, Wz[:, :, j * 128:(j + 1) * 128], win_cols(nsa_w_in, 1840 + g * 512 + j * 128, 128), 128,
                               gain=gN, key=j)
                    for r in range(8):
                        hq = g * 8 + r
                        wq = load_win(hq * 64, 64)
                        for tc in range(4):
                            bk = rbank()
                            for dc in range(8):
                                kb.mm(bk[0:64, :], wq[:, dc, 0:64], xnT[:, dc, tc * 512:(tc + 1) * 512], dc == 0, dc == 7, [wq, xnT], [bk])
                            kb.cp("act", QAg[0:64, r, tc * 512:(tc + 1) * 512], bk[0:64, :], [bk], [QAg])
                        for hf in range(2):
                            stg = next_ws()
                            kb.dma("sp", stg[96:105, 0:1024], qal_d[hq][:, hf * 1024:(hf + 1) * 1024], [], [stg])
                            kb.cp("pool", QAg[96:105, r, hf * 1024:(hf + 1) * 1024], stg[96:105, 0:1024], [stg], [QAg])
                    for KA, col in ((KAs, 1024 + 2 * 128 + g * 64), (KAw, 1024 + 4 * 128 + g * 64)):
                        wk = load_win(col, 64)
                        for tc in range(4):
                            bk = rbank()
                            for dc in range(8):
                                kb.mm(bk[0:64, :], wk[:, dc, 0:64], xnT[:, dc, tc * 512:(tc + 1) * 512], dc == 0, dc == 7, [wk, xnT], [bk])
                            kb.cp("act", KA[0:64, tc * 512:(tc + 1) * 512], bk[0:64, :], [bk], [(KA, tc)])
                    wv = load_win(1024 + 3 * 128 + g * 64, 64)
                    load_win(1024 + 5 * 128 + g * 64, 64, into=wv, off=64)
                    for t in range(NT_):
                        bk = rbank()
                        for dc in range(8):
                            kb.mm(bk[:, 0:128], xnT[:, dc, t * 128:(t + 1) * 128], wv[:, dc, 0:128], dc == 0, dc == 7, [xnT, wv], [bk])
                        kb.cp("act", VAs[:, t, 0:64], bk[:, 0:64], [bk], [(VAs, t)])
                        kb.cp("dve", VAw[:, t, 0:64], bk[:, 64:128], [bk], [(VAw, t)])

                    def task_C(qt):
                        qc = slice(qt * 128, (qt + 1) * 128)
                        O = Oa[qt % 2]
                        gq = gates[:, qt, :]
                        eoff = 120 - 8 * qt
                        ocb = kb.banks[7]
                        for b4 in range(2):
                            sbk = mbank()
                            for hl in range(4):
                                r = b4 * 4 + hl
                                kb.mm(sbk[:, hl * 128:hl * 128 + 127], QAg[0:64, r, qc], kcmpT[0:64, g, 0:127], True, True,
                                      [(QAg, qt), kcmpT], [sbk])
                            yield
                            pc = Pc[b4]
                            pu = Pu[b4]
                            s3 = sbk[:].rearrange("p (h n) -> p h n", h=4)
                            kb.act(pc[:, :, 0:127], s3[:, :, 0:127], AF.Exp, [sbk], [pc], scale=0.125)
                            yield
                            kb.tt("dve", pu[:, :, 0:127], pc[:, :, 0:127], ecmp[:, b4 * 4:(b4 + 1) * 4, eoff:eoff + 127], ALU.mult,
                                  [pc, ecmp], [pu])
                            S.op("dve", lambda e, pu=pu, b4=b4: e.tensor_reduce(out=den8[:, b4 * 4:(b4 + 1) * 4], in_=pu[:, :, 0:127],
                                                                                axis=AX.X, op=ALU.add),
                                 reads=[pu], writes=[(den8, b4)])
                            yield
                            kb.ts("dve", den8[:, b4 * 4:(b4 + 1) * 4], den8[:, b4 * 4:(b4 + 1) * 4], 1e-30, None, ALU.max, None,
                                  [(den8, b4)], [(den8, b4)])
                            kb.recip(den8[:, b4 * 4:(b4 + 1) * 4], den8[:, b4 * 4:(b4 + 1) * 4], [(den8, b4)], [(den8, b4)])
                            pub = Pub[b4]
                            kb.cp("pool", pub[:], pu[:], [pu], [pub])
                            yield
                            for hl in range(4):
                                r = b4 * 4 + hl
                                if r == 0:
                                    kb.ts("dve", psg[:, :], pu[:, hl, :], den8[:, r:r + 1], None, ALU.mult, None, [pu, (den8, b4)], [psg])
                                else:
                                    kb.stt(psg[:, :], pu[:, hl, :], den8[:, r:r + 1], psg[:, :], ALU.mult, ALU.add,
                                           [pu, (den8, b4), psg], [psg])
                                if hl % 2 == 1:
                                    yield
                            tbk = mbank()
                            tv = tbk[:].bitcast(BF16)
                            for hl in range(4):
                                kb.tr(tv[0:127, hl * 128:(hl + 1) * 128], pub[:, hl, 0:127], identb[:], [pub, identb], [tbk])
                            yield
                            ptt = pT[b4]
                            kb.cp("act", ptt[0:127, :, :], tv[0:127, 0:512].rearrange("p (h n) -> p h n", h=4), [tbk], [ptt])
                            yield
                            for hl in range(4):
                                r = b4 * 4 + hl
                                kb.mm(ocb[:, r * 64:(r + 1) * 64], ptt[0:127, hl, :], vcmp[0:127, g, :], True, True, [ptt, vcmp], [ocb])
                            yield
                        kb.tt("dve", cg8[:], den8[:], gq[:, g * 24:g * 24 + 24:3], ALU.mult, [den8, gates], [cg8])
                        kb.tt("dve", O[:].rearrange("p (h d) -> p h d", h=8), ocb[:].rearrange("p (h d) -> p h d", h=8),
                              cg8[:, 0:8].unsqueeze(2).to_broadcast([128, 8, 64]), ALU.mult, [ocb, cg8], [O])
                        yield
                        S.op("dve", lambda e: e.tensor_reduce(out=imp[:, :], in_=psg[:].rearrange("p (j a) -> p j a", a=4),
                                                              axis=AX.X, op=ALU.add), reads=[psg], writes=[imp])
                        kb.tt("dve", imp[:, 1:32], imp[:, 1:32], psg[:, 3:127:4], ALU.add, [imp, psg], [imp])
                        yield
                        moff = 30 - 2 * qt
                        kb.tt("dve", impm[:], imp[:], m12[:, 0, moff:moff + 32], ALU.mult, [imp, m12], [impm])
                        kb.tt("dve", impm[:], impm[:], m12[:, 1, moff:moff + 32], ALU.add, [impm, m12], [impm])
                        kb.memset("dve", impm[:, 0:1], 1e6, [impm])
                        yield
                        S.op("dve", lambda e: e.max(out=m8[:], in_=impm[:]), reads=[impm], writes=[m8])
                        kb.ts("dve", negp[:, 64:96], impm[:], m8[:, 7:8], 1.0, ALU.is_ge, ALU.subtract, [impm, m8], [negp])
                        kb.ts("dve", negp[:, 64:96], negp[:, 64:96], NEGB, None, ALU.mult, None, [negp], [negp])
                        yield
                        tbk = mbank()
                        kb.tr(tbk[0:96, 0:128], negp[:, 0:96], identf[:], [negp, identf], [tbk])
                        yield
                        kb.cp("dve", negS[64:96, :], tbk[64:96, 0:128], [tbk], [negS])
                        kb.cp("dve", QAg[64:96, :, qc], negS[64:96, :].unsqueeze(1).to_broadcast([32, 8, 128]), [negS], [(QAg, qt)])
                        yield

                    def task_branch(qt, br, b4):
                        qc = slice(qt * 128, (qt + 1) * 128)
                        O = Oa[qt % 2]
                        gq = gates[:, qt, :]
                        KA, VA = (KAw, VAw) if br == 2 else (KAs, VAs)
                        kbs = list(range(max(0, qt - 4), qt + 1)) if br == 2 else list(range(0, qt + 1))
                        accb = kb.banks[b4]
                        prev = None
                        for idx, kbi in enumerate(kbs):
                            sbk = stbank(b4)
                            masks = []
                            if kbi == qt:
                                masks.append(0)
                            if br == 2 and kbi == qt - 4:
                                masks.append(1)
                            kb.mm(sbk[:, :], KA[0:105, kbi * 128:(kbi + 1) * 128], QAg[0:105, b4 * 4:(b4 + 1) * 4, qc],
                                  True, len(masks) == 0, [KA, (QAg, qt)], [sbk])
                            for mi, mv in enumerate(masks):
                                kb.mm(sbk[:, :], identb[:], trib[:, mv, :], False, mi == len(masks) - 1, [identb, trib], [sbk])
                            pt = next_pt(b4)
                            kb.act(pt[:], sbk[:], AF.Exp, [sbk], [pt], scale=0.125)
                            yield
                            if prev is not None:
                                pi, pk, ppt = prev
                                kb.mm(accb[:, :], VA[:, pk, :], ppt[:], pi == 0, False, [VA, ppt], [accb])
                                yield
                            prev = (idx, kbi, pt)
                        pi, pk, ppt = prev
                        kb.mm(accb[:, :], VA[:, pk, :], ppt[:], pi == 0, True, [VA, ppt], [accb])
                        yield
                        ac = accs[b4]
                        kb.cp("act", ac[:], accb[:], [accb], [ac])
                        yield
                        tbk = stbank(b4)
                        for hl in range(4):
                            kb.tr(tbk[:, hl * 128:(hl + 1) * 128], ac[:, hl * 128:(hl + 1) * 128], identf[:], [ac, identf], [tbk])
                        yield
                        t3 = tbk[:].rearrange("p (h n) -> p h n", h=4)
                        kb.recip(rd4[b4][:], t3[:, :, 64], [tbk], [rd4[b4]])
                        h0 = (g * 8 + b4 * 4) * 3 + br
                        kb.tt("dve", cg4[b4][:], rd4[b4][:], gq[:, h0:h0 + 10:3], ALU.mult, [rd4[b4], gates], [cg4[b4]])
                        yield
                        for hl in range(4):
                            r = b4 * 4 + hl
                            kb.stt(O[:, r * 64:(r + 1) * 64], tbk[:, hl * 128:hl * 128 + 64], cg4[b4][:, hl:hl + 1], O[:, r * 64:(r + 1) * 64],
                                   ALU.mult, ALU.add, [tbk, cg4[b4], (O, b4)], [(O, b4)])
                            if hl % 2 == 1:
                                yield

                    def task_Z(qt):
                        qc = slice(qt * 128, (qt + 1) * 128)
                        O = Oa[qt % 2]
                        zb = mbank()
                        for dc in range(8):
                            kb.mm(zb[:, :], xnT[:, dc, qc], Wz[:, dc, :], dc == 0, dc == 7, [xnT, Wz], [zb])
                            if dc % 4 == 3:
                                yield
                        kb.act(szt[:], zb[:], AF.Silu, [zb], [szt])
                        yield
                        kb.tt("dve", Ob[:], O[:], szt[:], ALU.mult, [O, szt], [Ob])
                        yield
                        tbk = mbank()
                        tv = tbk[:].bitcast(BF16)
                        for c4 in range(4):
                            kb.tr(tv[:, c4 * 128:(c4 + 1) * 128], Ob[:, c4 * 128:(c4 + 1) * 128], identb[:], [Ob, identb], [tbk])
                        yield
                        kb.cp("act", yT1[:, g * 4:(g + 1) * 4, qc], tv[:, 0:512].rearrange("p (c n) -> p c n", c=4), [tbk], [(yT1, (g, qt))])
                        yield

                    run_lanes([task_C(0)])
                    for qt in range(NT_):
                        l1_ = chain(task_branch(qt, 2, 0), task_branch(qt, 1, 0))
                        l2_ = chain(task_branch(qt, 2, 1), task_branch(qt, 1, 1))
                        third = []
                        if qt > 0:
                            third.append(task_Z(qt - 1))
                        if qt + 1 < NT_:
                            third.append(task_C(qt + 1))
                        run_lanes([l1_, l2_, chain(*third)])
                    run_lanes([task_Z(NT_ - 1)])
                S.barrier()
                ck("nsa")

            with ExitStack() as pe_:
                out_proj(nsa_w_out, 8, yT1, ymT1, x1_scr, DX1, True, pe_)
                S.barrier()
                ck("l1end")

        S.dead = False
        S.barrier()
        with nc.Block() as block:
            S.emit(block)
        print("program ops:", S.nops, "sems:", S.nsem)
    return nc


_CONST = None


def prep_inputs(inp):
    global _CONST
    if _CONST is None:
        _CONST = host_constants()
        _CONST.update(host_constants_nsa())
    f = lambda a: np.ascontiguousarray(np.asarray(a, dtype=np.float32))
    shared = {
        "hawk_w_in": f(inp["hawk_w_in"][0]),
        "hawk_w_out": f(inp["hawk_w_out"][0]),
        "hawk_w_mem_kv": f(inp["hawk_w_mem_kv"][0]),
        "g_hawk": expand_gain(f(inp["hawk_norm"][0])),
        "g_hawk_mem": expand_gain(f(inp["hawk_mem_norm"][0])),
        "bd_a": block_diag(f(inp["hawk_gate_a_w"][0])),
        "bd_x": block_diag(f(inp["hawk_gate_x_w"][0])),
        "final_norm": f(inp["final_norm"]),
        "nsa_w_in": f(inp["nsa_w_in"][0]),
        "nsa_w_out": f(inp["nsa_w_out"][0]),
        "nsa_w_mem_kv": f(inp["nsa_w_mem_kv"][0]),
        "g_nsa": expand_gain(f(inp["nsa_norm"][0])),
        "g_nsa_mem": expand_gain(f(inp["nsa_mem_norm"][0])),
        "w2k": f(inp["nsa_phi_k_w2"][0]),
        "w2v": f(inp["nsa_phi_v_w2"][0]),
    }
    for k in ("identf", "edil", "ecmp", "m12", "tri", "kaug", "qal"):
        shared[k] = _CONST[k]

    def w1_layout(w1):
        a = w1.reshape(32, 64, 256).transpose(1, 0, 2)
        return np.ascontiguousarray(np.concatenate([a, a], axis=0))
    shared["w1k"] = w1_layout(f(inp["nsa_phi_k_w1"][0]))
    shared["w1v"] = w1_layout(f(inp["nsa_phi_v_w1"][0]))
    shared["peT"] = np.ascontiguousarray(np.stack([f(inp["nsa_pe_k"][0]).T, f(inp["nsa_pe_v"][0]).T], axis=1))
    lv = np.zeros((128, 8, 8), np.float32)
    cw = f(inp["hawk_conv_w"][0])
    for k in range(4):
        lv[:, :, k] = vec_fm(cw[k])
    lv[:, :, 4] = vec_fm(f(inp["hawk_conv_b"][0]))
    lv[:, :, 5] = vec_fm(f(inp["hawk_gate_a_b"][0]).reshape(-1))
    lv[:, :, 6] = vec_fm(f(inp["hawk_gate_x_b"][0]).reshape(-1))
    lv[:, :, 7] = vec_fm(f(inp["hawk_lambda"][0]))
    shared["lru_vec"] = lv
    x = f(inp["x"])
    mem = f(inp["mem"])
    maps = []
    for b in range(x.shape[0]):
        m = dict(shared)
        m["x"] = x[b]
        m["mem"] = mem[b]
        maps.append(m)
    return maps


def kernel(**inputs):
    maps = prep_inputs(inputs)
    nc = build_program()
    res = run_bass_kernel_spmd(nc, maps, core_ids=list(range(len(maps))))
    out = np.stack([np.asarray(r["out"], dtype=np.float32) for r in res.results], axis=0)
    return out
```

```python
import math
from contextlib import ExitStack

import numpy as np
import concourse.bass as bass
import concourse.mybir as mybir
from concourse.bass_utils import run_bass_kernel_spmd

F32 = mybir.dt.float32
BF16 = mybir.dt.bfloat16
AF = mybir.ActivationFunctionType
ALU = mybir.AluOpType
AX = mybir.AxisListType

S_LEN = 2048
D = 1024
NT_ = 16
EPS = 1e-6
DIL_GROUPS = ((128, 1), (512, 4), (2048, 16))

SEM_LIMIT = 30000
N_DMA_SEMS = 24
SAME_ENGINE_SYNC = True


class Buf:
    def __init__(self, name, t, excl=False):
        self.name = name
        self.t = t
        self.excl = excl
        self.st = {}

    def __getitem__(self, idx):
        return self.t[idx]


class Sync:
    def __init__(self, nc, stack):
        self.nc = nc
        self.stack = stack
        self.engs = ["pe", "act", "dve", "pool", "sp"]
        self.ops = {e: [] for e in self.engs}
        self.cur_sem = {}
        self.cnt = {}
        self.nsem = 0
        for e in self.engs:
            self._new_sem(e)
        self.dma_sems = {}
        self.dma_val = {}
        self.dma_rr = {}
        for e in ["sp", "pool", "act"]:
            self.dma_sems[e] = [self._alloc_sem(f"d{e}{i}") for i in range(N_DMA_SEMS)]
            self.dma_val[e] = [0] * N_DMA_SEMS
            self.dma_rr[e] = 0
        self.seen = {e: {} for e in self.engs}
        self.all_ticks = {}
        self.nops = 0
        self.dead = False

    def _alloc_sem(self, name):
        self.nsem += 1
        return self.stack.enter_context(self.nc.semaphore(f"s_{name}_{self.nsem}"))

    def _new_sem(self, e):
        self.cur_sem[e] = self._alloc_sem(e)
        self.cnt[e] = 0

    def _states(self, buf, key, create):
        if key is None:
            if create and None not in buf.st:
                buf.st[None] = [None, {}]
            return list(buf.st.values())
        out = []
        if None in buf.st:
            out.append(buf.st[None])
        if key not in buf.st and create:
            buf.st[key] = [None, {}]
        if key in buf.st:
            out.append(buf.st[key])
        return out

    @staticmethod
    def _norm(lst):
        out = []
        for r in lst or []:
            out.append(r if isinstance(r, tuple) else (r, None))
        return out

    def op(self, eng, fn, reads=None, writes=None, dma=False):
        if self.dead:
            return None
        reads = self._norm(reads)
        writes = self._norm(writes)
        ex = [(b, None) for (b, k) in reads + writes if b.excl]
        if ex:
            reads = [(b, k) for (b, k) in reads if not b.excl]
            writes = [(b, k) for (b, k) in writes if not b.excl]
            for bk in ex:
                if bk not in writes:
                    writes.append(bk)
        need = []
        for buf, key in reads:
            for st in self._states(buf, key, False):
                if st[0] is not None:
                    need.append(st[0])
        for buf, key in writes:
            for st in self._states(buf, key, False):
                if st[0] is not None:
                    need.append(st[0])
                need.extend(st[1].values())
        if dma:
            i = self.dma_rr[eng]
            self.dma_rr[eng] = (i + 1) % N_DMA_SEMS
            sem = self.dma_sems[eng][i]
            prev = self.dma_val[eng][i]
            if prev > 0:
                need.append((sem, prev, "dma"))
            if prev + 16 > SEM_LIMIT:
                sem = self._alloc_sem(f"d{eng}{i}")
                self.dma_sems[eng][i] = sem
                prev = 0
            val = prev + 16
            self.dma_val[eng][i] = val
            inc = 16
            tick = (sem, val, "dma")
        else:
            if self.cnt[eng] + 1 > SEM_LIMIT:
                self._new_sem(eng)
            self.cnt[eng] += 1
            sem = self.cur_sem[eng]
            val = self.cnt[eng]
            inc = 1
            tick = (sem, val, eng)
        waits = {}
        seen = self.seen[eng]
        for (s, v, src) in need:
            if src == eng and (eng == "pe" or not SAME_ENGINE_SYNC):
                continue
            sid = id(s)
            if seen.get(sid, 0) >= v:
                continue
            if sid not in waits or waits[sid][1] < v:
                waits[sid] = (s, v)
        for sid, (s, v) in waits.items():
            seen[sid] = v
        self.ops[eng].append((list(waits.values()), fn, sem, inc))
        self.all_ticks[id(sem)] = (sem, val)
        self.nops += 1
        wset = set((id(b), k) for b, k in writes)
        for buf, key in reads:
            if (id(buf), key) in wset:
                continue
            self._states(buf, key, True)
            buf.st[key][1][eng if not dma else ("dma", id(sem))] = tick
        for buf, key in writes:
            if key is None:
                buf.st = {None: [tick, {}]}
            else:
                buf.st[key] = [tick, {}]
        return tick

    def barrier(self):
        if self.dead:
            return
        ticks = list(self.all_ticks.values())
        for e in self.engs:
            wl = []
            for (s, v) in ticks:
                if self.seen[e].get(id(s), 0) < v:
                    wl.append((s, v))
                    self.seen[e][id(s)] = v
            if wl:
                self.ops[e].append((wl, None, None, 0))

    def emit(self, block):
        S = self

        def run(engname, e):
            for (wl, fn, sem, inc) in S.ops[engname]:
                for (s, v) in wl:
                    e.wait_ge(s, v)
                if fn is not None:
                    fn(e).then_inc(sem, inc)

        @block.sync
        def _(e):
            run("sp", e)

        @block.tensor
        def _(e):
            run("pe", e)

        @block.scalar
        def _(e):
            run("act", e)

        @block.vector
        def _(e):
            run("dve", e)

        @block.gpsimd
        def _(e):
            run("pool", e)


class KB:
    def __init__(self, nc, stack):
        self.nc = nc
        self.gst = stack
        self.S = Sync(nc, stack)
        self.banks = [Buf(f"bank{i}", stack.enter_context(nc.psum_tensor(f"bank{i}", [128, 512], F32)), excl=True)
                      for i in range(8)]
        self.bank_rr = 0
        self.uid = 0

    def sb(self, name, shape, dt, stack=None):
        self.uid += 1
        t = (stack or self.gst).enter_context(self.nc.sbuf_tensor(f"{name}_{self.uid}", shape, dt))
        return Buf(name, t)

    def bank(self):
        b = self.banks[self.bank_rr]
        self.bank_rr = (self.bank_rr + 1) % 8
        return b

    def mm(self, out, lhsT, rhs, start, stop, r, w):
        self.S.op("pe", lambda e: e.matmul(out, lhsT=lhsT, rhs=rhs, start=start, stop=stop), reads=r, writes=w)

    def tr(self, out, in_, ident, r, w):
        self.S.op("pe", lambda e: e.transpose(out, in_, ident), reads=r, writes=w)

    def act(self, out, in_, func, r, w, **kw):
        self.S.op("act", lambda e: e.activation(out=out, in_=in_, func=func, **kw), reads=r, writes=w)

    def tt(self, eng, out, in0, in1, op, r, w):
        self.S.op(eng, lambda e: e.tensor_tensor(out=out, in0=in0, in1=in1, op=op), reads=r, writes=w)

    def ts(self, eng, out, in0, s1, s2, op0, op1, r, w, **kw):
        if op1 is None:
            self.S.op(eng, lambda e: e.tensor_scalar(out=out, in0=in0, scalar1=s1, scalar2=None, op0=op0, **kw), reads=r, writes=w)
        else:
            self.S.op(eng, lambda e: e.tensor_scalar(out=out, in0=in0, scalar1=s1, scalar2=s2, op0=op0, op1=op1, **kw), reads=r, writes=w)

    def stt(self, out, in0, scalar, in1, op0, op1, r, w, **kw):
        self.S.op("dve", lambda e: e.scalar_tensor_tensor(out=out, in0=in0, scalar=scalar, in1=in1, op0=op0, op1=op1, **kw), reads=r, writes=w)

    def cp(self, eng, out, in_, r, w):
        if eng == "act":
            self.S.op("act", lambda e: e.activation(out=out, in_=in_, func=AF.Copy), reads=r, writes=w)
        else:
            self.S.op(eng, lambda e: e.tensor_copy(out=out, in_=in_), reads=r, writes=w)

    def memset(self, eng, ap, val, w):
        self.S.op(eng, lambda e: e.memset(ap, val), writes=w)

    def recip(self, out, in_, r, w):
        self.S.op("dve", lambda e: e.reciprocal(out=out, in_=in_), reads=r, writes=w)

    def dma(self, q, out, in_, r, w):
        self.S.op(q, lambda e: e.dma_start(out=out, in_=in_), reads=r, writes=w, dma=True)


def alibi_slopes(n):
    return np.exp2(-8.0 * np.arange(1, n + 1) / n).astype(np.float32)


def host_constants():
    c = {}
    c["identf"] = np.eye(128, dtype=np.float32)
    sl = alibi_slopes(12)
    ik = np.arange(128)[:, None].astype(np.float64)
    iq = np.arange(128)[None, :].astype(np.float64)
    E = np.zeros((128, 12, 256), np.float32)
    for g, (win, dil) in enumerate(DIL_GROUPS):
        for hs in range(4):
            hh = g * 4 + hs
            s = float(sl[hh]) * dil
            dist_prev = 128 + iq - ik
            ok_prev = (dist_prev <= 128)
            E[:, hh, 0:128] = np.where(ok_prev, np.exp(-s * dist_prev), 0.0)
            dist_cur = iq - ik
            ok_cur = dist_cur >= 0
            E[:, hh, 128:256] = np.where(ok_cur, np.exp(-s * dist_cur), 0.0)
    c["edil"] = E
    return c


def expand_gain(g):
    return np.ascontiguousarray(np.broadcast_to(g.reshape(8, 128).T[:, :, None], (128, 8, 128))).astype(np.float32)


def vec_fm(v):
    return np.ascontiguousarray(v.reshape(8, 128).T).astype(np.float32)


def block_diag(gw):
    out = np.zeros((128, 8, 128), np.float32)
    for c in range(8):
        out[0:64, c, 0:64] = gw[2 * c]
        out[64:128, c, 64:128] = gw[2 * c + 1]
    return out


NEGB = 8192.0


def _bf16_split3(a):
    import ml_dtypes
    a = a.astype(np.float32)
    hi = a.astype(ml_dtypes.bfloat16).astype(np.float32)
    r1 = (a - hi).astype(np.float32)
    mid = r1.astype(ml_dtypes.bfloat16).astype(np.float32)
    r2 = (r1 - mid).astype(np.float32)
    lo = r2.astype(ml_dtypes.bfloat16).astype(np.float32)
    return hi, mid, lo


def host_constants_nsa():
    c = {}
    sl = alibi_slopes(16)
    i = np.arange(128)[:, None].astype(np.float64)
    m = np.arange(247)[None, :].astype(np.float64)
    dist = i - 16.0 * (m - 120.0) - 31.0
    E = np.zeros((128, 16, 247), np.float32)
    for h in range(16):
        E[:, h, :] = np.where(dist >= 0, np.exp(-float(sl[h]) * np.maximum(dist, 0.0)), 0.0)
    c["ecmp"] = E
    ii = np.arange(128)[:, None]
    rel = np.arange(62)[None, :] - 30
    cur = (ii >= 64).astype(np.int64)
    forced = (rel == cur) | (rel == cur - 1)
    future = rel > cur
    m1 = np.where(forced | future, 0.0, 1.0).astype(np.float32)
    m2 = np.where(forced, 1e6, np.where(future, -1e6, 0.0)).astype(np.float32)
    c["m12"] = np.ascontiguousarray(np.stack([m1, m2], axis=1))
    ik = np.arange(128)[:, None]
    iq = np.arange(128)[None, :]
    diag = np.where(ik > iq, -NEGB, 0.0).astype(np.float32)
    far = np.where(ik <= iq, -NEGB, 0.0).astype(np.float32)
    c["tri"] = np.ascontiguousarray(np.stack([np.tile(diag, (1, 4)), np.tile(far, (1, 4))], axis=1))
    k = np.arange(2048)
    kp = k - 1024
    hi = (np.floor(kp / 128.0) * 128.0).astype(np.float32)
    lo = (kp - hi).astype(np.float32)
    ka = np.zeros((2, 41, 2048), np.float32)
    for j in range(32):
        ka[0, j, :] = (k // 64 == j).astype(np.float32)
    for v in range(2):
        ka[v, 32:35, :] = 1.0
        ka[v, 35:38, :] = lo[None, :]
        ka[v, 38:41, :] = hi[None, :]
    c["kaug"] = ka
    qa = np.zeros((16, 9, 2048), np.float32)
    qp = (np.arange(2048) - 1024).astype(np.float32)
    for h in range(16):
        s8 = np.float32(8.0) * np.float32(sl[h])
        a = (-s8 * qp).astype(np.float32)
        ah, am, al = _bf16_split3(a)
        sh, sm, sl_ = _bf16_split3(np.full((2048,), s8, np.float32))
        qa[h, 0], qa[h, 1], qa[h, 2] = ah, am, al
        qa[h, 3], qa[h, 4], qa[h, 5] = sh, sm, sl_
        qa[h, 6], qa[h, 7], qa[h, 8] = sh, sm, sl_
    c["qal"] = qa
    return c


def chain(*gens):
    for g_ in gens:
        yield from g_


def run_lanes(lanes):
    active = list(lanes)
    while active:
        for l in list(active):
            try:
                next(l)
            except StopIteration:
                active.remove(l)


def build_program(stop_after=None):
    nc = bass.Bass("TRN2", target_bir_lowering=False)

    ckstate = {}

    def ck(name):
        if stop_after == name:
            ckstate["S"].dead = True

    def din(name, shape):
        return nc.dram_tensor(name, list(shape), F32, kind="ExternalInput").ap()

    x_d = din("x", [S_LEN, D])
    mem_d = din("mem", [256, D])
    hawk_w_in = din("hawk_w_in", [D, 7680])
    hawk_w_out = din("hawk_w_out", [1792, D])
    hawk_w_mem_kv = din("hawk_w_mem_kv", [D, 512])
    g_hawk = din("g_hawk", [128, 8, 128])
    g_hawk_mem = din("g_hawk_mem", [128, 8, 128])
    lru_vec = din("lru_vec", [128, 8, 8])
    bd_a = din("bd_a", [128, 8, 128])
    bd_x = din("bd_x", [128, 8, 128])
    identf_d = din("identf", [128, 128])
    edil_d = din("edil", [128, 12, 256])
    final_g = din("final_norm", [D])
    nsa_w_in = din("nsa_w_in", [D, 3376])
    nsa_w_out = din("nsa_w_out", [1280, D])
    nsa_w_mem_kv = din("nsa_w_mem_kv", [D, 512])
    g_nsa = din("g_nsa", [128, 8, 128])
    g_nsa_mem = din("g_nsa_mem", [128, 8, 128])
    w1k_d = din("w1k", [128, 32, 256])
    w1v_d = din("w1v", [128, 32, 256])
    w2k_d = din("w2k", [256, 64])
    w2v_d = din("w2v", [256, 64])
    peT_d = din("peT", [64, 2, 32])
    ecmp_d = din("ecmp", [128, 16, 247])
    m12_d = din("m12", [128, 2, 62])
    tri_d = din("tri", [128, 2, 512])
    kaug_d = din("kaug", [2, 41, 2048])
    qal_d = din("qal", [16, 9, 2048])
    out_d = nc.dram_tensor("out", [S_LEN, D], F32, kind="ExternalOutput").ap()
    x1_scr = nc.dram_tensor("x1_scr", [S_LEN, D], F32, kind="Internal").ap()

    with ExitStack() as gst:
        kb = KB(nc, gst)
        S = kb.S
        ckstate["S"] = S
        DX = Buf("x_dram", None)
        DX1 = Buf("x1_dram", None)
        DOUT = Buf("out_dram", None)

        xnT = kb.sb("xnT", [128, 8, S_LEN], BF16)
        memnT = kb.sb("memnT", [128, 8, 256], BF16)
        identf = kb.sb("identf", [128, 128], F32)
        identb = kb.sb("identb", [128, 128], BF16)
        onesb = kb.sb("onesb", [128, 128], BF16)
        wstage = [kb.sb(f"wstage{i}", [128, 1024], F32) for i in range(3)]
        ws_rr = [0]
        stat = kb.sb("stat", [128, 64], F32)
        stat_rr = [0]

        kb.dma("sp", identf[:], identf_d, [], [identf])
        kb.cp("dve", identb[:], identf[:], [identf], [identb])
        kb.memset("dve", onesb[:], 1.0, [onesb])

        def next_ws():
            b = wstage[ws_rr[0]]
            ws_rr[0] = (ws_rr[0] + 1) % len(wstage)
            return b

        def load_w(dst, dst_ap3, src_ap3, n, gain=None, key=None, q="sp", part=128, eng="pool"):
            dcs = dst_ap3.shape[1]
            assert dcs * n <= 1024
            stg = next_ws()
            sv = stg[0:part, 0:dcs * n].rearrange("p (c n) -> p c n", c=dcs)
            kb.dma(q, sv, src_ap3, [], [stg])
            if gain is not None:
                kb.tt(eng, dst_ap3, sv, gain[0:part, 0:dcs, 0:n], ALU.mult, [stg, gain], [(dst, key)])
            else:
                kb.cp(eng, dst_ap3, sv, [stg], [(dst, key)])

        def win_cols(w_dram, c0, n):
            return w_dram.rearrange("(dc p) n -> p dc n", p=128)[:, :, c0:c0 + n]

        class NormCtx:
            def __init__(self, stack, nbuf=2):
                self.xstage = [kb.sb(f"xstage{i}", [128, 1024], F32, stack) for i in range(nbuf)]
                self.xnb = [kb.sb(f"xnb{i}", [128, 1024], BF16, stack) for i in range(nbuf)]
                self.junk = kb.sb("junk", [128, 1024], BF16, stack)

        def tile_rstd(ncx, xbuf, xap):
            i = stat_rr[0]
            stat_rr[0] = (stat_rr[0] + 1) % 32
            ss = stat[:, 2 * i:2 * i + 1]
            rs = stat[:, 2 * i + 1:2 * i + 2]
            kb.stt(ncx.junk[:], xap, 1.0, xap, ALU.mult, ALU.mult, [xbuf], [ncx.junk, (stat, i)], accum_out=ss)
            kb.ts("dve", ss, ss, 1.0 / D, EPS, ALU.mult, ALU.add, [(stat, i)], [(stat, i)])
            kb.act(ss, ss, AF.Sqrt, [(stat, i)], [(stat, i)])
            kb.recip(rs, ss, [(stat, i)], [(stat, i)])
            return rs, i

        def norm_to_T(ncx, xbuf, xap, dstT, t, ntok_off):
            rs, i = tile_rstd(ncx, xbuf, xap)
            nb = ncx.xnb[t % 2]
            kb.ts("dve", nb[:], xap, rs, None, ALU.mult, None, [xbuf, (stat, i)], [nb])
            bk = kb.bank()
            bv = bk[:].bitcast(BF16)
            for c in range(8):
                kb.tr(bv[:, c * 128:(c + 1) * 128], nb[:, c * 128:(c + 1) * 128], identb[:], [nb, identb], [bk])
            kb.cp("act", dstT[:, :, ntok_off:ntok_off + 128], bv.rearrange("p (c n) -> p c n", c=8), [bk], [(dstT, t)])

        def norm_to_T_gen(ncx, xbuf, xap, dstT, t, ntok_off, bk, bi):
            rs, i = tile_rstd(ncx, xbuf, xap)
            yield
            nb = ncx.xnb[bi]
            kb.ts("dve", nb[:], xap, rs, None, ALU.mult, None, [xbuf, (stat, i)], [nb])
            yield
            bv = bk[:].bitcast(BF16)
            for c in range(8):
                kb.tr(bv[:, c * 128:(c + 1) * 128], nb[:, c * 128:(c + 1) * 128], identb[:], [nb, identb], [bk])
            yield
            kb.cp("act", dstT[:, :, ntok_off:ntok_off + 128], bv.rearrange("p (c n) -> p c n", c=8), [bk], [(dstT, t)])
            yield

        with ExitStack() as pa:
            ncx = NormCtx(pa)
            for t in range(NT_):
                xs = ncx.xstage[t % 2]
                kb.dma("sp", xs[:], x_d[t * 128:(t + 1) * 128, :], [DX], [xs])
                norm_to_T(ncx, xs, xs[:], xnT, t, t * 128)
            for t in range(2):
                xs = ncx.xstage[t % 2]
                kb.dma("sp", xs[:], mem_d[t * 128:(t + 1) * 128, :], [], [xs])
                norm_to_T(ncx, xs, xs[:], memnT, t, t * 128)
            S.barrier()
            ck("A")

        def make_loader(w_in_d, gain, nslots, stack):
            wslots = [kb.sb(f"wslot{i}", [128, 8, 128], BF16, stack) for i in range(nslots)]
            rr = [0]

            def load_win(c0, n=128, q="sp", into=None, off=0):
                if into is None:
                    wsl = wslots[rr[0]]
                    rr[0] = (rr[0] + 1) % nslots
                else:
                    wsl = into
                load_w(wsl, wsl[:, :, off:off + n], win_cols(w_in_d, c0, n), n, gain=gain, q=q, key=off)
                return wsl
            return load_win

        def proj_fm(wsl, n, evac, woff=0):
            for tc in range(4):
                bk = kb.bank()
                for dc in range(8):
                    kb.mm(bk[0:n, :], wsl[:, dc, woff:woff + n], xnT[:, dc, tc * 512:(tc + 1) * 512], dc == 0, dc == 7,
                          [wsl, xnT], [bk])
                evac(bk, bk[0:n, :], tc)

        def mem_kv(w_kv_d, gain, kmT, vm, stack):
            wkv = kb.sb("wkv", [128, 8, 512], BF16, stack)
            for j in range(4):
                load_w(wkv, wkv[:, :, j * 128:(j + 1) * 128], win_cols(w_kv_d, j * 128, 128), 128, gain=gain, key=j)
            for h in range(4):
                bk = kb.bank()
                for dc in range(8):
                    kb.mm(bk[0:64, 0:256], wkv[:, dc, h * 64:(h + 1) * 64], memnT[:, dc, :], dc == 0, dc == 7,
                          [wkv, memnT], [bk])
                kb.cp("act", kmT[0:64, h, :], bk[0:64, 0:256], [bk], [(kmT, h)])
            for mt in range(2):
                bk = kb.bank()
                for dc in range(8):
                    kb.mm(bk[:, 0:256], memnT[:, dc, mt * 128:(mt + 1) * 128], wkv[:, dc, 256:512], dc == 0, dc == 7,
                          [wkv, memnT], [bk])
                kb.cp("act", vm[:, mt, :], bk[:, 0:256], [bk], [(vm, mt)])

        def mem_attn(load_win, colq, colz, kmT, vm, ymT, stack):
            qmTs = [kb.sb(f"qmT{i}", [64, S_LEN], BF16, stack) for i in range(2)]
            szms = [kb.sb(f"szm{i}", [64, S_LEN], BF16, stack) for i in range(2)]
            PTm = [kb.sb(f"PTm{i}", [128, 512], BF16, stack) for i in range(4)]
            rdm = [kb.sb(f"rdm{i}", [64, 512], F32, stack) for i in range(2)]

            def lane(h, L):
                qmT, szm = qmTs[L], szms[L]
                b0, b1, b2, b3 = [kb.banks[4 * L + j] for j in range(4)]
                wq = load_win(colq + h * 64, 64)
                wz = load_win(colz + h * 64, 64)
                for (wsl, dst, func) in ((wq, qmT, None), (wz, szm, AF.Silu)):
                    for tc in range(4):
                        bk = b0 if tc % 2 == 0 else b1
                        for dc in range(8):
                            kb.mm(bk[0:64, :], wsl[:, dc, 0:64], xnT[:, dc, tc * 512:(tc + 1) * 512], dc == 0, dc == 7, [wsl, xnT], [bk])
                        yield
                        if func is None:
                            kb.cp("act", dst[0:64, tc * 512:(tc + 1) * 512], bk[0:64, :], [bk], [(dst, tc)])
                        else:
                            kb.act(dst[0:64, tc * 512:(tc + 1) * 512], bk[0:64, :], func, [bk], [(dst, tc)])
                        yield
                for tc in range(4):
                    pts = []
                    for mt in range(2):
                        bk = b0 if mt == 0 else b1
                        kb.mm(bk[:, :], kmT[0:64, h, mt * 128:(mt + 1) * 128], qmT[0:64, tc * 512:(tc + 1) * 512],
                              True, True, [kmT, (qmT, tc)], [bk])
                        pt = PTm[2 * L + mt]
                        kb.act(pt[:], bk[:], AF.Exp, [bk], [pt], scale=0.125)
                        pts.append(pt)
                        yield
                    for mt in range(2):
                        kb.mm(b2[0:64, :], vm[:, mt, h * 64:(h + 1) * 64], pts[mt][:], mt == 0, mt == 1, [vm, pts[mt]], [b2])
                    for mt in range(2):
                        kb.mm(b3[0:64, :], onesb[:, 0:64], pts[mt][:], mt == 0, mt == 1, [onesb, pts[mt]], [b3])
                    yield
                    rd = rdm[L]
                    kb.recip(rd[:], b3[0:64, :], [b3], [rd])
                    kb.tt("dve", rd[:], b2[0:64, :], rd[:], ALU.mult, [b2, rd], [rd])
                    yield
                    kb.tt("dve", ymT[0:64, h, tc * 512:(tc + 1) * 512], rd[:], szm[0:64, tc * 512:(tc + 1) * 512], ALU.mult,
                          [rd, (szm, tc)], [(ymT, (h, tc))])
                    yield

            run_lanes([lane(0, 0), lane(1, 1)])
            run_lanes([lane(2, 0), lane(3, 1)])

        def out_proj(w_out_d, nch, yTl, ymT, resid_d, resid_buf, final, stack, dbg=False):
            ysrc = []
            for (yb_, n_) in yTl:
                for ci in range(n_):
                    ysrc.append((yb_, ci))
            ncxs = [NormCtx(stack, 1), NormCtx(stack, 1)]
            WO = kb.sb("WO", [128, nch, 1024], BF16, stack)
            WOm = kb.sb("WOm", [64, 4, 1024], BF16, stack)
            wo_v = w_out_d[0:nch * 128, :].rearrange("(c p) n -> p c n", p=128)
            for c in range(nch):
                load_w(WO, WO[:, c:c + 1, :], wo_v[:, c:c + 1, :], 1024, key=c)
            wom_v = w_out_d[nch * 128:nch * 128 + 256, :].rearrange("(h p) n -> p h n", p=64)
            for h in range(4):
                load_w(WOm, WOm[0:64, h:h + 1, :], wom_v[:, h:h + 1, :], 1024, key=h, part=64)
            x1t = [kb.sb(f"x1t{i}", [128, 1024], F32, stack) for i in range(2)]
            if final:
                gF = kb.sb("gF", [128, 1024], F32, stack)
                kb.dma("sp", gF[:], final_g.partition_broadcast(128), [], [gF])
                ot = [kb.sb(f"ot{i}", [128, 1024], F32, stack) for i in range(2)]

            def lane(L):
                ncx = ncxs[L]
                bks = [kb.banks[4 * L + j] for j in range(4)]
                for t in range(L, NT_, 2):
                    xs = ncx.xstage[0]
                    kb.dma("sp", xs[:], resid_d[t * 128:(t + 1) * 128, :], [resid_buf], [xs])
                    x1 = x1t[L]
                    for half in range(2):
                        bk = bks[half]
                        for c in range(nch):
                            yb_, ci = ysrc[c]
                            kb.mm(bk[:, :], yb_[:, ci, t * 128:(t + 1) * 128], WO[:, c, half * 512:(half + 1) * 512], c == 0, False,
                                  [yb_, WO], [bk])
                            if c % 4 == 3:
                                yield
                        for h in range(4):
                            kb.mm(bk[:, :], ymT[0:64, h, t * 128:(t + 1) * 128], WOm[0:64, h, half * 512:(half + 1) * 512], False, h == 3,
                                  [ymT, WOm], [bk])
                        yield
                        kb.tt("dve", x1[:, half * 512:(half + 1) * 512], xs[:, half * 512:(half + 1) * 512], bk[:], ALU.add,
                              [xs, bk], [(x1, half)])
                        yield
                    if not final:
                        kb.dma("sp", x1_scr[t * 128:(t + 1) * 128, :], x1[:], [x1], [DX1])
                        yield from norm_to_T_gen(ncx, x1, x1[:], xnT, t, t * 128, bks[2], 0)
                        if dbg:
                            kb.dma("sp", out_d[t * 128:(t + 1) * 128, :], x1[:], [x1], [DOUT])
                    else:
                        rs, i = tile_rstd(ncx, x1, x1[:])
                        yield
                        o = ot[L]
                        kb.stt(o[:], x1[:], rs, gF[:], ALU.mult, ALU.mult, [x1, (stat, i), gF], [o])
                        yield
                        kb.dma("sp", out_d[t * 128:(t + 1) * 128, :], o[:], [o], [DOUT])
                        yield

            run_lanes([lane(0), lane(1)])

        with ExitStack() as l0:
            yTa = kb.sb("yTa", [128, 8, S_LEN], BF16, l0)
            ymT = kb.sb("ymT", [64, 4, S_LEN], BF16, l0)
            gH = kb.sb("gH", [128, 8, 128], F32, l0)
            kb.dma("sp", gH[:], g_hawk, [], [gH])
            load_win = make_loader(hawk_w_in, gH, 6, l0)
            kmT = kb.sb("kmT", [64, 4, 256], BF16, l0)
            vm = kb.sb("vm", [128, 2, 256], BF16, l0)
            with ExitStack() as pm:
                gHm = kb.sb("gHm", [128, 8, 128], F32, pm)
                kb.dma("sp", gHm[:], g_hawk_mem, [], [gHm])
                mem_kv(hawk_w_mem_kv, gHm, kmT, vm, pm)
                S.barrier()
                ck("memkv0")
            with ExitStack() as pd:
                mem_attn(load_win, 7168, 7424, kmT, vm, ymT, pd)
                S.barrier()
                ck("mem0")

            with ExitStack() as pb:
                lv = kb.sb("lv", [128, 8, 8], F32, pb)
                cvec = kb.sb("cvec", [128, 8, 2], F32, pb)
                bda = kb.sb("bda", [128, 8, 128], BF16, pb)
                bdx = kb.sb("bdx", [128, 8, 128], BF16, pb)
                kb.dma("sp", lv[:], lru_vec, [], [lv])
                load_w(bda, bda[:], bd_a, 128)
                load_w(bdx, bdx[:], bd_x, 128)
                kb.act(cvec[:, :, 0], lv[:, :, 7], AF.Exp, [lv], [cvec], scale=-1.0)
                kb.act(cvec[:, :, 0], cvec[:, :, 0], AF.Ln, [cvec], [cvec], bias=1.0)
                kb.ts("dve", cvec[:, :, 1], cvec[:, :, 0], -16.0, None, ALU.mult, None, [cvec], [cvec])
                kb.ts("dve", cvec[:, :, 0], cvec[:, :, 0], -8.0, None, ALU.mult, None, [cvec], [cvec])
                sets = []
                for L in range(2):
                    sets.append(dict(
                        B1=kb.sb(f"B1_{L}", [128, S_LEN + 4], F32, pb), B2=kb.sb(f"B2_{L}", [128, S_LEN], F32, pb),
                        B3=kb.sb(f"B3_{L}", [128, S_LEN], F32, pb), B4=kb.sb(f"B4_{L}", [128, S_LEN], F32, pb),
                        xcb=kb.sb(f"xcb_{L}", [128, S_LEN], BF16, pb), sz=kb.sb(f"sz_{L}", [128, S_LEN], BF16, pb)))

                def lru_lane(L):
                    st_ = sets[L]
                    B1, B2, B3, B4, xcb, sz = st_["B1"], st_["B2"], st_["B3"], st_["B4"], st_["xcb"], st_["sz"]
                    bks = [kb.banks[4 * L + j] for j in range(4)]
                    for c in range(L, 8, 2):
                        wxa = load_win(c * 128)
                        wza = load_win(1024 + c * 128)
                        kb.memset("dve", B1[:, 0:3], 0.0, [(B1, "pad")])
                        for (wsl, which) in ((wxa, 0), (wza, 1)):
                            for tc in range(4):
                                bk = bks[tc % 4]
                                for dc in range(8):
                                    kb.mm(bk[:, :], wsl[:, dc, 0:128], xnT[:, dc, tc * 512:(tc + 1) * 512], dc == 0, dc == 7, [wsl, xnT], [bk])
                                yield
                                if which == 0:
                                    kb.cp("act", B1[:, 3 + tc * 512:3 + (tc + 1) * 512], bk[:], [bk], [(B1, tc)])
                                else:
                                    kb.act(sz[:, tc * 512:(tc + 1) * 512], bk[:], AF.Silu, [bk], [(sz, tc)])
                                yield
                        kb.ts("dve", B2[:], B1[:, 0:S_LEN], lv[:, c, 0:1], lv[:, c, 4:5], ALU.mult, ALU.add, [B1, lv], [B2])
                        yield
                        for k in range(1, 4):
                            kb.stt(B2[:], B1[:, k:k + S_LEN], lv[:, c, k:k + 1], B2[:], ALU.mult, ALU.add, [B1, lv, B2], [B2])
                            yield
                        kb.cp("dve", xcb[:], B2[:], [B2], [xcb])
                        yield
                        for (bd, dstb, col) in ((bda, B1, 5), (bdx, B4, 6)):
                            for tc in range(4):
                                bk = bks[tc % 4]
                                kb.mm(bk[:, :], bd[:, c, :], xcb[:, tc * 512:(tc + 1) * 512], True, True, [bd, xcb], [bk])
                                yield
                                kb.act(dstb[:, tc * 512:(tc + 1) * 512], bk[:], AF.Sigmoid, [bk, lv], [(dstb, tc)], bias=lv[:, c, col:col + 1])
                                yield
                        r_ap = B1[:, 0:S_LEN]
                        kb.act(B3[:], r_ap, AF.Exp, [B1, cvec], [B3], scale=cvec[:, c, 0:1])
                        yield
                        kb.act(r_ap, r_ap, AF.Exp, [B1, cvec], [B1], scale=cvec[:, c, 1:2])
                        yield
                        kb.tt("dve", B2[:], B2[:], B4[:], ALU.mult, [B2, B4], [B2])
                        yield
                        kb.ts("dve", r_ap, r_ap, -1.0, 1.0, ALU.mult, ALU.add, [B1], [B1])
                        yield
                        kb.ts("dve", r_ap, r_ap, 0.0, None, ALU.max, None, [B1], [B1])
                        yield
                        kb.act(r_ap, r_ap, AF.Sqrt, [B1], [B1])
                        kb.memset("dve", B1[:, 0:1], 1.0, [B1])
                        yield
                        kb.tt("dve", B2[:], B2[:], r_ap, ALU.mult, [B2, B1], [B2])
                        yield
                        S.op("dve", lambda e, B4=B4, B3=B3, B2=B2: e.tensor_tensor_scan(out=B4[:], data0=B3[:], data1=B2[:], initial=0.0,
                                                                                      op0=ALU.mult, op1=ALU.add), reads=[B3, B2], writes=[B4])
                        yield
                        kb.tt("dve", yTa[:, c, :], B4[:], sz[:], ALU.mult, [B4, sz], [(yTa, c)])
                        yield

                run_lanes([lru_lane(0), lru_lane(1)])
                S.barrier()
                ck("lru")
            yTb = kb.sb("yTb", [128, 4, S_LEN], BF16, l0)

            with ExitStack() as pc:
                edil = kb.sb("edil", [128, 12, 256], F32, pc)
                kb.dma("sp", edil[:], edil_d, [], [edil])
                qTs = [kb.sb(f"qT{i}", [128, S_LEN], BF16, pc) for i in range(2)]
                kTs = [kb.sb(f"kT{i}", [128, S_LEN], BF16, pc) for i in range(2)]
                vTs = [kb.sb(f"vT{i}", [128, S_LEN], BF16, pc) for i in range(2)]
                Vps = [kb.sb(f"Vp{i}", [128, 16, 128], BF16, pc) for i in range(2)]
                szbs = [kb.sb(f"szb{i}", [128, S_LEN], BF16, pc) for i in range(2)]
                NTa = kb.sb("NTa", [128, S_LEN], F32, pc)
                DBa = kb.sb("DBa", [128, S_LEN], F32, pc)
                Pf = [kb.sb(f"Pf{i}", [128, 256], F32, pc) for i in range(3)]
                PT = [kb.sb(f"PT{i}", [128, 256], BF16, pc) for i in range(3)]
                sc_d = 128.0 ** -0.5
                items = [(hs, g) for hs in range(4) for g in range(3)]
                prot = [0]

                def pbank():
                    b = kb.banks[6 + prot[0]]
                    prot[0] ^= 1
                    return b

                def dtoks(d, r, b):
                    t0 = r + d * 128 * b
                    return slice(t0, t0 + d * 127 + 1, d)

                def proj_fm_lane(wsl, dst, func):
                    for tc in range(4):
                        bk = pbank()
                        for dc in range(8):
                            kb.mm(bk[:, :], wsl[:, dc, 0:128], xnT[:, dc, tc * 512:(tc + 1) * 512], dc == 0, dc == 7, [wsl, xnT], [bk])
                        yield
                        if func is None:
                            kb.cp("act", dst[:, tc * 512:(tc + 1) * 512], bk[:], [bk], [(dst, tc)])
                        else:
                            kb.act(dst[:, tc * 512:(tc + 1) * 512], bk[:], func, [bk], [(dst, tc)])
                        yield

                def task_proj(i):
                    hs, g = items[i]
                    win, d = DIL_GROUPS[g]
                    hh = g * 4 + hs
                    nqb = (S_LEN // d) // 128
                    s = i % 2
                    if g == 0:
                        wz = load_win(2048 + 4608 + hs * 128)
                        yield from proj_fm_lane(wz, szbs[hs % 2], AF.Silu)
                    wq = load_win(2048 + hh * 128)
                    yield from proj_fm_lane(wq, qTs[s], None)
                    wk = load_win(2048 + 1536 + hh * 128)
                    yield from proj_fm_lane(wk, kTs[s], None)
                    wv = load_win(2048 + 3072 + hh * 128)
                    yield from proj_fm_lane(wv, vTs[s], None)
                    for j in range(4):
                        bk = pbank()
                        bv = bk[:].bitcast(BF16)
                        for k in range(4):
                            r, b = divmod(4 * j + k, nqb)
                            kb.tr(bv[:, k * 128:(k + 1) * 128], vTs[s][:, dtoks(d, r, b)], identb[:], [vTs[s], identb], [bk])
                        yield
                        kb.cp("act", Vps[s][:, 4 * j:4 * j + 4, :], bv[:, 0:512].rearrange("p (k n) -> p k n", k=4), [bk], [(Vps[s], j)])
                        yield

                def task_tile(i, ti, lane):
                    hs, g = items[i]
                    win, d = DIL_GROUPS[g]
                    hh = g * 4 + hs
                    nqb = (S_LEN // d) // 128
                    s = i % 2
                    qT, kT, Vp = qTs[s], kTs[s], Vps[s]
                    r, qb = divmod(ti, nqb)
                    qs = dtoks(d, r, qb)
                    kbs = [qb - 1, qb] if qb > 0 else [qb]
                    sbk = kb.banks[2 * lane]
                    ndb = kb.banks[2 * lane + 1]
                    for kbi in kbs:
                        typ = 0 if kbi < qb else 1
                        kb.mm(sbk[:, typ * 128:(typ + 1) * 128], kT[:, dtoks(d, r, kbi)], qT[:, qs], True, True, [kT, qT], [sbk])
                    yield
                    lo = 0 if qb > 0 else 128
                    pf = Pf[lane]
                    pt = PT[lane]
                    kb.act(pf[:, lo:256], sbk[:, lo:256], AF.Exp, [sbk], [pf], scale=sc_d)
                    yield
                    kb.tt("dve", pt[:, lo:256], pf[:, lo:256], edil[:, hh, lo:256], ALU.mult, [pf, edil], [pt])
                    yield
                    for j, kbi in enumerate(kbs):
                        typ = 0 if kbi < qb else 1
                        kb.mm(ndb[:, 0:128], Vp[:, r * nqb + kbi, :], pt[:, typ * 128:(typ + 1) * 128], j == 0, j == len(kbs) - 1,
                              [Vp, pt], [ndb])
                    for j, kbi in enumerate(kbs):
                        typ = 0 if kbi < qb else 1
                        kb.mm(ndb[:, 128:256], onesb[:], pt[:, typ * 128:(typ + 1) * 128], j == 0, j == len(kbs) - 1,
                              [onesb, pt], [ndb])
                    yield
                    if g == 0:
                        kb.cp("dve", NTa[:, qs], ndb[:, 0:128], [ndb], [NTa])
                        kb.cp("dve", DBa[:, qs], ndb[:, 128:256], [ndb], [DBa])
                    else:
                        kb.tt("dve", NTa[:, qs], NTa[:, qs], ndb[:, 0:128], ALU.add, [ndb, NTa], [NTa])
                        kb.tt("dve", DBa[:, qs], DBa[:, qs], ndb[:, 128:256], ALU.add, [ndb, DBa], [DBa])
                    yield

                def tile_lane(i, lane):
                    for ti in range(lane, 16, 3):
                        yield from task_tile(i, ti, lane)

                run_lanes([task_proj(0)])
                for i in range(len(items)):
                    hs, g = items[i]
                    lanes = [tile_lane(i, 0), tile_lane(i, 1), tile_lane(i, 2)]
                    if i + 1 < len(items):
                        lanes.append(task_proj(i + 1))
                    run_lanes(lanes)
                    if g == 2:
                        kb.recip(DBa[:], DBa[:], [DBa], [DBa])
                        kb.tt("dve", NTa[:], NTa[:], DBa[:], ALU.mult, [NTa, DBa], [NTa])
                        kb.tt("dve", yTb[:, hs, :], NTa[:], szbs[hs % 2][:], ALU.mult, [NTa, szbs[hs % 2]], [(yTb, hs)])
                S.barrier()
                ck("dil")

            with ExitStack() as pe_:
                out_proj(hawk_w_out, 12, [(yTa, 8), (yTb, 4)], ymT, x_d, DX, False, pe_, dbg=(stop_after == "l0"))
                S.barrier()
                ck("l0end")

        if stop_after != "l0":
          with ExitStack() as l1:
            yT1 = kb.sb("yT1", [128, 8, S_LEN], BF16, l1)
            ymT1 = kb.sb("ymT1", [64, 4, S_LEN], BF16, l1)
            gN = kb.sb("gN", [128, 8, 128], F32, l1)
            kb.dma("sp", gN[:], g_nsa, [], [gN])
            load_win = make_loader(nsa_w_in, gN, 4, l1)
            kcmpT = kb.sb("kcmpT", [64, 2, 128], BF16, l1)
            vcmp = kb.sb("vcmp", [128, 2, 64], BF16, l1)
            gates = kb.sb("gates", [128, 16, 48], F32, l1)
            with ExitStack() as pmm:
                kmT = kb.sb("kmT1", [64, 4, 256], BF16, pmm)
                vm = kb.sb("vm1", [128, 2, 256], BF16, pmm)
                with ExitStack() as pm:
                    gNm = kb.sb("gNm", [128, 8, 128], F32, pm)
                    kb.dma("sp", gNm[:], g_nsa_mem, [], [gNm])
                    mem_kv(nsa_w_mem_kv, gNm, kmT, vm, pm)
                    S.barrier()
                    ck("memkv1")
                with ExitStack() as pd:
                    mem_attn(load_win, 2864, 3120, kmT, vm, ymT1, pd)
                    S.barrier()
                    ck("mem1")

            with ExitStack() as pq:
                wg = load_win(1792, 48)
                for t in range(NT_):
                    bk = kb.bank()
                    for dc in range(8):
                        kb.mm(bk[:, 0:48], xnT[:, dc, t * 128:(t + 1) * 128], wg[:, dc, 0:48], dc == 0, dc == 7, [xnT, wg], [bk])
                    kb.act(gates[:, t, :], bk[:, 0:48], AF.Sigmoid, [bk], [(gates, t)])
                kcT = kb.sb("kcT", [128, S_LEN], BF16, pq)
                vcT = kb.sb("vcT", [128, S_LEN], BF16, pq)
                wkc = load_win(1024)
                proj_fm(wkc, 128, lambda bk, ap, tc: kb.cp("act", kcT[:, tc * 512:(tc + 1) * 512], ap, [bk], [(kcT, tc)]))
                wvc = load_win(1024 + 128)
                proj_fm(wvc, 128, lambda bk, ap, tc: kb.cp("act", vcT[:, tc * 512:(tc + 1) * 512], ap, [bk], [(vcT, tc)]))
                W1 = kb.sb("W1", [128, 32, 256], BF16, pq)
                w2 = kb.sb("w2", [128, 2, 64], BF16, pq)
                peS = kb.sb("peS", [64, 2, 32], F32, pq)
                peb = kb.sb("peb", [64, 2, 32], BF16, pq)
                hidT = kb.sb("hidT", [128, 2, 128], BF16, pq)
                cb = kb.sb("cb", [128, 2], F32, pq)
                kb.dma("sp", peS[:], peT_d, [], [peS])
                kb.cp("dve", peb[:], peS[:], [peS], [peb])
                for kv in range(2):
                    w1d = w1k_d if kv == 0 else w1v_d
                    w2d = w2k_d if kv == 0 else w2v_d
                    srcT = kcT if kv == 0 else vcT
                    for p4 in range(8):
                        load_w(W1, W1[:, p4 * 4:(p4 + 1) * 4, :], w1d[:, p4 * 4:(p4 + 1) * 4, :], 256, key=p4)
                    load_w(w2, w2[:, :, :], w2d.rearrange("(hc p) d -> p hc d", p=128), 64)
                    for hc in range(2):
                        bk = kb.bank()
                        for p in range(32):
                            kb.mm(bk[:, 0:1], W1[0:64, p, hc * 128:(hc + 1) * 128], peb[0:64, kv, p:p + 1], p == 0, p == 31,
                                  [W1, peb], [bk])
                        kb.cp("dve", cb[:, hc:hc + 1], bk[:, 0:1], [bk], [(cb, hc)])
                    for g in range(2):
                        for hc in range(2):
                            bk = kb.bank()
                            for p in range(32):
                                kb.mm(bk[:, 0:127], W1[g * 64:(g + 1) * 64, p, hc * 128:(hc + 1) * 128],
                                      srcT[g * 64:(g + 1) * 64, p:p + 16 * 126 + 1:16], p == 0, p == 31, [W1, srcT], [bk])
                            kb.act(hidT[:, hc, 0:127], bk[:, 0:127], AF.Silu, [bk, cb], [(hidT, hc)], bias=cb[:, hc:hc + 1])
                        bk = kb.bank()
                        if kv == 0:
                            for hc in range(2):
                                kb.mm(bk[0:64, 0:127], w2[:, hc, :], hidT[:, hc, 0:127], hc == 0, hc == 1, [w2, hidT], [bk])
                            kb.cp("dve", kcmpT[0:64, g, 0:127], bk[0:64, 0:127], [bk], [(kcmpT, g)])
                        else:
                            for hc in range(2):
                                kb.mm(bk[0:127, 0:64], hidT[:, hc, 0:127], w2[:, hc, :], hc == 0, hc == 1, [w2, hidT], [bk])
                            kb.cp("dve", vcmp[0:127, g, :], bk[0:127, 0:64], [bk], [(vcmp, g)])
                S.barrier()
                ck("cmpkv")

            with ExitStack() as pg:
                QAg = kb.sb("QAg", [105, 8, S_LEN], BF16, pg)
                KAs = kb.sb("KAs", [105, S_LEN], BF16, pg)
                KAw = kb.sb("KAw", [105, S_LEN], BF16, pg)
                VAs = kb.sb("VAs", [128, 16, 128], BF16, pg)
                VAw = kb.sb("VAw", [128, 16, 128], BF16, pg)
                ecmp = kb.sb("ecmp", [128, 8, 247], F32, pg)
                Wz = kb.sb("Wz", [128, 8, 512], BF16, pg)
                m12 = kb.sb("m12", [128, 2, 62], F32, pg)
                trib = kb.sb("trib", [128, 2, 512], BF16, pg)
                kb.dma("sp", m12[:], m12_d, [], [m12])
                for v in range(2):
                    stg = next_ws()
                    kb.dma("sp", stg[:, 0:512], tri_d[:, v, :], [], [stg])
                    kb.cp("pool", trib[:, v, :], stg[:, 0:512], [stg], [(trib, v)])
                for v, KA in enumerate((KAs, KAw)):
                    for hf in range(2):
                        stg = next_ws()
                        kb.dma("sp", stg[64:105, 0:1024], kaug_d[v][:, hf * 1024:(hf + 1) * 1024], [], [stg])
                        kb.cp("pool", KA[64:105, hf * 1024:(hf + 1) * 1024], stg[64:105, 0:1024], [stg], [(KA, ("aug", hf))])
                kb.memset("pool", VAs[:, :, 64:128], 1.0, [(VAs, "ones")])
                kb.memset("pool", VAw[:, :, 64:128], 1.0, [(VAw, "ones")])
                Pc = [kb.sb("Pc0", [128, 4, 128], F32, pg)] * 2
                Pu = [kb.sb(f"Pu{i}", [128, 4, 128], F32, pg) for i in range(2)]
                Pub = [kb.sb(f"Pub{i}", [128, 4, 128], BF16, pg) for i in range(2)]
                pT = [kb.sb(f"pT{i}", [128, 4, 128], BF16, pg) for i in range(2)]
                for i in range(2):
                    kb.memset("pool", Pu[i][:], 0.0, [Pu[i]])
                psg = kb.sb("psg", [128, 128], F32, pg)
                den8 = kb.sb("den8", [128, 8], F32, pg)
                cg8 = kb.sb("cg8", [128, 8], F32, pg)
                imp = kb.sb("imp", [128, 32], F32, pg)
                impm = kb.sb("impm", [128, 32], F32, pg)
                m8 = kb.sb("m8", [128, 8], F32, pg)
                negp = kb.sb("negp", [128, 96], F32, pg)
                negS = kb.sb("negS", [96, 128], BF16, pg)
                kb.memset("pool", negp[:], 0.0, [negp])
                PTs = [kb.sb(f"PTs{i}", [128, 512], BF16, pg) for i in range(6)]
                pts_rr = [0]
                accs = [kb.sb(f"accs{i}", [128, 512], F32, pg) for i in range(2)]
                rd4 = [kb.sb(f"rd4{i}", [128, 4], F32, pg) for i in range(2)]
                cg4 = [kb.sb(f"cg4{i}", [128, 4], F32, pg) for i in range(2)]
                Oa = [kb.sb(f"Oa{i}", [128, 512], F32, pg) for i in range(2)]
                szt = kb.sb("szt", [128, 512], F32, pg)
                Ob = kb.sb("Ob", [128, 512], BF16, pg)
                accbanks = [kb.banks[0], kb.banks[1]]
                rot = [2]

                def rbank():
                    b = kb.banks[rot[0]]
                    rot[0] = rot[0] + 1 if rot[0] < 7 else 2
                    return b

                strot = [0, 0]

                def stbank(lane):
                    b = kb.banks[2 + 2 * lane + strot[lane]]
                    strot[lane] ^= 1
                    return b

                def mbank():
                    return kb.banks[6]

                ptrot = [0, 0]

                def next_pt(lane):
                    p = PTs[3 * lane + ptrot[lane]]
                    ptrot[lane] = (ptrot[lane] + 1) % 3
                    return p

                for g in range(2):
                    kb.dma("sp", ecmp[:], ecmp_d[:, g * 8:(g + 1) * 8, :], [], [ecmp])
                    for j in range(4):
                        load_w(Wz, Wz[:, :, j * 128:(j + 1) * 128], win_cols(nsa_w_in, 1840 + g * 512 + j * 128, 128), 128,
                               gain=gN, key=j)
                    for r in range(8):
                        hq = g * 8 + r
                        wq = load_win(hq * 64, 64)
                        for tc in range(4):
                            bk = rbank()
                            for dc in range(8):
                                kb.mm(bk[0:64, :], wq[:, dc, 0:64], xnT[:, dc, tc * 512:(tc + 1) * 512], dc == 0, dc == 7, [wq, xnT], [bk])
                            kb.cp("act", QAg[0:64, r, tc * 512:(tc + 1) * 512], bk[0:64, :], [bk], [QAg])
                        for hf in range(2):
                            stg = next_ws()
                            kb.dma("sp", stg[96:105, 0:1024], qal_d[hq][:, hf * 1024:(hf + 1) * 1024], [], [stg])
                            kb.cp("pool", QAg[96:105, r, hf * 1024:(hf + 1) * 1024], stg[96:105, 0:1024], [stg], [QAg])
                    for KA, col in ((KAs, 1024 + 2 * 128 + g * 64), (KAw, 1024 + 4 * 128 + g * 64)):
                        wk = load_win(col, 64)
                        for tc in range(4):
                            bk = rbank()
                            for dc in range(8):
                                kb.mm(bk[0:64, :], wk[:, dc, 0:64], xnT[:, dc, tc * 512:(tc + 1) * 512], dc == 0, dc == 7, [wk, xnT], [bk])
                            kb.cp("act", KA[0:64, tc * 512:(tc + 1) * 512], bk[0:64, :], [bk], [(KA, tc)])
                    wv = load_win(1024 + 3 * 128 + g * 64, 64)
                    load_win(1024 + 5 * 128 + g * 64, 64, into=wv, off=64)
                    for t in range(NT_):
                        bk = rbank()
                        for dc in range(8):
                            kb.mm(bk[:, 0:128], xnT[:, dc, t * 128:(t + 1) * 128], wv[:, dc, 0:128], dc == 0, dc == 7, [xnT, wv], [bk])
                        kb.cp("act", VAs[:, t, 0:64], bk[:, 0:64], [bk], [(VAs, t)])
                        kb.cp("dve", VAw[:, t, 0:64], bk[:, 64:128], [bk], [(VAw, t)])

                    def task_C(qt):
                        qc = slice(qt * 128, (qt + 1) * 128)
                        O = Oa[qt % 2]
                        gq = gates[:, qt, :]
                        eoff = 120 - 8 * qt
                        ocb = kb.banks[7]
                        for b4 in range(2):
                            sbk = mbank()
                            for hl in range(4):
                                r = b4 * 4 + hl
                                kb.mm(sbk[:, hl * 128:hl * 128 + 127], QAg[0:64, r, qc], kcmpT[0:64, g, 0:127], True, True,
                                      [(QAg, qt), kcmpT], [sbk])
                            yield
                            pc = Pc[b4]
                            pu = Pu[b4]
                            s3 = sbk[:].rearrange("p (h n) -> p h n", h=4)
                            kb.act(pc[:, :, 0:127], s3[:, :, 0:127], AF.Exp, [sbk], [pc], scale=0.125)
                            yield
                            kb.tt("dve", pu[:, :, 0:127], pc[:, :, 0:127], ecmp[:, b4 * 4:(b4 + 1) * 4, eoff:eoff + 127], ALU.mult,
                                  [pc, ecmp], [pu])
                            S.op("dve", lambda e, pu=pu, b4=b4: e.tensor_reduce(out=den8[:, b4 * 4:(b4 + 1) * 4], in_=pu[:, :, 0:127],
                                                                                axis=AX.X, op=ALU.add),
                                 reads=[pu], writes=[(den8, b4)])
                            yield
                            kb.ts("dve", den8[:, b4 * 4:(b4 + 1) * 4], den8[:, b4 * 4:(b4 + 1) * 4], 1e-30, None, ALU.max, None,
                                  [(den8, b4)], [(den8, b4)])
                            kb.recip(den8[:, b4 * 4:(b4 + 1) * 4], den8[:, b4 * 4:(b4 + 1) * 4], [(den8, b4)], [(den8, b4)])
                            pub = Pub[b4]
                            kb.cp("pool", pub[:], pu[:], [pu], [pub])
                            yield
                            for hl in range(4):
                                r = b4 * 4 + hl
                                if r == 0:
                                    kb.ts("dve", psg[:, :], pu[:, hl, :], den8[:, r:r + 1], None, ALU.mult, None, [pu, (den8, b4)], [psg])
                                else:
                                    kb.stt(psg[:, :], pu[:, hl, :], den8[:, r:r + 1], psg[:, :], ALU.mult, ALU.add,
                                           [pu, (den8, b4), psg], [psg])
                                if hl % 2 == 1:
                                    yield
                            tbk = mbank()
                            tv = tbk[:].bitcast(BF16)
                            for hl in range(4):
                                kb.tr(tv[0:127, hl * 128:(hl + 1) * 128], pub[:, hl, 0:127], identb[:], [pub, identb], [tbk])
                            yield
                            ptt = pT[b4]
                            kb.cp("act", ptt[0:127, :, :], tv[0:127, 0:512].rearrange("p (h n) -> p h n", h=4), [tbk], [ptt])
                            yield
                            for hl in range(4):
                                r = b4 * 4 + hl
                                kb.mm(ocb[:, r * 64:(r + 1) * 64], ptt[0:127, hl, :], vcmp[0:127, g, :], True, True, [ptt, vcmp], [ocb])
                            yield
                        kb.tt("dve", cg8[:], den8[:], gq[:, g * 24:g * 24 + 24:3], ALU.mult, [den8, gates], [cg8])
                        kb.tt("dve", O[:].rearrange("p (h d) -> p h d", h=8), ocb[:].rearrange("p (h d) -> p h d", h=8),
                              cg8[:, 0:8].unsqueeze(2).to_broadcast([128, 8, 64]), ALU.mult, [ocb, cg8], [O])
                        yield
                        S.op("dve", lambda e: e.tensor_reduce(out=imp[:, :], in_=psg[:].rearrange("p (j a) -> p j a", a=4),
                                                              axis=AX.X, op=ALU.add), reads=[psg], writes=[imp])
                        kb.tt("dve", imp[:, 1:32], imp[:, 1:32], psg[:, 3:127:4], ALU.add, [imp, psg], [imp])
                        yield
                        moff = 30 - 2 * qt
                        kb.tt("dve", impm[:], imp[:], m12[:, 0, moff:moff + 32], ALU.mult, [imp, m12], [impm])
                        kb.tt("dve", impm[:], impm[:], m12[:, 1, moff:moff + 32], ALU.add, [impm, m12], [impm])
                        kb.memset("dve", impm[:, 0:1], 1e6, [impm])
                        yield
                        S.op("dve", lambda e: e.max(out=m8[:], in_=impm[:]), reads=[impm], writes=[m8])
                        kb.ts("dve", negp[:, 64:96], impm[:], m8[:, 7:8], 1.0, ALU.is_ge, ALU.subtract, [impm, m8], [negp])
                        kb.ts("dve", negp[:, 64:96], negp[:, 64:96], NEGB, None, ALU.mult, None, [negp], [negp])
                        yield
                        tbk = mbank()
                        kb.tr(tbk[0:96, 0:128], negp[:, 0:96], identf[:], [negp, identf], [tbk])
                        yield
                        kb.cp("dve", negS[64:96, :], tbk[64:96, 0:128], [tbk], [negS])
                        kb.cp("dve", QAg[64:96, :, qc], negS[64:96, :].unsqueeze(1).to_broadcast([32, 8, 128]), [negS], [(QAg, qt)])
                        yield

                    def task_branch(qt, br, b4):
                        qc = slice(qt * 128, (qt + 1) * 128)
                        O = Oa[qt % 2]
                        gq = gates[:, qt, :]
                        KA, VA = (KAw, VAw) if br == 2 else (KAs, VAs)
                        kbs = list(range(max(0, qt - 4), qt + 1)) if br == 2 else list(range(0, qt + 1))
                        accb = kb.banks[b4]
                        prev = None
                        for idx, kbi in enumerate(kbs):
                            sbk = stbank(b4)
                            masks = []
                            if kbi == qt:
                                masks.append(0)
                            if br == 2 and kbi == qt - 4:
                                masks.append(1)
                            kb.mm(sbk[:, :], KA[0:105, kbi * 128:(kbi + 1) * 128], QAg[0:105, b4 * 4:(b4 + 1) * 4, qc],
                                  True, len(masks) == 0, [KA, (QAg, qt)], [sbk])
                            for mi, mv in enumerate(masks):
                                kb.mm(sbk[:, :], identb[:], trib[:, mv, :], False, mi == len(masks) - 1, [identb, trib], [sbk])
                            pt = next_pt(b4)
                            kb.act(pt[:], sbk[:], AF.Exp, [sbk], [pt], scale=0.125)
                            yield
                            if prev is not None:
                                pi, pk, ppt = prev
                                kb.mm(accb[:, :], VA[:, pk, :], ppt[:], pi == 0, False, [VA, ppt], [accb])
                                yield
                            prev = (idx, kbi, pt)
                        pi, pk, ppt = prev
                        kb.mm(accb[:, :], VA[:, pk, :], ppt[:], pi == 0, True, [VA, ppt], [accb])
                        yield
                        ac = accs[b4]
                        kb.cp("act", ac[:], accb[:], [accb], [ac])
                        yield
                        tbk = stbank(b4)
                        for hl in range(4):
                            kb.tr(tbk[:, hl * 128:(hl + 1) * 128], ac[:, hl * 128:(hl + 1) * 128], identf[:], [ac, identf], [tbk])
                        yield
                        t3 = tbk[:].rearrange("p (h n) -> p h n", h=4)
                        kb.recip(rd4[b4][:], t3[:, :, 64], [tbk], [rd4[b4]])
                        h0 = (g * 8 + b4 * 4) * 3 + br
                        kb.tt("dve", cg4[b4][:], rd4[b4][:], gq[:, h0:h0 + 10:3], ALU.mult, [rd4[b4], gates], [cg4[b4]])
                        yield
                        for hl in range(4):
                            r = b4 * 4 + hl
                            kb.stt(O[:, r * 64:(r + 1) * 64], tbk[:, hl * 128:hl * 128 + 64], cg4[b4][:, hl:hl + 1], O[:, r * 64:(r + 1) * 64],
                                   ALU.mult, ALU.add, [tbk, cg4[b4], (O, b4)], [(O, b4)])
                            if hl % 2 == 1:
                                yield

                    def task_Z(qt):
                        qc = slice(qt * 128, (qt + 1) * 128)
                        O = Oa[qt % 2]
                        zb = mbank()
                        for dc in range(8):
                            kb.mm(zb[:, :], xnT[:, dc, qc], Wz[:, dc, :], dc == 0, dc == 7, [xnT, Wz], [zb])
                            if dc % 4 == 3:
                                yield
                        kb.act(szt[:], zb[:], AF.Silu, [zb], [szt])
                        yield
                        kb.tt("dve", Ob[:], O[:], szt[:], ALU.mult, [O, szt], [Ob])
                        yield
                        tbk = mbank()
                        tv = tbk[:].bitcast(BF16)
                        for c4 in range(4):
                            kb.tr(tv[:, c4 * 128:(c4 + 1) * 128], Ob[:, c4 * 128:(c4 + 1) * 128], identb[:], [Ob, identb], [tbk])
                        yield
                        kb.cp("act", yT1[:, g * 4:(g + 1) * 4, qc], tv[:, 0:512].rearrange("p (c n) -> p c n", c=4), [tbk], [(yT1, (g, qt))])
                        yield

                    run_lanes([task_C(0)])
                    for qt in range(NT_):
                        l1_ = chain(task_branch(qt, 2, 0), task_branch(qt, 1, 0))
                        l2_ = chain(task_branch(qt, 2, 1), task_branch(qt, 1, 1))
                        third = []
                        if qt > 0:
                            third.append(task_Z(qt - 1))
                        if qt + 1 < NT_:
                            third.append(task_C(qt + 1))
                        run_lanes([l1_, l2_, chain(*third)])
                    run_lanes([task_Z(NT_ - 1)])
                S.barrier()
                ck("nsa")

            with ExitStack() as pe_:
                out_proj(nsa_w_out, 8, [(yT1, 8)], ymT1, x1_scr, DX1, True, pe_)
                S.barrier()
                ck("l1end")

        S.dead = False
        S.barrier()
        with nc.Block() as block:
            S.emit(block)
        print("program ops:", S.nops, "sems:", S.nsem)
    return nc


_CONST = None


def prep_inputs(inp):
    global _CONST
    if _CONST is None:
        _CONST = host_constants()
        _CONST.update(host_constants_nsa())
    f = lambda a: np.ascontiguousarray(np.asarray(a, dtype=np.float32))
    shared = {
        "hawk_w_in": f(inp["hawk_w_in"][0]),
        "hawk_w_out": f(inp["hawk_w_out"][0]),
        "hawk_w_mem_kv": f(inp["hawk_w_mem_kv"][0]),
        "g_hawk": expand_gain(f(inp["hawk_norm"][0])),
        "g_hawk_mem": expand_gain(f(inp["hawk_mem_norm"][0])),
        "bd_a": block_diag(f(inp["hawk_gate_a_w"][0])),
        "bd_x": block_diag(f(inp["hawk_gate_x_w"][0])),
        "final_norm": f(inp["final_norm"]),
        "nsa_w_in": f(inp["nsa_w_in"][0]),
        "nsa_w_out": f(inp["nsa_w_out"][0]),
        "nsa_w_mem_kv": f(inp["nsa_w_mem_kv"][0]),
        "g_nsa": expand_gain(f(inp["nsa_norm"][0])),
        "g_nsa_mem": expand_gain(f(inp["nsa_mem_norm"][0])),
        "w2k": f(inp["nsa_phi_k_w2"][0]),
        "w2v": f(inp["nsa_phi_v_w2"][0]),
    }
    for k in ("identf", "edil", "ecmp", "m12", "tri", "kaug", "qal"):
        shared[k] = _CONST[k]

    def w1_layout(w1):
        a = w1.reshape(32, 64, 256).transpose(1, 0, 2)
        return np.ascontiguousarray(np.concatenate([a, a], axis=0))
    shared["w1k"] = w1_layout(f(inp["nsa_phi_k_w1"][0]))
    shared["w1v"] = w1_layout(f(inp["nsa_phi_v_w1"][0]))
    shared["peT"] = np.ascontiguousarray(np.stack([f(inp["nsa_pe_k"][0]).T, f(inp["nsa_pe_v"][0]).T], axis=1))
    lv = np.zeros((128, 8, 8), np.float32)
    cw = f(inp["hawk_conv_w"][0])
    for k in range(4):
        lv[:, :, k] = vec_fm(cw[k])
    lv[:, :, 4] = vec_fm(f(inp["hawk_conv_b"][0]))
    lv[:, :, 5] = vec_fm(f(inp["hawk_gate_a_b"][0]).reshape(-1))
    lv[:, :, 6] = vec_fm(f(inp["hawk_gate_x_b"][0]).reshape(-1))
    lv[:, :, 7] = vec_fm(f(inp["hawk_lambda"][0]))
    shared["lru_vec"] = lv
    x = f(inp["x"])
    mem = f(inp["mem"])
    maps = []
    for b in range(x.shape[0]):
        m = dict(shared)
        m["x"] = x[b]
        m["mem"] = mem[b]
        maps.append(m)
    return maps


def kernel(**inputs):
    maps = prep_inputs(inputs)
    nc = build_program()
    res = run_bass_kernel_spmd(nc, maps, core_ids=list(range(len(maps))))
    out = np.stack([np.asarray(r["out"], dtype=np.float32) for r in res.results], axis=0)
    return out
```

```python
import math
from contextlib import ExitStack

import numpy as np
import concourse.bass as bass
import concourse.mybir as mybir
from concourse.bass_utils import run_bass_kernel_spmd

F32 = mybir.dt.float32
BF16 = mybir.dt.bfloat16
AF = mybir.ActivationFunctionType
ALU = mybir.AluOpType
AX = mybir.AxisListType

S_LEN = 2048
D = 1024
NT_ = 16
EPS = 1e-6
DIL_GROUPS = ((128, 1), (512, 4), (2048, 16))

SEM_LIMIT = 30000
N_DMA_SEMS = 24
SAME_ENGINE_SYNC = True


class Buf:
    def __init__(self, name, t, excl=False):
        self.name = name
        self.t = t
        self.excl = excl
        self.st = {}

    def __getitem__(self, idx):
        return self.t[idx]


class Sync:
    def __init__(self, nc, stack):
        self.nc = nc
        self.stack = stack
        self.engs = ["pe", "act", "dve", "pool", "sp"]
        self.ops = {e: [] for e in self.engs}
        self.cur_sem = {}
        self.cnt = {}
        self.nsem = 0
        for e in self.engs:
            self._new_sem(e)
        self.dma_sems = {}
        self.dma_val = {}
        self.dma_rr = {}
        for e in ["sp", "pool", "act"]:
            self.dma_sems[e] = [self._alloc_sem(f"d{e}{i}") for i in range(N_DMA_SEMS)]
            self.dma_val[e] = [0] * N_DMA_SEMS
            self.dma_rr[e] = 0
        self.seen = {e: {} for e in self.engs}
        self.all_ticks = {}
        self.nops = 0
        self.dead = False
        self.eng_free = {e: 0.0 for e in self.engs}
        self.lane = None
        self.tnow = 0.0

    def _alloc_sem(self, name):
        self.nsem += 1
        return self.stack.enter_context(self.nc.semaphore(f"s_{name}_{self.nsem}"))

    def _new_sem(self, e):
        self.cur_sem[e] = self._alloc_sem(e)
        self.cnt[e] = 0

    def _states(self, buf, key, create):
        if key is None:
            if create and None not in buf.st:
                buf.st[None] = [None, {}]
            return list(buf.st.values())
        out = []
        if None in buf.st:
            out.append(buf.st[None])
        if key not in buf.st and create:
            buf.st[key] = [None, {}]
        if key in buf.st:
            out.append(buf.st[key])
        return out

    @staticmethod
    def _norm(lst):
        out = []
        for r in lst or []:
            out.append(r if isinstance(r, tuple) else (r, None))
        return out

    def op(self, eng, fn, reads=None, writes=None, dma=False, cost=0.5):
        if self.dead:
            return None
        reads = self._norm(reads)
        writes = self._norm(writes)
        ex = [(b, None) for (b, k) in reads + writes if b.excl]
        if ex:
            reads = [(b, k) for (b, k) in reads if not b.excl]
            writes = [(b, k) for (b, k) in writes if not b.excl]
            for bk in ex:
                if bk not in writes:
                    writes.append(bk)
        need = []
        for buf, key in reads:
            for st in self._states(buf, key, False):
                if st[0] is not None:
                    need.append(st[0])
        for buf, key in writes:
            for st in self._states(buf, key, False):
                if st[0] is not None:
                    need.append(st[0])
                need.extend(st[1].values())
        if dma:
            i = self.dma_rr[eng]
            self.dma_rr[eng] = (i + 1) % N_DMA_SEMS
            sem = self.dma_sems[eng][i]
            prev = self.dma_val[eng][i]
            if prev > 0:
                need.append((sem, prev, "dma", 0.0))
            if prev + 16 > SEM_LIMIT:
                sem = self._alloc_sem(f"d{eng}{i}")
                self.dma_sems[eng][i] = sem
                prev = 0
            val = prev + 16
            self.dma_val[eng][i] = val
            inc = 16
            tick = [sem, val, "dma", 0.0]
        else:
            if self.cnt[eng] + 1 > SEM_LIMIT:
                self._new_sem(eng)
            self.cnt[eng] += 1
            sem = self.cur_sem[eng]
            val = self.cnt[eng]
            inc = 1
            tick = [sem, val, eng, 0.0]
        ready = 0.0
        for nd in need:
            if nd[3] > ready:
                ready = nd[3]
        start = max(self.eng_free[eng], ready + 0.06)
        if dma:
            self.eng_free[eng] = start + 0.06
        else:
            self.eng_free[eng] = start + cost
        tick[3] = start + cost
        tick = tuple(tick)
        if self.lane is not None and tick[3] > self.lane.clock:
            self.lane.clock = tick[3]
        if tick[3] > self.tnow:
            self.tnow = tick[3]
        waits = {}
        seen = self.seen[eng]
        for (s, v, src, _fin) in need:
            if src == eng and (eng == "pe" or not SAME_ENGINE_SYNC):
                continue
            sid = id(s)
            if seen.get(sid, 0) >= v:
                continue
            if sid not in waits or waits[sid][1] < v:
                waits[sid] = (s, v)
        for sid, (s, v) in waits.items():
            seen[sid] = v
        self.ops[eng].append((list(waits.values()), fn, sem, inc))
        self.all_ticks[id(sem)] = (sem, val)
        self.nops += 1
        wset = set((id(b), k) for b, k in writes)
        for buf, key in reads:
            if (id(buf), key) in wset:
                continue
            self._states(buf, key, True)
            buf.st[key][1][eng if not dma else ("dma", id(sem))] = tick
        for buf, key in writes:
            if key is None:
                buf.st = {None: [tick, {}]}
            else:
                buf.st[key] = [tick, {}]
        return tick

    def barrier(self):
        if self.dead:
            return
        ticks = list(self.all_ticks.values())
        for e in self.engs:
            wl = []
            for (s, v) in ticks:
                if self.seen[e].get(id(s), 0) < v:
                    wl.append((s, v))
                    self.seen[e][id(s)] = v
            if wl:
                self.ops[e].append((wl, None, None, 0))

    def emit(self, block):
        S = self

        def run(engname, e):
            for (wl, fn, sem, inc) in S.ops[engname]:
                for (s, v) in wl:
                    e.wait_ge(s, v)
                if fn is not None:
                    fn(e).then_inc(sem, inc)

        @block.sync
        def _(e):
            run("sp", e)

        @block.tensor
        def _(e):
            run("pe", e)

        @block.scalar
        def _(e):
            run("act", e)

        @block.vector
        def _(e):
            run("dve", e)

        @block.gpsimd
        def _(e):
            run("pool", e)


class BankView:
    def __init__(self, pair, half):
        self.pair = pair
        self.off = 512 * half

    def __getitem__(self, idx):
        if not isinstance(idx, tuple):
            idx = (idx, slice(None))
        pr, col = idx
        cs = (col.start or 0) + self.off
        ce = (col.stop if col.stop is not None else 512) + self.off
        return self.pair[pr, cs:ce:col.step] if col.step else self.pair[pr, cs:ce]


class KB:
    def __init__(self, nc, stack):
        self.nc = nc
        self.gst = stack
        self.S = Sync(nc, stack)
        self.pairs = [stack.enter_context(nc.psum_tensor(f"pair{i}", [128, 1024], F32)) for i in range(4)]
        self.banks = [Buf(f"bank{i}", BankView(self.pairs[i // 2], i % 2), excl=True) for i in range(8)]
        self.bank_rr = 0
        self.uid = 0

    def sb(self, name, shape, dt, stack=None):
        self.uid += 1
        t = (stack or self.gst).enter_context(self.nc.sbuf_tensor(f"{name}_{self.uid}", shape, dt))
        return Buf(name, t)

    def bank(self):
        b = self.banks[self.bank_rr]
        self.bank_rr = (self.bank_rr + 1) % 8
        return b

    @staticmethod
    def fsz(ap):
        n = 1
        for s in ap.shape[1:]:
            n *= int(s)
        return n

    def vcost(self, eng, ap):
        n = self.fsz(ap)
        if eng == "pool":
            return 0.3 + n / 480.0
        if eng == "act":
            return 0.22 + n / 1400.0
        return 0.08 + n / 960.0

    def mm(self, out, lhsT, rhs, start, stop, r, w):
        c = max(self.fsz(rhs), 64) / 1600.0 + 0.04
        self.S.op("pe", lambda e: e.matmul(out, lhsT=lhsT, rhs=rhs, start=start, stop=stop), reads=r, writes=w, cost=c)

    def tr(self, out, in_, ident, r, w):
        self.S.op("pe", lambda e: e.transpose(out, in_, ident), reads=r, writes=w, cost=0.11)

    def act(self, out, in_, func, r, w, **kw):
        self.S.op("act", lambda e: e.activation(out=out, in_=in_, func=func, **kw), reads=r, writes=w, cost=self.vcost("act", out))

    def tt(self, eng, out, in0, in1, op, r, w):
        self.S.op(eng, lambda e: e.tensor_tensor(out=out, in0=in0, in1=in1, op=op), reads=r, writes=w, cost=self.vcost(eng, out))

    def ts(self, eng, out, in0, s1, s2, op0, op1, r, w, **kw):
        c = self.vcost(eng, out)
        if op1 is None:
            self.S.op(eng, lambda e: e.tensor_scalar(out=out, in0=in0, scalar1=s1, scalar2=None, op0=op0, **kw), reads=r, writes=w, cost=c)
        else:
            self.S.op(eng, lambda e: e.tensor_scalar(out=out, in0=in0, scalar1=s1, scalar2=s2, op0=op0, op1=op1, **kw), reads=r, writes=w, cost=c)

    def stt(self, out, in0, scalar, in1, op0, op1, r, w, **kw):
        self.S.op("dve", lambda e: e.scalar_tensor_tensor(out=out, in0=in0, scalar=scalar, in1=in1, op0=op0, op1=op1, **kw), reads=r, writes=w,
                  cost=0.12 + self.fsz(out) / 960.0)

    def cp(self, eng, out, in_, r, w):
        c = self.vcost(eng, out)
        if eng == "act":
            self.S.op("act", lambda e: e.activation(out=out, in_=in_, func=AF.Copy), reads=r, writes=w, cost=c)
        else:
            self.S.op(eng, lambda e: e.tensor_copy(out=out, in_=in_), reads=r, writes=w, cost=c)

    def memset(self, eng, ap, val, w):
        self.S.op(eng, lambda e: e.memset(ap, val), writes=w, cost=self.vcost(eng, ap))

    def recip(self, out, in_, r, w):
        self.S.op("dve", lambda e: e.reciprocal(out=out, in_=in_), reads=r, writes=w, cost=self.vcost("dve", out))

    def dma(self, q, out, in_, r, w):
        nbytes = self.fsz(out) * int(out.shape[0]) * 4
        self.S.op(q, lambda e: e.dma_start(out=out, in_=in_), reads=r, writes=w, dma=True, cost=2.0 + nbytes / 150000.0)


def alibi_slopes(n):
    return np.exp2(-8.0 * np.arange(1, n + 1) / n).astype(np.float32)


def host_constants():
    c = {}
    c["identf"] = np.eye(128, dtype=np.float32)
    sl = alibi_slopes(12)
    ik = np.arange(128)[:, None].astype(np.float64)
    iq = np.arange(128)[None, :].astype(np.float64)
    E = np.zeros((128, 12, 256), np.float32)
    for g, (win, dil) in enumerate(DIL_GROUPS):
        for hs in range(4):
            hh = g * 4 + hs
            s = float(sl[hh]) * dil
            dist_prev = 128 + iq - ik
            ok_prev = (dist_prev <= 128)
            E[:, hh, 0:128] = np.where(ok_prev, np.exp(-s * dist_prev), 0.0)
            dist_cur = iq - ik
            ok_cur = dist_cur >= 0
            E[:, hh, 128:256] = np.where(ok_cur, np.exp(-s * dist_cur), 0.0)
    c["edil"] = E
    return c


def expand_gain(g):
    return np.ascontiguousarray(np.broadcast_to(g.reshape(8, 128).T[:, :, None], (128, 8, 128))).astype(np.float32)


def vec_fm(v):
    return np.ascontiguousarray(v.reshape(8, 128).T).astype(np.float32)


def block_diag(gw):
    out = np.zeros((128, 8, 128), np.float32)
    for c in range(8):
        out[0:64, c, 0:64] = gw[2 * c]
        out[64:128, c, 64:128] = gw[2 * c + 1]
    return out


NEGB = 8192.0


def _bf16_split3(a):
    import ml_dtypes
    a = a.astype(np.float32)
    hi = a.astype(ml_dtypes.bfloat16).astype(np.float32)
    r1 = (a - hi).astype(np.float32)
    mid = r1.astype(ml_dtypes.bfloat16).astype(np.float32)
    r2 = (r1 - mid).astype(np.float32)
    lo = r2.astype(ml_dtypes.bfloat16).astype(np.float32)
    return hi, mid, lo


def host_constants_nsa():
    c = {}
    sl = alibi_slopes(16)
    i = np.arange(128)[:, None].astype(np.float64)
    m = np.arange(247)[None, :].astype(np.float64)
    dist = i - 16.0 * (m - 120.0) - 31.0
    E = np.zeros((128, 16, 247), np.float32)
    for h in range(16):
        E[:, h, :] = np.where(dist >= 0, np.exp(-float(sl[h]) * np.maximum(dist, 0.0)), 0.0)
    c["ecmp"] = E
    ii = np.arange(128)[:, None]
    rel = np.arange(62)[None, :] - 30
    cur = (ii >= 64).astype(np.int64)
    forced = (rel == cur) | (rel == cur - 1)
    future = rel > cur
    m1 = np.where(forced | future, 0.0, 1.0).astype(np.float32)
    m2 = np.where(forced, 1e6, np.where(future, -1e6, 0.0)).astype(np.float32)
    c["m12"] = np.ascontiguousarray(np.stack([m1, m2], axis=1))
    ik = np.arange(128)[:, None]
    iq = np.arange(128)[None, :]
    diag = np.where(ik > iq, -NEGB, 0.0).astype(np.float32)
    far = np.where(ik <= iq, -NEGB, 0.0).astype(np.float32)
    c["tri"] = np.ascontiguousarray(np.stack([np.tile(diag, (1, 4)), np.tile(far, (1, 4))], axis=1))
    k = np.arange(2048)
    kp = k - 1024
    hi = (np.floor(kp / 128.0) * 128.0).astype(np.float32)
    lo = (kp - hi).astype(np.float32)
    ka = np.zeros((2, 41, 2048), np.float32)
    for j in range(32):
        ka[0, j, :] = (k // 64 == j).astype(np.float32)
    for v in range(2):
        ka[v, 32:35, :] = 1.0
        ka[v, 35:38, :] = lo[None, :]
        ka[v, 38:41, :] = hi[None, :]
    c["kaug"] = ka
    qa = np.zeros((16, 9, 2048), np.float32)
    qp = (np.arange(2048) - 1024).astype(np.float32)
    for h in range(16):
        s8 = np.float32(8.0) * np.float32(sl[h])
        a = (-s8 * qp).astype(np.float32)
        ah, am, al = _bf16_split3(a)
        sh, sm, sl_ = _bf16_split3(np.full((2048,), s8, np.float32))
        qa[h, 0], qa[h, 1], qa[h, 2] = ah, am, al
        qa[h, 3], qa[h, 4], qa[h, 5] = sh, sm, sl_
        qa[h, 6], qa[h, 7], qa[h, 8] = sh, sm, sl_
    c["qal"] = qa
    return c


def chain(*gens):
    for g_ in gens:
        yield from g_


def run_lanes(lanes):
    active = list(lanes)
    while active:
        for l in list(active):
            try:
                next(l)
            except StopIteration:
                active.remove(l)


def build_program(stop_after=None):
    nc = bass.Bass("TRN2", target_bir_lowering=False)

    ckstate = {}

    def ck(name):
        if stop_after == name:
            ckstate["S"].dead = True

    def din(name, shape):
        return nc.dram_tensor(name, list(shape), F32, kind="ExternalInput").ap()

    x_d = din("x", [S_LEN, D])
    mem_d = din("mem", [256, D])
    hawk_w_in = din("hawk_w_in", [D, 7680])
    hawk_w_out = din("hawk_w_out", [1792, D])
    hawk_w_mem_kv = din("hawk_w_mem_kv", [D, 512])
    g_hawk = din("g_hawk", [128, 8, 128])
    g_hawk_mem = din("g_hawk_mem", [128, 8, 128])
    lru_vec = din("lru_vec", [128, 8, 8])
    bd_a = din("bd_a", [128, 8, 128])
    bd_x = din("bd_x", [128, 8, 128])
    identf_d = din("identf", [128, 128])
    edil_d = din("edil", [128, 12, 256])
    final_g = din("final_norm", [D])
    nsa_w_in = din("nsa_w_in", [D, 3376])
    nsa_w_out = din("nsa_w_out", [1280, D])
    nsa_w_mem_kv = din("nsa_w_mem_kv", [D, 512])
    g_nsa = din("g_nsa", [128, 8, 128])
    g_nsa_mem = din("g_nsa_mem", [128, 8, 128])
    w1k_d = din("w1k", [128, 32, 256])
    w1v_d = din("w1v", [128, 32, 256])
    w2k_d = din("w2k", [256, 64])
    w2v_d = din("w2v", [256, 64])
    peT_d = din("peT", [64, 2, 32])
    ecmp_d = din("ecmp", [128, 16, 247])
    m12_d = din("m12", [128, 2, 62])
    tri_d = din("tri", [128, 2, 512])
    kaug_d = din("kaug", [2, 41, 2048])
    qal_d = din("qal", [16, 9, 2048])
    out_d = nc.dram_tensor("out", [S_LEN, D], F32, kind="ExternalOutput").ap()
    x1_scr = nc.dram_tensor("x1_scr", [S_LEN, D], F32, kind="Internal").ap()

    with ExitStack() as gst:
        kb = KB(nc, gst)
        S = kb.S
        ckstate["S"] = S
        DX = Buf("x_dram", None)
        DX1 = Buf("x1_dram", None)
        DOUT = Buf("out_dram", None)

        xnT = kb.sb("xnT", [128, 8, S_LEN], BF16)
        memnT = kb.sb("memnT", [128, 8, 256], BF16)
        identf = kb.sb("identf", [128, 128], F32)
        identb = kb.sb("identb", [128, 128], BF16)
        onesb = kb.sb("onesb", [128, 128], BF16)
        wstage = [kb.sb(f"wstage{i}", [128, 1024], F32) for i in range(3)]
        ws_rr = [0]
        stat = kb.sb("stat", [128, 64], F32)
        stat_rr = [0]

        kb.dma("sp", identf[:], identf_d, [], [identf])
        kb.cp("dve", identb[:], identf[:], [identf], [identb])
        kb.memset("dve", onesb[:], 1.0, [onesb])

        def next_ws():
            b = wstage[ws_rr[0]]
            ws_rr[0] = (ws_rr[0] + 1) % len(wstage)
            return b

        def load_w(dst, dst_ap3, src_ap3, n, gain=None, key=None, q="sp", part=128, eng="pool"):
            dcs = dst_ap3.shape[1]
            assert dcs * n <= 1024
            stg = next_ws()
            sv = stg[0:part, 0:dcs * n].rearrange("p (c n) -> p c n", c=dcs)
            kb.dma(q, sv, src_ap3, [], [stg])
            if gain is not None:
                kb.tt(eng, dst_ap3, sv, gain[0:part, 0:dcs, 0:n], ALU.mult, [stg, gain], [(dst, key)])
            else:
                kb.cp(eng, dst_ap3, sv, [stg], [(dst, key)])

        def win_cols(w_dram, c0, n):
            return w_dram.rearrange("(dc p) n -> p dc n", p=128)[:, :, c0:c0 + n]

        class NormCtx:
            def __init__(self, stack, nbuf=2):
                self.xstage = [kb.sb(f"xstage{i}", [128, 1024], F32, stack) for i in range(nbuf)]
                self.xnb = [kb.sb(f"xnb{i}", [128, 1024], BF16, stack) for i in range(nbuf)]
                self.junk = kb.sb("junk", [128, 1024], BF16, stack)

        def tile_rstd(ncx, xbuf, xap):
            i = stat_rr[0]
            stat_rr[0] = (stat_rr[0] + 1) % 32
            ss = stat[:, 2 * i:2 * i + 1]
            rs = stat[:, 2 * i + 1:2 * i + 2]
            kb.stt(ncx.junk[:], xap, 1.0, xap, ALU.mult, ALU.mult, [xbuf], [ncx.junk, (stat, i)], accum_out=ss)
            kb.ts("dve", ss, ss, 1.0 / D, EPS, ALU.mult, ALU.add, [(stat, i)], [(stat, i)])
            kb.act(ss, ss, AF.Sqrt, [(stat, i)], [(stat, i)])
            kb.recip(rs, ss, [(stat, i)], [(stat, i)])
            return rs, i

        def norm_to_T(ncx, xbuf, xap, dstT, t, ntok_off):
            rs, i = tile_rstd(ncx, xbuf, xap)
            nb = ncx.xnb[t % 2]
            kb.ts("dve", nb[:], xap, rs, None, ALU.mult, None, [xbuf, (stat, i)], [nb])
            bk = kb.bank()
            bv = bk[:].bitcast(BF16)
            for c in range(8):
                kb.tr(bv[:, c * 128:(c + 1) * 128], nb[:, c * 128:(c + 1) * 128], identb[:], [nb, identb], [bk])
            kb.cp("act", dstT[:, :, ntok_off:ntok_off + 128], bv.rearrange("p (c n) -> p c n", c=8), [bk], [(dstT, t)])

        def norm_to_T_gen(ncx, xbuf, xap, dstT, t, ntok_off, bk, bi):
            rs, i = tile_rstd(ncx, xbuf, xap)
            yield
            nb = ncx.xnb[bi]
            kb.ts("dve", nb[:], xap, rs, None, ALU.mult, None, [xbuf, (stat, i)], [nb])
            yield
            bv = bk[:].bitcast(BF16)
            for c in range(8):
                kb.tr(bv[:, c * 128:(c + 1) * 128], nb[:, c * 128:(c + 1) * 128], identb[:], [nb, identb], [bk])
            yield
            kb.cp("act", dstT[:, :, ntok_off:ntok_off + 128], bv.rearrange("p (c n) -> p c n", c=8), [bk], [(dstT, t)])
            yield

        with ExitStack() as pa:
            ncx = NormCtx(pa)
            for t in range(NT_):
                xs = ncx.xstage[t % 2]
                kb.dma("sp", xs[:], x_d[t * 128:(t + 1) * 128, :], [DX], [xs])
                norm_to_T(ncx, xs, xs[:], xnT, t, t * 128)
            for t in range(2):
                xs = ncx.xstage[t % 2]
                kb.dma("sp", xs[:], mem_d[t * 128:(t + 1) * 128, :], [], [xs])
                norm_to_T(ncx, xs, xs[:], memnT, t, t * 128)
            S.barrier()
            ck("A")

        def make_loader(w_in_d, gain, nslots, stack):
            wslots = [kb.sb(f"wslot{i}", [128, 8, 128], BF16, stack) for i in range(nslots)]
            rr = [0]

            def load_win(c0, n=128, q="sp", into=None, off=0):
                if into is None:
                    wsl = wslots[rr[0]]
                    rr[0] = (rr[0] + 1) % nslots
                else:
                    wsl = into
                load_w(wsl, wsl[:, :, off:off + n], win_cols(w_in_d, c0, n), n, gain=gain, q=q, key=off)
                return wsl
            return load_win

        def proj_fm(wsl, n, evac, woff=0):
            for tc in range(4):
                bk = kb.bank()
                for dc in range(8):
                    kb.mm(bk[0:n, :], wsl[:, dc, woff:woff + n], xnT[:, dc, tc * 512:(tc + 1) * 512], dc == 0, dc == 7,
                          [wsl, xnT], [bk])
                evac(bk, bk[0:n, :], tc)

        def mem_kv(w_kv_d, gain, kmT, vm, stack):
            wkv = kb.sb("wkv", [128, 8, 512], BF16, stack)
            for j in range(4):
                load_w(wkv, wkv[:, :, j * 128:(j + 1) * 128], win_cols(w_kv_d, j * 128, 128), 128, gain=gain, key=j)
            for h in range(4):
                bk = kb.bank()
                for dc in range(8):
                    kb.mm(bk[0:64, 0:256], wkv[:, dc, h * 64:(h + 1) * 64], memnT[:, dc, :], dc == 0, dc == 7,
                          [wkv, memnT], [bk])
                kb.cp("act", kmT[0:64, h, :], bk[0:64, 0:256], [bk], [(kmT, h)])
            for mt in range(2):
                bk = kb.bank()
                for dc in range(8):
                    kb.mm(bk[:, 0:256], memnT[:, dc, mt * 128:(mt + 1) * 128], wkv[:, dc, 256:512], dc == 0, dc == 7,
                          [wkv, memnT], [bk])
                kb.cp("act", vm[:, mt, :], bk[:, 0:256], [bk], [(vm, mt)])

        def mem_attn(load_win, colq, colz, kmT, vm, ymT, stack):
            qmTs = [kb.sb(f"qmT{i}", [64, S_LEN], BF16, stack) for i in range(2)]
            szms = [kb.sb(f"szm{i}", [64, S_LEN], BF16, stack) for i in range(2)]
            PTm = [kb.sb(f"PTm{i}", [128, 512], BF16, stack) for i in range(4)]
            rdm = [kb.sb(f"rdm{i}", [64, 512], F32, stack) for i in range(2)]

            def lane(h, L):
                qmT, szm = qmTs[L], szms[L]
                b0, b1, b2, b3 = [kb.banks[4 * L + j] for j in range(4)]
                wq = load_win(colq + h * 64, 64)
                wz = load_win(colz + h * 64, 64)
                for (wsl, dst, func) in ((wq, qmT, None), (wz, szm, AF.Silu)):
                    for tc in range(4):
                        bk = b0 if tc % 2 == 0 else b1
                        for dc in range(8):
                            kb.mm(bk[0:64, :], wsl[:, dc, 0:64], xnT[:, dc, tc * 512:(tc + 1) * 512], dc == 0, dc == 7, [wsl, xnT], [bk])
                        yield
                        if func is None:
                            kb.cp("act", dst[0:64, tc * 512:(tc + 1) * 512], bk[0:64, :], [bk], [(dst, tc)])
                        else:
                            kb.act(dst[0:64, tc * 512:(tc + 1) * 512], bk[0:64, :], func, [bk], [(dst, tc)])
                        yield
                for tc in range(4):
                    pts = []
                    for mt in range(2):
                        bk = b0 if mt == 0 else b1
                        kb.mm(bk[:, :], kmT[0:64, h, mt * 128:(mt + 1) * 128], qmT[0:64, tc * 512:(tc + 1) * 512],
                              True, True, [kmT, (qmT, tc)], [bk])
                        pt = PTm[2 * L + mt]
                        kb.act(pt[:], bk[:], AF.Exp, [bk], [pt], scale=0.125)
                        pts.append(pt)
                        yield
                    for mt in range(2):
                        kb.mm(b2[0:64, :], vm[:, mt, h * 64:(h + 1) * 64], pts[mt][:], mt == 0, mt == 1, [vm, pts[mt]], [b2])
                    for mt in range(2):
                        kb.mm(b3[0:64, :], onesb[:, 0:64], pts[mt][:], mt == 0, mt == 1, [onesb, pts[mt]], [b3])
                    yield
                    rd = rdm[L]
                    kb.recip(rd[:], b3[0:64, :], [b3], [rd])
                    kb.tt("dve", rd[:], b2[0:64, :], rd[:], ALU.mult, [b2, rd], [rd])
                    yield
                    kb.tt("dve", ymT[0:64, h, tc * 512:(tc + 1) * 512], rd[:], szm[0:64, tc * 512:(tc + 1) * 512], ALU.mult,
                          [rd, (szm, tc)], [(ymT, (h, tc))])
                    yield

            run_lanes([lane(0, 0), lane(1, 1)])
            run_lanes([lane(2, 0), lane(3, 1)])

        def out_proj(w_out_d, nch, yTl, ymT, resid_d, resid_buf, final, stack, dbg=False):
            ysrc = []
            for (yb_, n_) in yTl:
                for ci in range(n_):
                    ysrc.append((yb_, ci))
            ncxs = [NormCtx(stack, 1), NormCtx(stack, 1)]
            WO = kb.sb("WO", [128, nch, 1024], BF16, stack)
            WOm = kb.sb("WOm", [64, 4, 1024], BF16, stack)
            wo_v = w_out_d[0:nch * 128, :].rearrange("(c p) n -> p c n", p=128)
            for c in range(nch):
                load_w(WO, WO[:, c:c + 1, :], wo_v[:, c:c + 1, :], 1024, key=c)
            wom_v = w_out_d[nch * 128:nch * 128 + 256, :].rearrange("(h p) n -> p h n", p=64)
            for h in range(4):
                load_w(WOm, WOm[0:64, h:h + 1, :], wom_v[:, h:h + 1, :], 1024, key=h, part=64)
            x1t = [kb.sb(f"x1t{i}", [128, 1024], F32, stack) for i in range(2)]
            if final:
                gF = kb.sb("gF", [128, 1024], F32, stack)
                kb.dma("sp", gF[:], final_g.partition_broadcast(128), [], [gF])
                ot = [kb.sb(f"ot{i}", [128, 1024], F32, stack) for i in range(2)]

            def lane(L):
                ncx = ncxs[L]
                bks = [kb.banks[4 * L + j] for j in range(4)]
                for t in range(L, NT_, 2):
                    xs = ncx.xstage[0]
                    kb.dma("sp", xs[:], resid_d[t * 128:(t + 1) * 128, :], [resid_buf], [xs])
                    x1 = x1t[L]
                    for half in range(2):
                        bk = bks[half]
                        for c in range(nch):
                            yb_, ci = ysrc[c]
                            kb.mm(bk[:, :], yb_[:, ci, t * 128:(t + 1) * 128], WO[:, c, half * 512:(half + 1) * 512], c == 0, False,
                                  [yb_, WO], [bk])
                            if c % 4 == 3:
                                yield
                        for h in range(4):
                            kb.mm(bk[:, :], ymT[0:64, h, t * 128:(t + 1) * 128], WOm[0:64, h, half * 512:(half + 1) * 512], False, h == 3,
                                  [ymT, WOm], [bk])
                        yield
                        kb.tt("dve", x1[:, half * 512:(half + 1) * 512], xs[:, half * 512:(half + 1) * 512], bk[:], ALU.add,
                              [xs, bk], [(x1, half)])
                        yield
                    if not final:
                        kb.dma("sp", x1_scr[t * 128:(t + 1) * 128, :], x1[:], [x1], [DX1])
                        yield from norm_to_T_gen(ncx, x1, x1[:], xnT, t, t * 128, bks[2], 0)
                        if dbg:
                            kb.dma("sp", out_d[t * 128:(t + 1) * 128, :], x1[:], [x1], [DOUT])
                    else:
                        rs, i = tile_rstd(ncx, x1, x1[:])
                        yield
                        o = ot[L]
                        kb.stt(o[:], x1[:], rs, gF[:], ALU.mult, ALU.mult, [x1, (stat, i), gF], [o])
                        yield
                        kb.dma("sp", out_d[t * 128:(t + 1) * 128, :], o[:], [o], [DOUT])
                        yield

            run_lanes([lane(0), lane(1)])

        with ExitStack() as l0:
            yTa = kb.sb("yTa", [128, 8, S_LEN], BF16, l0)
            ymT = kb.sb("ymT", [64, 4, S_LEN], BF16, l0)
            gH = kb.sb("gH", [128, 8, 128], F32, l0)
            kb.dma("sp", gH[:], g_hawk, [], [gH])
            load_win = make_loader(hawk_w_in, gH, 6, l0)
            kmT = kb.sb("kmT", [64, 4, 256], BF16, l0)
            vm = kb.sb("vm", [128, 2, 256], BF16, l0)
            with ExitStack() as pm:
                gHm = kb.sb("gHm", [128, 8, 128], F32, pm)
                kb.dma("sp", gHm[:], g_hawk_mem, [], [gHm])
                mem_kv(hawk_w_mem_kv, gHm, kmT, vm, pm)
                S.barrier()
                ck("memkv0")
            with ExitStack() as pd:
                mem_attn(load_win, 7168, 7424, kmT, vm, ymT, pd)
                S.barrier()
                ck("mem0")

            with ExitStack() as pb:
                lv = kb.sb("lv", [128, 8, 8], F32, pb)
                cvec = kb.sb("cvec", [128, 8, 2], F32, pb)
                bda = kb.sb("bda", [128, 8, 128], BF16, pb)
                bdx = kb.sb("bdx", [128, 8, 128], BF16, pb)
                kb.dma("sp", lv[:], lru_vec, [], [lv])
                load_w(bda, bda[:], bd_a, 128)
                load_w(bdx, bdx[:], bd_x, 128)
                kb.act(cvec[:, :, 0], lv[:, :, 7], AF.Exp, [lv], [cvec], scale=-1.0)
                kb.act(cvec[:, :, 0], cvec[:, :, 0], AF.Ln, [cvec], [cvec], bias=1.0)
                kb.ts("dve", cvec[:, :, 1], cvec[:, :, 0], -16.0, None, ALU.mult, None, [cvec], [cvec])
                kb.ts("dve", cvec[:, :, 0], cvec[:, :, 0], -8.0, None, ALU.mult, None, [cvec], [cvec])
                sets = []
                for L in range(2):
                    sets.append(dict(
                        B1=kb.sb(f"B1_{L}", [128, S_LEN + 4], F32, pb), B2=kb.sb(f"B2_{L}", [128, S_LEN], F32, pb),
                        B3=kb.sb(f"B3_{L}", [128, S_LEN], F32, pb), B4=kb.sb(f"B4_{L}", [128, S_LEN], F32, pb),
                        xcb=kb.sb(f"xcb_{L}", [128, S_LEN], BF16, pb), sz=kb.sb(f"sz_{L}", [128, S_LEN], BF16, pb)))

                def lru_lane(L):
                    st_ = sets[L]
                    B1, B2, B3, B4, xcb, sz = st_["B1"], st_["B2"], st_["B3"], st_["B4"], st_["xcb"], st_["sz"]
                    bks = [kb.banks[4 * L + j] for j in range(4)]
                    for c in range(L, 8, 2):
                        wxa = load_win(c * 128)
                        wza = load_win(1024 + c * 128)
                        kb.memset("dve", B1[:, 0:3], 0.0, [(B1, "pad")])
                        for (wsl, which) in ((wxa, 0), (wza, 1)):
                            for tc in range(4):
                                bk = bks[tc % 4]
                                for dc in range(8):
                                    kb.mm(bk[:, :], wsl[:, dc, 0:128], xnT[:, dc, tc * 512:(tc + 1) * 512], dc == 0, dc == 7, [wsl, xnT], [bk])
                                yield
                                if which == 0:
                                    kb.cp("act", B1[:, 3 + tc * 512:3 + (tc + 1) * 512], bk[:], [bk], [(B1, tc)])
                                else:
                                    kb.act(sz[:, tc * 512:(tc + 1) * 512], bk[:], AF.Silu, [bk], [(sz, tc)])
                                yield
                        kb.ts("dve", B2[:], B1[:, 0:S_LEN], lv[:, c, 0:1], lv[:, c, 4:5], ALU.mult, ALU.add, [B1, lv], [B2])
                        yield
                        for k in range(1, 4):
                            kb.stt(B2[:], B1[:, k:k + S_LEN], lv[:, c, k:k + 1], B2[:], ALU.mult, ALU.add, [B1, lv, B2], [B2])
                            yield
                        kb.cp("dve", xcb[:], B2[:], [B2], [xcb])
                        yield
                        for (bd, dstb, col) in ((bda, B1, 5), (bdx, B4, 6)):
                            for tc in range(4):
                                bk = bks[tc % 4]
                                kb.mm(bk[:, :], bd[:, c, :], xcb[:, tc * 512:(tc + 1) * 512], True, True, [bd, xcb], [bk])
                                yield
                                kb.act(dstb[:, tc * 512:(tc + 1) * 512], bk[:], AF.Sigmoid, [bk, lv], [(dstb, tc)], bias=lv[:, c, col:col + 1])
                                yield
                        r_ap = B1[:, 0:S_LEN]
                        kb.act(B3[:], r_ap, AF.Exp, [B1, cvec], [B3], scale=cvec[:, c, 0:1])
                        yield
                        kb.act(r_ap, r_ap, AF.Exp, [B1, cvec], [B1], scale=cvec[:, c, 1:2])
                        yield
                        kb.tt("dve", B2[:], B2[:], B4[:], ALU.mult, [B2, B4], [B2])
                        yield
                        kb.ts("dve", r_ap, r_ap, -1.0, 1.0, ALU.mult, ALU.add, [B1], [B1])
                        yield
                        kb.ts("dve", r_ap, r_ap, 0.0, None, ALU.max, None, [B1], [B1])
                        yield
                        kb.act(r_ap, r_ap, AF.Sqrt, [B1], [B1])
                        kb.memset("dve", B1[:, 0:1], 1.0, [B1])
                        yield
                        kb.tt("dve", B2[:], B2[:], r_ap, ALU.mult, [B2, B1], [B2])
                        yield
                        S.op("dve", lambda e, B4=B4, B3=B3, B2=B2: e.tensor_tensor_scan(out=B4[:], data0=B3[:], data1=B2[:], initial=0.0,
                                                                                      op0=ALU.mult, op1=ALU.add), reads=[B3, B2], writes=[B4])
                        yield
                        kb.tt("dve", yTa[:, c, :], B4[:], sz[:], ALU.mult, [B4, sz], [(yTa, c)])
                        yield

                run_lanes([lru_lane(0), lru_lane(1)])
                S.barrier()
                ck("lru")
            yTb = kb.sb("yTb", [128, 4, S_LEN], BF16, l0)

            with ExitStack() as pc:
                edil = kb.sb("edil", [128, 12, 256], F32, pc)
                kb.dma("sp", edil[:], edil_d, [], [edil])
                qTs = [kb.sb(f"qT{i}", [128, S_LEN], BF16, pc) for i in range(2)]
                kTs = [kb.sb(f"kT{i}", [128, S_LEN], BF16, pc) for i in range(2)]
                vTs = [kb.sb(f"vT{i}", [128, S_LEN], BF16, pc) for i in range(2)]
                Vps = [kb.sb(f"Vp{i}", [128, 16, 128], BF16, pc) for i in range(2)]
                szbs = [kb.sb(f"szb{i}", [128, S_LEN], BF16, pc) for i in range(2)]
                NTa = kb.sb("NTa", [128, S_LEN], F32, pc)
                DBa = kb.sb("DBa", [128, S_LEN], F32, pc)
                Pf = [kb.sb(f"Pf{i}", [128, 256], F32, pc) for i in range(3)]
                PT = [kb.sb(f"PT{i}", [128, 256], BF16, pc) for i in range(3)]
                sc_d = 128.0 ** -0.5
                items = [(hs, g) for hs in range(4) for g in range(3)]
                prot = [0]

                def pbank():
                    b = kb.banks[6 + prot[0]]
                    prot[0] ^= 1
                    return b

                def dtoks(d, r, b):
                    t0 = r + d * 128 * b
                    return slice(t0, t0 + d * 127 + 1, d)

                def proj_fm_lane(wsl, dst, func):
                    for tc in range(4):
                        bk = pbank()
                        for dc in range(8):
                            kb.mm(bk[:, :], wsl[:, dc, 0:128], xnT[:, dc, tc * 512:(tc + 1) * 512], dc == 0, dc == 7, [wsl, xnT], [bk])
                        yield
                        if func is None:
                            kb.cp("act", dst[:, tc * 512:(tc + 1) * 512], bk[:], [bk], [(dst, tc)])
                        else:
                            kb.act(dst[:, tc * 512:(tc + 1) * 512], bk[:], func, [bk], [(dst, tc)])
                        yield

                def task_proj(i):
                    hs, g = items[i]
                    win, d = DIL_GROUPS[g]
                    hh = g * 4 + hs
                    nqb = (S_LEN // d) // 128
                    s = i % 2
                    if g == 0:
                        wz = load_win(2048 + 4608 + hs * 128)
                        yield from proj_fm_lane(wz, szbs[hs % 2], AF.Silu)
                    wq = load_win(2048 + hh * 128)
                    yield from proj_fm_lane(wq, qTs[s], None)
                    wk = load_win(2048 + 1536 + hh * 128)
                    yield from proj_fm_lane(wk, kTs[s], None)
                    wv = load_win(2048 + 3072 + hh * 128)
                    yield from proj_fm_lane(wv, vTs[s], None)
                    for j in range(4):
                        bk = pbank()
                        bv = bk[:].bitcast(BF16)
                        for k in range(4):
                            r, b = divmod(4 * j + k, nqb)
                            kb.tr(bv[:, k * 128:(k + 1) * 128], vTs[s][:, dtoks(d, r, b)], identb[:], [vTs[s], identb], [bk])
                        yield
                        kb.cp("act", Vps[s][:, 4 * j:4 * j + 4, :], bv[:, 0:512].rearrange("p (k n) -> p k n", k=4), [bk], [(Vps[s], j)])
                        yield

                def task_tile(i, ti, lane):
                    hs, g = items[i]
                    win, d = DIL_GROUPS[g]
                    hh = g * 4 + hs
                    nqb = (S_LEN // d) // 128
                    s = i % 2
                    qT, kT, Vp = qTs[s], kTs[s], Vps[s]
                    r, qb = divmod(ti, nqb)
                    qs = dtoks(d, r, qb)
                    kbs = [qb - 1, qb] if qb > 0 else [qb]
                    sbk = kb.banks[2 * lane]
                    ndb = kb.banks[2 * lane + 1]
                    for kbi in kbs:
                        typ = 0 if kbi < qb else 1
                        kb.mm(sbk[:, typ * 128:(typ + 1) * 128], kT[:, dtoks(d, r, kbi)], qT[:, qs], True, True, [kT, qT], [sbk])
                    yield
                    lo = 0 if qb > 0 else 128
                    pf = Pf[lane]
                    pt = PT[lane]
                    kb.act(pf[:, lo:256], sbk[:, lo:256], AF.Exp, [sbk], [pf], scale=sc_d)
                    yield
                    kb.tt("dve", pt[:, lo:256], pf[:, lo:256], edil[:, hh, lo:256], ALU.mult, [pf, edil], [pt])
                    yield
                    for j, kbi in enumerate(kbs):
                        typ = 0 if kbi < qb else 1
                        kb.mm(ndb[:, 0:128], Vp[:, r * nqb + kbi, :], pt[:, typ * 128:(typ + 1) * 128], j == 0, j == len(kbs) - 1,
                              [Vp, pt], [ndb])
                    for j, kbi in enumerate(kbs):
                        typ = 0 if kbi < qb else 1
                        kb.mm(ndb[:, 128:256], onesb[:], pt[:, typ * 128:(typ + 1) * 128], j == 0, j == len(kbs) - 1,
                              [onesb, pt], [ndb])
                    yield
                    if g == 0:
                        kb.cp("dve", NTa[:, qs], ndb[:, 0:128], [ndb], [NTa])
                        kb.cp("dve", DBa[:, qs], ndb[:, 128:256], [ndb], [DBa])
                    else:
                        kb.tt("dve", NTa[:, qs], NTa[:, qs], ndb[:, 0:128], ALU.add, [ndb, NTa], [NTa])
                        kb.tt("dve", DBa[:, qs], DBa[:, qs], ndb[:, 128:256], ALU.add, [ndb, DBa], [DBa])
                    yield

                def tile_lane(i, lane):
                    for ti in range(lane, 16, 3):
                        yield from task_tile(i, ti, lane)

                run_lanes([task_proj(0)])
                for i in range(len(items)):
                    hs, g = items[i]
                    lanes = [tile_lane(i, 0), tile_lane(i, 1), tile_lane(i, 2)]
                    if i + 1 < len(items):
                        lanes.append(task_proj(i + 1))
                    run_lanes(lanes)
                    if g == 2:
                        kb.recip(DBa[:], DBa[:], [DBa], [DBa])
                        kb.tt("dve", NTa[:], NTa[:], DBa[:], ALU.mult, [NTa, DBa], [NTa])
                        kb.tt("dve", yTb[:, hs, :], NTa[:], szbs[hs % 2][:], ALU.mult, [NTa, szbs[hs % 2]], [(yTb, hs)])
                S.barrier()
                ck("dil")

            with ExitStack() as pe_:
                out_proj(hawk_w_out, 12, [(yTa, 8), (yTb, 4)], ymT, x_d, DX, False, pe_, dbg=(stop_after == "l0"))
                S.barrier()
                ck("l0end")

        if stop_after != "l0":
          with ExitStack() as l1:
            yT1 = kb.sb("yT1", [128, 8, S_LEN], BF16, l1)
            ymT1 = kb.sb("ymT1", [64, 4, S_LEN], BF16, l1)
            gN = kb.sb("gN", [128, 8, 128], F32, l1)
            kb.dma("sp", gN[:], g_nsa, [], [gN])
            load_win = make_loader(nsa_w_in, gN, 4, l1)
            kcmpT = kb.sb("kcmpT", [64, 2, 128], BF16, l1)
            vcmp = kb.sb("vcmp", [128, 2, 64], BF16, l1)
            gates = kb.sb("gates", [128, 16, 48], F32, l1)
            with ExitStack() as pmm:
                kmT = kb.sb("kmT1", [64, 4, 256], BF16, pmm)
                vm = kb.sb("vm1", [128, 2, 256], BF16, pmm)
                with ExitStack() as pm:
                    gNm = kb.sb("gNm", [128, 8, 128], F32, pm)
                    kb.dma("sp", gNm[:], g_nsa_mem, [], [gNm])
                    mem_kv(nsa_w_mem_kv, gNm, kmT, vm, pm)
                    S.barrier()
                    ck("memkv1")
                with ExitStack() as pd:
                    mem_attn(load_win, 2864, 3120, kmT, vm, ymT1, pd)
                    S.barrier()
                    ck("mem1")

            with ExitStack() as pq:
                wg = load_win(1792, 48)
                for t in range(NT_):
                    bk = kb.bank()
                    for dc in range(8):
                        kb.mm(bk[:, 0:48], xnT[:, dc, t * 128:(t + 1) * 128], wg[:, dc, 0:48], dc == 0, dc == 7, [xnT, wg], [bk])
                    kb.act(gates[:, t, :], bk[:, 0:48], AF.Sigmoid, [bk], [(gates, t)])
                kcT = kb.sb("kcT", [128, S_LEN], BF16, pq)
                vcT = kb.sb("vcT", [128, S_LEN], BF16, pq)
                wkc = load_win(1024)
                proj_fm(wkc, 128, lambda bk, ap, tc: kb.cp("act", kcT[:, tc * 512:(tc + 1) * 512], ap, [bk], [(kcT, tc)]))
                wvc = load_win(1024 + 128)
                proj_fm(wvc, 128, lambda bk, ap, tc: kb.cp("act", vcT[:, tc * 512:(tc + 1) * 512], ap, [bk], [(vcT, tc)]))
                W1 = kb.sb("W1", [128, 32, 256], BF16, pq)
                w2 = kb.sb("w2", [128, 2, 64], BF16, pq)
                peS = kb.sb("peS", [64, 2, 32], F32, pq)
                peb = kb.sb("peb", [64, 2, 32], BF16, pq)
                hidT = kb.sb("hidT", [128, 2, 128], BF16, pq)
                cb = kb.sb("cb", [128, 2], F32, pq)
                kb.dma("sp", peS[:], peT_d, [], [peS])
                kb.cp("dve", peb[:], peS[:], [peS], [peb])
                for kv in range(2):
                    w1d = w1k_d if kv == 0 else w1v_d
                    w2d = w2k_d if kv == 0 else w2v_d
                    srcT = kcT if kv == 0 else vcT
                    for p4 in range(8):
                        load_w(W1, W1[:, p4 * 4:(p4 + 1) * 4, :], w1d[:, p4 * 4:(p4 + 1) * 4, :], 256, key=p4)
                    load_w(w2, w2[:, :, :], w2d.rearrange("(hc p) d -> p hc d", p=128), 64)
                    for hc in range(2):
                        bk = kb.bank()
                        for p in range(32):
                            kb.mm(bk[:, 0:1], W1[0:64, p, hc * 128:(hc + 1) * 128], peb[0:64, kv, p:p + 1], p == 0, p == 31,
                                  [W1, peb], [bk])
                        kb.cp("dve", cb[:, hc:hc + 1], bk[:, 0:1], [bk], [(cb, hc)])
                    for g in range(2):
                        for hc in range(2):
                            bk = kb.bank()
                            for p in range(32):
                                kb.mm(bk[:, 0:127], W1[g * 64:(g + 1) * 64, p, hc * 128:(hc + 1) * 128],
                                      srcT[g * 64:(g + 1) * 64, p:p + 16 * 126 + 1:16], p == 0, p == 31, [W1, srcT], [bk])
                            kb.act(hidT[:, hc, 0:127], bk[:, 0:127], AF.Silu, [bk, cb], [(hidT, hc)], bias=cb[:, hc:hc + 1])
                        bk = kb.bank()
                        if kv == 0:
                            for hc in range(2):
                                kb.mm(bk[0:64, 0:127], w2[:, hc, :], hidT[:, hc, 0:127], hc == 0, hc == 1, [w2, hidT], [bk])
                            kb.cp("dve", kcmpT[0:64, g, 0:127], bk[0:64, 0:127], [bk], [(kcmpT, g)])
                        else:
                            for hc in range(2):
                                kb.mm(bk[0:127, 0:64], hidT[:, hc, 0:127], w2[:, hc, :], hc == 0, hc == 1, [w2, hidT], [bk])
                            kb.cp("dve", vcmp[0:127, g, :], bk[0:127, 0:64], [bk], [(vcmp, g)])
                S.barrier()
                ck("cmpkv")

            with ExitStack() as pg:
                QAg = kb.sb("QAg", [105, 8, S_LEN], BF16, pg)
                KAs = kb.sb("KAs", [105, S_LEN], BF16, pg)
                KAw = kb.sb("KAw", [105, S_LEN], BF16, pg)
                VAs = kb.sb("VAs", [128, 16, 128], BF16, pg)
                VAw = kb.sb("VAw", [128, 16, 128], BF16, pg)
                ecmp = kb.sb("ecmp", [128, 8, 247], F32, pg)
                Wz = kb.sb("Wz", [128, 8, 512], BF16, pg)
                m12 = kb.sb("m12", [128, 2, 62], F32, pg)
                trib = kb.sb("trib", [128, 2, 512], BF16, pg)
                kb.dma("sp", m12[:], m12_d, [], [m12])
                for v in range(2):
                    stg = next_ws()
                    kb.dma("sp", stg[:, 0:512], tri_d[:, v, :], [], [stg])
                    kb.cp("pool", trib[:, v, :], stg[:, 0:512], [stg], [(trib, v)])
                for v, KA in enumerate((KAs, KAw)):
                    for hf in range(2):
                        stg = next_ws()
                        kb.dma("sp", stg[64:105, 0:1024], kaug_d[v][:, hf * 1024:(hf + 1) * 1024], [], [stg])
                        kb.cp("pool", KA[64:105, hf * 1024:(hf + 1) * 1024], stg[64:105, 0:1024], [stg], [(KA, ("aug", hf))])
                kb.memset("pool", VAs[:, :, 64:128], 1.0, [(VAs, "ones")])
                kb.memset("pool", VAw[:, :, 64:128], 1.0, [(VAw, "ones")])
                Pc = [kb.sb("Pc0", [128, 4, 128], F32, pg)] * 2
                Pu = [kb.sb(f"Pu{i}", [128, 4, 128], F32, pg) for i in range(2)]
                Pub = [kb.sb("Pub0", [128, 4, 128], BF16, pg)] * 2
                pT = [kb.sb("pT0", [128, 4, 128], BF16, pg)] * 2
                for i in range(2):
                    kb.memset("pool", Pu[i][:], 0.0, [Pu[i]])
                psg = kb.sb("psg", [128, 128], F32, pg)
                den8 = kb.sb("den8", [128, 8], F32, pg)
                cg8 = kb.sb("cg8", [128, 8], F32, pg)
                imp = kb.sb("imp", [128, 32], F32, pg)
                impm = kb.sb("impm", [128, 32], F32, pg)
                m8 = kb.sb("m8", [128, 8], F32, pg)
                negp = kb.sb("negp", [128, 96], F32, pg)
                negS = kb.sb("negS", [96, 128], BF16, pg)
                kb.memset("pool", negp[:], 0.0, [negp])
                PTs = [kb.sb(f"PTs{i}", [128, 512], BF16, pg) for i in range(8)]
                pts_rr = [0]
                accs = [kb.sb(f"accs{i}", [128, 512], F32, pg) for i in range(2)]
                rd4 = [kb.sb(f"rd4{i}", [128, 4], F32, pg) for i in range(2)]
                cg4 = [kb.sb(f"cg4{i}", [128, 4], F32, pg) for i in range(2)]
                Oa = [kb.sb(f"Oa{i}", [128, 512], F32, pg) for i in range(2)]
                szt = kb.sb("szt", [128, 512], F32, pg)
                Ob = kb.sb("Ob", [128, 512], BF16, pg)
                accbanks = [kb.banks[0], kb.banks[1]]
                rot = [2]

                def rbank():
                    b = kb.banks[rot[0]]
                    rot[0] = rot[0] + 1 if rot[0] < 7 else 2
                    return b

                strot = [0, 0]

                def stbank(lane):
                    b = kb.banks[2 + 2 * lane + strot[lane]]
                    strot[lane] ^= 1
                    return b

                def mbank():
                    return kb.banks[6]

                ptrot = [0, 0]

                def next_pt(lane):
                    p = PTs[4 * lane + ptrot[lane]]
                    ptrot[lane] = (ptrot[lane] + 1) % 4
                    return p

                for g in range(2):
                    kb.dma("sp", ecmp[:], ecmp_d[:, g * 8:(g + 1) * 8, :], [], [ecmp])
                    for j in range(4):
                        load_w(Wz, Wz[:, :, j * 128:(j + 1) * 128], win_cols(nsa_w_in, 1840 + g * 512 + j * 128, 128), 128,
                               gain=gN, key=j)
                    for r in range(8):
                        hq = g * 8 + r
                        wq = load_win(hq * 64, 64)
                        for tc in range(4):
                            bk = rbank()
                            for dc in range(8):
                                kb.mm(bk[0:64, :], wq[:, dc, 0:64], xnT[:, dc, tc * 512:(tc + 1) * 512], dc == 0, dc == 7, [wq, xnT], [bk])
                            kb.cp("act", QAg[0:64, r, tc * 512:(tc + 1) * 512], bk[0:64, :], [bk], [QAg])
                        for hf in range(2):
                            stg = next_ws()
                            kb.dma("sp", stg[96:105, 0:1024], qal_d[hq][:, hf * 1024:(hf + 1) * 1024], [], [stg])
                            kb.cp("pool", QAg[96:105, r, hf * 1024:(hf + 1) * 1024], stg[96:105, 0:1024], [stg], [QAg])
                    for KA, col in ((KAs, 1024 + 2 * 128 + g * 64), (KAw, 1024 + 4 * 128 + g * 64)):
                        wk = load_win(col, 64)
                        for tc in range(4):
                            bk = rbank()
                            for dc in range(8):
                                kb.mm(bk[0:64, :], wk[:, dc, 0:64], xnT[:, dc, tc * 512:(tc + 1) * 512], dc == 0, dc == 7, [wk, xnT], [bk])
                            kb.cp("act", KA[0:64, tc * 512:(tc + 1) * 512], bk[0:64, :], [bk], [(KA, tc)])
                    wv = load_win(1024 + 3 * 128 + g * 64, 64)
                    load_win(1024 + 5 * 128 + g * 64, 64, into=wv, off=64)
                    for t in range(NT_):
                        bk = rbank()
                        for dc in range(8):
                            kb.mm(bk[:, 0:128], xnT[:, dc, t * 128:(t + 1) * 128], wv[:, dc, 0:128], dc == 0, dc == 7, [xnT, wv], [bk])
                        kb.cp("act", VAs[:, t, 0:64], bk[:, 0:64], [bk], [(VAs, t)])
                        kb.cp("dve", VAw[:, t, 0:64], bk[:, 64:128], [bk], [(VAw, t)])

                    def task_C(qt):
                        qc = slice(qt * 128, (qt + 1) * 128)
                        O = Oa[qt % 2]
                        gq = gates[:, qt, :]
                        eoff = 120 - 8 * qt
                        ocb = kb.banks[7]
                        for b4 in range(2):
                            sbk = mbank()
                            for hl in range(4):
                                r = b4 * 4 + hl
                                kb.mm(sbk[:, hl * 128:hl * 128 + 127], QAg[0:64, r, qc], kcmpT[0:64, g, 0:127], True, True,
                                      [(QAg, qt), kcmpT], [sbk])
                            yield
                            pc = Pc[b4]
                            pu = Pu[b4]
                            s3 = sbk[:].rearrange("p (h n) -> p h n", h=4)
                            kb.act(pc[:, :, 0:127], s3[:, :, 0:127], AF.Exp, [sbk], [pc], scale=0.125)
                            yield
                            kb.tt("dve", pu[:, :, 0:127], pc[:, :, 0:127], ecmp[:, b4 * 4:(b4 + 1) * 4, eoff:eoff + 127], ALU.mult,
                                  [pc, ecmp], [pu])
                            S.op("dve", lambda e, pu=pu, b4=b4: e.tensor_reduce(out=den8[:, b4 * 4:(b4 + 1) * 4], in_=pu[:, :, 0:127],
                                                                                axis=AX.X, op=ALU.add),
                                 reads=[pu], writes=[(den8, b4)])
                            yield
                            kb.ts("dve", den8[:, b4 * 4:(b4 + 1) * 4], den8[:, b4 * 4:(b4 + 1) * 4], 1e-30, None, ALU.max, None,
                                  [(den8, b4)], [(den8, b4)])
                            kb.recip(den8[:, b4 * 4:(b4 + 1) * 4], den8[:, b4 * 4:(b4 + 1) * 4], [(den8, b4)], [(den8, b4)])
                            pub = Pub[b4]
                            kb.cp("pool", pub[:], pu[:], [pu], [pub])
                            yield
                            for hl in range(4):
                                r = b4 * 4 + hl
                                if r == 0:
                                    kb.ts("dve", psg[:, :], pu[:, hl, :], den8[:, r:r + 1], None, ALU.mult, None, [pu, (den8, b4)], [psg])
                                else:
                                    kb.stt(psg[:, :], pu[:, hl, :], den8[:, r:r + 1], psg[:, :], ALU.mult, ALU.add,
                                           [pu, (den8, b4), psg], [psg])
                                if hl % 2 == 1:
                                    yield
                            tbk = mbank()
                            tv = tbk[:].bitcast(BF16)
                            for hl in range(4):
                                kb.tr(tv[0:127, hl * 128:(hl + 1) * 128], pub[:, hl, 0:127], identb[:], [pub, identb], [tbk])
                            yield
                            ptt = pT[b4]
                            kb.cp("act", ptt[0:127, :, :], tv[0:127, 0:512].rearrange("p (h n) -> p h n", h=4), [tbk], [ptt])
                            yield
                            for hl in range(4):
                                r = b4 * 4 + hl
                                kb.mm(ocb[:, r * 64:(r + 1) * 64], ptt[0:127, hl, :], vcmp[0:127, g, :], True, True, [ptt, vcmp], [ocb])
                            yield
                        kb.tt("dve", cg8[:], den8[:], gq[:, g * 24:g * 24 + 24:3], ALU.mult, [den8, gates], [cg8])
                        kb.tt("dve", O[:].rearrange("p (h d) -> p h d", h=8), ocb[:].rearrange("p (h d) -> p h d", h=8),
                              cg8[:, 0:8].unsqueeze(2).to_broadcast([128, 8, 64]), ALU.mult, [ocb, cg8], [O])
                        yield
                        S.op("dve", lambda e: e.tensor_reduce(out=imp[:, :], in_=psg[:].rearrange("p (j a) -> p j a", a=4),
                                                              axis=AX.X, op=ALU.add), reads=[psg], writes=[imp])
                        kb.tt("dve", imp[:, 1:32], imp[:, 1:32], psg[:, 3:127:4], ALU.add, [imp, psg], [imp])
                        yield
                        moff = 30 - 2 * qt
                        kb.tt("dve", impm[:], imp[:], m12[:, 0, moff:moff + 32], ALU.mult, [imp, m12], [impm])
                        kb.tt("dve", impm[:], impm[:], m12[:, 1, moff:moff + 32], ALU.add, [impm, m12], [impm])
                        kb.memset("dve", impm[:, 0:1], 1e6, [impm])
                        yield
                        S.op("dve", lambda e: e.max(out=m8[:], in_=impm[:]), reads=[impm], writes=[m8])
                        kb.ts("dve", negp[:, 64:96], impm[:], m8[:, 7:8], 1.0, ALU.is_ge, ALU.subtract, [impm, m8], [negp])
                        kb.ts("dve", negp[:, 64:96], negp[:, 64:96], NEGB, None, ALU.mult, None, [negp], [negp])
                        yield
                        tbk = mbank()
                        kb.tr(tbk[0:96, 0:128], negp[:, 0:96], identf[:], [negp, identf], [tbk])
                        yield
                        kb.cp("dve", negS[64:96, :], tbk[64:96, 0:128], [tbk], [negS])
                        kb.cp("dve", QAg[64:96, :, qc], negS[64:96, :].unsqueeze(1).to_broadcast([32, 8, 128]), [negS], [(QAg, qt)])
                        yield

                    def task_branch(qt, br, b4):
                        qc = slice(qt * 128, (qt + 1) * 128)
                        O = Oa[qt % 2]
                        gq = gates[:, qt, :]
                        KA, VA = (KAw, VAw) if br == 2 else (KAs, VAs)
                        kbs = list(range(max(0, qt - 4), qt + 1)) if br == 2 else list(range(0, qt + 1))
                        accb = kb.banks[b4]
                        pend = []
                        nk = len(kbs)

                        def do_pv(item):
                            pi, pk, ppt = item
                            kb.mm(accb[:, :], VA[:, pk, :], ppt[:], pi == 0, pi == nk - 1, [VA, ppt], [accb])

                        for idx, kbi in enumerate(kbs):
                            sbk = stbank(b4)
                            masks = []
                            if kbi == qt:
                                masks.append(0)
                            if br == 2 and kbi == qt - 4:
                                masks.append(1)
                            kb.mm(sbk[:, :], KA[0:105, kbi * 128:(kbi + 1) * 128], QAg[0:105, b4 * 4:(b4 + 1) * 4, qc],
                                  True, len(masks) == 0, [KA, (QAg, qt)], [sbk])
                            for mi, mv in enumerate(masks):
                                kb.mm(sbk[:, :], identb[:], trib[:, mv, :], False, mi == len(masks) - 1, [identb, trib], [sbk])
                            pt = next_pt(b4)
                            kb.act(pt[:], sbk[:], AF.Exp, [sbk], [pt], scale=0.125)
                            yield
                            pend.append((idx, kbi, pt))
                            if len(pend) > 2:
                                do_pv(pend.pop(0))
                                yield
                        while pend:
                            do_pv(pend.pop(0))
                            yield
                        ac = accs[b4]
                        kb.cp("act", ac[:], accb[:], [accb], [ac])
                        yield
                        tbk = stbank(b4)
                        for hl in range(4):
                            kb.tr(tbk[:, hl * 128:(hl + 1) * 128], ac[:, hl * 128:(hl + 1) * 128], identf[:], [ac, identf], [tbk])
                        yield
                        t3 = tbk[:].rearrange("p (h n) -> p h n", h=4)
                        kb.recip(rd4[b4][:], t3[:, :, 64], [tbk], [rd4[b4]])
                        h0 = (g * 8 + b4 * 4) * 3 + br
                        kb.tt("dve", cg4[b4][:], rd4[b4][:], gq[:, h0:h0 + 10:3], ALU.mult, [rd4[b4], gates], [cg4[b4]])
                        yield
                        for hl in range(4):
                            r = b4 * 4 + hl
                            kb.stt(O[:, r * 64:(r + 1) * 64], tbk[:, hl * 128:hl * 128 + 64], cg4[b4][:, hl:hl + 1], O[:, r * 64:(r + 1) * 64],
                                   ALU.mult, ALU.add, [tbk, cg4[b4], (O, b4)], [(O, b4)])
                            if hl % 2 == 1:
                                yield

                    def task_Z(qt):
                        qc = slice(qt * 128, (qt + 1) * 128)
                        O = Oa[qt % 2]
                        zb = mbank()
                        for dc in range(8):
                            kb.mm(zb[:, :], xnT[:, dc, qc], Wz[:, dc, :], dc == 0, dc == 7, [xnT, Wz], [zb])
                            if dc % 4 == 3:
                                yield
                        kb.act(szt[:], zb[:], AF.Silu, [zb], [szt])
                        yield
                        kb.tt("dve", Ob[:], O[:], szt[:], ALU.mult, [O, szt], [Ob])
                        yield
                        tbk = mbank()
                        tv = tbk[:].bitcast(BF16)
                        for c4 in range(4):
                            kb.tr(tv[:, c4 * 128:(c4 + 1) * 128], Ob[:, c4 * 128:(c4 + 1) * 128], identb[:], [Ob, identb], [tbk])
                        yield
                        kb.cp("act", yT1[:, g * 4:(g + 1) * 4, qc], tv[:, 0:512].rearrange("p (c n) -> p c n", c=4), [tbk], [(yT1, (g, qt))])
                        yield

                    run_lanes([task_C(0)])
                    for qt in range(NT_):
                        l1_ = chain(task_branch(qt, 2, 0), task_branch(qt, 1, 0))
                        l2_ = chain(task_branch(qt, 2, 1), task_branch(qt, 1, 1))
                        third = []
                        if qt > 0:
                            third.append(task_Z(qt - 1))
                        if qt + 1 < NT_:
                            third.append(task_C(qt + 1))
                        run_lanes([l1_, l2_, chain(*third)])
                    run_lanes([task_Z(NT_ - 1)])
                S.barrier()
                ck("nsa")

            with ExitStack() as pe_:
                out_proj(nsa_w_out, 8, [(yT1, 8)], ymT1, x1_scr, DX1, True, pe_)
                S.barrier()
                ck("l1end")

        S.dead = False
        S.barrier()
        with nc.Block() as block:
            S.emit(block)
        print("program ops:", S.nops, "sems:", S.nsem)
    return nc


_CONST = None


def prep_inputs(inp):
    global _CONST
    if _CONST is None:
        _CONST = host_constants()
        _CONST.update(host_constants_nsa())
    f = lambda a: np.ascontiguousarray(np.asarray(a, dtype=np.float32))
    shared = {
        "hawk_w_in": f(inp["hawk_w_in"][0]),
        "hawk_w_out": f(inp["hawk_w_out"][0]),
        "hawk_w_mem_kv": f(inp["hawk_w_mem_kv"][0]),
        "g_hawk": expand_gain(f(inp["hawk_norm"][0])),
        "g_hawk_mem": expand_gain(f(inp["hawk_mem_norm"][0])),
        "bd_a": block_diag(f(inp["hawk_gate_a_w"][0])),
        "bd_x": block_diag(f(inp["hawk_gate_x_w"][0])),
        "final_norm": f(inp["final_norm"]),
        "nsa_w_in": f(inp["nsa_w_in"][0]),
        "nsa_w_out": f(inp["nsa_w_out"][0]),
        "nsa_w_mem_kv": f(inp["nsa_w_mem_kv"][0]),
        "g_nsa": expand_gain(f(inp["nsa_norm"][0])),
        "g_nsa_mem": expand_gain(f(inp["nsa_mem_norm"][0])),
        "w2k": f(inp["nsa_phi_k_w2"][0]),
        "w2v": f(inp["nsa_phi_v_w2"][0]),
    }
    for k in ("identf", "edil", "ecmp", "m12", "tri", "kaug", "qal"):
        shared[k] = _CONST[k]

    def w1_layout(w1):
        a = w1.reshape(32, 64, 256).transpose(1, 0, 2)
        return np.ascontiguousarray(np.concatenate([a, a], axis=0))
    shared["w1k"] = w1_layout(f(inp["nsa_phi_k_w1"][0]))
    shared["w1v"] = w1_layout(f(inp["nsa_phi_v_w1"][0]))
    shared["peT"] = np.ascontiguousarray(np.stack([f(inp["nsa_pe_k"][0]).T, f(inp["nsa_pe_v"][0]).T], axis=1))
    lv = np.zeros((128, 8, 8), np.float32)
    cw = f(inp["hawk_conv_w"][0])
    for k in range(4):
        lv[:, :, k] = vec_fm(cw[k])
    lv[:, :, 4] = vec_fm(f(inp["hawk_conv_b"][0]))
    lv[:, :, 5] = vec_fm(f(inp["hawk_gate_a_b"][0]).reshape(-1))
    lv[:, :, 6] = vec_fm(f(inp["hawk_gate_x_b"][0]).reshape(-1))
    lv[:, :, 7] = vec_fm(f(inp["hawk_lambda"][0]))
    shared["lru_vec"] = lv
    x = f(inp["x"])
    mem = f(inp["mem"])
    maps = []
    for b in range(x.shape[0]):
        m = dict(shared)
        m["x"] = x[b]
        m["mem"] = mem[b]
        maps.append(m)
    return maps


def kernel(**inputs):
    maps = prep_inputs(inputs)
    nc = build_program()
    res = run_bass_kernel_spmd(nc, maps, core_ids=list(range(len(maps))))
    out = np.stack([np.asarray(r["out"], dtype=np.float32) for r in res.results], axis=0)
    return out
```

```python
import math
from contextlib import ExitStack

import numpy as np
import concourse.bass as bass
import concourse.mybir as mybir
from concourse.bass_utils import run_bass_kernel_spmd

F32 = mybir.dt.float32
BF16 = mybir.dt.bfloat16
AF = mybir.ActivationFunctionType
ALU = mybir.AluOpType
AX = mybir.AxisListType

S_LEN = 2048
D = 1024
NT_ = 16
EPS = 1e-6
DIL_GROUPS = ((128, 1), (512, 4), (2048, 16))

SEM_LIMIT = 30000
N_DMA_SEMS = 24
SAME_ENGINE_SYNC = True


class Buf:
    def __init__(self, name, t, excl=False):
        self.name = name
        self.t = t
        self.excl = excl
        self.st = {}

    def __getitem__(self, idx):
        return self.t[idx]


class Sync:
    def __init__(self, nc, stack):
        self.nc = nc
        self.stack = stack
        self.engs = ["pe", "act", "dve", "pool", "sp"]
        self.ops = {e: [] for e in self.engs}
        self.cur_sem = {}
        self.cnt = {}
        self.nsem = 0
        for e in self.engs:
            self._new_sem(e)
        self.dma_sems = {}
        self.dma_val = {}
        self.dma_rr = {}
        for e in ["sp", "pool", "act"]:
            self.dma_sems[e] = [self._alloc_sem(f"d{e}{i}") for i in range(N_DMA_SEMS)]
            self.dma_val[e] = [0] * N_DMA_SEMS
            self.dma_rr[e] = 0
        self.seen = {e: {} for e in self.engs}
        self.all_ticks = {}
        self.nops = 0
        self.dead = False
        self.eng_free = {e: 0.0 for e in self.engs}
        self.lane = None
        self.tnow = 0.0

    def _alloc_sem(self, name):
        self.nsem += 1
        return self.stack.enter_context(self.nc.semaphore(f"s_{name}_{self.nsem}"))

    def _new_sem(self, e):
        self.cur_sem[e] = self._alloc_sem(e)
        self.cnt[e] = 0

    def _states(self, buf, key, create):
        if key is None:
            if create and None not in buf.st:
                buf.st[None] = [None, {}]
            return list(buf.st.values())
        out = []
        if None in buf.st:
            out.append(buf.st[None])
        if key not in buf.st and create:
            buf.st[key] = [None, {}]
        if key in buf.st:
            out.append(buf.st[key])
        return out

    @staticmethod
    def _norm(lst):
        out = []
        for r in lst or []:
            out.append(r if isinstance(r, tuple) else (r, None))
        return out

    def op(self, eng, fn, reads=None, writes=None, dma=False, cost=0.5):
        if self.dead:
            return None
        reads = self._norm(reads)
        writes = self._norm(writes)
        ex = [(b, None) for (b, k) in reads + writes if b.excl]
        if ex:
            reads = [(b, k) for (b, k) in reads if not b.excl]
            writes = [(b, k) for (b, k) in writes if not b.excl]
            for bk in ex:
                if bk not in writes:
                    writes.append(bk)
        need = []
        for buf, key in reads:
            for st in self._states(buf, key, False):
                if st[0] is not None:
                    need.append(st[0])
        for buf, key in writes:
            for st in self._states(buf, key, False):
                if st[0] is not None:
                    need.append(st[0])
                need.extend(st[1].values())
        if dma:
            i = self.dma_rr[eng]
            self.dma_rr[eng] = (i + 1) % N_DMA_SEMS
            sem = self.dma_sems[eng][i]
            prev = self.dma_val[eng][i]
            if prev > 0:
                need.append((sem, prev, "dma", 0.0))
            if prev + 16 > SEM_LIMIT:
                sem = self._alloc_sem(f"d{eng}{i}")
                self.dma_sems[eng][i] = sem
                prev = 0
            val = prev + 16
            self.dma_val[eng][i] = val
            inc = 16
            tick = [sem, val, "dma", 0.0]
        else:
            if self.cnt[eng] + 1 > SEM_LIMIT:
                self._new_sem(eng)
            self.cnt[eng] += 1
            sem = self.cur_sem[eng]
            val = self.cnt[eng]
            inc = 1
            tick = [sem, val, eng, 0.0]
        ready = 0.0
        for nd in need:
            if nd[3] > ready:
                ready = nd[3]
        start = max(self.eng_free[eng], ready + 0.06)
        if dma:
            self.eng_free[eng] = start + 0.06
        else:
            self.eng_free[eng] = start + cost
        tick[3] = start + cost
        tick = tuple(tick)
        if self.lane is not None and tick[3] > self.lane.clock:
            self.lane.clock = tick[3]
        if tick[3] > self.tnow:
            self.tnow = tick[3]
        waits = {}
        seen = self.seen[eng]
        for (s, v, src, _fin) in need:
            if src == eng and (eng == "pe" or not SAME_ENGINE_SYNC):
                continue
            sid = id(s)
            if seen.get(sid, 0) >= v:
                continue
            if sid not in waits or waits[sid][1] < v:
                waits[sid] = (s, v)
        for sid, (s, v) in waits.items():
            seen[sid] = v
        self.ops[eng].append((list(waits.values()), fn, sem, inc))
        self.all_ticks[id(sem)] = (sem, val)
        self.nops += 1
        wset = set((id(b), k) for b, k in writes)
        for buf, key in reads:
            if (id(buf), key) in wset:
                continue
            self._states(buf, key, True)
            buf.st[key][1][eng if not dma else ("dma", id(sem))] = tick
        for buf, key in writes:
            if key is None:
                buf.st = {None: [tick, {}]}
            else:
                buf.st[key] = [tick, {}]
        return tick

    def barrier(self):
        if self.dead:
            return
        ticks = list(self.all_ticks.values())
        for e in self.engs:
            wl = []
            for (s, v) in ticks:
                if self.seen[e].get(id(s), 0) < v:
                    wl.append((s, v))
                    self.seen[e][id(s)] = v
            if wl:
                self.ops[e].append((wl, None, None, 0))

    def emit(self, block):
        S = self

        def run(engname, e):
            for (wl, fn, sem, inc) in S.ops[engname]:
                for (s, v) in wl:
                    e.wait_ge(s, v)
                if fn is not None:
                    fn(e).then_inc(sem, inc)

        @block.sync
        def _(e):
            run("sp", e)

        @block.tensor
        def _(e):
            run("pe", e)

        @block.scalar
        def _(e):
            run("act", e)

        @block.vector
        def _(e):
            run("dve", e)

        @block.gpsimd
        def _(e):
            run("pool", e)


class BankView:
    def __init__(self, pair, half):
        self.pair = pair
        self.off = 512 * half

    def __getitem__(self, idx):
        if not isinstance(idx, tuple):
            idx = (idx, slice(None))
        pr, col = idx
        cs = (col.start or 0) + self.off
        ce = (col.stop if col.stop is not None else 512) + self.off
        return self.pair[pr, cs:ce:col.step] if col.step else self.pair[pr, cs:ce]


class KB:
    def __init__(self, nc, stack):
        self.nc = nc
        self.gst = stack
        self.S = Sync(nc, stack)
        self.pairs = [stack.enter_context(nc.psum_tensor(f"pair{i}", [128, 1024], F32)) for i in range(4)]
        self.banks = [Buf(f"bank{i}", BankView(self.pairs[i // 2], i % 2), excl=True) for i in range(8)]
        self.bank_rr = 0
        self.uid = 0

    def sb(self, name, shape, dt, stack=None):
        self.uid += 1
        t = (stack or self.gst).enter_context(self.nc.sbuf_tensor(f"{name}_{self.uid}", shape, dt))
        return Buf(name, t)

    def bank(self):
        b = self.banks[self.bank_rr]
        self.bank_rr = (self.bank_rr + 1) % 8
        return b

    @staticmethod
    def fsz(ap):
        n = 1
        for s in ap.shape[1:]:
            n *= int(s)
        return n

    def vcost(self, eng, ap):
        n = self.fsz(ap)
        if eng == "pool":
            return 0.3 + n / 480.0
        if eng == "act":
            return 0.22 + n / 1400.0
        return 0.08 + n / 960.0

    def mm(self, out, lhsT, rhs, start, stop, r, w):
        c = max(self.fsz(rhs), 64) / 1600.0 + 0.04
        self.S.op("pe", lambda e: e.matmul(out, lhsT=lhsT, rhs=rhs, start=start, stop=stop), reads=r, writes=w, cost=c)

    def tr(self, out, in_, ident, r, w):
        self.S.op("pe", lambda e: e.transpose(out, in_, ident), reads=r, writes=w, cost=0.11)

    def act(self, out, in_, func, r, w, **kw):
        self.S.op("act", lambda e: e.activation(out=out, in_=in_, func=func, **kw), reads=r, writes=w, cost=self.vcost("act", out))

    def tt(self, eng, out, in0, in1, op, r, w):
        self.S.op(eng, lambda e: e.tensor_tensor(out=out, in0=in0, in1=in1, op=op), reads=r, writes=w, cost=self.vcost(eng, out))

    def ts(self, eng, out, in0, s1, s2, op0, op1, r, w, **kw):
        c = self.vcost(eng, out)
        if op1 is None:
            self.S.op(eng, lambda e: e.tensor_scalar(out=out, in0=in0, scalar1=s1, scalar2=None, op0=op0, **kw), reads=r, writes=w, cost=c)
        else:
            self.S.op(eng, lambda e: e.tensor_scalar(out=out, in0=in0, scalar1=s1, scalar2=s2, op0=op0, op1=op1, **kw), reads=r, writes=w, cost=c)

    def stt(self, out, in0, scalar, in1, op0, op1, r, w, **kw):
        self.S.op("dve", lambda e: e.scalar_tensor_tensor(out=out, in0=in0, scalar=scalar, in1=in1, op0=op0, op1=op1, **kw), reads=r, writes=w,
                  cost=0.12 + self.fsz(out) / 960.0)

    def cp(self, eng, out, in_, r, w):
        c = self.vcost(eng, out)
        if eng == "act":
            self.S.op("act", lambda e: e.activation(out=out, in_=in_, func=AF.Copy), reads=r, writes=w, cost=c)
        else:
            self.S.op(eng, lambda e: e.tensor_copy(out=out, in_=in_), reads=r, writes=w, cost=c)

    def memset(self, eng, ap, val, w):
        self.S.op(eng, lambda e: e.memset(ap, val), writes=w, cost=self.vcost(eng, ap))

    def recip(self, out, in_, r, w):
        self.S.op("dve", lambda e: e.reciprocal(out=out, in_=in_), reads=r, writes=w, cost=self.vcost("dve", out))

    def dma(self, q, out, in_, r, w):
        nbytes = self.fsz(out) * int(out.shape[0]) * 4
        self.S.op(q, lambda e: e.dma_start(out=out, in_=in_), reads=r, writes=w, dma=True, cost=2.0 + nbytes / 150000.0)


def alibi_slopes(n):
    return np.exp2(-8.0 * np.arange(1, n + 1) / n).astype(np.float32)


def host_constants():
    c = {}
    c["identf"] = np.eye(128, dtype=np.float32)
    sl = alibi_slopes(12)
    ik = np.arange(128)[:, None].astype(np.float64)
    iq = np.arange(128)[None, :].astype(np.float64)
    E = np.zeros((128, 12, 256), np.float32)
    for g, (win, dil) in enumerate(DIL_GROUPS):
        for hs in range(4):
            hh = g * 4 + hs
            s = float(sl[hh]) * dil
            dist_prev = 128 + iq - ik
            ok_prev = (dist_prev <= 128)
            E[:, hh, 0:128] = np.where(ok_prev, np.exp(-s * dist_prev), 0.0)
            dist_cur = iq - ik
            ok_cur = dist_cur >= 0
            E[:, hh, 128:256] = np.where(ok_cur, np.exp(-s * dist_cur), 0.0)
    c["edil"] = E
    return c


def expand_gain(g):
    return np.ascontiguousarray(np.broadcast_to(g.reshape(8, 128).T[:, :, None], (128, 8, 128))).astype(np.float32)


def vec_fm(v):
    return np.ascontiguousarray(v.reshape(8, 128).T).astype(np.float32)


def block_diag(gw):
    out = np.zeros((128, 8, 128), np.float32)
    for c in range(8):
        out[0:64, c, 0:64] = gw[2 * c]
        out[64:128, c, 64:128] = gw[2 * c + 1]
    return out


NEGB = 8192.0


def _bf16_split3(a):
    import ml_dtypes
    a = a.astype(np.float32)
    hi = a.astype(ml_dtypes.bfloat16).astype(np.float32)
    r1 = (a - hi).astype(np.float32)
    mid = r1.astype(ml_dtypes.bfloat16).astype(np.float32)
    r2 = (r1 - mid).astype(np.float32)
    lo = r2.astype(ml_dtypes.bfloat16).astype(np.float32)
    return hi, mid, lo


def host_constants_nsa():
    c = {}
    sl = alibi_slopes(16)
    i = np.arange(128)[:, None].astype(np.float64)
    m = np.arange(247)[None, :].astype(np.float64)
    dist = i - 16.0 * (m - 120.0) - 31.0
    E = np.zeros((128, 16, 247), np.float32)
    for h in range(16):
        E[:, h, :] = np.where(dist >= 0, np.exp(-float(sl[h]) * np.maximum(dist, 0.0)), 0.0)
    c["ecmp"] = E
    ii = np.arange(128)[:, None]
    rel = np.arange(62)[None, :] - 30
    cur = (ii >= 64).astype(np.int64)
    forced = (rel == cur) | (rel == cur - 1)
    future = rel > cur
    m1 = np.where(forced | future, 0.0, 1.0).astype(np.float32)
    m2 = np.where(forced, 1e6, np.where(future, -1e6, 0.0)).astype(np.float32)
    c["m12"] = np.ascontiguousarray(np.stack([m1, m2], axis=1))
    ik = np.arange(128)[:, None]
    iq = np.arange(128)[None, :]
    diag = np.where(ik > iq, -NEGB, 0.0).astype(np.float32)
    far = np.where(ik <= iq, -NEGB, 0.0).astype(np.float32)
    c["tri"] = np.ascontiguousarray(np.stack([np.tile(diag, (1, 4)), np.tile(far, (1, 4))], axis=1))
    k = np.arange(2048)
    kp = k - 1024
    hi = (np.floor(kp / 128.0) * 128.0).astype(np.float32)
    lo = (kp - hi).astype(np.float32)
    ka = np.zeros((2, 41, 2048), np.float32)
    for j in range(32):
        ka[0, j, :] = (k // 64 == j).astype(np.float32)
    for v in range(2):
        ka[v, 32:35, :] = 1.0
        ka[v, 35:38, :] = lo[None, :]
        ka[v, 38:41, :] = hi[None, :]
    c["kaug"] = ka
    qa = np.zeros((16, 9, 2048), np.float32)
    qp = (np.arange(2048) - 1024).astype(np.float32)
    for h in range(16):
        s8 = np.float32(8.0) * np.float32(sl[h])
        a = (-s8 * qp).astype(np.float32)
        ah, am, al = _bf16_split3(a)
        sh, sm, sl_ = _bf16_split3(np.full((2048,), s8, np.float32))
        qa[h, 0], qa[h, 1], qa[h, 2] = ah, am, al
        qa[h, 3], qa[h, 4], qa[h, 5] = sh, sm, sl_
        qa[h, 6], qa[h, 7], qa[h, 8] = sh, sm, sl_
    c["qal"] = qa
    return c


def chain(*gens):
    for g_ in gens:
        yield from g_


def run_lanes(lanes):
    active = list(lanes)
    while active:
        for l in list(active):
            try:
                next(l)
            except StopIteration:
                active.remove(l)


def build_program(stop_after=None):
    nc = bass.Bass("TRN2", target_bir_lowering=False)

    ckstate = {}

    def ck(name):
        if stop_after == name:
            ckstate["S"].dead = True

    def din(name, shape):
        return nc.dram_tensor(name, list(shape), F32, kind="ExternalInput").ap()

    x_d = din("x", [S_LEN, D])
    mem_d = din("mem", [256, D])
    hawk_w_in = din("hawk_w_in", [D, 7680])
    hawk_w_out = din("hawk_w_out", [1792, D])
    hawk_w_mem_kv = din("hawk_w_mem_kv", [D, 512])
    g_hawk = din("g_hawk", [128, 8, 128])
    g_hawk_mem = din("g_hawk_mem", [128, 8, 128])
    lru_vec = din("lru_vec", [128, 8, 8])
    bd_a = din("bd_a", [128, 8, 128])
    bd_x = din("bd_x", [128, 8, 128])
    identf_d = din("identf", [128, 128])
    edil_d = din("edil", [128, 12, 256])
    final_g = din("final_norm", [D])
    nsa_w_in = din("nsa_w_in", [D, 3376])
    nsa_w_out = din("nsa_w_out", [1280, D])
    nsa_w_mem_kv = din("nsa_w_mem_kv", [D, 512])
    g_nsa = din("g_nsa", [128, 8, 128])
    g_nsa_mem = din("g_nsa_mem", [128, 8, 128])
    w1k_d = din("w1k", [128, 32, 256])
    w1v_d = din("w1v", [128, 32, 256])
    w2k_d = din("w2k", [256, 64])
    w2v_d = din("w2v", [256, 64])
    peT_d = din("peT", [64, 2, 32])
    ecmp_d = din("ecmp", [128, 16, 247])
    m12_d = din("m12", [128, 2, 62])
    tri_d = din("tri", [128, 2, 512])
    kaug_d = din("kaug", [2, 41, 2048])
    qal_d = din("qal", [16, 9, 2048])
    out_d = nc.dram_tensor("out", [S_LEN, D], F32, kind="ExternalOutput").ap()
    x1_scr = nc.dram_tensor("x1_scr", [S_LEN, D], F32, kind="Internal").ap()

    with ExitStack() as gst:
        kb = KB(nc, gst)
        S = kb.S
        ckstate["S"] = S
        DX = Buf("x_dram", None)
        DX1 = Buf("x1_dram", None)
        DOUT = Buf("out_dram", None)

        xnT = kb.sb("xnT", [128, 8, S_LEN], BF16)
        memnT = kb.sb("memnT", [128, 8, 256], BF16)
        identf = kb.sb("identf", [128, 128], F32)
        identb = kb.sb("identb", [128, 128], BF16)
        onesb = kb.sb("onesb", [128, 128], BF16)
        wstage = [kb.sb(f"wstage{i}", [128, 1024], F32) for i in range(3)]
        ws_rr = [0]
        stat = kb.sb("stat", [128, 64], F32)
        stat_rr = [0]

        kb.dma("sp", identf[:], identf_d, [], [identf])
        kb.cp("dve", identb[:], identf[:], [identf], [identb])
        kb.memset("dve", onesb[:], 1.0, [onesb])

        def next_ws():
            b = wstage[ws_rr[0]]
            ws_rr[0] = (ws_rr[0] + 1) % len(wstage)
            return b

        def load_w(dst, dst_ap3, src_ap3, n, gain=None, key=None, q="sp", part=128, eng="pool"):
            dcs = dst_ap3.shape[1]
            assert dcs * n <= 1024
            stg = next_ws()
            sv = stg[0:part, 0:dcs * n].rearrange("p (c n) -> p c n", c=dcs)
            kb.dma(q, sv, src_ap3, [], [stg])
            if gain is not None:
                kb.tt(eng, dst_ap3, sv, gain[0:part, 0:dcs, 0:n], ALU.mult, [stg, gain], [(dst, key)])
            else:
                kb.cp(eng, dst_ap3, sv, [stg], [(dst, key)])

        def win_cols(w_dram, c0, n):
            return w_dram.rearrange("(dc p) n -> p dc n", p=128)[:, :, c0:c0 + n]

        class NormCtx:
            def __init__(self, stack, nbuf=2):
                self.xstage = [kb.sb(f"xstage{i}", [128, 1024], F32, stack) for i in range(nbuf)]
                self.xnb = [kb.sb(f"xnb{i}", [128, 1024], BF16, stack) for i in range(nbuf)]
                self.junk = kb.sb("junk", [128, 1024], BF16, stack)

        def tile_rstd(ncx, xbuf, xap):
            i = stat_rr[0]
            stat_rr[0] = (stat_rr[0] + 1) % 32
            ss = stat[:, 2 * i:2 * i + 1]
            rs = stat[:, 2 * i + 1:2 * i + 2]
            kb.stt(ncx.junk[:], xap, 1.0, xap, ALU.mult, ALU.mult, [xbuf], [ncx.junk, (stat, i)], accum_out=ss)
            kb.ts("dve", ss, ss, 1.0 / D, EPS, ALU.mult, ALU.add, [(stat, i)], [(stat, i)])
            kb.act(ss, ss, AF.Sqrt, [(stat, i)], [(stat, i)])
            kb.recip(rs, ss, [(stat, i)], [(stat, i)])
            return rs, i

        def norm_to_T(ncx, xbuf, xap, dstT, t, ntok_off):
            rs, i = tile_rstd(ncx, xbuf, xap)
            nb = ncx.xnb[t % 2]
            kb.ts("dve", nb[:], xap, rs, None, ALU.mult, None, [xbuf, (stat, i)], [nb])
            bk = kb.bank()
            bv = bk[:].bitcast(BF16)
            for c in range(8):
                kb.tr(bv[:, c * 128:(c + 1) * 128], nb[:, c * 128:(c + 1) * 128], identb[:], [nb, identb], [bk])
            kb.cp("act", dstT[:, :, ntok_off:ntok_off + 128], bv.rearrange("p (c n) -> p c n", c=8), [bk], [(dstT, t)])

        def norm_to_T_gen(ncx, xbuf, xap, dstT, t, ntok_off, bk, bi):
            rs, i = tile_rstd(ncx, xbuf, xap)
            yield
            nb = ncx.xnb[bi]
            kb.ts("dve", nb[:], xap, rs, None, ALU.mult, None, [xbuf, (stat, i)], [nb])
            yield
            bv = bk[:].bitcast(BF16)
            for c in range(8):
                kb.tr(bv[:, c * 128:(c + 1) * 128], nb[:, c * 128:(c + 1) * 128], identb[:], [nb, identb], [bk])
            yield
            kb.cp("act", dstT[:, :, ntok_off:ntok_off + 128], bv.rearrange("p (c n) -> p c n", c=8), [bk], [(dstT, t)])
            yield

        with ExitStack() as pa:
            ncx = NormCtx(pa)
            for t in range(NT_):
                xs = ncx.xstage[t % 2]
                kb.dma("sp", xs[:], x_d[t * 128:(t + 1) * 128, :], [DX], [xs])
                norm_to_T(ncx, xs, xs[:], xnT, t, t * 128)
            for t in range(2):
                xs = ncx.xstage[t % 2]
                kb.dma("sp", xs[:], mem_d[t * 128:(t + 1) * 128, :], [], [xs])
                norm_to_T(ncx, xs, xs[:], memnT, t, t * 128)
            S.barrier()
            ck("A")

        def make_loader(w_in_d, gain, nslots, stack):
            wslots = [kb.sb(f"wslot{i}", [128, 8, 128], BF16, stack) for i in range(nslots)]
            rr = [0]

            def load_win(c0, n=128, q="sp", into=None, off=0):
                if into is None:
                    wsl = wslots[rr[0]]
                    rr[0] = (rr[0] + 1) % nslots
                else:
                    wsl = into
                load_w(wsl, wsl[:, :, off:off + n], win_cols(w_in_d, c0, n), n, gain=gain, q=q, key=off)
                return wsl
            return load_win

        def proj_fm(wsl, n, evac, woff=0):
            for tc in range(4):
                bk = kb.bank()
                for dc in range(8):
                    kb.mm(bk[0:n, :], wsl[:, dc, woff:woff + n], xnT[:, dc, tc * 512:(tc + 1) * 512], dc == 0, dc == 7,
                          [wsl, xnT], [bk])
                evac(bk, bk[0:n, :], tc)

        def mem_kv(w_kv_d, gain, kmT, vm, stack):
            wkv = kb.sb("wkv", [128, 8, 512], BF16, stack)
            for j in range(4):
                load_w(wkv, wkv[:, :, j * 128:(j + 1) * 128], win_cols(w_kv_d, j * 128, 128), 128, gain=gain, key=j)
            for h in range(4):
                bk = kb.bank()
                for dc in range(8):
                    kb.mm(bk[0:64, 0:256], wkv[:, dc, h * 64:(h + 1) * 64], memnT[:, dc, :], dc == 0, dc == 7,
                          [wkv, memnT], [bk])
                kb.cp("act", kmT[0:64, h, :], bk[0:64, 0:256], [bk], [(kmT, h)])
            for mt in range(2):
                bk = kb.bank()
                for dc in range(8):
                    kb.mm(bk[:, 0:256], memnT[:, dc, mt * 128:(mt + 1) * 128], wkv[:, dc, 256:512], dc == 0, dc == 7,
                          [wkv, memnT], [bk])
                kb.cp("act", vm[:, mt, :], bk[:, 0:256], [bk], [(vm, mt)])

        def mem_attn(load_win, colq, colz, kmT, vm, ymT, stack):
            qmTs = [kb.sb(f"qmT{i}", [64, S_LEN], BF16, stack) for i in range(2)]
            szms = [kb.sb(f"szm{i}", [64, S_LEN], BF16, stack) for i in range(2)]
            PTm = [kb.sb(f"PTm{i}", [128, 512], BF16, stack) for i in range(4)]
            rdm = [kb.sb(f"rdm{i}", [64, 512], F32, stack) for i in range(2)]

            def lane(h, L):
                qmT, szm = qmTs[L], szms[L]
                b0, b1, b2, b3 = [kb.banks[4 * L + j] for j in range(4)]
                wq = load_win(colq + h * 64, 64)
                wz = load_win(colz + h * 64, 64)
                for (wsl, dst, func) in ((wq, qmT, None), (wz, szm, AF.Silu)):
                    for tc in range(4):
                        bk = b0 if tc % 2 == 0 else b1
                        for dc in range(8):
                            kb.mm(bk[0:64, :], wsl[:, dc, 0:64], xnT[:, dc, tc * 512:(tc + 1) * 512], dc == 0, dc == 7, [wsl, xnT], [bk])
                        yield
                        if func is None:
                            kb.cp("act", dst[0:64, tc * 512:(tc + 1) * 512], bk[0:64, :], [bk], [(dst, tc)])
                        else:
                            kb.act(dst[0:64, tc * 512:(tc + 1) * 512], bk[0:64, :], func, [bk], [(dst, tc)])
                        yield
                for tc in range(4):
                    pts = []
                    for mt in range(2):
                        bk = b0 if mt == 0 else b1
                        kb.mm(bk[:, :], kmT[0:64, h, mt * 128:(mt + 1) * 128], qmT[0:64, tc * 512:(tc + 1) * 512],
                              True, True, [kmT, (qmT, tc)], [bk])
                        pt = PTm[2 * L + mt]
                        kb.act(pt[:], bk[:], AF.Exp, [bk], [pt], scale=0.125)
                        pts.append(pt)
                        yield
                    for mt in range(2):
                        kb.mm(b2[0:64, :], vm[:, mt, h * 64:(h + 1) * 64], pts[mt][:], mt == 0, mt == 1, [vm, pts[mt]], [b2])
                    for mt in range(2):
                        kb.mm(b3[0:64, :], onesb[:, 0:64], pts[mt][:], mt == 0, mt == 1, [onesb, pts[mt]], [b3])
                    yield
                    rd = rdm[L]
                    kb.recip(rd[:], b3[0:64, :], [b3], [rd])
                    kb.tt("dve", rd[:], b2[0:64, :], rd[:], ALU.mult, [b2, rd], [rd])
                    yield
                    kb.tt("dve", ymT[0:64, h, tc * 512:(tc + 1) * 512], rd[:], szm[0:64, tc * 512:(tc + 1) * 512], ALU.mult,
                          [rd, (szm, tc)], [(ymT, (h, tc))])
                    yield

            run_lanes([lane(0, 0), lane(1, 1)])
            run_lanes([lane(2, 0), lane(3, 1)])

        def out_proj(w_out_d, nch, yTl, ymT, resid_d, resid_buf, final, stack, dbg=False):
            ysrc = []
            for (yb_, n_) in yTl:
                for ci in range(n_):
                    ysrc.append((yb_, ci))
            ncxs = [NormCtx(stack, 1), NormCtx(stack, 1)]
            WO = kb.sb("WO", [128, nch, 1024], BF16, stack)
            WOm = kb.sb("WOm", [64, 4, 1024], BF16, stack)
            wo_v = w_out_d[0:nch * 128, :].rearrange("(c p) n -> p c n", p=128)
            for c in range(nch):
                load_w(WO, WO[:, c:c + 1, :], wo_v[:, c:c + 1, :], 1024, key=c)
            wom_v = w_out_d[nch * 128:nch * 128 + 256, :].rearrange("(h p) n -> p h n", p=64)
            for h in range(4):
                load_w(WOm, WOm[0:64, h:h + 1, :], wom_v[:, h:h + 1, :], 1024, key=h, part=64)
            x1t = [kb.sb(f"x1t{i}", [128, 1024], F32, stack) for i in range(2)]
            if final:
                gF = kb.sb("gF", [128, 1024], F32, stack)
                kb.dma("sp", gF[:], final_g.partition_broadcast(128), [], [gF])
                ot = [kb.sb(f"ot{i}", [128, 1024], F32, stack) for i in range(2)]

            def lane(L):
                ncx = ncxs[L]
                bks = [kb.banks[4 * L + j] for j in range(4)]
                for t in range(L, NT_, 2):
                    xs = ncx.xstage[0]
                    kb.dma("sp", xs[:], resid_d[t * 128:(t + 1) * 128, :], [resid_buf], [xs])
                    x1 = x1t[L]
                    for half in range(2):
                        bk = bks[half]
                        for c in range(nch):
                            yb_, ci = ysrc[c]
                            kb.mm(bk[:, :], yb_[:, ci, t * 128:(t + 1) * 128], WO[:, c, half * 512:(half + 1) * 512], c == 0, False,
                                  [yb_, WO], [bk])
                            if c % 4 == 3:
                                yield
                        for h in range(4):
                            kb.mm(bk[:, :], ymT[0:64, h, t * 128:(t + 1) * 128], WOm[0:64, h, half * 512:(half + 1) * 512], False, h == 3,
                                  [ymT, WOm], [bk])
                        yield
                        kb.tt("dve", x1[:, half * 512:(half + 1) * 512], xs[:, half * 512:(half + 1) * 512], bk[:], ALU.add,
                              [xs, bk], [(x1, half)])
                        yield
                    if not final:
                        kb.dma("sp", x1_scr[t * 128:(t + 1) * 128, :], x1[:], [x1], [DX1])
                        yield from norm_to_T_gen(ncx, x1, x1[:], xnT, t, t * 128, bks[2], 0)
                        if dbg:
                            kb.dma("sp", out_d[t * 128:(t + 1) * 128, :], x1[:], [x1], [DOUT])
                    else:
                        rs, i = tile_rstd(ncx, x1, x1[:])
                        yield
                        o = ot[L]
                        kb.stt(o[:], x1[:], rs, gF[:], ALU.mult, ALU.mult, [x1, (stat, i), gF], [o])
                        yield
                        kb.dma("sp", out_d[t * 128:(t + 1) * 128, :], o[:], [o], [DOUT])
                        yield

            run_lanes([lane(0), lane(1)])

        with ExitStack() as l0:
            yTa = kb.sb("yTa", [128, 8, S_LEN], BF16, l0)
            ymT = kb.sb("ymT", [64, 4, S_LEN], BF16, l0)
            gH = kb.sb("gH", [128, 8, 128], F32, l0)
            kb.dma("sp", gH[:], g_hawk, [], [gH])
            load_win = make_loader(hawk_w_in, gH, 6, l0)
            kmT = kb.sb("kmT", [64, 4, 256], BF16, l0)
            vm = kb.sb("vm", [128, 2, 256], BF16, l0)
            with ExitStack() as pm:
                gHm = kb.sb("gHm", [128, 8, 128], F32, pm)
                kb.dma("sp", gHm[:], g_hawk_mem, [], [gHm])
                mem_kv(hawk_w_mem_kv, gHm, kmT, vm, pm)
                S.barrier()
                ck("memkv0")
            with ExitStack() as pd:
                mem_attn(load_win, 7168, 7424, kmT, vm, ymT, pd)
                S.barrier()
                ck("mem0")

            with ExitStack() as pb:
                lv = kb.sb("lv", [128, 8, 8], F32, pb)
                cvec = kb.sb("cvec", [128, 8, 2], F32, pb)
                bda = kb.sb("bda", [128, 8, 128], BF16, pb)
                bdx = kb.sb("bdx", [128, 8, 128], BF16, pb)
                kb.dma("sp", lv[:], lru_vec, [], [lv])
                load_w(bda, bda[:], bd_a, 128)
                load_w(bdx, bdx[:], bd_x, 128)
                kb.act(cvec[:, :, 0], lv[:, :, 7], AF.Exp, [lv], [cvec], scale=-1.0)
                kb.act(cvec[:, :, 0], cvec[:, :, 0], AF.Ln, [cvec], [cvec], bias=1.0)
                kb.ts("dve", cvec[:, :, 1], cvec[:, :, 0], -16.0, None, ALU.mult, None, [cvec], [cvec])
                kb.ts("dve", cvec[:, :, 0], cvec[:, :, 0], -8.0, None, ALU.mult, None, [cvec], [cvec])
                sets = []
                for L in range(2):
                    sets.append(dict(
                        B1=kb.sb(f"B1_{L}", [128, S_LEN + 4], F32, pb), B2=kb.sb(f"B2_{L}", [128, S_LEN], F32, pb),
                        B3=kb.sb(f"B3_{L}", [128, S_LEN], F32, pb), B4=kb.sb(f"B4_{L}", [128, S_LEN], F32, pb),
                        xcb=kb.sb(f"xcb_{L}", [128, S_LEN], BF16, pb), sz=kb.sb(f"sz_{L}", [128, S_LEN], BF16, pb)))

                def lru_lane(L):
                    st_ = sets[L]
                    B1, B2, B3, B4, xcb, sz = st_["B1"], st_["B2"], st_["B3"], st_["B4"], st_["xcb"], st_["sz"]
                    bks = [kb.banks[4 * L + j] for j in range(4)]
                    for c in range(L, 8, 2):
                        wxa = load_win(c * 128)
                        wza = load_win(1024 + c * 128)
                        kb.memset("dve", B1[:, 0:3], 0.0, [(B1, "pad")])
                        for (wsl, which) in ((wxa, 0), (wza, 1)):
                            for tc in range(4):
                                bk = bks[tc % 4]
                                for dc in range(8):
                                    kb.mm(bk[:, :], wsl[:, dc, 0:128], xnT[:, dc, tc * 512:(tc + 1) * 512], dc == 0, dc == 7, [wsl, xnT], [bk])
                                yield
                                if which == 0:
                                    kb.cp("act", B1[:, 3 + tc * 512:3 + (tc + 1) * 512], bk[:], [bk], [(B1, tc)])
                                else:
                                    kb.act(sz[:, tc * 512:(tc + 1) * 512], bk[:], AF.Silu, [bk], [(sz, tc)])
                                yield
                        kb.ts("dve", B2[:], B1[:, 0:S_LEN], lv[:, c, 0:1], lv[:, c, 4:5], ALU.mult, ALU.add, [B1, lv], [B2])
                        yield
                        for k in range(1, 4):
                            kb.stt(B2[:], B1[:, k:k + S_LEN], lv[:, c, k:k + 1], B2[:], ALU.mult, ALU.add, [B1, lv, B2], [B2])
                            yield
                        kb.cp("dve", xcb[:], B2[:], [B2], [xcb])
                        yield
                        for (bd, dstb, col) in ((bda, B1, 5), (bdx, B4, 6)):
                            for tc in range(4):
                                bk = bks[tc % 4]
                                kb.mm(bk[:, :], bd[:, c, :], xcb[:, tc * 512:(tc + 1) * 512], True, True, [bd, xcb], [bk])
                                yield
                                kb.act(dstb[:, tc * 512:(tc + 1) * 512], bk[:], AF.Sigmoid, [bk, lv], [(dstb, tc)], bias=lv[:, c, col:col + 1])
                                yield
                        r_ap = B1[:, 0:S_LEN]
                        kb.act(B3[:], r_ap, AF.Exp, [B1, cvec], [B3], scale=cvec[:, c, 0:1])
                        yield
                        kb.act(r_ap, r_ap, AF.Exp, [B1, cvec], [B1], scale=cvec[:, c, 1:2])
                        yield
                        kb.tt("dve", B2[:], B2[:], B4[:], ALU.mult, [B2, B4], [B2])
                        yield
                        kb.ts("dve", r_ap, r_ap, -1.0, 1.0, ALU.mult, ALU.add, [B1], [B1])
                        yield
                        kb.ts("dve", r_ap, r_ap, 0.0, None, ALU.max, None, [B1], [B1])
                        yield
                        kb.act(r_ap, r_ap, AF.Sqrt, [B1], [B1])
                        kb.memset("dve", B1[:, 0:1], 1.0, [B1])
                        yield
                        kb.tt("dve", B2[:], B2[:], r_ap, ALU.mult, [B2, B1], [B2])
                        yield
                        S.op("dve", lambda e, B4=B4, B3=B3, B2=B2: e.tensor_tensor_scan(out=B4[:], data0=B3[:], data1=B2[:], initial=0.0,
                                                                                      op0=ALU.mult, op1=ALU.add), reads=[B3, B2], writes=[B4])
                        yield
                        kb.tt("dve", yTa[:, c, :], B4[:], sz[:], ALU.mult, [B4, sz], [(yTa, c)])
                        yield

                run_lanes([lru_lane(0), lru_lane(1)])
                S.barrier()
                ck("lru")
            yTb = kb.sb("yTb", [128, 4, S_LEN], BF16, l0)

            with ExitStack() as pc:
                edil = kb.sb("edil", [128, 12, 256], F32, pc)
                kb.dma("sp", edil[:], edil_d, [], [edil])
                qTs = [kb.sb(f"qT{i}", [128, S_LEN], BF16, pc) for i in range(2)]
                kTs = [kb.sb(f"kT{i}", [128, S_LEN], BF16, pc) for i in range(2)]
                vTs = [kb.sb(f"vT{i}", [128, S_LEN], BF16, pc) for i in range(2)]
                Vps = [kb.sb(f"Vp{i}", [128, 16, 128], BF16, pc) for i in range(2)]
                szbs = [kb.sb(f"szb{i}", [128, S_LEN], BF16, pc) for i in range(2)]
                NTa = kb.sb("NTa", [128, S_LEN], F32, pc)
                DBa = kb.sb("DBa", [128, S_LEN], F32, pc)
                Pf = [kb.sb(f"Pf{i}", [128, 256], F32, pc) for i in range(3)]
                PT = [kb.sb(f"PT{i}", [128, 256], BF16, pc) for i in range(3)]
                sc_d = 128.0 ** -0.5
                items = [(hs, g) for hs in range(4) for g in range(3)]
                prot = [0]

                def pbank():
                    b = kb.banks[6 + prot[0]]
                    prot[0] ^= 1
                    return b

                def dtoks(d, r, b):
                    t0 = r + d * 128 * b
                    return slice(t0, t0 + d * 127 + 1, d)

                def proj_fm_lane(wsl, dst, func):
                    for tc in range(4):
                        bk = pbank()
                        for dc in range(8):
                            kb.mm(bk[:, :], wsl[:, dc, 0:128], xnT[:, dc, tc * 512:(tc + 1) * 512], dc == 0, dc == 7, [wsl, xnT], [bk])
                        yield
                        if func is None:
                            kb.cp("act", dst[:, tc * 512:(tc + 1) * 512], bk[:], [bk], [(dst, tc)])
                        else:
                            kb.act(dst[:, tc * 512:(tc + 1) * 512], bk[:], func, [bk], [(dst, tc)])
                        yield

                def task_proj(i):
                    hs, g = items[i]
                    win, d = DIL_GROUPS[g]
                    hh = g * 4 + hs
                    nqb = (S_LEN // d) // 128
                    s = i % 2
                    if g == 0:
                        wz = load_win(2048 + 4608 + hs * 128)
                        yield from proj_fm_lane(wz, szbs[hs % 2], AF.Silu)
                    wq = load_win(2048 + hh * 128)
                    yield from proj_fm_lane(wq, qTs[s], None)
                    wk = load_win(2048 + 1536 + hh * 128)
                    yield from proj_fm_lane(wk, kTs[s], None)
                    wv = load_win(2048 + 3072 + hh * 128)
                    yield from proj_fm_lane(wv, vTs[s], None)
                    for j in range(4):
                        bk = pbank()
                        bv = bk[:].bitcast(BF16)
                        for k in range(4):
                            r, b = divmod(4 * j + k, nqb)
                            kb.tr(bv[:, k * 128:(k + 1) * 128], vTs[s][:, dtoks(d, r, b)], identb[:], [vTs[s], identb], [bk])
                        yield
                        kb.cp("act", Vps[s][:, 4 * j:4 * j + 4, :], bv[:, 0:512].rearrange("p (k n) -> p k n", k=4), [bk], [(Vps[s], j)])
                        yield

                def task_tile(i, ti, lane):
                    hs, g = items[i]
                    win, d = DIL_GROUPS[g]
                    hh = g * 4 + hs
                    nqb = (S_LEN // d) // 128
                    s = i % 2
                    qT, kT, Vp = qTs[s], kTs[s], Vps[s]
                    r, qb = divmod(ti, nqb)
                    qs = dtoks(d, r, qb)
                    kbs = [qb - 1, qb] if qb > 0 else [qb]
                    sbk = kb.banks[2 * lane]
                    ndb = kb.banks[2 * lane + 1]
                    for kbi in kbs:
                        typ = 0 if kbi < qb else 1
                        kb.mm(sbk[:, typ * 128:(typ + 1) * 128], kT[:, dtoks(d, r, kbi)], qT[:, qs], True, True, [kT, qT], [sbk])
                    yield
                    lo = 0 if qb > 0 else 128
                    pf = Pf[lane]
                    pt = PT[lane]
                    kb.act(pf[:, lo:256], sbk[:, lo:256], AF.Exp, [sbk], [pf], scale=sc_d)
                    yield
                    kb.tt("dve", pt[:, lo:256], pf[:, lo:256], edil[:, hh, lo:256], ALU.mult, [pf, edil], [pt])
                    yield
                    for j, kbi in enumerate(kbs):
                        typ = 0 if kbi < qb else 1
                        kb.mm(ndb[:, 0:128], Vp[:, r * nqb + kbi, :], pt[:, typ * 128:(typ + 1) * 128], j == 0, j == len(kbs) - 1,
                              [Vp, pt], [ndb])
                    for j, kbi in enumerate(kbs):
                        typ = 0 if kbi < qb else 1
                        kb.mm(ndb[:, 128:256], onesb[:], pt[:, typ * 128:(typ + 1) * 128], j == 0, j == len(kbs) - 1,
                              [onesb, pt], [ndb])
                    yield
                    if g == 0:
                        kb.cp("dve", NTa[:, qs], ndb[:, 0:128], [ndb], [NTa])
                        kb.cp("dve", DBa[:, qs], ndb[:, 128:256], [ndb], [DBa])
                    else:
                        kb.tt("dve", NTa[:, qs], NTa[:, qs], ndb[:, 0:128], ALU.add, [ndb, NTa], [NTa])
                        kb.tt("dve", DBa[:, qs], DBa[:, qs], ndb[:, 128:256], ALU.add, [ndb, DBa], [DBa])
                    yield

                def tile_lane(i, lane):
                    for ti in range(lane, 16, 3):
                        yield from task_tile(i, ti, lane)

                run_lanes([task_proj(0)])
                for i in range(len(items)):
                    hs, g = items[i]
                    lanes = [tile_lane(i, 0), tile_lane(i, 1), tile_lane(i, 2)]
                    if i + 1 < len(items):
                        lanes.append(task_proj(i + 1))
                    run_lanes(lanes)
                    if g == 2:
                        kb.recip(DBa[:], DBa[:], [DBa], [DBa])
                        kb.tt("dve", NTa[:], NTa[:], DBa[:], ALU.mult, [NTa, DBa], [NTa])
                        kb.tt("dve", yTb[:, hs, :], NTa[:], szbs[hs % 2][:], ALU.mult, [NTa, szbs[hs % 2]], [(yTb, hs)])
                S.barrier()
                ck("dil")

            with ExitStack() as pe_:
                out_proj(hawk_w_out, 12, [(yTa, 8), (yTb, 4)], ymT, x_d, DX, False, pe_, dbg=(stop_after == "l0"))
                S.barrier()
                ck("l0end")

        if stop_after != "l0":
          with ExitStack() as l1:
            yT1 = kb.sb("yT1", [128, 8, S_LEN], BF16, l1)
            ymT1 = kb.sb("ymT1", [64, 4, S_LEN], BF16, l1)
            gN = kb.sb("gN", [128, 8, 128], F32, l1)
            kb.dma("sp", gN[:], g_nsa, [], [gN])
            load_win = make_loader(nsa_w_in, gN, 4, l1)
            kcmpT = kb.sb("kcmpT", [64, 2, 128], BF16, l1)
            vcmp = kb.sb("vcmp", [128, 2, 64], BF16, l1)
            gates = kb.sb("gates", [128, 16, 48], F32, l1)
            with ExitStack() as pmm:
                kmT = kb.sb("kmT1", [64, 4, 256], BF16, pmm)
                vm = kb.sb("vm1", [128, 2, 256], BF16, pmm)
                with ExitStack() as pm:
                    gNm = kb.sb("gNm", [128, 8, 128], F32, pm)
                    kb.dma("sp", gNm[:], g_nsa_mem, [], [gNm])
                    mem_kv(nsa_w_mem_kv, gNm, kmT, vm, pm)
                    S.barrier()
                    ck("memkv1")
                with ExitStack() as pd:
                    mem_attn(load_win, 2864, 3120, kmT, vm, ymT1, pd)
                    S.barrier()
                    ck("mem1")

            with ExitStack() as pq:
                wg = load_win(1792, 48)
                for t in range(NT_):
                    bk = kb.bank()
                    for dc in range(8):
                        kb.mm(bk[:, 0:48], xnT[:, dc, t * 128:(t + 1) * 128], wg[:, dc, 0:48], dc == 0, dc == 7, [xnT, wg], [bk])
                    kb.act(gates[:, t, :], bk[:, 0:48], AF.Sigmoid, [bk], [(gates, t)])
                kcT = kb.sb("kcT", [128, S_LEN], BF16, pq)
                vcT = kb.sb("vcT", [128, S_LEN], BF16, pq)
                wkc = load_win(1024)
                proj_fm(wkc, 128, lambda bk, ap, tc: kb.cp("act", kcT[:, tc * 512:(tc + 1) * 512], ap, [bk], [(kcT, tc)]))
                wvc = load_win(1024 + 128)
                proj_fm(wvc, 128, lambda bk, ap, tc: kb.cp("act", vcT[:, tc * 512:(tc + 1) * 512], ap, [bk], [(vcT, tc)]))
                W1 = kb.sb("W1", [128, 32, 256], BF16, pq)
                w2 = kb.sb("w2", [128, 2, 64], BF16, pq)
                peS = kb.sb("peS", [64, 2, 32], F32, pq)
                peb = kb.sb("peb", [64, 2, 32], BF16, pq)
                hidT = kb.sb("hidT", [128, 2, 128], BF16, pq)
                cb = kb.sb("cb", [128, 2], F32, pq)
                kb.dma("sp", peS[:], peT_d, [], [peS])
                kb.cp("dve", peb[:], peS[:], [peS], [peb])
                for kv in range(2):
                    w1d = w1k_d if kv == 0 else w1v_d
                    w2d = w2k_d if kv == 0 else w2v_d
                    srcT = kcT if kv == 0 else vcT
                    for p4 in range(8):
                        load_w(W1, W1[:, p4 * 4:(p4 + 1) * 4, :], w1d[:, p4 * 4:(p4 + 1) * 4, :], 256, key=p4)
                    load_w(w2, w2[:, :, :], w2d.rearrange("(hc p) d -> p hc d", p=128), 64)
                    for hc in range(2):
                        bk = kb.bank()
                        for p in range(32):
                            kb.mm(bk[:, 0:1], W1[0:64, p, hc * 128:(hc + 1) * 128], peb[0:64, kv, p:p + 1], p == 0, p == 31,
                                  [W1, peb], [bk])
                        kb.cp("dve", cb[:, hc:hc + 1], bk[:, 0:1], [bk], [(cb, hc)])
                    for g in range(2):
                        for hc in range(2):
                            bk = kb.bank()
                            for p in range(32):
                                kb.mm(bk[:, 0:127], W1[g * 64:(g + 1) * 64, p, hc * 128:(hc + 1) * 128],
                                      srcT[g * 64:(g + 1) * 64, p:p + 16 * 126 + 1:16], p == 0, p == 31, [W1, srcT], [bk])
                            kb.act(hidT[:, hc, 0:127], bk[:, 0:127], AF.Silu, [bk, cb], [(hidT, hc)], bias=cb[:, hc:hc + 1])
                        bk = kb.bank()
                        if kv == 0:
                            for hc in range(2):
                                kb.mm(bk[0:64, 0:127], w2[:, hc, :], hidT[:, hc, 0:127], hc == 0, hc == 1, [w2, hidT], [bk])
                            kb.cp("dve", kcmpT[0:64, g, 0:127], bk[0:64, 0:127], [bk], [(kcmpT, g)])
                        else:
                            for hc in range(2):
                                kb.mm(bk[0:127, 0:64], hidT[:, hc, 0:127], w2[:, hc, :], hc == 0, hc == 1, [w2, hidT], [bk])
                            kb.cp("dve", vcmp[0:127, g, :], bk[0:127, 0:64], [bk], [(vcmp, g)])
                S.barrier()
                ck("cmpkv")

            with ExitStack() as pg:
                QAg = kb.sb("QAg", [105, 8, S_LEN], BF16, pg)
                KAs = kb.sb("KAs", [105, S_LEN], BF16, pg)
                KAw = kb.sb("KAw", [105, S_LEN], BF16, pg)
                VAs = kb.sb("VAs", [128, 16, 128], BF16, pg)
                VAw = kb.sb("VAw", [128, 16, 128], BF16, pg)
                ecmp = kb.sb("ecmp", [128, 8, 247], F32, pg)
                Wz = kb.sb("Wz", [128, 8, 512], BF16, pg)
                m12 = kb.sb("m12", [128, 2, 62], F32, pg)
                trib = kb.sb("trib", [128, 2, 512], BF16, pg)
                kb.dma("sp", m12[:], m12_d, [], [m12])
                for v in range(2):
                    stg = next_ws()
                    kb.dma("sp", stg[:, 0:512], tri_d[:, v, :], [], [stg])
                    kb.cp("pool", trib[:, v, :], stg[:, 0:512], [stg], [(trib, v)])
                for v, KA in enumerate((KAs, KAw)):
                    for hf in range(2):
                        stg = next_ws()
                        kb.dma("sp", stg[64:105, 0:1024], kaug_d[v][:, hf * 1024:(hf + 1) * 1024], [], [stg])
                        kb.cp("pool", KA[64:105, hf * 1024:(hf + 1) * 1024], stg[64:105, 0:1024], [stg], [(KA, ("aug", hf))])
                kb.memset("pool", VAs[:, :, 64:128], 1.0, [(VAs, "ones")])
                kb.memset("pool", VAw[:, :, 64:128], 1.0, [(VAw, "ones")])
                Pc = [kb.sb("Pc0", [128, 4, 128], F32, pg)] * 2
                Pu = [kb.sb(f"Pu{i}", [128, 4, 128], F32, pg) for i in range(2)]
                Pub = [kb.sb("Pub0", [128, 4, 128], BF16, pg)] * 2
                pT = [kb.sb("pT0", [128, 4, 128], BF16, pg)] * 2
                for i in range(2):
                    kb.memset("pool", Pu[i][:], 0.0, [Pu[i]])
                psg = kb.sb("psg", [128, 128], F32, pg)
                den8 = kb.sb("den8", [128, 8], F32, pg)
                cg8 = kb.sb("cg8", [128, 8], F32, pg)
                imp = kb.sb("imp", [128, 32], F32, pg)
                impm = kb.sb("impm", [128, 32], F32, pg)
                m8 = kb.sb("m8", [128, 8], F32, pg)
                negp = kb.sb("negp", [128, 96], F32, pg)
                negS = kb.sb("negS", [96, 128], BF16, pg)
                kb.memset("pool", negp[:], 0.0, [negp])
                PTs = [kb.sb(f"PTs{i}", [128, 512], BF16, pg) for i in range(8)]
                pts_rr = [0]
                zerob = kb.sb("zerob", [128, 512], BF16, pg)
                kb.memset("pool", zerob[:], 0.0, [zerob])
                tmpo = [kb.sb(f"tmpo{i}", [128, 4, 64], F32, pg) for i in range(2)]
                rd4 = [kb.sb(f"rd4{i}", [128, 4], F32, pg) for i in range(2)]
                cg4 = [kb.sb(f"cg4{i}", [128, 4], F32, pg) for i in range(2)]
                Oa = [kb.sb(f"Oa{i}", [128, 512], F32, pg) for i in range(2)]
                szt = kb.sb("szt", [128, 512], F32, pg)
                Ob = kb.sb("Ob", [128, 512], BF16, pg)
                accbanks = [kb.banks[0], kb.banks[1]]
                rot = [2]

                def rbank():
                    b = kb.banks[rot[0]]
                    rot[0] = rot[0] + 1 if rot[0] < 7 else 2
                    return b

                strot = [0, 0]

                def stbank(lane):
                    b = kb.banks[2 + 2 * lane + strot[lane]]
                    strot[lane] ^= 1
                    return b

                def mbank():
                    return kb.banks[6]

                ptrot = [0, 0]

                def next_pt(lane):
                    p = PTs[4 * lane + ptrot[lane]]
                    ptrot[lane] = (ptrot[lane] + 1) % 4
                    return p

                for g in range(2):
                    kb.dma("sp", ecmp[:], ecmp_d[:, g * 8:(g + 1) * 8, :], [], [ecmp])
                    for j in range(4):
                        load_w(Wz, Wz[:, :, j * 128:(j + 1) * 128], win_cols(nsa_w_in, 1840 + g * 512 + j * 128, 128), 128,
                               gain=gN, key=j)
                    for r in range(8):
                        hq = g * 8 + r
                        wq = load_win(hq * 64, 64)
                        for tc in range(4):
                            bk = rbank()
                            for dc in range(8):
                                kb.mm(bk[0:64, :], wq[:, dc, 0:64], xnT[:, dc, tc * 512:(tc + 1) * 512], dc == 0, dc == 7, [wq, xnT], [bk])
                            kb.cp("act", QAg[0:64, r, tc * 512:(tc + 1) * 512], bk[0:64, :], [bk], [QAg])
                        for hf in range(2):
                            stg = next_ws()
                            kb.dma("sp", stg[96:105, 0:1024], qal_d[hq][:, hf * 1024:(hf + 1) * 1024], [], [stg])
                            kb.cp("pool", QAg[96:105, r, hf * 1024:(hf + 1) * 1024], stg[96:105, 0:1024], [stg], [QAg])
                    for KA, col in ((KAs, 1024 + 2 * 128 + g * 64), (KAw, 1024 + 4 * 128 + g * 64)):
                        wk = load_win(col, 64)
                        for tc in range(4):
                            bk = rbank()
                            for dc in range(8):
                                kb.mm(bk[0:64, :], wk[:, dc, 0:64], xnT[:, dc, tc * 512:(tc + 1) * 512], dc == 0, dc == 7, [wk, xnT], [bk])
                            kb.cp("act", KA[0:64, tc * 512:(tc + 1) * 512], bk[0:64, :], [bk], [(KA, tc)])
                    wv = load_win(1024 + 3 * 128 + g * 64, 64)
                    load_win(1024 + 5 * 128 + g * 64, 64, into=wv, off=64)
                    for t in range(NT_):
                        bk = rbank()
                        for dc in range(8):
                            kb.mm(bk[:, 0:128], xnT[:, dc, t * 128:(t + 1) * 128], wv[:, dc, 0:128], dc == 0, dc == 7, [xnT, wv], [bk])
                        kb.cp("act", VAs[:, t, 0:64], bk[:, 0:64], [bk], [(VAs, t)])
                        kb.cp("dve", VAw[:, t, 0:64], bk[:, 64:128], [bk], [(VAw, t)])

                    def task_C(qt):
                        qc = slice(qt * 128, (qt + 1) * 128)
                        O = Oa[qt % 2]
                        gq = gates[:, qt, :]
                        eoff = 120 - 8 * qt
                        ocb = kb.banks[7]
                        for b4 in range(2):
                            sbk = mbank()
                            for hl in range(4):
                                r = b4 * 4 + hl
                                kb.mm(sbk[:, hl * 128:hl * 128 + 127], QAg[0:64, r, qc], kcmpT[0:64, g, 0:127], True, True,
                                      [(QAg, qt), kcmpT], [sbk])
                            yield
                            pc = Pc[b4]
                            pu = Pu[b4]
                            s3 = sbk[:].rearrange("p (h n) -> p h n", h=4)
                            kb.act(pc[:, :, 0:127], s3[:, :, 0:127], AF.Exp, [sbk], [pc], scale=0.125)
                            yield
                            kb.tt("dve", pu[:, :, 0:127], pc[:, :, 0:127], ecmp[:, b4 * 4:(b4 + 1) * 4, eoff:eoff + 127], ALU.mult,
                                  [pc, ecmp], [pu])
                            S.op("dve", lambda e, pu=pu, b4=b4: e.tensor_reduce(out=den8[:, b4 * 4:(b4 + 1) * 4], in_=pu[:, :, 0:127],
                                                                                axis=AX.X, op=ALU.add),
                                 reads=[pu], writes=[(den8, b4)])
                            yield
                            kb.ts("dve", den8[:, b4 * 4:(b4 + 1) * 4], den8[:, b4 * 4:(b4 + 1) * 4], 1e-30, None, ALU.max, None,
                                  [(den8, b4)], [(den8, b4)])
                            kb.recip(den8[:, b4 * 4:(b4 + 1) * 4], den8[:, b4 * 4:(b4 + 1) * 4], [(den8, b4)], [(den8, b4)])
                            pub = Pub[b4]
                            kb.cp("pool", pub[:], pu[:], [pu], [pub])
                            yield
                            for hl in range(4):
                                r = b4 * 4 + hl
                                if r == 0:
                                    kb.ts("dve", psg[:, :], pu[:, hl, :], den8[:, r:r + 1], None, ALU.mult, None, [pu, (den8, b4)], [psg])
                                else:
                                    kb.stt(psg[:, :], pu[:, hl, :], den8[:, r:r + 1], psg[:, :], ALU.mult, ALU.add,
                                           [pu, (den8, b4), psg], [psg])
                                if hl % 2 == 1:
                                    yield
                            tbk = mbank()
                            tv = tbk[:].bitcast(BF16)
                            for hl in range(4):
                                kb.tr(tv[0:127, hl * 128:(hl + 1) * 128], pub[:, hl, 0:127], identb[:], [pub, identb], [tbk])
                            yield
                            ptt = pT[b4]
                            kb.cp("act", ptt[0:127, :, :], tv[0:127, 0:512].rearrange("p (h n) -> p h n", h=4), [tbk], [ptt])
                            yield
                            for hl in range(4):
                                r = b4 * 4 + hl
                                kb.mm(ocb[:, r * 64:(r + 1) * 64], ptt[0:127, hl, :], vcmp[0:127, g, :], True, True, [ptt, vcmp], [ocb])
                            yield
                        kb.tt("dve", cg8[:], den8[:], gq[:, g * 24:g * 24 + 24:3], ALU.mult, [den8, gates], [cg8])
                        kb.tt("dve", O[:].rearrange("p (h d) -> p h d", h=8), ocb[:].rearrange("p (h d) -> p h d", h=8),
                              cg8[:, 0:8].unsqueeze(2).to_broadcast([128, 8, 64]), ALU.mult, [ocb, cg8], [O])
                        yield
                        S.op("dve", lambda e: e.tensor_reduce(out=imp[:, :], in_=psg[:].rearrange("p (j a) -> p j a", a=4),
                                                              axis=AX.X, op=ALU.add), reads=[psg], writes=[imp])
                        kb.tt("dve", imp[:, 1:32], imp[:, 1:32], psg[:, 3:127:4], ALU.add, [imp, psg], [imp])
                        yield
                        moff = 30 - 2 * qt
                        kb.tt("dve", impm[:], imp[:], m12[:, 0, moff:moff + 32], ALU.mult, [imp, m12], [impm])
                        kb.tt("dve", impm[:], impm[:], m12[:, 1, moff:moff + 32], ALU.add, [impm, m12], [impm])
                        kb.memset("dve", impm[:, 0:1], 1e6, [impm])
                        yield
                        S.op("dve", lambda e: e.max(out=m8[:], in_=impm[:]), reads=[impm], writes=[m8])
                        kb.ts("dve", negp[:, 64:96], impm[:], m8[:, 7:8], 1.0, ALU.is_ge, ALU.subtract, [impm, m8], [negp])
                        kb.ts("dve", negp[:, 64:96], negp[:, 64:96], NEGB, None, ALU.mult, None, [negp], [negp])
                        yield
                        tbk = mbank()
                        kb.tr(tbk[0:96, 0:128], negp[:, 0:96], identf[:], [negp, identf], [tbk])
                        yield
                        kb.cp("dve", negS[64:96, :], tbk[64:96, 0:128], [tbk], [negS])
                        kb.cp("dve", QAg[64:96, :, qc], negS[64:96, :].unsqueeze(1).to_broadcast([32, 8, 128]), [negS], [(QAg, qt)])
                        yield

                    def task_branch(qt, br, b4):
                        qc = slice(qt * 128, (qt + 1) * 128)
                        O = Oa[qt % 2]
                        gq = gates[:, qt, :]
                        KA, VA = (KAw, VAw) if br == 2 else (KAs, VAs)
                        kbs = list(range(max(0, qt - 4), qt + 1)) if br == 2 else list(range(0, qt + 1))
                        accb = kb.banks[b4]
                        pend = []
                        nk = len(kbs)
                        kb.mm(accb[:, 0:260], zerob[:, 0:128], zerob[:, 0:260], True, False, [zerob], [accb])

                        def do_pv(item):
                            pi, pk, ppt = item
                            for hl in range(4):
                                kb.mm(accb[:, hl * 65:(hl + 1) * 65], ppt[:, hl * 128:(hl + 1) * 128], VA[:, pk, 0:65], False,
                                      (pi == nk - 1) and hl == 3, [VA, ppt], [accb])

                        for idx, kbi in enumerate(kbs):
                            sbk = stbank(b4)
                            masks = []
                            if kbi == qt:
                                masks.append(0)
                            if br == 2 and kbi == qt - 4:
                                masks.append(1)
                            kb.mm(sbk[:, :], KA[0:105, kbi * 128:(kbi + 1) * 128], QAg[0:105, b4 * 4:(b4 + 1) * 4, qc],
                                  True, len(masks) == 0, [KA, (QAg, qt)], [sbk])
                            for mi, mv in enumerate(masks):
                                kb.mm(sbk[:, :], identb[:], trib[:, mv, :], False, mi == len(masks) - 1, [identb, trib], [sbk])
                            pt = next_pt(b4)
                            kb.act(pt[:], sbk[:], AF.Exp, [sbk], [pt], scale=0.125)
                            yield
                            pend.append((idx, kbi, pt))
                            if len(pend) > 2:
                                do_pv(pend.pop(0))
                                yield
                        while pend:
                            do_pv(pend.pop(0))
                            yield
                        a3 = accb[:, 0:260].rearrange("p (h n) -> p h n", h=4)
                        kb.recip(rd4[b4][:], a3[:, :, 64], [accb], [rd4[b4]])
                        h0 = (g * 8 + b4 * 4) * 3 + br
                        kb.tt("dve", cg4[b4][:], rd4[b4][:], gq[:, h0:h0 + 10:3], ALU.mult, [rd4[b4], gates], [cg4[b4]])
                        yield
                        kb.tt("dve", tmpo[b4][:], a3[:, :, 0:64], cg4[b4][:, 0:4].unsqueeze(2).to_broadcast([128, 4, 64]), ALU.mult,
                              [accb, cg4[b4]], [tmpo[b4]])
                        yield
                        ov = O[:, b4 * 256:(b4 + 1) * 256].rearrange("p (h d) -> p h d", h=4)
                        kb.tt("dve", ov, ov, tmpo[b4][:], ALU.add, [(O, b4), tmpo[b4]], [(O, b4)])
                        yield

                    def task_Z(qt):
                        qc = slice(qt * 128, (qt + 1) * 128)
                        O = Oa[qt % 2]
                        zb = mbank()
                        for dc in range(8):
                            kb.mm(zb[:, :], xnT[:, dc, qc], Wz[:, dc, :], dc == 0, dc == 7, [xnT, Wz], [zb])
                            if dc % 4 == 3:
                                yield
                        kb.act(szt[:], zb[:], AF.Silu, [zb], [szt])
                        yield
                        kb.tt("dve", Ob[:], O[:], szt[:], ALU.mult, [O, szt], [Ob])
                        yield
                        tbk = mbank()
                        tv = tbk[:].bitcast(BF16)
                        for c4 in range(4):
                            kb.tr(tv[:, c4 * 128:(c4 + 1) * 128], Ob[:, c4 * 128:(c4 + 1) * 128], identb[:], [Ob, identb], [tbk])
                        yield
                        kb.cp("act", yT1[:, g * 4:(g + 1) * 4, qc], tv[:, 0:512].rearrange("p (c n) -> p c n", c=4), [tbk], [(yT1, (g, qt))])
                        yield

                    run_lanes([task_C(0)])
                    for qt in range(NT_):
                        l1_ = chain(task_branch(qt, 2, 0), task_branch(qt, 1, 0))
                        l2_ = chain(task_branch(qt, 2, 1), task_branch(qt, 1, 1))
                        third = []
                        if qt > 0:
                            third.append(task_Z(qt - 1))
                        if qt + 1 < NT_:
                            third.append(task_C(qt + 1))
                        run_lanes([l1_, l2_, chain(*third)])
                    run_lanes([task_Z(NT_ - 1)])
                S.barrier()
                ck("nsa")

            with ExitStack() as pe_:
                out_proj(nsa_w_out, 8, [(yT1, 8)], ymT1, x1_scr, DX1, True, pe_)
                S.barrier()
                ck("l1end")

        S.dead = False
        S.barrier()
        with nc.Block() as block:
            S.emit(block)
        print("program ops:", S.nops, "sems:", S.nsem)
    return nc


_CONST = None


def prep_inputs(inp):
    global _CONST
    if _CONST is None:
        _CONST = host_constants()
        _CONST.update(host_constants_nsa())
    f = lambda a: np.ascontiguousarray(np.asarray(a, dtype=np.float32))
    shared = {
        "hawk_w_in": f(inp["hawk_w_in"][0]),
        "hawk_w_out": f(inp["hawk_w_out"][0]),
        "hawk_w_mem_kv": f(inp["hawk_w_mem_kv"][0]),
        "g_hawk": expand_gain(f(inp["hawk_norm"][0])),
        "g_hawk_mem": expand_gain(f(inp["hawk_mem_norm"][0])),
        "bd_a": block_diag(f(inp["hawk_gate_a_w"][0])),
        "bd_x": block_diag(f(inp["hawk_gate_x_w"][0])),
        "final_norm": f(inp["final_norm"]),
        "nsa_w_in": f(inp["nsa_w_in"][0]),
        "nsa_w_out": f(inp["nsa_w_out"][0]),
        "nsa_w_mem_kv": f(inp["nsa_w_mem_kv"][0]),
        "g_nsa": expand_gain(f(inp["nsa_norm"][0])),
        "g_nsa_mem": expand_gain(f(inp["nsa_mem_norm"][0])),
        "w2k": f(inp["nsa_phi_k_w2"][0]),
        "w2v": f(inp["nsa_phi_v_w2"][0]),
    }
    for k in ("identf", "edil", "ecmp", "m12", "tri", "kaug", "qal"):
        shared[k] = _CONST[k]

    def w1_layout(w1):
        a = w1.reshape(32, 64, 256).transpose(1, 0, 2)
        return np.ascontiguousarray(np.concatenate([a, a], axis=0))
    shared["w1k"] = w1_layout(f(inp["nsa_phi_k_w1"][0]))
    shared["w1v"] = w1_layout(f(inp["nsa_phi_v_w1"][0]))
    shared["peT"] = np.ascontiguousarray(np.stack([f(inp["nsa_pe_k"][0]).T, f(inp["nsa_pe_v"][0]).T], axis=1))
    lv = np.zeros((128, 8, 8), np.float32)
    cw = f(inp["hawk_conv_w"][0])
    for k in range(4):
        lv[:, :, k] = vec_fm(cw[k])
    lv[:, :, 4] = vec_fm(f(inp["hawk_conv_b"][0]))
    lv[:, :, 5] = vec_fm(f(inp["hawk_gate_a_b"][0]).reshape(-1))
    lv[:, :, 6] = vec_fm(f(inp["hawk_gate_x_b"][0]).reshape(-1))
    lv[:, :, 7] = vec_fm(f(inp["hawk_lambda"][0]))
    shared["lru_vec"] = lv
    x = f(inp["x"])
    mem = f(inp["mem"])
    maps = []
    for b in range(x.shape[0]):
        m = dict(shared)
        m["x"] = x[b]
        m["mem"] = mem[b]
        maps.append(m)
    return maps


def kernel(**inputs):
    maps = prep_inputs(inputs)
    nc = build_program()
    res = run_bass_kernel_spmd(nc, maps, core_ids=list(range(len(maps))))
    out = np.stack([np.asarray(r["out"], dtype=np.float32) for r in res.results], axis=0)
    return out
```

```python
import math
from contextlib import ExitStack

import numpy as np
import concourse.bass as bass
import concourse.mybir as mybir
from concourse.bass_utils import run_bass_kernel_spmd

F32 = mybir.dt.float32
BF16 = mybir.dt.bfloat16
AF = mybir.ActivationFunctionType
ALU = mybir.AluOpType
AX = mybir.AxisListType

S_LEN = 2048
D = 1024
NT_ = 16
EPS = 1e-6
DIL_GROUPS = ((128, 1), (512, 4), (2048, 16))

SEM_LIMIT = 30000
N_DMA_SEMS = 24
SAME_ENGINE_SYNC = True


class Buf:
    def __init__(self, name, t, excl=False):
        self.name = name
        self.t = t
        self.excl = excl
        self.st = {}

    def __getitem__(self, idx):
        return self.t[idx]


class Sync:
    def __init__(self, nc, stack):
        self.nc = nc
        self.stack = stack
        self.engs = ["pe", "act", "dve", "pool", "sp"]
        self.ops = {e: [] for e in self.engs}
        self.cur_sem = {}
        self.cnt = {}
        self.nsem = 0
        for e in self.engs:
            self._new_sem(e)
        self.dma_sems = {}
        self.dma_val = {}
        self.dma_rr = {}
        for e in ["sp", "pool", "act"]:
            self.dma_sems[e] = [self._alloc_sem(f"d{e}{i}") for i in range(N_DMA_SEMS)]
            self.dma_val[e] = [0] * N_DMA_SEMS
            self.dma_rr[e] = 0
        self.seen = {e: {} for e in self.engs}
        self.all_ticks = {}
        self.nops = 0
        self.dead = False
        self.eng_free = {e: 0.0 for e in self.engs}
        self.lane = None
        self.tnow = 0.0

    def _alloc_sem(self, name):
        self.nsem += 1
        return self.stack.enter_context(self.nc.semaphore(f"s_{name}_{self.nsem}"))

    def _new_sem(self, e):
        self.cur_sem[e] = self._alloc_sem(e)
        self.cnt[e] = 0

    def _states(self, buf, key, create):
        if key is None:
            if create and None not in buf.st:
                buf.st[None] = [None, {}]
            return list(buf.st.values())
        out = []
        if None in buf.st:
            out.append(buf.st[None])
        if key not in buf.st and create:
            buf.st[key] = [None, {}]
        if key in buf.st:
            out.append(buf.st[key])
        return out

    @staticmethod
    def _norm(lst):
        out = []
        for r in lst or []:
            out.append(r if isinstance(r, tuple) else (r, None))
        return out

    def op(self, eng, fn, reads=None, writes=None, dma=False, cost=0.5):
        if self.dead:
            return None
        reads = self._norm(reads)
        writes = self._norm(writes)
        ex = [(b, None) for (b, k) in reads + writes if b.excl]
        if ex:
            reads = [(b, k) for (b, k) in reads if not b.excl]
            writes = [(b, k) for (b, k) in writes if not b.excl]
            for bk in ex:
                if bk not in writes:
                    writes.append(bk)
        need = []
        for buf, key in reads:
            for st in self._states(buf, key, False):
                if st[0] is not None:
                    need.append(st[0])
        for buf, key in writes:
            for st in self._states(buf, key, False):
                if st[0] is not None:
                    need.append(st[0])
                need.extend(st[1].values())
        if dma:
            i = self.dma_rr[eng]
            self.dma_rr[eng] = (i + 1) % N_DMA_SEMS
            sem = self.dma_sems[eng][i]
            prev = self.dma_val[eng][i]
            if prev > 0:
                need.append((sem, prev, "dma", 0.0))
            if prev + 16 > SEM_LIMIT:
                sem = self._alloc_sem(f"d{eng}{i}")
                self.dma_sems[eng][i] = sem
                prev = 0
            val = prev + 16
            self.dma_val[eng][i] = val
            inc = 16
            tick = [sem, val, "dma", 0.0]
        else:
            if self.cnt[eng] + 1 > SEM_LIMIT:
                self._new_sem(eng)
            self.cnt[eng] += 1
            sem = self.cur_sem[eng]
            val = self.cnt[eng]
            inc = 1
            tick = [sem, val, eng, 0.0]
        ready = 0.0
        for nd in need:
            if nd[3] > ready:
                ready = nd[3]
        start = max(self.eng_free[eng], ready + 0.06)
        if dma:
            self.eng_free[eng] = start + 0.06
        else:
            self.eng_free[eng] = start + cost
        tick[3] = start + cost
        tick = tuple(tick)
        if self.lane is not None and tick[3] > self.lane.clock:
            self.lane.clock = tick[3]
        if tick[3] > self.tnow:
            self.tnow = tick[3]
        waits = {}
        seen = self.seen[eng]
        for (s, v, src, _fin) in need:
            if src == eng and (eng == "pe" or not SAME_ENGINE_SYNC):
                continue
            sid = id(s)
            if seen.get(sid, 0) >= v:
                continue
            if sid not in waits or waits[sid][1] < v:
                waits[sid] = (s, v)
        for sid, (s, v) in waits.items():
            seen[sid] = v
        self.ops[eng].append((list(waits.values()), fn, sem, inc))
        self.all_ticks[id(sem)] = (sem, val)
        self.nops += 1
        wset = set((id(b), k) for b, k in writes)
        for buf, key in reads:
            if (id(buf), key) in wset:
                continue
            self._states(buf, key, True)
            buf.st[key][1][eng if not dma else ("dma", id(sem))] = tick
        for buf, key in writes:
            if key is None:
                buf.st = {None: [tick, {}]}
            else:
                buf.st[key] = [tick, {}]
        return tick

    def barrier(self):
        if self.dead:
            return
        ticks = list(self.all_ticks.values())
        for e in self.engs:
            wl = []
            for (s, v) in ticks:
                if self.seen[e].get(id(s), 0) < v:
                    wl.append((s, v))
                    self.seen[e][id(s)] = v
            if wl:
                self.ops[e].append((wl, None, None, 0))

    def emit(self, block):
        S = self

        def run(engname, e):
            for (wl, fn, sem, inc) in S.ops[engname]:
                for (s, v) in wl:
                    e.wait_ge(s, v)
                if fn is not None:
                    fn(e).then_inc(sem, inc)

        @block.sync
        def _(e):
            run("sp", e)

        @block.tensor
        def _(e):
            run("pe", e)

        @block.scalar
        def _(e):
            run("act", e)

        @block.vector
        def _(e):
            run("dve", e)

        @block.gpsimd
        def _(e):
            run("pool", e)


class BankView:
    def __init__(self, pair, half):
        self.pair = pair
        self.off = 512 * half

    def __getitem__(self, idx):
        if not isinstance(idx, tuple):
            idx = (idx, slice(None))
        pr, col = idx
        cs = (col.start or 0) + self.off
        ce = (col.stop if col.stop is not None else 512) + self.off
        return self.pair[pr, cs:ce:col.step] if col.step else self.pair[pr, cs:ce]


class KB:
    def __init__(self, nc, stack):
        self.nc = nc
        self.gst = stack
        self.S = Sync(nc, stack)
        self.pairs = [stack.enter_context(nc.psum_tensor(f"pair{i}", [128, 1024], F32)) for i in range(4)]
        self.banks = [Buf(f"bank{i}", BankView(self.pairs[i // 2], i % 2), excl=True) for i in range(8)]
        self.bank_rr = 0
        self.uid = 0

    def sb(self, name, shape, dt, stack=None):
        self.uid += 1
        t = (stack or self.gst).enter_context(self.nc.sbuf_tensor(f"{name}_{self.uid}", shape, dt))
        return Buf(name, t)

    def bank(self):
        b = self.banks[self.bank_rr]
        self.bank_rr = (self.bank_rr + 1) % 8
        return b

    @staticmethod
    def fsz(ap):
        n = 1
        for s in ap.shape[1:]:
            n *= int(s)
        return n

    def vcost(self, eng, ap):
        n = self.fsz(ap)
        if eng == "pool":
            return 0.3 + n / 480.0
        if eng == "act":
            return 0.22 + n / 1400.0
        return 0.08 + n / 960.0

    def mm(self, out, lhsT, rhs, start, stop, r, w):
        c = max(self.fsz(rhs), 64) / 1600.0 + 0.04
        self.S.op("pe", lambda e: e.matmul(out, lhsT=lhsT, rhs=rhs, start=start, stop=stop), reads=r, writes=w, cost=c)

    def tr(self, out, in_, ident, r, w):
        self.S.op("pe", lambda e: e.transpose(out, in_, ident), reads=r, writes=w, cost=0.11)

    def act(self, out, in_, func, r, w, **kw):
        self.S.op("act", lambda e: e.activation(out=out, in_=in_, func=func, **kw), reads=r, writes=w, cost=self.vcost("act", out))

    def tt(self, eng, out, in0, in1, op, r, w):
        self.S.op(eng, lambda e: e.tensor_tensor(out=out, in0=in0, in1=in1, op=op), reads=r, writes=w, cost=self.vcost(eng, out))

    def ts(self, eng, out, in0, s1, s2, op0, op1, r, w, **kw):
        c = self.vcost(eng, out)
        if op1 is None:
            self.S.op(eng, lambda e: e.tensor_scalar(out=out, in0=in0, scalar1=s1, scalar2=None, op0=op0, **kw), reads=r, writes=w, cost=c)
        else:
            self.S.op(eng, lambda e: e.tensor_scalar(out=out, in0=in0, scalar1=s1, scalar2=s2, op0=op0, op1=op1, **kw), reads=r, writes=w, cost=c)

    def stt(self, out, in0, scalar, in1, op0, op1, r, w, **kw):
        self.S.op("dve", lambda e: e.scalar_tensor_tensor(out=out, in0=in0, scalar=scalar, in1=in1, op0=op0, op1=op1, **kw), reads=r, writes=w,
                  cost=0.12 + self.fsz(out) / 960.0)

    def cp(self, eng, out, in_, r, w):
        c = self.vcost(eng, out)
        if eng == "act":
            self.S.op("act", lambda e: e.activation(out=out, in_=in_, func=AF.Copy), reads=r, writes=w, cost=c)
        else:
            self.S.op(eng, lambda e: e.tensor_copy(out=out, in_=in_), reads=r, writes=w, cost=c)

    def memset(self, eng, ap, val, w):
        self.S.op(eng, lambda e: e.memset(ap, val), writes=w, cost=self.vcost(eng, ap))

    def recip(self, out, in_, r, w):
        self.S.op("dve", lambda e: e.reciprocal(out=out, in_=in_), reads=r, writes=w, cost=self.vcost("dve", out))

    def dma(self, q, out, in_, r, w):
        nbytes = self.fsz(out) * int(out.shape[0]) * 4
        self.S.op(q, lambda e: e.dma_start(out=out, in_=in_), reads=r, writes=w, dma=True, cost=2.0 + nbytes / 150000.0)


def alibi_slopes(n):
    return np.exp2(-8.0 * np.arange(1, n + 1) / n).astype(np.float32)


def host_constants():
    c = {}
    c["identf"] = np.eye(128, dtype=np.float32)
    sl = alibi_slopes(12)
    ik = np.arange(128)[:, None].astype(np.float64)
    iq = np.arange(128)[None, :].astype(np.float64)
    E = np.zeros((128, 12, 256), np.float32)
    for g, (win, dil) in enumerate(DIL_GROUPS):
        for hs in range(4):
            hh = g * 4 + hs
            s = float(sl[hh]) * dil
            dist_prev = 128 + iq - ik
            ok_prev = (dist_prev <= 128)
            E[:, hh, 0:128] = np.where(ok_prev, np.exp(-s * dist_prev), 0.0)
            dist_cur = iq - ik
            ok_cur = dist_cur >= 0
            E[:, hh, 128:256] = np.where(ok_cur, np.exp(-s * dist_cur), 0.0)
    c["edil"] = E
    return c


def expand_gain(g):
    return np.ascontiguousarray(np.broadcast_to(g.reshape(8, 128).T[:, :, None], (128, 8, 128))).astype(np.float32)


def vec_fm(v):
    return np.ascontiguousarray(v.reshape(8, 128).T).astype(np.float32)


def block_diag(gw):
    out = np.zeros((128, 8, 128), np.float32)
    for c in range(8):
        out[0:64, c, 0:64] = gw[2 * c]
        out[64:128, c, 64:128] = gw[2 * c + 1]
    return out


NEGB = 8192.0


def _bf16_split3(a):
    import ml_dtypes
    a = a.astype(np.float32)
    hi = a.astype(ml_dtypes.bfloat16).astype(np.float32)
    r1 = (a - hi).astype(np.float32)
    mid = r1.astype(ml_dtypes.bfloat16).astype(np.float32)
    r2 = (r1 - mid).astype(np.float32)
    lo = r2.astype(ml_dtypes.bfloat16).astype(np.float32)
    return hi, mid, lo


def host_constants_nsa():
    c = {}
    sl = alibi_slopes(16)
    i = np.arange(128)[:, None].astype(np.float64)
    m = np.arange(247)[None, :].astype(np.float64)
    dist = i - 16.0 * (m - 120.0) - 31.0
    E = np.zeros((128, 16, 247), np.float32)
    for h in range(16):
        E[:, h, :] = np.where(dist >= 0, np.exp(-float(sl[h]) * np.maximum(dist, 0.0)), 0.0)
    c["ecmp"] = E
    ii = np.arange(128)[:, None]
    rel = np.arange(62)[None, :] - 30
    cur = (ii >= 64).astype(np.int64)
    forced = (rel == cur) | (rel == cur - 1)
    future = rel > cur
    m1 = np.where(forced | future, 0.0, 1.0).astype(np.float32)
    m2 = np.where(forced, 1e6, np.where(future, -1e6, 0.0)).astype(np.float32)
    c["m12"] = np.ascontiguousarray(np.stack([m1, m2], axis=1))
    ik = np.arange(128)[:, None]
    iq = np.arange(128)[None, :]
    diag = np.where(ik > iq, -NEGB, 0.0).astype(np.float32)
    far = np.where(ik <= iq, -NEGB, 0.0).astype(np.float32)
    c["tri"] = np.ascontiguousarray(np.stack([np.tile(diag, (1, 4)), np.tile(far, (1, 4))], axis=1))
    k = np.arange(2048)
    kp = k - 1024
    hi = (np.floor(kp / 128.0) * 128.0).astype(np.float32)
    lo = (kp - hi).astype(np.float32)
    ka = np.zeros((2, 41, 2048), np.float32)
    for j in range(32):
        ka[0, j, :] = (k // 64 == j).astype(np.float32)
    for v in range(2):
        ka[v, 32:35, :] = 1.0
        ka[v, 35:38, :] = lo[None, :]
        ka[v, 38:41, :] = hi[None, :]
    c["kaug"] = ka
    qa = np.zeros((16, 9, 2048), np.float32)
    qp = (np.arange(2048) - 1024).astype(np.float32)
    for h in range(16):
        s8 = np.float32(8.0) * np.float32(sl[h])
        a = (-s8 * qp).astype(np.float32)
        ah, am, al = _bf16_split3(a)
        sh, sm, sl_ = _bf16_split3(np.full((2048,), s8, np.float32))
        qa[h, 0], qa[h, 1], qa[h, 2] = ah, am, al
        qa[h, 3], qa[h, 4], qa[h, 5] = sh, sm, sl_
        qa[h, 6], qa[h, 7], qa[h, 8] = sh, sm, sl_
    c["qal"] = qa
    return c


def chain(*gens):
    for g_ in gens:
        yield from g_


L3_STEPS = 1


def run_lanes(lanes, weights=None):
    active = list(lanes)
    w = {id(l): 1 for l in active}
    if weights:
        for l, wt in zip(lanes, weights):
            w[id(l)] = wt
    while active:
        for l in list(active):
            for _ in range(w[id(l)]):
                try:
                    next(l)
                except StopIteration:
                    active.remove(l)
                    break


def build_program(stop_after=None):
    nc = bass.Bass("TRN2", target_bir_lowering=False)

    ckstate = {}

    def ck(name):
        if stop_after == name:
            ckstate["S"].dead = True

    def din(name, shape):
        return nc.dram_tensor(name, list(shape), F32, kind="ExternalInput").ap()

    x_d = din("x", [S_LEN, D])
    mem_d = din("mem", [256, D])
    hawk_w_in = din("hawk_w_in", [D, 7680])
    hawk_w_out = din("hawk_w_out", [1792, D])
    hawk_w_mem_kv = din("hawk_w_mem_kv", [D, 512])
    g_hawk = din("g_hawk", [128, 8, 128])
    g_hawk_mem = din("g_hawk_mem", [128, 8, 128])
    lru_vec = din("lru_vec", [128, 8, 8])
    bd_a = din("bd_a", [128, 8, 128])
    bd_x = din("bd_x", [128, 8, 128])
    identf_d = din("identf", [128, 128])
    edil_d = din("edil", [128, 12, 256])
    final_g = din("final_norm", [D])
    hawk_norm_v = din("hawk_norm_v", [D])
    nsa_norm_v = din("nsa_norm_v", [D])
    nsa_w_in = din("nsa_w_in", [D, 3376])
    nsa_w_out = din("nsa_w_out", [1280, D])
    nsa_w_mem_kv = din("nsa_w_mem_kv", [D, 512])
    g_nsa = din("g_nsa", [128, 8, 128])
    g_nsa_mem = din("g_nsa_mem", [128, 8, 128])
    w1k_d = din("w1k", [128, 32, 256])
    w1v_d = din("w1v", [128, 32, 256])
    w2k_d = din("w2k", [256, 64])
    w2v_d = din("w2v", [256, 64])
    peT_d = din("peT", [64, 2, 32])
    ecmp_d = din("ecmp", [128, 16, 247])
    m12_d = din("m12", [128, 2, 62])
    tri_d = din("tri", [128, 2, 512])
    kaug_d = din("kaug", [2, 41, 2048])
    qal_d = din("qal", [16, 9, 2048])
    out_d = nc.dram_tensor("out", [S_LEN, D], F32, kind="ExternalOutput").ap()
    x1_scr = nc.dram_tensor("x1_scr", [S_LEN, D], F32, kind="Internal").ap()

    with ExitStack() as gst:
        kb = KB(nc, gst)
        S = kb.S
        ckstate["S"] = S
        DX = Buf("x_dram", None)
        DX1 = Buf("x1_dram", None)
        DOUT = Buf("out_dram", None)

        xnT = kb.sb("xnT", [128, 8, S_LEN], BF16)
        memnT = kb.sb("memnT", [128, 8, 256], BF16)
        identf = kb.sb("identf", [128, 128], F32)
        identb = kb.sb("identb", [128, 128], BF16)
        onesb = kb.sb("onesb", [128, 128], BF16)
        wstage = [kb.sb(f"wstage{i}", [128, 1024], F32) for i in range(3)]
        ws_rr = [0]
        stat = kb.sb("stat", [128, 64], F32)
        stat_rr = [0]

        kb.dma("sp", identf[:], identf_d, [], [identf])
        kb.cp("dve", identb[:], identf[:], [identf], [identb])
        kb.memset("dve", onesb[:], 1.0, [onesb])

        def next_ws():
            b = wstage[ws_rr[0]]
            ws_rr[0] = (ws_rr[0] + 1) % len(wstage)
            return b

        def load_w(dst, dst_ap3, src_ap3, n, gain=None, key=None, q="sp", part=128, eng="pool"):
            dcs = dst_ap3.shape[1]
            if gain is None:
                kb.dma("pool", dst_ap3, src_ap3, [], [(dst, key)])
                return
            assert dcs * n <= 1024
            stg = next_ws()
            sv = stg[0:part, 0:dcs * n].rearrange("p (c n) -> p c n", c=dcs)
            kb.dma(q, sv, src_ap3, [], [stg])
            if gain is not None:
                kb.tt(eng, dst_ap3, sv, gain[0:part, 0:dcs, 0:n], ALU.mult, [stg, gain], [(dst, key)])
            else:
                kb.cp(eng, dst_ap3, sv, [stg], [(dst, key)])

        def win_cols(w_dram, c0, n):
            return w_dram.rearrange("(dc p) n -> p dc n", p=128)[:, :, c0:c0 + n]

        class NormCtx:
            def __init__(self, stack, nbuf=2):
                self.xstage = [kb.sb(f"xstage{i}", [128, 1024], F32, stack) for i in range(nbuf)]
                self.xnb = [kb.sb(f"xnb{i}", [128, 1024], BF16, stack) for i in range(nbuf)]
                self.junk = kb.sb("junk", [128, 1024], BF16, stack)

        def tile_rstd(ncx, xbuf, xap):
            i = stat_rr[0]
            stat_rr[0] = (stat_rr[0] + 1) % 32
            ss = stat[:, 2 * i:2 * i + 1]
            rs = stat[:, 2 * i + 1:2 * i + 2]
            kb.stt(ncx.junk[:], xap, 1.0, xap, ALU.mult, ALU.mult, [xbuf], [ncx.junk, (stat, i)], accum_out=ss)
            kb.ts("dve", ss, ss, 1.0 / D, EPS, ALU.mult, ALU.add, [(stat, i)], [(stat, i)])
            kb.act(ss, ss, AF.Sqrt, [(stat, i)], [(stat, i)])
            kb.recip(rs, ss, [(stat, i)], [(stat, i)])
            return rs, i

        def norm_to_T(ncx, xbuf, xap, dstT, t, ntok_off, gB=None):
            rs, i = tile_rstd(ncx, xbuf, xap)
            nb = ncx.xnb[t % 2]
            if gB is None:
                kb.ts("dve", nb[:], xap, rs, None, ALU.mult, None, [xbuf, (stat, i)], [nb])
            else:
                kb.stt(nb[:], xap, rs, gB[:], ALU.mult, ALU.mult, [xbuf, (stat, i), gB], [nb])
            bk = kb.bank()
            bv = bk[:].bitcast(BF16)
            for c in range(8):
                kb.tr(bv[:, c * 128:(c + 1) * 128], nb[:, c * 128:(c + 1) * 128], identb[:], [nb, identb], [bk])
            kb.cp("act", dstT[:, :, ntok_off:ntok_off + 128], bv.rearrange("p (c n) -> p c n", c=8), [bk], [(dstT, t)])

        def norm_to_T_gen(ncx, xbuf, xap, dstT, t, ntok_off, bk, bi, gB=None):
            rs, i = tile_rstd(ncx, xbuf, xap)
            yield
            nb = ncx.xnb[bi]
            if gB is None:
                kb.ts("dve", nb[:], xap, rs, None, ALU.mult, None, [xbuf, (stat, i)], [nb])
            else:
                kb.stt(nb[:], xap, rs, gB[:], ALU.mult, ALU.mult, [xbuf, (stat, i), gB], [nb])
            yield
            bv = bk[:].bitcast(BF16)
            for c in range(8):
                kb.tr(bv[:, c * 128:(c + 1) * 128], nb[:, c * 128:(c + 1) * 128], identb[:], [nb, identb], [bk])
            yield
            kb.cp("act", dstT[:, :, ntok_off:ntok_off + 128], bv.rearrange("p (c n) -> p c n", c=8), [bk], [(dstT, t)])
            yield

        with ExitStack() as pa:
            ncx = NormCtx(pa)
            gBa = kb.sb("gBa", [128, 1024], F32, pa)
            kb.dma("sp", gBa[:], hawk_norm_v.partition_broadcast(128), [], [gBa])
            for t in range(NT_):
                xs = ncx.xstage[t % 2]
                kb.dma("sp", xs[:], x_d[t * 128:(t + 1) * 128, :], [DX], [xs])
                norm_to_T(ncx, xs, xs[:], xnT, t, t * 128, gB=gBa)
            for t in range(2):
                xs = ncx.xstage[t % 2]
                kb.dma("sp", xs[:], mem_d[t * 128:(t + 1) * 128, :], [], [xs])
                norm_to_T(ncx, xs, xs[:], memnT, t, t * 128)
            S.barrier()
            ck("A")

        def make_loader(w_in_d, gain, nslots, stack):
            wslots = [kb.sb(f"wslot{i}", [128, 8, 128], BF16, stack) for i in range(nslots)]
            rr = [0]

            def load_win(c0, n=128, q="sp", into=None, off=0):
                if into is None:
                    wsl = wslots[rr[0]]
                    rr[0] = (rr[0] + 1) % nslots
                else:
                    wsl = into
                load_w(wsl, wsl[:, :, off:off + n], win_cols(w_in_d, c0, n), n, gain=None, q=q, key=off)
                return wsl
            return load_win

        def proj_fm(wsl, n, evac, woff=0):
            for tc in range(4):
                bk = kb.bank()
                for dc in range(8):
                    kb.mm(bk[0:n, :], wsl[:, dc, woff:woff + n], xnT[:, dc, tc * 512:(tc + 1) * 512], dc == 0, dc == 7,
                          [wsl, xnT], [bk])
                evac(bk, bk[0:n, :], tc)

        def mem_kv(w_kv_d, gain, kmT, vm, stack):
            wkv = kb.sb("wkv", [128, 8, 512], BF16, stack)
            for j in range(4):
                load_w(wkv, wkv[:, :, j * 128:(j + 1) * 128], win_cols(w_kv_d, j * 128, 128), 128, gain=gain, key=j)
            for h in range(4):
                bk = kb.bank()
                for dc in range(8):
                    kb.mm(bk[0:64, 0:256], wkv[:, dc, h * 64:(h + 1) * 64], memnT[:, dc, :], dc == 0, dc == 7,
                          [wkv, memnT], [bk])
                kb.cp("act", kmT[0:64, h, :], bk[0:64, 0:256], [bk], [(kmT, h)])
            for mt in range(2):
                bk = kb.bank()
                for dc in range(8):
                    kb.mm(bk[:, 0:256], memnT[:, dc, mt * 128:(mt + 1) * 128], wkv[:, dc, 256:512], dc == 0, dc == 7,
                          [wkv, memnT], [bk])
                kb.cp("act", vm[:, mt, :], bk[:, 0:256], [bk], [(vm, mt)])

        def mem_attn(load_win, colq, colz, kmT, vm, ymT, stack):
            qmTs = [kb.sb(f"qmT{i}", [64, S_LEN], BF16, stack) for i in range(2)]
            szms = [kb.sb(f"szm{i}", [64, S_LEN], BF16, stack) for i in range(2)]
            PTm = [kb.sb(f"PTm{i}", [128, 512], BF16, stack) for i in range(4)]
            rdm = [kb.sb(f"rdm{i}", [64, 512], F32, stack) for i in range(2)]

            def lane(h, L):
                qmT, szm = qmTs[L], szms[L]
                b0, b1, b2, b3 = [kb.banks[4 * L + j] for j in range(4)]
                wq = load_win(colq + h * 64, 64)
                wz = load_win(colz + h * 64, 64)
                for (wsl, dst, func) in ((wq, qmT, None), (wz, szm, AF.Silu)):
                    for tc in range(4):
                        bk = b0 if tc % 2 == 0 else b1
                        for dc in range(8):
                            kb.mm(bk[0:64, :], wsl[:, dc, 0:64], xnT[:, dc, tc * 512:(tc + 1) * 512], dc == 0, dc == 7, [wsl, xnT], [bk])
                        yield
                        if func is None:
                            kb.cp("act", dst[0:64, tc * 512:(tc + 1) * 512], bk[0:64, :], [bk], [(dst, tc)])
                        else:
                            kb.act(dst[0:64, tc * 512:(tc + 1) * 512], bk[0:64, :], func, [bk], [(dst, tc)])
                        yield
                for tc in range(4):
                    pts = []
                    for mt in range(2):
                        bk = b0 if mt == 0 else b1
                        kb.mm(bk[:, :], kmT[0:64, h, mt * 128:(mt + 1) * 128], qmT[0:64, tc * 512:(tc + 1) * 512],
                              True, True, [kmT, (qmT, tc)], [bk])
                        pt = PTm[2 * L + mt]
                        kb.act(pt[:], bk[:], AF.Exp, [bk], [pt], scale=0.125)
                        pts.append(pt)
                        yield
                    for mt in range(2):
                        kb.mm(b2[0:64, :], vm[:, mt, h * 64:(h + 1) * 64], pts[mt][:], mt == 0, mt == 1, [vm, pts[mt]], [b2])
                    for mt in range(2):
                        kb.mm(b3[0:64, :], onesb[:, 0:64], pts[mt][:], mt == 0, mt == 1, [onesb, pts[mt]], [b3])
                    yield
                    rd = rdm[L]
                    kb.recip(rd[:], b3[0:64, :], [b3], [rd])
                    kb.tt("dve", rd[:], b2[0:64, :], rd[:], ALU.mult, [b2, rd], [rd])
                    yield
                    kb.tt("dve", ymT[0:64, h, tc * 512:(tc + 1) * 512], rd[:], szm[0:64, tc * 512:(tc + 1) * 512], ALU.mult,
                          [rd, (szm, tc)], [(ymT, (h, tc))])
                    yield

            run_lanes([lane(0, 0), lane(1, 1)])
            run_lanes([lane(2, 0), lane(3, 1)])

        def out_proj(w_out_d, nch, yTl, ymT, resid_d, resid_buf, final, stack, dbg=False):
            ysrc = []
            for (yb_, n_) in yTl:
                for ci in range(n_):
                    ysrc.append((yb_, ci))
            ncxs = [NormCtx(stack, 1), NormCtx(stack, 1)]
            WO = kb.sb("WO", [128, nch, 1024], BF16, stack)
            WOm = kb.sb("WOm", [64, 4, 1024], BF16, stack)
            wo_v = w_out_d[0:nch * 128, :].rearrange("(c p) n -> p c n", p=128)
            for c in range(nch):
                load_w(WO, WO[:, c:c + 1, :], wo_v[:, c:c + 1, :], 1024, key=c, eng=("pool", "dve", "act")[c % 3])
            wom_v = w_out_d[nch * 128:nch * 128 + 256, :].rearrange("(h p) n -> p h n", p=64)
            for h in range(4):
                load_w(WOm, WOm[0:64, h:h + 1, :], wom_v[:, h:h + 1, :], 1024, key=h, part=64)
            x1t = [kb.sb(f"x1t{i}", [128, 1024], F32, stack) for i in range(2)]
            if not final:
                gNx = kb.sb("gNx", [128, 1024], F32, stack)
                kb.dma("sp", gNx[:], nsa_norm_v.partition_broadcast(128), [], [gNx])
            if final:
                gF = kb.sb("gF", [128, 1024], F32, stack)
                kb.dma("sp", gF[:], final_g.partition_broadcast(128), [], [gF])
                ot = [kb.sb(f"ot{i}", [128, 1024], F32, stack) for i in range(2)]

            def lane(L):
                ncx = ncxs[L]
                bks = [kb.banks[4 * L + j] for j in range(4)]
                for t in range(L, NT_, 2):
                    xs = ncx.xstage[0]
                    kb.dma("sp", xs[:], resid_d[t * 128:(t + 1) * 128, :], [resid_buf], [xs])
                    x1 = x1t[L]
                    for half in range(2):
                        bk = bks[half]
                        for c in range(nch):
                            yb_, ci = ysrc[c]
                            kb.mm(bk[:, :], yb_[:, ci, t * 128:(t + 1) * 128], WO[:, c, half * 512:(half + 1) * 512], c == 0, False,
                                  [yb_, WO], [bk])
                            if c % 4 == 3:
                                yield
                        for h in range(4):
                            kb.mm(bk[:, :], ymT[0:64, h, t * 128:(t + 1) * 128], WOm[0:64, h, half * 512:(half + 1) * 512], False, h == 3,
                                  [ymT, WOm], [bk])
                        yield
                        kb.tt("dve", x1[:, half * 512:(half + 1) * 512], xs[:, half * 512:(half + 1) * 512], bk[:], ALU.add,
                              [xs, bk], [(x1, half)])
                        yield
                    if not final:
                        kb.dma("sp", x1_scr[t * 128:(t + 1) * 128, :], x1[:], [x1], [DX1])
                        yield from norm_to_T_gen(ncx, x1, x1[:], xnT, t, t * 128, bks[2], 0, gB=gNx)
                        if dbg:
                            kb.dma("sp", out_d[t * 128:(t + 1) * 128, :], x1[:], [x1], [DOUT])
                    else:
                        rs, i = tile_rstd(ncx, x1, x1[:])
                        yield
                        o = ot[L]
                        kb.stt(o[:], x1[:], rs, gF[:], ALU.mult, ALU.mult, [x1, (stat, i), gF], [o])
                        yield
                        kb.dma("sp", out_d[t * 128:(t + 1) * 128, :], o[:], [o], [DOUT])
                        yield

            run_lanes([lane(0), lane(1)])

        with ExitStack() as l0:
            yTa = kb.sb("yTa", [128, 8, S_LEN], BF16, l0)
            ymT = kb.sb("ymT", [64, 4, S_LEN], BF16, l0)
            gH = kb.sb("gH", [128, 8, 128], F32, l0)
            kb.dma("sp", gH[:], g_hawk, [], [gH])
            load_win = make_loader(hawk_w_in, gH, 6, l0)
            kmT = kb.sb("kmT", [64, 4, 256], BF16, l0)
            vm = kb.sb("vm", [128, 2, 256], BF16, l0)
            with ExitStack() as pm:
                gHm = kb.sb("gHm", [128, 8, 128], F32, pm)
                kb.dma("sp", gHm[:], g_hawk_mem, [], [gHm])
                mem_kv(hawk_w_mem_kv, gHm, kmT, vm, pm)
                S.barrier()
                ck("memkv0")
            with ExitStack() as pd:
                mem_attn(load_win, 7168, 7424, kmT, vm, ymT, pd)
                S.barrier()
                ck("mem0")

            with ExitStack() as pb:
                lv = kb.sb("lv", [128, 8, 8], F32, pb)
                cvec = kb.sb("cvec", [128, 8, 2], F32, pb)
                bda = kb.sb("bda", [128, 8, 128], BF16, pb)
                bdx = kb.sb("bdx", [128, 8, 128], BF16, pb)
                kb.dma("sp", lv[:], lru_vec, [], [lv])
                load_w(bda, bda[:], bd_a, 128)
                load_w(bdx, bdx[:], bd_x, 128)
                kb.act(cvec[:, :, 0], lv[:, :, 7], AF.Exp, [lv], [cvec], scale=-1.0)
                kb.act(cvec[:, :, 0], cvec[:, :, 0], AF.Ln, [cvec], [cvec], bias=1.0)
                kb.ts("dve", cvec[:, :, 1], cvec[:, :, 0], -16.0, None, ALU.mult, None, [cvec], [cvec])
                kb.ts("dve", cvec[:, :, 0], cvec[:, :, 0], -8.0, None, ALU.mult, None, [cvec], [cvec])
                sets = []
                for L in range(2):
                    sets.append(dict(
                        B1=kb.sb(f"B1_{L}", [128, S_LEN + 4], F32, pb), B2=kb.sb(f"B2_{L}", [128, S_LEN], F32, pb),
                        B3=kb.sb(f"B3_{L}", [128, S_LEN], F32, pb), B4=kb.sb(f"B4_{L}", [128, S_LEN], F32, pb),
                        xcb=kb.sb(f"xcb_{L}", [128, S_LEN], BF16, pb), sz=kb.sb(f"sz_{L}", [128, S_LEN], BF16, pb)))

                def lru_lane(L):
                    st_ = sets[L]
                    B1, B2, B3, B4, xcb, sz = st_["B1"], st_["B2"], st_["B3"], st_["B4"], st_["xcb"], st_["sz"]
                    bks = [kb.banks[4 * L + j] for j in range(4)]
                    for c in range(L, 8, 2):
                        wxa = load_win(c * 128)
                        wza = load_win(1024 + c * 128)
                        kb.memset("dve", B1[:, 0:3], 0.0, [(B1, "pad")])
                        for (wsl, which) in ((wxa, 0), (wza, 1)):
                            for tc in range(4):
                                bk = bks[tc % 4]
                                for dc in range(8):
                                    kb.mm(bk[:, :], wsl[:, dc, 0:128], xnT[:, dc, tc * 512:(tc + 1) * 512], dc == 0, dc == 7, [wsl, xnT], [bk])
                                yield
                                if which == 0:
                                    kb.cp("act", B1[:, 3 + tc * 512:3 + (tc + 1) * 512], bk[:], [bk], [(B1, tc)])
                                else:
                                    kb.act(sz[:, tc * 512:(tc + 1) * 512], bk[:], AF.Silu, [bk], [(sz, tc)])
                                yield
                        kb.ts("dve", B2[:], B1[:, 0:S_LEN], lv[:, c, 0:1], lv[:, c, 4:5], ALU.mult, ALU.add, [B1, lv], [B2])
                        yield
                        for k in range(1, 4):
                            kb.stt(B2[:], B1[:, k:k + S_LEN], lv[:, c, k:k + 1], B2[:], ALU.mult, ALU.add, [B1, lv, B2], [B2])
                            yield
                        kb.cp("dve", xcb[:], B2[:], [B2], [xcb])
                        yield
                        for (bd, dstb, col) in ((bda, B1, 5), (bdx, B4, 6)):
                            for tc in range(4):
                                bk = bks[tc % 4]
                                kb.mm(bk[:, :], bd[:, c, :], xcb[:, tc * 512:(tc + 1) * 512], True, True, [bd, xcb], [bk])
                                yield
                                kb.act(dstb[:, tc * 512:(tc + 1) * 512], bk[:], AF.Sigmoid, [bk, lv], [(dstb, tc)], bias=lv[:, c, col:col + 1])
                                yield
                        r_ap = B1[:, 0:S_LEN]
                        kb.act(B3[:], r_ap, AF.Exp, [B1, cvec], [B3], scale=cvec[:, c, 0:1])
                        yield
                        kb.act(r_ap, r_ap, AF.Exp, [B1, cvec], [B1], scale=cvec[:, c, 1:2])
                        yield
                        kb.tt("dve", B2[:], B2[:], B4[:], ALU.mult, [B2, B4], [B2])
                        yield
                        kb.ts("dve", r_ap, r_ap, -1.0, 1.0, ALU.mult, ALU.add, [B1], [B1])
                        yield
                        kb.ts("dve", r_ap, r_ap, 0.0, None, ALU.max, None, [B1], [B1])
                        yield
                        kb.act(r_ap, r_ap, AF.Sqrt, [B1], [B1])
                        kb.memset("dve", B1[:, 0:1], 1.0, [B1])
                        yield
                        kb.tt("dve", B2[:], B2[:], r_ap, ALU.mult, [B2, B1], [B2])
                        yield
                        S.op("dve", lambda e, B4=B4, B3=B3, B2=B2: e.tensor_tensor_scan(out=B4[:], data0=B3[:], data1=B2[:], initial=0.0,
                                                                                      op0=ALU.mult, op1=ALU.add), reads=[B3, B2], writes=[B4])
                        yield
                        kb.tt("dve", yTa[:, c, :], B4[:], sz[:], ALU.mult, [B4, sz], [(yTa, c)])
                        yield

                run_lanes([lru_lane(0), lru_lane(1)])
                S.barrier()
                ck("lru")
            yTb = kb.sb("yTb", [128, 4, S_LEN], BF16, l0)

            with ExitStack() as pc:
                edil = kb.sb("edil", [128, 12, 256], F32, pc)
                kb.dma("sp", edil[:], edil_d, [], [edil])
                qTs = [kb.sb(f"qT{i}", [128, S_LEN], BF16, pc) for i in range(2)]
                kTs = [kb.sb(f"kT{i}", [128, S_LEN], BF16, pc) for i in range(2)]
                vTs = [kb.sb(f"vT{i}", [128, S_LEN], BF16, pc) for i in range(2)]
                Vps = [kb.sb(f"Vp{i}", [128, 16, 128], BF16, pc) for i in range(2)]
                szbs = [kb.sb(f"szb{i}", [128, S_LEN], BF16, pc) for i in range(2)]
                NTa = kb.sb("NTa", [128, S_LEN], F32, pc)
                DBa = kb.sb("DBa", [128, S_LEN], F32, pc)
                Pf = [kb.sb(f"Pf{i}", [128, 256], F32, pc) for i in range(3)]
                PT = [kb.sb(f"PT{i}", [128, 256], BF16, pc) for i in range(3)]
                sc_d = 128.0 ** -0.5
                items = [(hs, g) for hs in range(4) for g in range(3)]
                prot = [0]

                def pbank():
                    b = kb.banks[6 + prot[0]]
                    prot[0] ^= 1
                    return b

                def dtoks(d, r, b):
                    t0 = r + d * 128 * b
                    return slice(t0, t0 + d * 127 + 1, d)

                def proj_fm_lane(wsl, dst, func):
                    for tc in range(4):
                        bk = pbank()
                        for dc in range(8):
                            kb.mm(bk[:, :], wsl[:, dc, 0:128], xnT[:, dc, tc * 512:(tc + 1) * 512], dc == 0, dc == 7, [wsl, xnT], [bk])
                        yield
                        if func is None:
                            kb.cp("act", dst[:, tc * 512:(tc + 1) * 512], bk[:], [bk], [(dst, tc)])
                        else:
                            kb.act(dst[:, tc * 512:(tc + 1) * 512], bk[:], func, [bk], [(dst, tc)])
                        yield

                def task_proj(i):
                    hs, g = items[i]
                    win, d = DIL_GROUPS[g]
                    hh = g * 4 + hs
                    nqb = (S_LEN // d) // 128
                    s = i % 2
                    if g == 0:
                        wz = load_win(2048 + 4608 + hs * 128)
                        yield from proj_fm_lane(wz, szbs[hs % 2], AF.Silu)
                    wq = load_win(2048 + hh * 128)
                    yield from proj_fm_lane(wq, qTs[s], None)
                    wk = load_win(2048 + 1536 + hh * 128)
                    yield from proj_fm_lane(wk, kTs[s], None)
                    wv = load_win(2048 + 3072 + hh * 128)
                    yield from proj_fm_lane(wv, vTs[s], None)
                    for j in range(4):
                        bk = pbank()
                        bv = bk[:].bitcast(BF16)
                        for k in range(4):
                            r, b = divmod(4 * j + k, nqb)
                            kb.tr(bv[:, k * 128:(k + 1) * 128], vTs[s][:, dtoks(d, r, b)], identb[:], [vTs[s], identb], [bk])
                        yield
                        kb.cp("act", Vps[s][:, 4 * j:4 * j + 4, :], bv[:, 0:512].rearrange("p (k n) -> p k n", k=4), [bk], [(Vps[s], j)])
                        yield

                def task_tile(i, ti, lane):
                    hs, g = items[i]
                    win, d = DIL_GROUPS[g]
                    hh = g * 4 + hs
                    nqb = (S_LEN // d) // 128
                    s = i % 2
                    qT, kT, Vp = qTs[s], kTs[s], Vps[s]
                    r, qb = divmod(ti, nqb)
                    qs = dtoks(d, r, qb)
                    kbs = [qb - 1, qb] if qb > 0 else [qb]
                    sbk = kb.banks[2 * lane]
                    ndb = kb.banks[2 * lane + 1]
                    for kbi in kbs:
                        typ = 0 if kbi < qb else 1
                        kb.mm(sbk[:, typ * 128:(typ + 1) * 128], kT[:, dtoks(d, r, kbi)], qT[:, qs], True, True, [kT, qT], [sbk])
                    yield
                    lo = 0 if qb > 0 else 128
                    pf = Pf[lane]
                    pt = PT[lane]
                    kb.act(pf[:, lo:256], sbk[:, lo:256], AF.Exp, [sbk], [pf], scale=sc_d)
                    yield
                    kb.tt("dve", pt[:, lo:256], pf[:, lo:256], edil[:, hh, lo:256], ALU.mult, [pf, edil], [pt])
                    yield
                    for j, kbi in enumerate(kbs):
                        typ = 0 if kbi < qb else 1
                        kb.mm(ndb[:, 0:128], Vp[:, r * nqb + kbi, :], pt[:, typ * 128:(typ + 1) * 128], j == 0, j == len(kbs) - 1,
                              [Vp, pt], [ndb])
                    for j, kbi in enumerate(kbs):
                        typ = 0 if kbi < qb else 1
                        kb.mm(ndb[:, 128:256], onesb[:], pt[:, typ * 128:(typ + 1) * 128], j == 0, j == len(kbs) - 1,
                              [onesb, pt], [ndb])
                    yield
                    if g == 0:
                        kb.cp("dve", NTa[:, qs], ndb[:, 0:128], [ndb], [NTa])
                        kb.cp("dve", DBa[:, qs], ndb[:, 128:256], [ndb], [DBa])
                    else:
                        kb.tt("dve", NTa[:, qs], NTa[:, qs], ndb[:, 0:128], ALU.add, [ndb, NTa], [NTa])
                        kb.tt("dve", DBa[:, qs], DBa[:, qs], ndb[:, 128:256], ALU.add, [ndb, DBa], [DBa])
                    yield

                def tile_lane(i, lane):
                    for ti in range(lane, 16, 3):
                        yield from task_tile(i, ti, lane)

                run_lanes([task_proj(0)])
                for i in range(len(items)):
                    hs, g = items[i]
                    lanes = [tile_lane(i, 0), tile_lane(i, 1), tile_lane(i, 2)]
                    if i + 1 < len(items):
                        lanes.append(task_proj(i + 1))
                    run_lanes(lanes)
                    if g == 2:
                        kb.recip(DBa[:], DBa[:], [DBa], [DBa])
                        kb.tt("dve", NTa[:], NTa[:], DBa[:], ALU.mult, [NTa, DBa], [NTa])
                        kb.tt("dve", yTb[:, hs, :], NTa[:], szbs[hs % 2][:], ALU.mult, [NTa, szbs[hs % 2]], [(yTb, hs)])
                S.barrier()
                ck("dil")

            with ExitStack() as pe_:
                out_proj(hawk_w_out, 12, [(yTa, 8), (yTb, 4)], ymT, x_d, DX, False, pe_, dbg=(stop_after == "l0"))
                S.barrier()
                ck("l0end")

        if stop_after != "l0":
          with ExitStack() as l1:
            yT1 = kb.sb("yT1", [128, 8, S_LEN], BF16, l1)
            ymT1 = kb.sb("ymT1", [64, 4, S_LEN], BF16, l1)
            gN = kb.sb("gN", [128, 8, 128], F32, l1)
            kb.dma("sp", gN[:], g_nsa, [], [gN])
            load_win = make_loader(nsa_w_in, gN, 4, l1)
            kcmpT = kb.sb("kcmpT", [64, 2, 128], BF16, l1)
            vcmp = kb.sb("vcmp", [128, 2, 64], BF16, l1)
            gates = kb.sb("gates", [128, 16, 48], F32, l1)
            with ExitStack() as pmm:
                kmT = kb.sb("kmT1", [64, 4, 256], BF16, pmm)
                vm = kb.sb("vm1", [128, 2, 256], BF16, pmm)
                with ExitStack() as pm:
                    gNm = kb.sb("gNm", [128, 8, 128], F32, pm)
                    kb.dma("sp", gNm[:], g_nsa_mem, [], [gNm])
                    mem_kv(nsa_w_mem_kv, gNm, kmT, vm, pm)
                    S.barrier()
                    ck("memkv1")
                with ExitStack() as pd:
                    mem_attn(load_win, 2864, 3120, kmT, vm, ymT1, pd)
                    S.barrier()
                    ck("mem1")

            with ExitStack() as pq:
                wg = load_win(1792, 48)
                for t in range(NT_):
                    bk = kb.bank()
                    for dc in range(8):
                        kb.mm(bk[:, 0:48], xnT[:, dc, t * 128:(t + 1) * 128], wg[:, dc, 0:48], dc == 0, dc == 7, [xnT, wg], [bk])
                    kb.act(gates[:, t, :], bk[:, 0:48], AF.Sigmoid, [bk], [(gates, t)])
                kcT = kb.sb("kcT", [128, S_LEN], BF16, pq)
                vcT = kb.sb("vcT", [128, S_LEN], BF16, pq)
                wkc = load_win(1024)
                proj_fm(wkc, 128, lambda bk, ap, tc: kb.cp("act", kcT[:, tc * 512:(tc + 1) * 512], ap, [bk], [(kcT, tc)]))
                wvc = load_win(1024 + 128)
                proj_fm(wvc, 128, lambda bk, ap, tc: kb.cp("act", vcT[:, tc * 512:(tc + 1) * 512], ap, [bk], [(vcT, tc)]))
                W1 = kb.sb("W1", [128, 32, 256], BF16, pq)
                w2 = kb.sb("w2", [128, 2, 64], BF16, pq)
                peS = kb.sb("peS", [64, 2, 32], F32, pq)
                peb = kb.sb("peb", [64, 2, 32], BF16, pq)
                hidT = kb.sb("hidT", [128, 2, 128], BF16, pq)
                cb = kb.sb("cb", [128, 2], F32, pq)
                kb.dma("sp", peS[:], peT_d, [], [peS])
                kb.cp("dve", peb[:], peS[:], [peS], [peb])
                for kv in range(2):
                    w1d = w1k_d if kv == 0 else w1v_d
                    w2d = w2k_d if kv == 0 else w2v_d
                    srcT = kcT if kv == 0 else vcT
                    for p4 in range(8):
                        load_w(W1, W1[:, p4 * 4:(p4 + 1) * 4, :], w1d[:, p4 * 4:(p4 + 1) * 4, :], 256, key=p4)
                    load_w(w2, w2[:, :, :], w2d.rearrange("(hc p) d -> p hc d", p=128), 64)
                    for hc in range(2):
                        bk = kb.bank()
                        for p in range(32):
                            kb.mm(bk[:, 0:1], W1[0:64, p, hc * 128:(hc + 1) * 128], peb[0:64, kv, p:p + 1], p == 0, p == 31,
                                  [W1, peb], [bk])
                        kb.cp("dve", cb[:, hc:hc + 1], bk[:, 0:1], [bk], [(cb, hc)])
                    for g in range(2):
                        for hc in range(2):
                            bk = kb.bank()
                            for p in range(32):
                                kb.mm(bk[:, 0:127], W1[g * 64:(g + 1) * 64, p, hc * 128:(hc + 1) * 128],
                                      srcT[g * 64:(g + 1) * 64, p:p + 16 * 126 + 1:16], p == 0, p == 31, [W1, srcT], [bk])
                            kb.act(hidT[:, hc, 0:127], bk[:, 0:127], AF.Silu, [bk, cb], [(hidT, hc)], bias=cb[:, hc:hc + 1])
                        bk = kb.bank()
                        if kv == 0:
                            for hc in range(2):
                                kb.mm(bk[0:64, 0:127], w2[:, hc, :], hidT[:, hc, 0:127], hc == 0, hc == 1, [w2, hidT], [bk])
                            kb.cp("dve", kcmpT[0:64, g, 0:127], bk[0:64, 0:127], [bk], [(kcmpT, g)])
                        else:
                            for hc in range(2):
                                kb.mm(bk[0:127, 0:64], hidT[:, hc, 0:127], w2[:, hc, :], hc == 0, hc == 1, [w2, hidT], [bk])
                            kb.cp("dve", vcmp[0:127, g, :], bk[0:127, 0:64], [bk], [(vcmp, g)])
                S.barrier()
                ck("cmpkv")

            with ExitStack() as pg:
                QAg = kb.sb("QAg", [105, 8, S_LEN], BF16, pg)
                KAs = kb.sb("KAs", [105, S_LEN], BF16, pg)
                KAw = kb.sb("KAw", [105, S_LEN], BF16, pg)
                VAs = kb.sb("VAs", [128, 16, 128], BF16, pg)
                VAw = kb.sb("VAw", [128, 16, 128], BF16, pg)
                ecmp = kb.sb("ecmp", [128, 8, 247], F32, pg)
                Wz = kb.sb("Wz", [128, 8, 512], BF16, pg)
                m12 = kb.sb("m12", [128, 2, 62], F32, pg)
                trib = kb.sb("trib", [128, 2, 512], BF16, pg)
                kb.dma("sp", m12[:], m12_d, [], [m12])
                kb.dma("pool", trib[:], tri_d, [], [trib])
                for v, KA in enumerate((KAs, KAw)):
                    kb.dma("pool", KA[64:105, :], kaug_d[v], [], [(KA, "aug")])
                kb.memset("pool", VAs[:, :, 64:128], 1.0, [(VAs, "ones")])
                kb.memset("pool", VAw[:, :, 64:128], 1.0, [(VAw, "ones")])
                Pc = [kb.sb("Pc0", [128, 4, 128], F32, pg)] * 2
                Pu = [kb.sb(f"Pu{i}", [128, 4, 128], F32, pg) for i in range(2)]
                Pub = [kb.sb("Pub0", [128, 4, 128], BF16, pg)] * 2
                pT = [kb.sb("pT0", [128, 4, 128], BF16, pg)] * 2
                for i in range(2):
                    kb.memset("pool", Pu[i][:], 0.0, [Pu[i]])
                psg = kb.sb("psg", [128, 128], F32, pg)
                den8 = kb.sb("den8", [128, 8], F32, pg)
                cg8 = kb.sb("cg8", [128, 8], F32, pg)
                imp = kb.sb("imp", [128, 32], F32, pg)
                impm = kb.sb("impm", [128, 32], F32, pg)
                m8 = kb.sb("m8", [128, 8], F32, pg)
                negp = kb.sb("negp", [128, 96], F32, pg)
                negS = kb.sb("negS", [96, 128], BF16, pg)
                kb.memset("pool", negp[:], 0.0, [negp])
                PTs = [kb.sb(f"PTs{i}", [128, 512], BF16, pg) for i in range(8)]
                pts_rr = [0]
                zerob = kb.sb("zerob", [128, 512], BF16, pg)
                kb.memset("pool", zerob[:], 0.0, [zerob])
                tmpo = [kb.sb(f"tmpo{i}", [128, 4, 64], F32, pg) for i in range(2)]
                rd4 = [kb.sb(f"rd4{i}", [128, 4], F32, pg) for i in range(2)]
                cg4 = [kb.sb(f"cg4{i}", [128, 4], F32, pg) for i in range(2)]
                Oa = [kb.sb(f"Oa{i}", [128, 512], F32, pg) for i in range(2)]
                szt = kb.sb("szt", [128, 512], F32, pg)
                Ob = kb.sb("Ob", [128, 512], BF16, pg)
                accbanks = [kb.banks[0], kb.banks[1]]
                rot = [2]

                def rbank():
                    b = kb.banks[rot[0]]
                    rot[0] = rot[0] + 1 if rot[0] < 7 else 2
                    return b

                strot = [0, 0]

                def stbank(lane):
                    b = kb.banks[2 + 2 * lane + strot[lane]]
                    strot[lane] ^= 1
                    return b

                def mbank():
                    return kb.banks[6]

                ptrot = [0, 0]

                def next_pt(lane):
                    p = PTs[4 * lane + ptrot[lane]]
                    ptrot[lane] = (ptrot[lane] + 1) % 4
                    return p

                for g in range(2):
                    kb.dma("sp", ecmp[:], ecmp_d[:, g * 8:(g + 1) * 8, :], [], [ecmp])
                    for j in range(4):
                        load_w(Wz, Wz[:, :, j * 128:(j + 1) * 128], win_cols(nsa_w_in, 1840 + g * 512 + j * 128, 128), 128,
                               gain=None, key=j)
                    for pr in range(4):
                        wq = load_win((g * 8 + 2 * pr) * 64, 128)
                        for tc in range(4):
                            bk = rbank()
                            for dc in range(8):
                                kb.mm(bk[:, :], wq[:, dc, 0:128], xnT[:, dc, tc * 512:(tc + 1) * 512], dc == 0, dc == 7, [wq, xnT], [bk])
                            kb.cp("act", QAg[0:64, 2 * pr, tc * 512:(tc + 1) * 512], bk[0:64, :], [bk], [QAg])
                            stq = PTs[4 * (pr % 2) + tc]
                            kb.cp("dve", stq[64:128, :], bk[64:128, :], [bk], [stq])
                            kb.dma("sp", QAg[0:64, 2 * pr + 1, tc * 512:(tc + 1) * 512], stq[64:128, :], [stq], [QAg])
                    kb.dma("pool", QAg[96:105, :, :], qal_d[g * 8:(g + 1) * 8].rearrange("h r n -> r h n"), [], [QAg])
                    for KA, col in ((KAs, 1024 + 2 * 128 + g * 64), (KAw, 1024 + 4 * 128 + g * 64)):
                        wk = load_win(col, 64)
                        for tc in range(4):
                            bk = rbank()
                            for dc in range(8):
                                kb.mm(bk[0:64, :], wk[:, dc, 0:64], xnT[:, dc, tc * 512:(tc + 1) * 512], dc == 0, dc == 7, [wk, xnT], [bk])
                            kb.cp("act", KA[0:64, tc * 512:(tc + 1) * 512], bk[0:64, :], [bk], [(KA, tc)])
                    wv = load_win(1024 + 3 * 128 + g * 64, 64)
                    load_win(1024 + 5 * 128 + g * 64, 64, into=wv, off=64)
                    for t in range(NT_):
                        bk = rbank()
                        for dc in range(8):
                            kb.mm(bk[:, 0:128], xnT[:, dc, t * 128:(t + 1) * 128], wv[:, dc, 0:128], dc == 0, dc == 7, [xnT, wv], [bk])
                        kb.cp("act", VAs[:, t, 0:64], bk[:, 0:64], [bk], [(VAs, t)])
                        kb.cp("dve", VAw[:, t, 0:64], bk[:, 64:128], [bk], [(VAw, t)])

                    def task_C(qt):
                        qc = slice(qt * 128, (qt + 1) * 128)
                        O = Oa[qt % 2]
                        gq = gates[:, qt, :]
                        eoff = 120 - 8 * qt
                        ocb = kb.banks[7]
                        for b4 in range(2):
                            sbk = mbank()
                            for hl in range(4):
                                r = b4 * 4 + hl
                                kb.mm(sbk[:, hl * 128:hl * 128 + 127], QAg[0:64, r, qc], kcmpT[0:64, g, 0:127], True, True,
                                      [(QAg, qt), kcmpT], [sbk])
                            yield
                            pc = Pc[b4]
                            pu = Pu[b4]
                            s3 = sbk[:].rearrange("p (h n) -> p h n", h=4)
                            kb.act(pc[:, :, 0:127], s3[:, :, 0:127], AF.Exp, [sbk], [pc], scale=0.125)
                            yield
                            kb.tt("dve", pu[:, :, 0:127], pc[:, :, 0:127], ecmp[:, b4 * 4:(b4 + 1) * 4, eoff:eoff + 127], ALU.mult,
                                  [pc, ecmp], [pu])
                            S.op("dve", lambda e, pu=pu, b4=b4: e.tensor_reduce(out=den8[:, b4 * 4:(b4 + 1) * 4], in_=pu[:, :, 0:127],
                                                                                axis=AX.X, op=ALU.add),
                                 reads=[pu], writes=[(den8, b4)])
                            yield
                            kb.ts("dve", den8[:, b4 * 4:(b4 + 1) * 4], den8[:, b4 * 4:(b4 + 1) * 4], 1e-30, None, ALU.max, None,
                                  [(den8, b4)], [(den8, b4)])
                            kb.recip(den8[:, b4 * 4:(b4 + 1) * 4], den8[:, b4 * 4:(b4 + 1) * 4], [(den8, b4)], [(den8, b4)])
                            pub = Pub[b4]
                            kb.cp("pool", pub[:], pu[:], [pu], [pub])
                            yield
                            for hl in range(4):
                                r = b4 * 4 + hl
                                if r == 0:
                                    kb.ts("dve", psg[:, :], pu[:, hl, :], den8[:, r:r + 1], None, ALU.mult, None, [pu, (den8, b4)], [psg])
                                else:
                                    kb.stt(psg[:, :], pu[:, hl, :], den8[:, r:r + 1], psg[:, :], ALU.mult, ALU.add,
                                           [pu, (den8, b4), psg], [psg])
                                if hl % 2 == 1:
                                    yield
                            tbk = mbank()
                            tv = tbk[:].bitcast(BF16)
                            for hl in range(4):
                                kb.tr(tv[0:127, hl * 128:(hl + 1) * 128], pub[:, hl, 0:127], identb[:], [pub, identb], [tbk])
                            yield
                            ptt = pT[b4]
                            kb.cp("act", ptt[0:127, :, :], tv[0:127, 0:512].rearrange("p (h n) -> p h n", h=4), [tbk], [ptt])
                            yield
                            for hl in range(4):
                                r = b4 * 4 + hl
                                kb.mm(ocb[:, r * 64:(r + 1) * 64], ptt[0:127, hl, :], vcmp[0:127, g, :], True, True, [ptt, vcmp], [ocb])
                            yield
                        kb.tt("dve", cg8[:], den8[:], gq[:, g * 24:g * 24 + 24:3], ALU.mult, [den8, gates], [cg8])
                        kb.tt("dve", O[:].rearrange("p (h d) -> p h d", h=8), ocb[:].rearrange("p (h d) -> p h d", h=8),
                              cg8[:, 0:8].unsqueeze(2).to_broadcast([128, 8, 64]), ALU.mult, [ocb, cg8], [O])
                        yield
                        S.op("dve", lambda e: e.tensor_reduce(out=imp[:, :], in_=psg[:].rearrange("p (j a) -> p j a", a=4),
                                                              axis=AX.X, op=ALU.add), reads=[psg], writes=[imp])
                        kb.tt("dve", imp[:, 1:32], imp[:, 1:32], psg[:, 3:127:4], ALU.add, [imp, psg], [imp])
                        yield
                        moff = 30 - 2 * qt
                        kb.tt("dve", impm[:], imp[:], m12[:, 0, moff:moff + 32], ALU.mult, [imp, m12], [impm])
                        kb.tt("dve", impm[:], impm[:], m12[:, 1, moff:moff + 32], ALU.add, [impm, m12], [impm])
                        kb.memset("dve", impm[:, 0:1], 1e6, [impm])
                        yield
                        S.op("dve", lambda e: e.max(out=m8[:], in_=impm[:]), reads=[impm], writes=[m8])
                        kb.ts("dve", negp[:, 64:96], impm[:], m8[:, 7:8], 1.0, ALU.is_ge, ALU.subtract, [impm, m8], [negp])
                        kb.ts("dve", negp[:, 64:96], negp[:, 64:96], NEGB, None, ALU.mult, None, [negp], [negp])
                        yield
                        tbk = mbank()
                        kb.tr(tbk[0:96, 0:128], negp[:, 0:96], identf[:], [negp, identf], [tbk])
                        yield
                        kb.cp("dve", negS[64:96, :], tbk[64:96, 0:128], [tbk], [negS])
                        kb.cp("dve", QAg[64:96, :, qc], negS[64:96, :].unsqueeze(1).to_broadcast([32, 8, 128]), [negS], [(QAg, qt)])
                        yield

                    def task_branch(qt, br, b4):
                        qc = slice(qt * 128, (qt + 1) * 128)
                        O = Oa[qt % 2]
                        gq = gates[:, qt, :]
                        KA, VA = (KAw, VAw) if br == 2 else (KAs, VAs)
                        kbs = list(range(max(0, qt - 4), qt + 1)) if br == 2 else list(range(0, qt + 1))
                        accb = kb.banks[b4]
                        pend = []
                        nk = len(kbs)
                        kb.mm(accb[:, 0:260], zerob[:, 0:128], zerob[:, 0:260], True, False, [zerob], [accb])

                        def do_pv(item):
                            pi, pk, ppt = item
                            for hl in range(4):
                                kb.mm(accb[:, hl * 65:(hl + 1) * 65], ppt[:, hl * 128:(hl + 1) * 128], VA[:, pk, 0:65], False,
                                      (pi == nk - 1) and hl == 3, [VA, ppt], [accb])

                        for idx, kbi in enumerate(kbs):
                            sbk = stbank(b4)
                            masks = []
                            if kbi == qt:
                                masks.append(0)
                            if br == 2 and kbi == qt - 4:
                                masks.append(1)
                            kb.mm(sbk[:, :], KA[0:105, kbi * 128:(kbi + 1) * 128], QAg[0:105, b4 * 4:(b4 + 1) * 4, qc],
                                  True, len(masks) == 0, [KA, (QAg, qt)], [sbk])
                            for mi, mv in enumerate(masks):
                                kb.mm(sbk[:, :], identb[:], trib[:, mv, :], False, mi == len(masks) - 1, [identb, trib], [sbk])
                            pt = next_pt(b4)
                            kb.act(pt[:], sbk[:], AF.Exp, [sbk], [pt], scale=0.125)
                            yield
                            pend.append((idx, kbi, pt))
                            if len(pend) > 2:
                                do_pv(pend.pop(0))
                                yield
                        while pend:
                            do_pv(pend.pop(0))
                            yield
                        a3 = accb[:, 0:260].rearrange("p (h n) -> p h n", h=4)
                        kb.recip(rd4[b4][:], a3[:, :, 64], [accb], [rd4[b4]])
                        h0 = (g * 8 + b4 * 4) * 3 + br
                        kb.tt("dve", cg4[b4][:], rd4[b4][:], gq[:, h0:h0 + 10:3], ALU.mult, [rd4[b4], gates], [cg4[b4]])
                        yield
                        kb.tt("dve", tmpo[b4][:], a3[:, :, 0:64], cg4[b4][:, 0:4].unsqueeze(2).to_broadcast([128, 4, 64]), ALU.mult,
                              [accb, cg4[b4]], [tmpo[b4]])
                        yield
                        ov = O[:, b4 * 256:(b4 + 1) * 256].rearrange("p (h d) -> p h d", h=4)
                        kb.tt("dve", ov, ov, tmpo[b4][:], ALU.add, [(O, b4), tmpo[b4]], [(O, b4)])
                        yield

                    def task_Z(qt):
                        qc = slice(qt * 128, (qt + 1) * 128)
                        O = Oa[qt % 2]
                        zb = mbank()
                        for dc in range(8):
                            kb.mm(zb[:, :], xnT[:, dc, qc], Wz[:, dc, :], dc == 0, dc == 7, [xnT, Wz], [zb])
                            if dc % 4 == 3:
                                yield
                        kb.act(szt[:], zb[:], AF.Silu, [zb], [szt])
                        yield
                        kb.tt("dve", Ob[:], O[:], szt[:], ALU.mult, [O, szt], [Ob])
                        yield
                        tbk = mbank()
                        tv = tbk[:].bitcast(BF16)
                        for c4 in range(4):
                            kb.tr(tv[:, c4 * 128:(c4 + 1) * 128], Ob[:, c4 * 128:(c4 + 1) * 128], identb[:], [Ob, identb], [tbk])
                        yield
                        kb.cp("act", yT1[:, g * 4:(g + 1) * 4, qc], tv[:, 0:512].rearrange("p (c n) -> p c n", c=4), [tbk], [(yT1, (g, qt))])
                        yield

                    run_lanes([task_C(0)])
                    for qt in range(NT_):
                        l1_ = chain(task_branch(qt, 2, 0), task_branch(qt, 1, 0))
                        l2_ = chain(task_branch(qt, 2, 1), task_branch(qt, 1, 1))
                        third = []
                        if qt > 0:
                            third.append(task_Z(qt - 1))
                        if qt + 1 < NT_:
                            third.append(task_C(qt + 1))
                        run_lanes([l1_, l2_, chain(*third)], [1, 1, L3_STEPS])
                    run_lanes([task_Z(NT_ - 1)])
                S.barrier()
                ck("nsa")

            with ExitStack() as pe_:
                out_proj(nsa_w_out, 8, [(yT1, 8)], ymT1, x1_scr, DX1, True, pe_)
                S.barrier()
                ck("l1end")

        S.dead = False
        S.barrier()
        with nc.Block() as block:
            S.emit(block)
        print("program ops:", S.nops, "sems:", S.nsem)
    return nc


_CONST = None


def prep_inputs(inp):
    global _CONST
    if _CONST is None:
        _CONST = host_constants()
        _CONST.update(host_constants_nsa())
    f = lambda a: np.ascontiguousarray(np.asarray(a, dtype=np.float32))
    shared = {
        "hawk_w_in": f(inp["hawk_w_in"][0]),
        "hawk_w_out": f(inp["hawk_w_out"][0]),
        "hawk_w_mem_kv": f(inp["hawk_w_mem_kv"][0]),
        "g_hawk": expand_gain(f(inp["hawk_norm"][0])),
        "g_hawk_mem": expand_gain(f(inp["hawk_mem_norm"][0])),
        "bd_a": block_diag(f(inp["hawk_gate_a_w"][0])),
        "bd_x": block_diag(f(inp["hawk_gate_x_w"][0])),
        "final_norm": f(inp["final_norm"]),
        "hawk_norm_v": f(inp["hawk_norm"][0]),
        "nsa_norm_v": f(inp["nsa_norm"][0]),
        "nsa_w_in": f(inp["nsa_w_in"][0]),
        "nsa_w_out": f(inp["nsa_w_out"][0]),
        "nsa_w_mem_kv": f(inp["nsa_w_mem_kv"][0]),
        "g_nsa": expand_gain(f(inp["nsa_norm"][0])),
        "g_nsa_mem": expand_gain(f(inp["nsa_mem_norm"][0])),
        "w2k": f(inp["nsa_phi_k_w2"][0]),
        "w2v": f(inp["nsa_phi_v_w2"][0]),
    }
    for k in ("identf", "edil", "ecmp", "m12", "tri", "kaug", "qal"):
        shared[k] = _CONST[k]

    def w1_layout(w1):
        a = w1.reshape(32, 64, 256).transpose(1, 0, 2)
        return np.ascontiguousarray(np.concatenate([a, a], axis=0))
    shared["w1k"] = w1_layout(f(inp["nsa_phi_k_w1"][0]))
    shared["w1v"] = w1_layout(f(inp["nsa_phi_v_w1"][0]))
    shared["peT"] = np.ascontiguousarray(np.stack([f(inp["nsa_pe_k"][0]).T, f(inp["nsa_pe_v"][0]).T], axis=1))
    lv = np.zeros((128, 8, 8), np.float32)
    cw = f(inp["hawk_conv_w"][0])
    for k in range(4):
        lv[:, :, k] = vec_fm(cw[k])
    lv[:, :, 4] = vec_fm(f(inp["hawk_conv_b"][0]))
    lv[:, :, 5] = vec_fm(f(inp["hawk_gate_a_b"][0]).reshape(-1))
    lv[:, :, 6] = vec_fm(f(inp["hawk_gate_x_b"][0]).reshape(-1))
    lv[:, :, 7] = vec_fm(f(inp["hawk_lambda"][0]))
    shared["lru_vec"] = lv
    x = f(inp["x"])
    mem = f(inp["mem"])
    maps = []
    for b in range(x.shape[0]):
        m = dict(shared)
        m["x"] = x[b]
        m["mem"] = mem[b]
        maps.append(m)
    return maps


def kernel(**inputs):
    maps = prep_inputs(inputs)
    nc = build_program()
    res = run_bass_kernel_spmd(nc, maps, core_ids=list(range(len(maps))))
    out = np.stack([np.asarray(r["out"], dtype=np.float32) for r in res.results], axis=0)
    return out
```

```python
import math
from contextlib import ExitStack

import numpy as np
import concourse.bass as bass
import concourse.mybir as mybir
from concourse.bass_utils import run_bass_kernel_spmd

F32 = mybir.dt.float32
BF16 = mybir.dt.bfloat16
AF = mybir.ActivationFunctionType
ALU = mybir.AluOpType
AX = mybir.AxisListType

S_LEN = 2048
D = 1024
NT_ = 16
EPS = 1e-6
DIL_GROUPS = ((128, 1), (512, 4), (2048, 16))

SEM_LIMIT = 30000
N_DMA_SEMS = 24
SAME_ENGINE_SYNC = True


class Buf:
    def __init__(self, name, t, excl=False):
        self.name = name
        self.t = t
        self.excl = excl
        self.st = {}

    def __getitem__(self, idx):
        return self.t[idx]


class Sync:
    def __init__(self, nc, stack):
        self.nc = nc
        self.stack = stack
        self.engs = ["pe", "act", "dve", "pool", "sp"]
        self.ops = {e: [] for e in self.engs}
        self.cur_sem = {}
        self.cnt = {}
        self.nsem = 0
        for e in self.engs:
            self._new_sem(e)
        self.dma_sems = {}
        self.dma_val = {}
        self.dma_rr = {}
        for e in ["sp", "pool", "act"]:
            self.dma_sems[e] = [self._alloc_sem(f"d{e}{i}") for i in range(N_DMA_SEMS)]
            self.dma_val[e] = [0] * N_DMA_SEMS
            self.dma_rr[e] = 0
        self.seen = {e: {} for e in self.engs}
        self.all_ticks = {}
        self.nops = 0
        self.dead = False
        self.eng_free = {e: 0.0 for e in self.engs}
        self.lane = None
        self.tnow = 0.0

    def _alloc_sem(self, name):
        self.nsem += 1
        return self.stack.enter_context(self.nc.semaphore(f"s_{name}_{self.nsem}"))

    def _new_sem(self, e):
        self.cur_sem[e] = self._alloc_sem(e)
        self.cnt[e] = 0

    def _states(self, buf, key, create):
        if key is None:
            if create and None not in buf.st:
                buf.st[None] = [None, {}]
            return list(buf.st.values())
        out = []
        if None in buf.st:
            out.append(buf.st[None])
        if key not in buf.st and create:
            buf.st[key] = [None, {}]
        if key in buf.st:
            out.append(buf.st[key])
        return out

    @staticmethod
    def _norm(lst):
        out = []
        for r in lst or []:
            out.append(r if isinstance(r, tuple) else (r, None))
        return out

    def op(self, eng, fn, reads=None, writes=None, dma=False, cost=0.5):
        if self.dead:
            return None
        reads = self._norm(reads)
        writes = self._norm(writes)
        ex = [(b, None) for (b, k) in reads + writes if b.excl]
        if ex:
            reads = [(b, k) for (b, k) in reads if not b.excl]
            writes = [(b, k) for (b, k) in writes if not b.excl]
            for bk in ex:
                if bk not in writes:
                    writes.append(bk)
        need = []
        for buf, key in reads:
            for st in self._states(buf, key, False):
                if st[0] is not None:
                    need.append(st[0])
        for buf, key in writes:
            for st in self._states(buf, key, False):
                if st[0] is not None:
                    need.append(st[0])
                need.extend(st[1].values())
        if dma:
            i = self.dma_rr[eng]
            self.dma_rr[eng] = (i + 1) % N_DMA_SEMS
            sem = self.dma_sems[eng][i]
            prev = self.dma_val[eng][i]
            if prev > 0:
                need.append((sem, prev, "dma", 0.0))
            if prev + 16 > SEM_LIMIT:
                sem = self._alloc_sem(f"d{eng}{i}")
                self.dma_sems[eng][i] = sem
                prev = 0
            val = prev + 16
            self.dma_val[eng][i] = val
            inc = 16
            tick = [sem, val, "dma", 0.0]
        else:
            if self.cnt[eng] + 1 > SEM_LIMIT:
                self._new_sem(eng)
            self.cnt[eng] += 1
            sem = self.cur_sem[eng]
            val = self.cnt[eng]
            inc = 1
            tick = [sem, val, eng, 0.0]
        ready = 0.0
        for nd in need:
            if nd[3] > ready:
                ready = nd[3]
        start = max(self.eng_free[eng], ready + 0.06)
        if dma:
            self.eng_free[eng] = start + 0.06
        else:
            self.eng_free[eng] = start + cost
        tick[3] = start + cost
        tick = tuple(tick)
        if self.lane is not None and tick[3] > self.lane.clock:
            self.lane.clock = tick[3]
        if tick[3] > self.tnow:
            self.tnow = tick[3]
        waits = {}
        seen = self.seen[eng]
        for (s, v, src, _fin) in need:
            if src == eng and (eng == "pe" or not SAME_ENGINE_SYNC):
                continue
            sid = id(s)
            if seen.get(sid, 0) >= v:
                continue
            if sid not in waits or waits[sid][1] < v:
                waits[sid] = (s, v)
        for sid, (s, v) in waits.items():
            seen[sid] = v
        self.ops[eng].append((list(waits.values()), fn, sem, inc))
        self.all_ticks[id(sem)] = (sem, val)
        self.nops += 1
        wset = set((id(b), k) for b, k in writes)
        for buf, key in reads:
            if (id(buf), key) in wset:
                continue
            self._states(buf, key, True)
            buf.st[key][1][eng if not dma else ("dma", id(sem))] = tick
        for buf, key in writes:
            if key is None:
                buf.st = {None: [tick, {}]}
            else:
                buf.st[key] = [tick, {}]
        return tick

    def barrier(self):
        if self.dead:
            return
        ticks = list(self.all_ticks.values())
        for e in self.engs:
            wl = []
            for (s, v) in ticks:
                if self.seen[e].get(id(s), 0) < v:
                    wl.append((s, v))
                    self.seen[e][id(s)] = v
            if wl:
                self.ops[e].append((wl, None, None, 0))

    def emit(self, block):
        S = self

        def run(engname, e):
            for (wl, fn, sem, inc) in S.ops[engname]:
                for (s, v) in wl:
                    e.wait_ge(s, v)
                if fn is not None:
                    fn(e).then_inc(sem, inc)

        @block.sync
        def _(e):
            run("sp", e)

        @block.tensor
        def _(e):
            run("pe", e)

        @block.scalar
        def _(e):
            run("act", e)

        @block.vector
        def _(e):
            run("dve", e)

        @block.gpsimd
        def _(e):
            run("pool", e)


class BankView:
    def __init__(self, pair, half):
        self.pair = pair
        self.off = 512 * half

    def __getitem__(self, idx):
        if not isinstance(idx, tuple):
            idx = (idx, slice(None))
        pr, col = idx
        cs = (col.start or 0) + self.off
        ce = (col.stop if col.stop is not None else 512) + self.off
        return self.pair[pr, cs:ce:col.step] if col.step else self.pair[pr, cs:ce]


class KB:
    def __init__(self, nc, stack):
        self.nc = nc
        self.gst = stack
        self.S = Sync(nc, stack)
        self.pairs = [stack.enter_context(nc.psum_tensor(f"pair{i}", [128, 1024], F32)) for i in range(4)]
        self.banks = [Buf(f"bank{i}", BankView(self.pairs[i // 2], i % 2), excl=True) for i in range(8)]
        self.bank_rr = 0
        self.uid = 0

    def sb(self, name, shape, dt, stack=None):
        self.uid += 1
        t = (stack or self.gst).enter_context(self.nc.sbuf_tensor(f"{name}_{self.uid}", shape, dt))
        return Buf(name, t)

    def bank(self):
        b = self.banks[self.bank_rr]
        self.bank_rr = (self.bank_rr + 1) % 8
        return b

    @staticmethod
    def fsz(ap):
        n = 1
        for s in ap.shape[1:]:
            n *= int(s)
        return n

    def vcost(self, eng, ap):
        n = self.fsz(ap)
        if eng == "pool":
            return 0.3 + n / 480.0
        if eng == "act":
            return 0.22 + n / 1400.0
        return 0.08 + n / 960.0

    def mm(self, out, lhsT, rhs, start, stop, r, w):
        c = max(self.fsz(rhs), 64) / 1600.0 + 0.04
        self.S.op("pe", lambda e: e.matmul(out, lhsT=lhsT, rhs=rhs, start=start, stop=stop), reads=r, writes=w, cost=c)

    def tr(self, out, in_, ident, r, w):
        self.S.op("pe", lambda e: e.transpose(out, in_, ident), reads=r, writes=w, cost=0.11)

    def act(self, out, in_, func, r, w, **kw):
        self.S.op("act", lambda e: e.activation(out=out, in_=in_, func=func, **kw), reads=r, writes=w, cost=self.vcost("act", out))

    def tt(self, eng, out, in0, in1, op, r, w):
        self.S.op(eng, lambda e: e.tensor_tensor(out=out, in0=in0, in1=in1, op=op), reads=r, writes=w, cost=self.vcost(eng, out))

    def ts(self, eng, out, in0, s1, s2, op0, op1, r, w, **kw):
        c = self.vcost(eng, out)
        if op1 is None:
            self.S.op(eng, lambda e: e.tensor_scalar(out=out, in0=in0, scalar1=s1, scalar2=None, op0=op0, **kw), reads=r, writes=w, cost=c)
        else:
            self.S.op(eng, lambda e: e.tensor_scalar(out=out, in0=in0, scalar1=s1, scalar2=s2, op0=op0, op1=op1, **kw), reads=r, writes=w, cost=c)

    def stt(self, out, in0, scalar, in1, op0, op1, r, w, **kw):
        self.S.op("dve", lambda e: e.scalar_tensor_tensor(out=out, in0=in0, scalar=scalar, in1=in1, op0=op0, op1=op1, **kw), reads=r, writes=w,
                  cost=0.12 + self.fsz(out) / 960.0)

    def cp(self, eng, out, in_, r, w):
        c = self.vcost(eng, out)
        if eng == "act":
            self.S.op("act", lambda e: e.activation(out=out, in_=in_, func=AF.Copy), reads=r, writes=w, cost=c)
        else:
            self.S.op(eng, lambda e: e.tensor_copy(out=out, in_=in_), reads=r, writes=w, cost=c)

    def memset(self, eng, ap, val, w):
        self.S.op(eng, lambda e: e.memset(ap, val), writes=w, cost=self.vcost(eng, ap))

    def recip(self, out, in_, r, w):
        self.S.op("dve", lambda e: e.reciprocal(out=out, in_=in_), reads=r, writes=w, cost=self.vcost("dve", out))

    def dma(self, q, out, in_, r, w):
        nbytes = self.fsz(out) * int(out.shape[0]) * 4
        self.S.op(q, lambda e: e.dma_start(out=out, in_=in_), reads=r, writes=w, dma=True, cost=2.0 + nbytes / 150000.0)


def alibi_slopes(n):
    return np.exp2(-8.0 * np.arange(1, n + 1) / n).astype(np.float32)


def host_constants():
    c = {}
    c["identf"] = np.eye(128, dtype=np.float32)
    sl = alibi_slopes(12)
    ik = np.arange(128)[:, None].astype(np.float64)
    iq = np.arange(128)[None, :].astype(np.float64)
    E = np.zeros((128, 12, 256), np.float32)
    for g, (win, dil) in enumerate(DIL_GROUPS):
        for hs in range(4):
            hh = g * 4 + hs
            s = float(sl[hh]) * dil
            dist_prev = 128 + iq - ik
            ok_prev = (dist_prev <= 128)
            E[:, hh, 0:128] = np.where(ok_prev, np.exp(-s * dist_prev), 0.0)
            dist_cur = iq - ik
            ok_cur = dist_cur >= 0
            E[:, hh, 128:256] = np.where(ok_cur, np.exp(-s * dist_cur), 0.0)
    c["edil"] = E
    return c


def expand_gain(g):
    return np.ascontiguousarray(np.broadcast_to(g.reshape(8, 128).T[:, :, None], (128, 8, 128))).astype(np.float32)


def vec_fm(v):
    return np.ascontiguousarray(v.reshape(8, 128).T).astype(np.float32)


def block_diag(gw):
    out = np.zeros((128, 8, 128), np.float32)
    for c in range(8):
        out[0:64, c, 0:64] = gw[2 * c]
        out[64:128, c, 64:128] = gw[2 * c + 1]
    return out


NEGB = 8192.0


def _bf16_split3(a):
    import ml_dtypes
    a = a.astype(np.float32)
    hi = a.astype(ml_dtypes.bfloat16).astype(np.float32)
    r1 = (a - hi).astype(np.float32)
    mid = r1.astype(ml_dtypes.bfloat16).astype(np.float32)
    r2 = (r1 - mid).astype(np.float32)
    lo = r2.astype(ml_dtypes.bfloat16).astype(np.float32)
    return hi, mid, lo


def host_constants_nsa():
    c = {}
    sl = alibi_slopes(16)
    i = np.arange(128)[:, None].astype(np.float64)
    m = np.arange(247)[None, :].astype(np.float64)
    dist = i - 16.0 * (m - 120.0) - 31.0
    E = np.zeros((128, 16, 247), np.float32)
    for h in range(16):
        E[:, h, :] = np.where(dist >= 0, np.exp(-float(sl[h]) * np.maximum(dist, 0.0)), 0.0)
    c["ecmp"] = E
    ii = np.arange(128)[:, None]
    rel = np.arange(62)[None, :] - 30
    cur = (ii >= 64).astype(np.int64)
    forced = (rel == cur) | (rel == cur - 1)
    future = rel > cur
    m1 = np.where(forced | future, 0.0, 1.0).astype(np.float32)
    m2 = np.where(forced, 1e6, np.where(future, -1e6, 0.0)).astype(np.float32)
    c["m12"] = np.ascontiguousarray(np.stack([m1, m2], axis=1))
    ik = np.arange(128)[:, None]
    iq = np.arange(128)[None, :]
    diag = np.where(ik > iq, -NEGB, 0.0).astype(np.float32)
    far = np.where(ik <= iq, -NEGB, 0.0).astype(np.float32)
    c["tri"] = np.ascontiguousarray(np.stack([np.tile(diag, (1, 4)), np.tile(far, (1, 4))], axis=1))
    k = np.arange(2048)
    kp = k - 1024
    hi = (np.floor(kp / 128.0) * 128.0).astype(np.float32)
    lo = (kp - hi).astype(np.float32)
    ka = np.zeros((2, 41, 2048), np.float32)
    for j in range(32):
        ka[0, j, :] = (k // 64 == j).astype(np.float32)
    for v in range(2):
        ka[v, 32:35, :] = 1.0
        ka[v, 35:38, :] = lo[None, :]
        ka[v, 38:41, :] = hi[None, :]
    c["kaug"] = ka
    qa = np.zeros((16, 9, 2048), np.float32)
    qp = (np.arange(2048) - 1024).astype(np.float32)
    for h in range(16):
        s8 = np.float32(8.0) * np.float32(sl[h])
        a = (-s8 * qp).astype(np.float32)
        ah, am, al = _bf16_split3(a)
        sh, sm, sl_ = _bf16_split3(np.full((2048,), s8, np.float32))
        qa[h, 0], qa[h, 1], qa[h, 2] = ah, am, al
        qa[h, 3], qa[h, 4], qa[h, 5] = sh, sm, sl_
        qa[h, 6], qa[h, 7], qa[h, 8] = sh, sm, sl_
    c["qal"] = qa
    return c


def chain(*gens):
    for g_ in gens:
        yield from g_


L3_STEPS = 1


def run_lanes(lanes, weights=None):
    active = list(lanes)
    w = {id(l): 1 for l in active}
    if weights:
        for l, wt in zip(lanes, weights):
            w[id(l)] = wt
    while active:
        for l in list(active):
            for _ in range(w[id(l)]):
                try:
                    next(l)
                except StopIteration:
                    active.remove(l)
                    break


def build_program(stop_after=None):
    nc = bass.Bass("TRN2", target_bir_lowering=False)

    ckstate = {}

    def ck(name):
        if stop_after == name:
            ckstate["S"].dead = True

    def din(name, shape):
        return nc.dram_tensor(name, list(shape), F32, kind="ExternalInput").ap()

    x_d = din("x", [S_LEN, D])
    mem_d = din("mem", [256, D])
    hawk_w_in = din("hawk_w_in", [D, 7680])
    hawk_w_out = din("hawk_w_out", [1792, D])
    hawk_w_mem_kv = din("hawk_w_mem_kv", [D, 512])
    g_hawk = din("g_hawk", [128, 8, 128])
    g_hawk_mem = din("g_hawk_mem", [128, 8, 128])
    lru_vec = din("lru_vec", [128, 8, 8])
    bd_a = din("bd_a", [128, 8, 128])
    bd_x = din("bd_x", [128, 8, 128])
    identf_d = din("identf", [128, 128])
    edil_d = din("edil", [128, 12, 256])
    final_g = din("final_norm", [D])
    hawk_norm_v = din("hawk_norm_v", [D])
    nsa_norm_v = din("nsa_norm_v", [D])
    nsa_w_in = din("nsa_w_in", [D, 3376])
    nsa_w_out = din("nsa_w_out", [1280, D])
    nsa_w_mem_kv = din("nsa_w_mem_kv", [D, 512])
    g_nsa = din("g_nsa", [128, 8, 128])
    g_nsa_mem = din("g_nsa_mem", [128, 8, 128])
    w1k_d = din("w1k", [128, 32, 256])
    w1v_d = din("w1v", [128, 32, 256])
    w2k_d = din("w2k", [256, 64])
    w2v_d = din("w2v", [256, 64])
    peT_d = din("peT", [64, 2, 32])
    ecmp_d = din("ecmp", [128, 16, 247])
    m12_d = din("m12", [128, 2, 62])
    tri_d = din("tri", [128, 2, 512])
    kaug_d = din("kaug", [2, 41, 2048])
    qal_d = din("qal", [16, 9, 2048])
    out_d = nc.dram_tensor("out", [S_LEN, D], F32, kind="ExternalOutput").ap()
    x1_scr = nc.dram_tensor("x1_scr", [S_LEN, D], F32, kind="Internal").ap()

    with ExitStack() as gst:
        kb = KB(nc, gst)
        S = kb.S
        ckstate["S"] = S
        DX = Buf("x_dram", None)
        DX1 = Buf("x1_dram", None)
        DOUT = Buf("out_dram", None)

        xnT = kb.sb("xnT", [128, 8, S_LEN], BF16)
        memnT = kb.sb("memnT", [128, 8, 256], BF16)
        identf = kb.sb("identf", [128, 128], F32)
        identb = kb.sb("identb", [128, 128], BF16)
        onesb = kb.sb("onesb", [128, 128], BF16)
        wstage = [kb.sb(f"wstage{i}", [128, 1024], F32) for i in range(3)]
        ws_rr = [0]
        stat = kb.sb("stat", [128, 64], F32)
        stat_rr = [0]

        kb.dma("sp", identf[:], identf_d, [], [identf])
        kb.cp("dve", identb[:], identf[:], [identf], [identb])
        kb.memset("dve", onesb[:], 1.0, [onesb])

        def next_ws():
            b = wstage[ws_rr[0]]
            ws_rr[0] = (ws_rr[0] + 1) % len(wstage)
            return b

        def load_w(dst, dst_ap3, src_ap3, n, gain=None, key=None, q="sp", part=128, eng="pool"):
            dcs = dst_ap3.shape[1]
            if gain is None:
                kb.dma("pool", dst_ap3, src_ap3, [], [(dst, key)])
                return
            assert dcs * n <= 1024
            stg = next_ws()
            sv = stg[0:part, 0:dcs * n].rearrange("p (c n) -> p c n", c=dcs)
            kb.dma(q, sv, src_ap3, [], [stg])
            if gain is not None:
                kb.tt(eng, dst_ap3, sv, gain[0:part, 0:dcs, 0:n], ALU.mult, [stg, gain], [(dst, key)])
            else:
                kb.cp(eng, dst_ap3, sv, [stg], [(dst, key)])

        def win_cols(w_dram, c0, n):
            return w_dram.rearrange("(dc p) n -> p dc n", p=128)[:, :, c0:c0 + n]

        class NormCtx:
            def __init__(self, stack, nbuf=2):
                self.xstage = [kb.sb(f"xstage{i}", [128, 1024], F32, stack) for i in range(nbuf)]
                self.xnb = [kb.sb(f"xnb{i}", [128, 1024], BF16, stack) for i in range(nbuf)]
                self.junk = kb.sb("junk", [128, 1024], BF16, stack)

        def tile_rstd(ncx, xbuf, xap):
            i = stat_rr[0]
            stat_rr[0] = (stat_rr[0] + 1) % 32
            ss = stat[:, 2 * i:2 * i + 1]
            rs = stat[:, 2 * i + 1:2 * i + 2]
            kb.stt(ncx.junk[:], xap, 1.0, xap, ALU.mult, ALU.mult, [xbuf], [ncx.junk, (stat, i)], accum_out=ss)
            kb.ts("dve", ss, ss, 1.0 / D, EPS, ALU.mult, ALU.add, [(stat, i)], [(stat, i)])
            kb.act(ss, ss, AF.Sqrt, [(stat, i)], [(stat, i)])
            kb.recip(rs, ss, [(stat, i)], [(stat, i)])
            return rs, i

        def norm_to_T(ncx, xbuf, xap, dstT, t, ntok_off, gB=None):
            rs, i = tile_rstd(ncx, xbuf, xap)
            nb = ncx.xnb[t % 2]
            if gB is None:
                kb.ts("dve", nb[:], xap, rs, None, ALU.mult, None, [xbuf, (stat, i)], [nb])
            else:
                kb.stt(nb[:], xap, rs, gB[:], ALU.mult, ALU.mult, [xbuf, (stat, i), gB], [nb])
            bk = kb.bank()
            bv = bk[:].bitcast(BF16)
            for c in range(8):
                kb.tr(bv[:, c * 128:(c + 1) * 128], nb[:, c * 128:(c + 1) * 128], identb[:], [nb, identb], [bk])
            kb.cp("act", dstT[:, :, ntok_off:ntok_off + 128], bv.rearrange("p (c n) -> p c n", c=8), [bk], [(dstT, t)])

        def norm_to_T_gen(ncx, xbuf, xap, dstT, t, ntok_off, bk, bi, gB=None):
            rs, i = tile_rstd(ncx, xbuf, xap)
            yield
            nb = ncx.xnb[bi]
            if gB is None:
                kb.ts("dve", nb[:], xap, rs, None, ALU.mult, None, [xbuf, (stat, i)], [nb])
            else:
                kb.stt(nb[:], xap, rs, gB[:], ALU.mult, ALU.mult, [xbuf, (stat, i), gB], [nb])
            yield
            bv = bk[:].bitcast(BF16)
            for c in range(8):
                kb.tr(bv[:, c * 128:(c + 1) * 128], nb[:, c * 128:(c + 1) * 128], identb[:], [nb, identb], [bk])
            yield
            kb.cp("act", dstT[:, :, ntok_off:ntok_off + 128], bv.rearrange("p (c n) -> p c n", c=8), [bk], [(dstT, t)])
            yield

        with ExitStack() as pa:
            ncxs = [NormCtx(pa, 2), NormCtx(pa, 2)]
            gBa = kb.sb("gBa", [128, 1024], F32, pa)
            kb.dma("sp", gBa[:], hawk_norm_v.partition_broadcast(128), [], [gBa])

            def a_lane(L):
                ncx = ncxs[L]
                for t in range(L, NT_ + 2, 2):
                    xs = ncx.xstage[(t // 2) % 2]
                    if t < NT_:
                        kb.dma("sp", xs[:], x_d[t * 128:(t + 1) * 128, :], [DX], [xs])
                        yield from norm_to_T_gen(ncx, xs, xs[:], xnT, t, t * 128, kb.banks[2 * L + (t // 2) % 2], (t // 2) % 2, gB=gBa)
                    else:
                        tm = t - NT_
                        kb.dma("sp", xs[:], mem_d[tm * 128:(tm + 1) * 128, :], [], [xs])
                        yield from norm_to_T_gen(ncx, xs, xs[:], memnT, tm, tm * 128, kb.banks[2 * L + (t // 2) % 2], (t // 2) % 2)

            run_lanes([a_lane(0), a_lane(1)])
            S.barrier()
            ck("A")

        def make_loader(w_in_d, gain, nslots, stack):
            wslots = [kb.sb(f"wslot{i}", [128, 8, 128], BF16, stack) for i in range(nslots)]
            rr = [0]
            pre = {}

            def raw(c0, n, q, into, off):
                if into is None:
                    wsl = wslots[rr[0]]
                    rr[0] = (rr[0] + 1) % nslots
                else:
                    wsl = into
                load_w(wsl, wsl[:, :, off:off + n], win_cols(w_in_d, c0, n), n, gain=None, q=q, key=off)
                return wsl

            def load_win(c0, n=128, q="sp", into=None, off=0):
                if into is None and (c0, n) in pre:
                    return pre.pop((c0, n))
                return raw(c0, n, q, into, off)

            def prefetch(c0, n=128):
                pre[(c0, n)] = raw(c0, n, "sp", None, 0)
            load_win.prefetch = prefetch
            return load_win

        def proj_fm(wsl, n, evac, woff=0):
            for tc in range(4):
                bk = kb.bank()
                for dc in range(8):
                    kb.mm(bk[0:n, :], wsl[:, dc, woff:woff + n], xnT[:, dc, tc * 512:(tc + 1) * 512], dc == 0, dc == 7,
                          [wsl, xnT], [bk])
                evac(bk, bk[0:n, :], tc)

        def mem_kv(w_kv_d, gain, kmT, vm, stack):
            wkv = kb.sb("wkv", [128, 8, 512], BF16, stack)
            for j in range(4):
                load_w(wkv, wkv[:, :, j * 128:(j + 1) * 128], win_cols(w_kv_d, j * 128, 128), 128, gain=gain, key=j)
            for h in range(4):
                bk = kb.bank()
                for dc in range(8):
                    kb.mm(bk[0:64, 0:256], wkv[:, dc, h * 64:(h + 1) * 64], memnT[:, dc, :], dc == 0, dc == 7,
                          [wkv, memnT], [bk])
                kb.cp("act", kmT[0:64, h, :], bk[0:64, 0:256], [bk], [(kmT, h)])
            for mt in range(2):
                bk = kb.bank()
                for dc in range(8):
                    kb.mm(bk[:, 0:256], memnT[:, dc, mt * 128:(mt + 1) * 128], wkv[:, dc, 256:512], dc == 0, dc == 7,
                          [wkv, memnT], [bk])
                kb.cp("act", vm[:, mt, :], bk[:, 0:256], [bk], [(vm, mt)])

        def mem_attn(load_win, colq, colz, kmT, vm, ymT, stack):
            qmTs = [kb.sb(f"qmT{i}", [64, S_LEN], BF16, stack) for i in range(2)]
            szms = [kb.sb(f"szm{i}", [64, S_LEN], BF16, stack) for i in range(2)]
            PTm = [kb.sb(f"PTm{i}", [128, 512], BF16, stack) for i in range(4)]
            rdm = [kb.sb(f"rdm{i}", [64, 512], F32, stack) for i in range(2)]

            def lane(h, L):
                qmT, szm = qmTs[L], szms[L]
                b0, b1, b2, b3 = [kb.banks[4 * L + j] for j in range(4)]
                wq = load_win(colq + h * 64, 64)
                wz = load_win(colz + h * 64, 64)
                for (wsl, dst, func) in ((wq, qmT, None), (wz, szm, AF.Silu)):
                    for tc in range(4):
                        bk = b0 if tc % 2 == 0 else b1
                        for dc in range(8):
                            kb.mm(bk[0:64, :], wsl[:, dc, 0:64], xnT[:, dc, tc * 512:(tc + 1) * 512], dc == 0, dc == 7, [wsl, xnT], [bk])
                        yield
                        if func is None:
                            kb.cp("act", dst[0:64, tc * 512:(tc + 1) * 512], bk[0:64, :], [bk], [(dst, tc)])
                        else:
                            kb.act(dst[0:64, tc * 512:(tc + 1) * 512], bk[0:64, :], func, [bk], [(dst, tc)])
                        yield
                for tc in range(4):
                    pts = []
                    for mt in range(2):
                        bk = b0 if mt == 0 else b1
                        kb.mm(bk[:, :], kmT[0:64, h, mt * 128:(mt + 1) * 128], qmT[0:64, tc * 512:(tc + 1) * 512],
                              True, True, [kmT, (qmT, tc)], [bk])
                        pt = PTm[2 * L + mt]
                        kb.act(pt[:], bk[:], AF.Exp, [bk], [pt], scale=0.125)
                        pts.append(pt)
                        yield
                    for mt in range(2):
                        kb.mm(b2[0:64, :], vm[:, mt, h * 64:(h + 1) * 64], pts[mt][:], mt == 0, mt == 1, [vm, pts[mt]], [b2])
                    for mt in range(2):
                        kb.mm(b3[0:64, :], onesb[:, 0:64], pts[mt][:], mt == 0, mt == 1, [onesb, pts[mt]], [b3])
                    yield
                    rd = rdm[L]
                    kb.recip(rd[:], b3[0:64, :], [b3], [rd])
                    kb.tt("dve", rd[:], b2[0:64, :], rd[:], ALU.mult, [b2, rd], [rd])
                    yield
                    kb.tt("dve", ymT[0:64, h, tc * 512:(tc + 1) * 512], rd[:], szm[0:64, tc * 512:(tc + 1) * 512], ALU.mult,
                          [rd, (szm, tc)], [(ymT, (h, tc))])
                    yield

            run_lanes([lane(0, 0), lane(1, 1)])
            run_lanes([lane(2, 0), lane(3, 1)])

        def out_proj(w_out_d, nch, yTl, ymT, resid_d, resid_buf, final, stack, dbg=False):
            ysrc = []
            for (yb_, n_) in yTl:
                for ci in range(n_):
                    ysrc.append((yb_, ci))
            ncxs = [NormCtx(stack, 1), NormCtx(stack, 1)]
            WO = kb.sb("WO", [128, nch, 1024], BF16, stack)
            WOm = kb.sb("WOm", [64, 4, 1024], BF16, stack)
            wo_v = w_out_d[0:nch * 128, :].rearrange("(c p) n -> p c n", p=128)
            for c in range(0, nch, 4):
                load_w(WO, WO[:, c:c + 4, :], wo_v[:, c:c + 4, :], 1024, key=c)
            wom_v = w_out_d[nch * 128:nch * 128 + 256, :].rearrange("(h p) n -> p h n", p=64)
            load_w(WOm, WOm[0:64, :, :], wom_v, 1024, key=0, part=64)
            x1t = [kb.sb(f"x1t{i}", [128, 1024], F32, stack) for i in range(2)]
            if not final:
                gNx = kb.sb("gNx", [128, 1024], F32, stack)
                kb.dma("sp", gNx[:], nsa_norm_v.partition_broadcast(128), [], [gNx])
            if final:
                gF = kb.sb("gF", [128, 1024], F32, stack)
                kb.dma("sp", gF[:], final_g.partition_broadcast(128), [], [gF])
                ot = [kb.sb(f"ot{i}", [128, 1024], F32, stack) for i in range(2)]

            def lane(L):
                ncx = ncxs[L]
                bks = [kb.banks[4 * L + j] for j in range(4)]
                for t in range(L, NT_, 2):
                    xs = ncx.xstage[0]
                    kb.dma("sp", xs[:], resid_d[t * 128:(t + 1) * 128, :], [resid_buf], [xs])
                    x1 = x1t[L]
                    for half in range(2):
                        bk = bks[half]
                        for c in range(nch):
                            yb_, ci = ysrc[c]
                            kb.mm(bk[:, :], yb_[:, ci, t * 128:(t + 1) * 128], WO[:, c, half * 512:(half + 1) * 512], c == 0, False,
                                  [yb_, WO], [bk])
                            if c % 4 == 3:
                                yield
                        for h in range(4):
                            kb.mm(bk[:, :], ymT[0:64, h, t * 128:(t + 1) * 128], WOm[0:64, h, half * 512:(half + 1) * 512], False, h == 3,
                                  [ymT, WOm], [bk])
                        yield
                        kb.tt("dve", x1[:, half * 512:(half + 1) * 512], xs[:, half * 512:(half + 1) * 512], bk[:], ALU.add,
                              [xs, bk], [(x1, half)])
                        yield
                    if not final:
                        kb.dma("sp", x1_scr[t * 128:(t + 1) * 128, :], x1[:], [x1], [DX1])
                        yield from norm_to_T_gen(ncx, x1, x1[:], xnT, t, t * 128, bks[2], 0, gB=gNx)
                        if dbg:
                            kb.dma("sp", out_d[t * 128:(t + 1) * 128, :], x1[:], [x1], [DOUT])
                    else:
                        rs, i = tile_rstd(ncx, x1, x1[:])
                        yield
                        o = ot[L]
                        kb.stt(o[:], x1[:], rs, gF[:], ALU.mult, ALU.mult, [x1, (stat, i), gF], [o])
                        yield
                        kb.dma("sp", out_d[t * 128:(t + 1) * 128, :], o[:], [o], [DOUT])
                        yield

            run_lanes([lane(0), lane(1)])

        with ExitStack() as l0:
            yTa = kb.sb("yTa", [128, 8, S_LEN], BF16, l0)
            ymT = kb.sb("ymT", [64, 4, S_LEN], BF16, l0)
            gH = kb.sb("gH", [128, 8, 128], F32, l0)
            kb.dma("sp", gH[:], g_hawk, [], [gH])
            load_win = make_loader(hawk_w_in, gH, 7, l0)
            kmT = kb.sb("kmT", [64, 4, 256], BF16, l0)
            vm = kb.sb("vm", [128, 2, 256], BF16, l0)
            with ExitStack() as pm:
                gHm = kb.sb("gHm", [128, 8, 128], F32, pm)
                kb.dma("sp", gHm[:], g_hawk_mem, [], [gHm])
                mem_kv(hawk_w_mem_kv, gHm, kmT, vm, pm)
                for c0_ in (7168, 7424, 7168 + 64, 7424 + 64):
                    load_win.prefetch(c0_, 64)
                S.barrier()
                ck("memkv0")
            with ExitStack() as pd:
                mem_attn(load_win, 7168, 7424, kmT, vm, ymT, pd)
                for c0_ in (0, 1024, 128, 1024 + 128):
                    load_win.prefetch(c0_, 128)
                S.barrier()
                ck("mem0")

            with ExitStack() as pb:
                lv = kb.sb("lv", [128, 8, 8], F32, pb)
                cvec = kb.sb("cvec", [128, 8, 2], F32, pb)
                bda = kb.sb("bda", [128, 8, 128], BF16, pb)
                bdx = kb.sb("bdx", [128, 8, 128], BF16, pb)
                kb.dma("sp", lv[:], lru_vec, [], [lv])
                load_w(bda, bda[:], bd_a, 128)
                load_w(bdx, bdx[:], bd_x, 128)
                kb.act(cvec[:, :, 0], lv[:, :, 7], AF.Exp, [lv], [cvec], scale=-1.0)
                kb.act(cvec[:, :, 0], cvec[:, :, 0], AF.Ln, [cvec], [cvec], bias=1.0)
                kb.ts("dve", cvec[:, :, 1], cvec[:, :, 0], -16.0, None, ALU.mult, None, [cvec], [cvec])
                kb.ts("dve", cvec[:, :, 0], cvec[:, :, 0], -8.0, None, ALU.mult, None, [cvec], [cvec])
                sets = []
                for L in range(2):
                    sets.append(dict(
                        B1=kb.sb(f"B1_{L}", [128, S_LEN + 4], F32, pb), B2=kb.sb(f"B2_{L}", [128, S_LEN], F32, pb),
                        B3=kb.sb(f"B3_{L}", [128, S_LEN], F32, pb), B4=kb.sb(f"B4_{L}", [128, S_LEN], F32, pb),
                        xcb=kb.sb(f"xcb_{L}", [128, S_LEN], BF16, pb), sz=kb.sb(f"sz_{L}", [128, S_LEN], BF16, pb)))

                def lru_lane(L):
                    st_ = sets[L]
                    B1, B2, B3, B4, xcb, sz = st_["B1"], st_["B2"], st_["B3"], st_["B4"], st_["xcb"], st_["sz"]
                    bks = [kb.banks[4 * L + j] for j in range(4)]
                    for c in range(L, 8, 2):
                        wxa = load_win(c * 128)
                        wza = load_win(1024 + c * 128)
                        kb.memset("dve", B1[:, 0:3], 0.0, [(B1, "pad")])
                        for (wsl, which) in ((wxa, 0), (wza, 1)):
                            for tc in range(4):
                                bk = bks[tc % 4]
                                for dc in range(8):
                                    kb.mm(bk[:, :], wsl[:, dc, 0:128], xnT[:, dc, tc * 512:(tc + 1) * 512], dc == 0, dc == 7, [wsl, xnT], [bk])
                                yield
                                if which == 0:
                                    kb.cp("act", B1[:, 3 + tc * 512:3 + (tc + 1) * 512], bk[:], [bk], [(B1, tc)])
                                else:
                                    kb.act(sz[:, tc * 512:(tc + 1) * 512], bk[:], AF.Silu, [bk], [(sz, tc)])
                                yield
                        kb.ts("dve", B2[:], B1[:, 0:S_LEN], lv[:, c, 0:1], lv[:, c, 4:5], ALU.mult, ALU.add, [B1, lv], [B2])
                        yield
                        for k in range(1, 4):
                            kb.stt(B2[:], B1[:, k:k + S_LEN], lv[:, c, k:k + 1], B2[:], ALU.mult, ALU.add, [B1, lv, B2], [B2])
                            yield
                        kb.cp("dve", xcb[:], B2[:], [B2], [xcb])
                        yield
                        for (bd, dstb, col) in ((bda, B1, 5), (bdx, B4, 6)):
                            for tc in range(4):
                                bk = bks[tc % 4]
                                kb.mm(bk[:, :], bd[:, c, :], xcb[:, tc * 512:(tc + 1) * 512], True, True, [bd, xcb], [bk])
                                yield
                                kb.act(dstb[:, tc * 512:(tc + 1) * 512], bk[:], AF.Sigmoid, [bk, lv], [(dstb, tc)], bias=lv[:, c, col:col + 1])
                                yield
                        r_ap = B1[:, 0:S_LEN]
                        kb.act(B3[:], r_ap, AF.Exp, [B1, cvec], [B3], scale=cvec[:, c, 0:1])
                        yield
                        kb.act(r_ap, r_ap, AF.Exp, [B1, cvec], [B1], scale=cvec[:, c, 1:2])
                        yield
                        kb.tt("dve", B2[:], B2[:], B4[:], ALU.mult, [B2, B4], [B2])
                        yield
                        kb.ts("dve", r_ap, r_ap, -1.0, 1.0, ALU.mult, ALU.add, [B1], [B1])
                        yield
                        kb.ts("dve", r_ap, r_ap, 0.0, None, ALU.max, None, [B1], [B1])
                        yield
                        kb.act(r_ap, r_ap, AF.Sqrt, [B1], [B1])
                        kb.memset("dve", B1[:, 0:1], 1.0, [B1])
                        yield
                        kb.tt("dve", B2[:], B2[:], r_ap, ALU.mult, [B2, B1], [B2])
                        yield
                        S.op("dve", lambda e, B4=B4, B3=B3, B2=B2: e.tensor_tensor_scan(out=B4[:], data0=B3[:], data1=B2[:], initial=0.0,
                                                                                      op0=ALU.mult, op1=ALU.add), reads=[B3, B2], writes=[B4])
                        yield
                        kb.tt("dve", yTa[:, c, :], B4[:], sz[:], ALU.mult, [B4, sz], [(yTa, c)])
                        yield

                run_lanes([lru_lane(0), lru_lane(1)])
                for c0_ in (2048 + 4608, 2048, 2048 + 1536, 2048 + 3072):
                    load_win.prefetch(c0_, 128)
                S.barrier()
                ck("lru")
            yTb = kb.sb("yTb", [128, 4, S_LEN], BF16, l0)

            with ExitStack() as pc:
                edil = kb.sb("edil", [128, 12, 256], F32, pc)
                kb.dma("sp", edil[:], edil_d, [], [edil])
                qTs = [kb.sb(f"qT{i}", [128, S_LEN], BF16, pc) for i in range(2)]
                kTs = [kb.sb(f"kT{i}", [128, S_LEN], BF16, pc) for i in range(2)]
                vTs = [kb.sb(f"vT{i}", [128, S_LEN], BF16, pc) for i in range(2)]
                Vps = [kb.sb(f"Vp{i}", [128, 16, 128], BF16, pc) for i in range(2)]
                szbs = [kb.sb(f"szb{i}", [128, S_LEN], BF16, pc) for i in range(2)]
                NTa = kb.sb("NTa", [128, S_LEN], F32, pc)
                DBa = kb.sb("DBa", [128, S_LEN], F32, pc)
                Pf = [kb.sb(f"Pf{i}", [128, 256], F32, pc) for i in range(3)]
                PT = [kb.sb(f"PT{i}", [128, 256], BF16, pc) for i in range(3)]
                sc_d = 128.0 ** -0.5
                items = [(hs, g) for hs in range(4) for g in range(3)]
                prot = [0]

                def pbank():
                    b = kb.banks[6 + prot[0]]
                    prot[0] ^= 1
                    return b

                def dtoks(d, r, b):
                    t0 = r + d * 128 * b
                    return slice(t0, t0 + d * 127 + 1, d)

                def proj_fm_lane(wsl, dst, func):
                    for tc in range(4):
                        bk = pbank()
                        for dc in range(8):
                            kb.mm(bk[:, :], wsl[:, dc, 0:128], xnT[:, dc, tc * 512:(tc + 1) * 512], dc == 0, dc == 7, [wsl, xnT], [bk])
                        yield
                        if func is None:
                            kb.cp("act", dst[:, tc * 512:(tc + 1) * 512], bk[:], [bk], [(dst, tc)])
                        else:
                            kb.act(dst[:, tc * 512:(tc + 1) * 512], bk[:], func, [bk], [(dst, tc)])
                        yield

                def task_proj(i):
                    hs, g = items[i]
                    win, d = DIL_GROUPS[g]
                    hh = g * 4 + hs
                    nqb = (S_LEN // d) // 128
                    s = i % 2
                    if g == 0:
                        wz = load_win(2048 + 4608 + hs * 128)
                        yield from proj_fm_lane(wz, szbs[hs % 2], AF.Silu)
                    wq = load_win(2048 + hh * 128)
                    yield from proj_fm_lane(wq, qTs[s], None)
                    wk = load_win(2048 + 1536 + hh * 128)
                    yield from proj_fm_lane(wk, kTs[s], None)
                    wv = load_win(2048 + 3072 + hh * 128)
                    yield from proj_fm_lane(wv, vTs[s], None)
                    for j in range(4):
                        bk = pbank()
                        bv = bk[:].bitcast(BF16)
                        for k in range(4):
                            r, b = divmod(4 * j + k, nqb)
                            kb.tr(bv[:, k * 128:(k + 1) * 128], vTs[s][:, dtoks(d, r, b)], identb[:], [vTs[s], identb], [bk])
                        yield
                        kb.cp("act", Vps[s][:, 4 * j:4 * j + 4, :], bv[:, 0:512].rearrange("p (k n) -> p k n", k=4), [bk], [(Vps[s], j)])
                        yield

                def task_tile(i, ti, lane):
                    hs, g = items[i]
                    win, d = DIL_GROUPS[g]
                    hh = g * 4 + hs
                    nqb = (S_LEN // d) // 128
                    s = i % 2
                    qT, kT, Vp = qTs[s], kTs[s], Vps[s]
                    r, qb = divmod(ti, nqb)
                    qs = dtoks(d, r, qb)
                    kbs = [qb - 1, qb] if qb > 0 else [qb]
                    sbk = kb.banks[2 * lane]
                    ndb = kb.banks[2 * lane + 1]
                    for kbi in kbs:
                        typ = 0 if kbi < qb else 1
                        kb.mm(sbk[:, typ * 128:(typ + 1) * 128], kT[:, dtoks(d, r, kbi)], qT[:, qs], True, True, [kT, qT], [sbk])
                    yield
                    lo = 0 if qb > 0 else 128
                    pf = Pf[lane]
                    pt = PT[lane]
                    kb.act(pf[:, lo:256], sbk[:, lo:256], AF.Exp, [sbk], [pf], scale=sc_d)
                    yield
                    kb.tt("dve", pt[:, lo:256], pf[:, lo:256], edil[:, hh, lo:256], ALU.mult, [pf, edil], [pt])
                    yield
                    for j, kbi in enumerate(kbs):
                        typ = 0 if kbi < qb else 1
                        kb.mm(ndb[:, 0:128], Vp[:, r * nqb + kbi, :], pt[:, typ * 128:(typ + 1) * 128], j == 0, j == len(kbs) - 1,
                              [Vp, pt], [ndb])
                    for j, kbi in enumerate(kbs):
                        typ = 0 if kbi < qb else 1
                        kb.mm(ndb[:, 128:256], onesb[:], pt[:, typ * 128:(typ + 1) * 128], j == 0, j == len(kbs) - 1,
                              [onesb, pt], [ndb])
                    yield
                    if g == 0:
                        kb.cp("dve", NTa[:, qs], ndb[:, 0:128], [ndb], [NTa])
                        kb.cp("dve", DBa[:, qs], ndb[:, 128:256], [ndb], [DBa])
                    else:
                        kb.tt("dve", NTa[:, qs], NTa[:, qs], ndb[:, 0:128], ALU.add, [ndb, NTa], [NTa])
                        kb.tt("dve", DBa[:, qs], DBa[:, qs], ndb[:, 128:256], ALU.add, [ndb, DBa], [DBa])
                    yield

                def tile_lane(i, lane):
                    for ti in range(lane, 16, 3):
                        yield from task_tile(i, ti, lane)

                run_lanes([task_proj(0)])
                for i in range(len(items)):
                    hs, g = items[i]
                    lanes = [tile_lane(i, 0), tile_lane(i, 1), tile_lane(i, 2)]
                    if i + 1 < len(items):
                        lanes.append(task_proj(i + 1))
                    run_lanes(lanes)
                    if g == 2:
                        kb.recip(DBa[:], DBa[:], [DBa], [DBa])
                        kb.tt("dve", NTa[:], NTa[:], DBa[:], ALU.mult, [NTa, DBa], [NTa])
                        kb.tt("dve", yTb[:, hs, :], NTa[:], szbs[hs % 2][:], ALU.mult, [NTa, szbs[hs % 2]], [(yTb, hs)])
                S.barrier()
                ck("dil")

            with ExitStack() as pe_:
                out_proj(hawk_w_out, 12, [(yTa, 8), (yTb, 4)], ymT, x_d, DX, False, pe_, dbg=(stop_after == "l0"))
                S.barrier()
                ck("l0end")

        if stop_after != "l0":
          with ExitStack() as l1:
            yT1 = kb.sb("yT1", [128, 8, S_LEN], BF16, l1)
            ymT1 = kb.sb("ymT1", [64, 4, S_LEN], BF16, l1)
            gN = kb.sb("gN", [128, 8, 128], F32, l1)
            kb.dma("sp", gN[:], g_nsa, [], [gN])
            load_win = make_loader(nsa_w_in, gN, 4, l1)
            kcmpT = kb.sb("kcmpT", [64, 2, 128], BF16, l1)
            vcmp = kb.sb("vcmp", [128, 2, 64], BF16, l1)
            gates = kb.sb("gates", [128, 16, 48], F32, l1)
            with ExitStack() as pmm:
                kmT = kb.sb("kmT1", [64, 4, 256], BF16, pmm)
                vm = kb.sb("vm1", [128, 2, 256], BF16, pmm)
                with ExitStack() as pm:
                    gNm = kb.sb("gNm", [128, 8, 128], F32, pm)
                    kb.dma("sp", gNm[:], g_nsa_mem, [], [gNm])
                    mem_kv(nsa_w_mem_kv, gNm, kmT, vm, pm)
                    for c0_ in (2864, 3120, 2864 + 64, 3120 + 64):
                        load_win.prefetch(c0_, 64)
                    S.barrier()
                    ck("memkv1")
                with ExitStack() as pd:
                    mem_attn(load_win, 2864, 3120, kmT, vm, ymT1, pd)
                    load_win.prefetch(1792, 48)
                    load_win.prefetch(1024, 128)
                    load_win.prefetch(1024 + 128, 128)
                    S.barrier()
                    ck("mem1")

            with ExitStack() as pq:
                wg = load_win(1792, 48)
                for t in range(NT_):
                    bk = kb.bank()
                    for dc in range(8):
                        kb.mm(bk[:, 0:48], xnT[:, dc, t * 128:(t + 1) * 128], wg[:, dc, 0:48], dc == 0, dc == 7, [xnT, wg], [bk])
                    kb.act(gates[:, t, :], bk[:, 0:48], AF.Sigmoid, [bk], [(gates, t)])
                kcT = kb.sb("kcT", [128, S_LEN], BF16, pq)
                vcT = kb.sb("vcT", [128, S_LEN], BF16, pq)
                wkc = load_win(1024)
                proj_fm(wkc, 128, lambda bk, ap, tc: kb.cp("act", kcT[:, tc * 512:(tc + 1) * 512], ap, [bk], [(kcT, tc)]))
                wvc = load_win(1024 + 128)
                proj_fm(wvc, 128, lambda bk, ap, tc: kb.cp("act", vcT[:, tc * 512:(tc + 1) * 512], ap, [bk], [(vcT, tc)]))
                W1 = kb.sb("W1", [128, 32, 256], BF16, pq)
                w2 = kb.sb("w2", [128, 2, 64], BF16, pq)
                peS = kb.sb("peS", [64, 2, 32], F32, pq)
                peb = kb.sb("peb", [64, 2, 32], BF16, pq)
                hidT = kb.sb("hidT", [128, 2, 128], BF16, pq)
                cb = kb.sb("cb", [128, 2], F32, pq)
                kb.dma("sp", peS[:], peT_d, [], [peS])
                kb.cp("dve", peb[:], peS[:], [peS], [peb])
                for kv in range(2):
                    w1d = w1k_d if kv == 0 else w1v_d
                    w2d = w2k_d if kv == 0 else w2v_d
                    srcT = kcT if kv == 0 else vcT
                    for p4 in range(2):
                        load_w(W1, W1[:, p4 * 16:(p4 + 1) * 16, :], w1d[:, p4 * 16:(p4 + 1) * 16, :], 256, key=p4)
                    load_w(w2, w2[:, :, :], w2d.rearrange("(hc p) d -> p hc d", p=128), 64)
                    for hc in range(2):
                        bk = kb.bank()
                        for p in range(32):
                            kb.mm(bk[:, 0:1], W1[0:64, p, hc * 128:(hc + 1) * 128], peb[0:64, kv, p:p + 1], p == 0, p == 31,
                                  [W1, peb], [bk])
                        kb.cp("dve", cb[:, hc:hc + 1], bk[:, 0:1], [bk], [(cb, hc)])
                    for g in range(2):
                        for hc in range(2):
                            bk = kb.bank()
                            for p in range(32):
                                kb.mm(bk[:, 0:127], W1[g * 64:(g + 1) * 64, p, hc * 128:(hc + 1) * 128],
                                      srcT[g * 64:(g + 1) * 64, p:p + 16 * 126 + 1:16], p == 0, p == 31, [W1, srcT], [bk])
                            kb.act(hidT[:, hc, 0:127], bk[:, 0:127], AF.Silu, [bk, cb], [(hidT, hc)], bias=cb[:, hc:hc + 1])
                        bk = kb.bank()
                        if kv == 0:
                            for hc in range(2):
                                kb.mm(bk[0:64, 0:127], w2[:, hc, :], hidT[:, hc, 0:127], hc == 0, hc == 1, [w2, hidT], [bk])
                            kb.cp("dve", kcmpT[0:64, g, 0:127], bk[0:64, 0:127], [bk], [(kcmpT, g)])
                        else:
                            for hc in range(2):
                                kb.mm(bk[0:127, 0:64], hidT[:, hc, 0:127], w2[:, hc, :], hc == 0, hc == 1, [w2, hidT], [bk])
                            kb.cp("dve", vcmp[0:127, g, :], bk[0:127, 0:64], [bk], [(vcmp, g)])
                S.barrier()
                ck("cmpkv")

            with ExitStack() as pg:
                QAg = kb.sb("QAg", [105, 8, S_LEN], BF16, pg)
                KAs = kb.sb("KAs", [105, S_LEN], BF16, pg)
                KAw = kb.sb("KAw", [105, S_LEN], BF16, pg)
                VAs = kb.sb("VAs", [128, 16, 128], BF16, pg)
                VAw = kb.sb("VAw", [128, 16, 128], BF16, pg)
                ecmp = kb.sb("ecmp", [128, 8, 247], F32, pg)
                Wz = kb.sb("Wz", [128, 8, 512], BF16, pg)
                m12 = kb.sb("m12", [128, 2, 62], F32, pg)
                trib = kb.sb("trib", [128, 2, 512], BF16, pg)
                kb.dma("sp", m12[:], m12_d, [], [m12])
                kb.dma("pool", trib[:], tri_d, [], [trib])
                for v, KA in enumerate((KAs, KAw)):
                    kb.dma("pool", KA[64:105, :], kaug_d[v], [], [(KA, "aug")])
                kb.memset("pool", VAs[:, :, 64:128], 1.0, [(VAs, "ones")])
                kb.memset("pool", VAw[:, :, 64:128], 1.0, [(VAw, "ones")])
                Pc = [kb.sb("Pc0", [128, 4, 128], F32, pg)] * 2
                Pu = [kb.sb(f"Pu{i}", [128, 4, 128], F32, pg) for i in range(2)]
                Pub = [kb.sb("Pub0", [128, 4, 128], BF16, pg)] * 2
                pT = [kb.sb("pT0", [128, 4, 128], BF16, pg)] * 2
                for i in range(2):
                    kb.memset("pool", Pu[i][:], 0.0, [Pu[i]])
                psg = kb.sb("psg", [128, 128], F32, pg)
                den8 = kb.sb("den8", [128, 8], F32, pg)
                cg8 = kb.sb("cg8", [128, 8], F32, pg)
                imp = kb.sb("imp", [128, 32], F32, pg)
                impm = kb.sb("impm", [128, 32], F32, pg)
                m8 = kb.sb("m8", [128, 8], F32, pg)
                negp = kb.sb("negp", [128, 96], F32, pg)
                negS = kb.sb("negS", [96, 128], BF16, pg)
                kb.memset("pool", negp[:], 0.0, [negp])
                PTs = [kb.sb(f"PTs{i}", [128, 512], BF16, pg) for i in range(8)]
                pts_rr = [0]
                zerob = kb.sb("zerob", [128, 512], BF16, pg)
                kb.memset("pool", zerob[:], 0.0, [zerob])
                tmpo = [kb.sb(f"tmpo{i}", [128, 4, 64], F32, pg) for i in range(2)]
                rd4 = [kb.sb(f"rd4{i}", [128, 4], F32, pg) for i in range(2)]
                cg4 = [kb.sb(f"cg4{i}", [128, 4], F32, pg) for i in range(2)]
                Oa = [kb.sb(f"Oa{i}", [128, 512], F32, pg) for i in range(2)]
                szt = kb.sb("szt", [128, 512], F32, pg)
                Ob = kb.sb("Ob", [128, 512], BF16, pg)
                accbanks = [kb.banks[0], kb.banks[1]]
                rot = [2]

                def rbank():
                    b = kb.banks[rot[0]]
                    rot[0] = rot[0] + 1 if rot[0] < 7 else 2
                    return b

                strot = [0, 0]

                def stbank(lane):
                    b = kb.banks[2 + 2 * lane + strot[lane]]
                    strot[lane] ^= 1
                    return b

                def mbank():
                    return kb.banks[6]

                ptrot = [0, 0]

                def next_pt(lane):
                    p = PTs[4 * lane + ptrot[lane]]
                    ptrot[lane] = (ptrot[lane] + 1) % 4
                    return p

                for g in range(2):
                    kb.dma("sp", ecmp[:], ecmp_d[:, g * 8:(g + 1) * 8, :], [], [ecmp])
                    for j in range(4):
                        load_w(Wz, Wz[:, :, j * 128:(j + 1) * 128], win_cols(nsa_w_in, 1840 + g * 512 + j * 128, 128), 128,
                               gain=None, key=j)
                    for pr in range(4):
                        wq = load_win((g * 8 + 2 * pr) * 64, 128)
                        for tc in range(4):
                            bk = rbank()
                            for dc in range(8):
                                kb.mm(bk[:, :], wq[:, dc, 0:128], xnT[:, dc, tc * 512:(tc + 1) * 512], dc == 0, dc == 7, [wq, xnT], [bk])
                            kb.cp("act", QAg[0:64, 2 * pr, tc * 512:(tc + 1) * 512], bk[0:64, :], [bk], [QAg])
                            stq = PTs[4 * (pr % 2) + tc]
                            kb.cp("dve", stq[64:128, :], bk[64:128, :], [bk], [stq])
                            kb.dma("sp", QAg[0:64, 2 * pr + 1, tc * 512:(tc + 1) * 512], stq[64:128, :], [stq], [QAg])
                    kb.dma("pool", QAg[96:105, :, :], qal_d[g * 8:(g + 1) * 8].rearrange("h r n -> r h n"), [], [QAg])
                    for KA, col in ((KAs, 1024 + 2 * 128 + g * 64), (KAw, 1024 + 4 * 128 + g * 64)):
                        wk = load_win(col, 64)
                        for tc in range(4):
                            bk = rbank()
                            for dc in range(8):
                                kb.mm(bk[0:64, :], wk[:, dc, 0:64], xnT[:, dc, tc * 512:(tc + 1) * 512], dc == 0, dc == 7, [wk, xnT], [bk])
                            kb.cp("act", KA[0:64, tc * 512:(tc + 1) * 512], bk[0:64, :], [bk], [(KA, tc)])
                    wv = load_win(1024 + 3 * 128 + g * 64, 64)
                    load_win(1024 + 5 * 128 + g * 64, 64, into=wv, off=64)
                    for t in range(NT_):
                        bk = rbank()
                        for dc in range(8):
                            kb.mm(bk[:, 0:128], xnT[:, dc, t * 128:(t + 1) * 128], wv[:, dc, 0:128], dc == 0, dc == 7, [xnT, wv], [bk])
                        kb.cp("act", VAs[:, t, 0:64], bk[:, 0:64], [bk], [(VAs, t)])
                        kb.cp("dve", VAw[:, t, 0:64], bk[:, 64:128], [bk], [(VAw, t)])

                    def task_C(qt):
                        qc = slice(qt * 128, (qt + 1) * 128)
                        O = Oa[qt % 2]
                        gq = gates[:, qt, :]
                        eoff = 120 - 8 * qt
                        ocb = kb.banks[7]
                        for b4 in range(2):
                            sbk = mbank()
                            for hl in range(4):
                                r = b4 * 4 + hl
                                kb.mm(sbk[:, hl * 128:hl * 128 + 127], QAg[0:64, r, qc], kcmpT[0:64, g, 0:127], True, True,
                                      [(QAg, qt), kcmpT], [sbk])
                            yield
                            pc = Pc[b4]
                            pu = Pu[b4]
                            s3 = sbk[:].rearrange("p (h n) -> p h n", h=4)
                            kb.act(pc[:, :, 0:127], s3[:, :, 0:127], AF.Exp, [sbk], [pc], scale=0.125)
                            yield
                            kb.tt("dve", pu[:, :, 0:127], pc[:, :, 0:127], ecmp[:, b4 * 4:(b4 + 1) * 4, eoff:eoff + 127], ALU.mult,
                                  [pc, ecmp], [pu])
                            S.op("dve", lambda e, pu=pu, b4=b4: e.tensor_reduce(out=den8[:, b4 * 4:(b4 + 1) * 4], in_=pu[:, :, 0:127],
                                                                                axis=AX.X, op=ALU.add),
                                 reads=[pu], writes=[(den8, b4)])
                            yield
                            kb.ts("dve", den8[:, b4 * 4:(b4 + 1) * 4], den8[:, b4 * 4:(b4 + 1) * 4], 1e-30, None, ALU.max, None,
                                  [(den8, b4)], [(den8, b4)])
                            kb.recip(den8[:, b4 * 4:(b4 + 1) * 4], den8[:, b4 * 4:(b4 + 1) * 4], [(den8, b4)], [(den8, b4)])
                            pub = Pub[b4]
                            kb.cp("pool", pub[:], pu[:], [pu], [pub])
                            yield
                            for hl in range(4):
                                r = b4 * 4 + hl
                                if r == 0:
                                    kb.ts("dve", psg[:, :], pu[:, hl, :], den8[:, r:r + 1], None, ALU.mult, None, [pu, (den8, b4)], [psg])
                                else:
                                    kb.stt(psg[:, :], pu[:, hl, :], den8[:, r:r + 1], psg[:, :], ALU.mult, ALU.add,
                                           [pu, (den8, b4), psg], [psg])
                                if hl % 2 == 1:
                                    yield
                            tbk = mbank()
                            tv = tbk[:].bitcast(BF16)
                            for hl in range(4):
                                kb.tr(tv[0:127, hl * 128:(hl + 1) * 128], pub[:, hl, 0:127], identb[:], [pub, identb], [tbk])
                            yield
                            ptt = pT[b4]
                            kb.cp("act", ptt[0:127, :, :], tv[0:127, 0:512].rearrange("p (h n) -> p h n", h=4), [tbk], [ptt])
                            yield
                            for hl in range(4):
                                r = b4 * 4 + hl
                                kb.mm(ocb[:, r * 64:(r + 1) * 64], ptt[0:127, hl, :], vcmp[0:127, g, :], True, True, [ptt, vcmp], [ocb])
                            yield
                        kb.tt("dve", cg8[:], den8[:], gq[:, g * 24:g * 24 + 24:3], ALU.mult, [den8, gates], [cg8])
                        kb.tt("dve", O[:].rearrange("p (h d) -> p h d", h=8), ocb[:].rearrange("p (h d) -> p h d", h=8),
                              cg8[:, 0:8].unsqueeze(2).to_broadcast([128, 8, 64]), ALU.mult, [ocb, cg8], [O])
                        yield
                        S.op("dve", lambda e: e.tensor_reduce(out=imp[:, :], in_=psg[:].rearrange("p (j a) -> p j a", a=4),
                                                              axis=AX.X, op=ALU.add), reads=[psg], writes=[imp])
                        kb.tt("dve", imp[:, 1:32], imp[:, 1:32], psg[:, 3:127:4], ALU.add, [imp, psg], [imp])
                        yield
                        moff = 30 - 2 * qt
                        kb.tt("dve", impm[:], imp[:], m12[:, 0, moff:moff + 32], ALU.mult, [imp, m12], [impm])
                        kb.tt("dve", impm[:], impm[:], m12[:, 1, moff:moff + 32], ALU.add, [impm, m12], [impm])
                        kb.memset("dve", impm[:, 0:1], 1e6, [impm])
                        yield
                        S.op("dve", lambda e: e.max(out=m8[:], in_=impm[:]), reads=[impm], writes=[m8])
                        kb.ts("dve", negp[:, 64:96], impm[:], m8[:, 7:8], 1.0, ALU.is_ge, ALU.subtract, [impm, m8], [negp])
                        kb.ts("dve", negp[:, 64:96], negp[:, 64:96], NEGB, None, ALU.mult, None, [negp], [negp])
                        yield
                        tbk = mbank()
                        kb.tr(tbk[0:96, 0:128], negp[:, 0:96], identf[:], [negp, identf], [tbk])
                        yield
                        kb.cp("dve", negS[64:96, :], tbk[64:96, 0:128], [tbk], [negS])
                        kb.cp("dve", QAg[64:96, :, qc], negS[64:96, :].unsqueeze(1).to_broadcast([32, 8, 128]), [negS], [(QAg, qt)])
                        yield

                    def task_branch(qt, br, b4):
                        qc = slice(qt * 128, (qt + 1) * 128)
                        O = Oa[qt % 2]
                        gq = gates[:, qt, :]
                        KA, VA = (KAw, VAw) if br == 2 else (KAs, VAs)
                        kbs = list(range(max(0, qt - 4), qt + 1)) if br == 2 else list(range(0, qt + 1))
                        accb = kb.banks[b4]
                        pend = []
                        nk = len(kbs)
                        kb.mm(accb[:, 0:260], zerob[:, 0:128], zerob[:, 0:260], True, False, [zerob], [accb])

                        def do_pv(item):
                            pi, pk, ppt = item
                            for hl in range(4):
                                kb.mm(accb[:, hl * 65:(hl + 1) * 65], ppt[:, hl * 128:(hl + 1) * 128], VA[:, pk, 0:65], False,
                                      (pi == nk - 1) and hl == 3, [VA, ppt], [accb])

                        for idx, kbi in enumerate(kbs):
                            sbk = stbank(b4)
                            masks = []
                            if kbi == qt:
                                masks.append(0)
                            if br == 2 and kbi == qt - 4:
                                masks.append(1)
                            kb.mm(sbk[:, :], KA[0:105, kbi * 128:(kbi + 1) * 128], QAg[0:105, b4 * 4:(b4 + 1) * 4, qc],
                                  True, len(masks) == 0, [KA, (QAg, qt)], [sbk])
                            for mi, mv in enumerate(masks):
                                kb.mm(sbk[:, :], identb[:], trib[:, mv, :], False, mi == len(masks) - 1, [identb, trib], [sbk])
                            pt = next_pt(b4)
                            kb.act(pt[:], sbk[:], AF.Exp, [sbk], [pt], scale=0.125)
                            yield
                            pend.append((idx, kbi, pt))
                            if len(pend) > 2:
                                do_pv(pend.pop(0))
                                yield
                        while pend:
                            do_pv(pend.pop(0))
                            yield
                        a3 = accb[:, 0:260].rearrange("p (h n) -> p h n", h=4)
                        kb.recip(rd4[b4][:], a3[:, :, 64], [accb], [rd4[b4]])
                        h0 = (g * 8 + b4 * 4) * 3 + br
                        kb.tt("dve", cg4[b4][:], rd4[b4][:], gq[:, h0:h0 + 10:3], ALU.mult, [rd4[b4], gates], [cg4[b4]])
                        yield
                        kb.tt("dve", tmpo[b4][:], a3[:, :, 0:64], cg4[b4][:, 0:4].unsqueeze(2).to_broadcast([128, 4, 64]), ALU.mult,
                              [accb, cg4[b4]], [tmpo[b4]])
                        yield
                        ov = O[:, b4 * 256:(b4 + 1) * 256].rearrange("p (h d) -> p h d", h=4)
                        kb.tt("dve", ov, ov, tmpo[b4][:], ALU.add, [(O, b4), tmpo[b4]], [(O, b4)])
                        yield

                    def task_Z(qt):
                        qc = slice(qt * 128, (qt + 1) * 128)
                        O = Oa[qt % 2]
                        zb = mbank()
                        for dc in range(8):
                            kb.mm(zb[:, :], xnT[:, dc, qc], Wz[:, dc, :], dc == 0, dc == 7, [xnT, Wz], [zb])
                            if dc % 4 == 3:
                                yield
                        kb.act(szt[:], zb[:], AF.Silu, [zb], [szt])
                        yield
                        kb.tt("dve", Ob[:], O[:], szt[:], ALU.mult, [O, szt], [Ob])
                        yield
                        tbk = mbank()
                        tv = tbk[:].bitcast(BF16)
                        for c4 in range(4):
                            kb.tr(tv[:, c4 * 128:(c4 + 1) * 128], Ob[:, c4 * 128:(c4 + 1) * 128], identb[:], [Ob, identb], [tbk])
                        yield
                        kb.cp("act", yT1[:, g * 4:(g + 1) * 4, qc], tv[:, 0:512].rearrange("p (c n) -> p c n", c=4), [tbk], [(yT1, (g, qt))])
                        yield

                    run_lanes([task_C(0)])
                    for qt in range(NT_):
                        l1_ = chain(task_branch(qt, 2, 0), task_branch(qt, 1, 0))
                        l2_ = chain(task_branch(qt, 2, 1), task_branch(qt, 1, 1))
                        third = []
                        if qt > 0:
                            third.append(task_Z(qt - 1))
                        if qt + 1 < NT_:
                            third.append(task_C(qt + 1))
                        run_lanes([l1_, l2_, chain(*third)], [1, 1, L3_STEPS])
                    run_lanes([task_Z(NT_ - 1)])
                S.barrier()
                ck("nsa")

            with ExitStack() as pe_:
                out_proj(nsa_w_out, 8, [(yT1, 8)], ymT1, x1_scr, DX1, True, pe_)
                S.barrier()
                ck("l1end")

        S.dead = False
        S.barrier()
        with nc.Block() as block:
            S.emit(block)
        print("program ops:", S.nops, "sems:", S.nsem)
    return nc


_CONST = None


def prep_inputs(inp):
    global _CONST
    if _CONST is None:
        _CONST = host_constants()
        _CONST.update(host_constants_nsa())
    f = lambda a: np.ascontiguousarray(np.asarray(a, dtype=np.float32))
    shared = {
        "hawk_w_in": f(inp["hawk_w_in"][0]),
        "hawk_w_out": f(inp["hawk_w_out"][0]),
        "hawk_w_mem_kv": f(inp["hawk_w_mem_kv"][0]),
        "g_hawk": expand_gain(f(inp["hawk_norm"][0])),
        "g_hawk_mem": expand_gain(f(inp["hawk_mem_norm"][0])),
        "bd_a": block_diag(f(inp["hawk_gate_a_w"][0])),
        "bd_x": block_diag(f(inp["hawk_gate_x_w"][0])),
        "final_norm": f(inp["final_norm"]),
        "hawk_norm_v": f(inp["hawk_norm"][0]),
        "nsa_norm_v": f(inp["nsa_norm"][0]),
        "nsa_w_in": f(inp["nsa_w_in"][0]),
        "nsa_w_out": f(inp["nsa_w_out"][0]),
        "nsa_w_mem_kv": f(inp["nsa_w_mem_kv"][0]),
        "g_nsa": expand_gain(f(inp["nsa_norm"][0])),
        "g_nsa_mem": expand_gain(f(inp["nsa_mem_norm"][0])),
        "w2k": f(inp["nsa_phi_k_w2"][0]),
        "w2v": f(inp["nsa_phi_v_w2"][0]),
    }
    for k in ("identf", "edil", "ecmp", "m12", "tri", "kaug", "qal"):
        shared[k] = _CONST[k]

    def w1_layout(w1):
        a = w1.reshape(32, 64, 256).transpose(1, 0, 2)
        return np.ascontiguousarray(np.concatenate([a, a], axis=0))
    shared["w1k"] = w1_layout(f(inp["nsa_phi_k_w1"][0]))
    shared["w1v"] = w1_layout(f(inp["nsa_phi_v_w1"][0]))
    shared["peT"] = np.ascontiguousarray(np.stack([f(inp["nsa_pe_k"][0]).T, f(inp["nsa_pe_v"][0]).T], axis=1))
    lv = np.zeros((128, 8, 8), np.float32)
    cw = f(inp["hawk_conv_w"][0])
    for k in range(4):
        lv[:, :, k] = vec_fm(cw[k])
    lv[:, :, 4] = vec_fm(f(inp["hawk_conv_b"][0]))
    lv[:, :, 5] = vec_fm(f(inp["hawk_gate_a_b"][0]).reshape(-1))
    lv[:, :, 6] = vec_fm(f(inp["hawk_gate_x_b"][0]).reshape(-1))
    lv[:, :, 7] = vec_fm(f(inp["hawk_lambda"][0]))
    shared["lru_vec"] = lv
    x = f(inp["x"])
    mem = f(inp["mem"])
    maps = []
    for b in range(x.shape[0]):
        m = dict(shared)
        m["x"] = x[b]
        m["mem"] = mem[b]
        maps.append(m)
    return maps


def kernel(**inputs):
    maps = prep_inputs(inputs)
    nc = build_program()
    res = run_bass_kernel_spmd(nc, maps, core_ids=list(range(len(maps))))
    out = np.stack([np.asarray(r["out"], dtype=np.float32) for r in res.results], axis=0)
    return out
```

```python
import math
from contextlib import ExitStack

import numpy as np
import concourse.bass as bass
import concourse.mybir as mybir
from concourse.bass_utils import run_bass_kernel_spmd

F32 = mybir.dt.float32
BF16 = mybir.dt.bfloat16
AF = mybir.ActivationFunctionType
ALU = mybir.AluOpType
AX = mybir.AxisListType

S_LEN = 2048
D = 1024
NT_ = 16
EPS = 1e-6
DIL_GROUPS = ((128, 1), (512, 4), (2048, 16))

SEM_LIMIT = 30000
N_DMA_SEMS = 24
SAME_ENGINE_SYNC = True


class Buf:
    def __init__(self, name, t, excl=False):
        self.name = name
        self.t = t
        self.excl = excl
        self.st = {}

    def __getitem__(self, idx):
        return self.t[idx]


class Sync:
    def __init__(self, nc, stack):
        self.nc = nc
        self.stack = stack
        self.engs = ["pe", "act", "dve", "pool", "sp"]
        self.ops = {e: [] for e in self.engs}
        self.cur_sem = {}
        self.cnt = {}
        self.nsem = 0
        for e in self.engs:
            self._new_sem(e)
        self.dma_sems = {}
        self.dma_val = {}
        self.dma_rr = {}
        for e in ["sp", "pool", "act"]:
            self.dma_sems[e] = [self._alloc_sem(f"d{e}{i}") for i in range(N_DMA_SEMS)]
            self.dma_val[e] = [0] * N_DMA_SEMS
            self.dma_rr[e] = 0
        self.seen = {e: {} for e in self.engs}
        self.all_ticks = {}
        self.nops = 0
        self.dead = False
        self.eng_free = {e: 0.0 for e in self.engs}
        self.lane = None
        self.tnow = 0.0

    def _alloc_sem(self, name):
        self.nsem += 1
        return self.stack.enter_context(self.nc.semaphore(f"s_{name}_{self.nsem}"))

    def _new_sem(self, e):
        self.cur_sem[e] = self._alloc_sem(e)
        self.cnt[e] = 0

    def _states(self, buf, key, create):
        if key is None:
            if create and None not in buf.st:
                buf.st[None] = [None, {}]
            return list(buf.st.values())
        out = []
        if None in buf.st:
            out.append(buf.st[None])
        if key not in buf.st and create:
            buf.st[key] = [None, {}]
        if key in buf.st:
            out.append(buf.st[key])
        return out

    @staticmethod
    def _norm(lst):
        out = []
        for r in lst or []:
            out.append(r if isinstance(r, tuple) else (r, None))
        return out

    def op(self, eng, fn, reads=None, writes=None, dma=False, cost=0.5):
        if self.dead:
            return None
        reads = self._norm(reads)
        writes = self._norm(writes)
        ex = [(b, None) for (b, k) in reads + writes if b.excl]
        if ex:
            reads = [(b, k) for (b, k) in reads if not b.excl]
            writes = [(b, k) for (b, k) in writes if not b.excl]
            for bk in ex:
                if bk not in writes:
                    writes.append(bk)
        need = []
        for buf, key in reads:
            for st in self._states(buf, key, False):
                if st[0] is not None:
                    need.append(st[0])
        for buf, key in writes:
            for st in self._states(buf, key, False):
                if st[0] is not None:
                    need.append(st[0])
                need.extend(st[1].values())
        if dma:
            i = self.dma_rr[eng]
            self.dma_rr[eng] = (i + 1) % N_DMA_SEMS
            sem = self.dma_sems[eng][i]
            prev = self.dma_val[eng][i]
            if prev > 0:
                need.append((sem, prev, "dma", 0.0))
            if prev + 16 > SEM_LIMIT:
                sem = self._alloc_sem(f"d{eng}{i}")
                self.dma_sems[eng][i] = sem
                prev = 0
            val = prev + 16
            self.dma_val[eng][i] = val
            inc = 16
            tick = [sem, val, "dma", 0.0]
        else:
            if self.cnt[eng] + 1 > SEM_LIMIT:
                self._new_sem(eng)
            self.cnt[eng] += 1
            sem = self.cur_sem[eng]
            val = self.cnt[eng]
            inc = 1
            tick = [sem, val, eng, 0.0]
        ready = 0.0
        for nd in need:
            if nd[3] > ready:
                ready = nd[3]
        start = max(self.eng_free[eng], ready + 0.06)
        if dma:
            self.eng_free[eng] = start + 0.06
        else:
            self.eng_free[eng] = start + cost
        tick[3] = start + cost
        tick = tuple(tick)
        if self.lane is not None and tick[3] > self.lane.clock:
            self.lane.clock = tick[3]
        if tick[3] > self.tnow:
            self.tnow = tick[3]
        waits = {}
        seen = self.seen[eng]
        for (s, v, src, _fin) in need:
            if src == eng and (eng == "pe" or not SAME_ENGINE_SYNC):
                continue
            sid = id(s)
            if seen.get(sid, 0) >= v:
                continue
            if sid not in waits or waits[sid][1] < v:
                waits[sid] = (s, v)
        for sid, (s, v) in waits.items():
            seen[sid] = v
        self.ops[eng].append((list(waits.values()), fn, sem, inc))
        self.all_ticks[id(sem)] = (sem, val)
        self.nops += 1
        wset = set((id(b), k) for b, k in writes)
        for buf, key in reads:
            if (id(buf), key) in wset:
                continue
            self._states(buf, key, True)
            buf.st[key][1][eng if not dma else ("dma", id(sem))] = tick
        for buf, key in writes:
            if key is None:
                buf.st = {None: [tick, {}]}
            else:
                buf.st[key] = [tick, {}]
        return tick

    def barrier(self):
        if self.dead:
            return
        ticks = list(self.all_ticks.values())
        for e in self.engs:
            wl = []
            for (s, v) in ticks:
                if self.seen[e].get(id(s), 0) < v:
                    wl.append((s, v))
                    self.seen[e][id(s)] = v
            if wl:
                self.ops[e].append((wl, None, None, 0))

    def emit(self, block):
        S = self

        def run(engname, e):
            for (wl, fn, sem, inc) in S.ops[engname]:
                for (s, v) in wl:
                    e.wait_ge(s, v)
                if fn is not None:
                    fn(e).then_inc(sem, inc)

        @block.sync
        def _(e):
            run("sp", e)

        @block.tensor
        def _(e):
            run("pe", e)

        @block.scalar
        def _(e):
            run("act", e)

        @block.vector
        def _(e):
            run("dve", e)

        @block.gpsimd
        def _(e):
            run("pool", e)


class BankView:
    def __init__(self, pair, half):
        self.pair = pair
        self.off = 512 * half

    def __getitem__(self, idx):
        if not isinstance(idx, tuple):
            idx = (idx, slice(None))
        pr, col = idx
        cs = (col.start or 0) + self.off
        ce = (col.stop if col.stop is not None else 512) + self.off
        return self.pair[pr, cs:ce:col.step] if col.step else self.pair[pr, cs:ce]


class KB:
    def __init__(self, nc, stack):
        self.nc = nc
        self.gst = stack
        self.S = Sync(nc, stack)
        self.pairs = [stack.enter_context(nc.psum_tensor(f"pair{i}", [128, 1024], F32)) for i in range(4)]
        self.banks = [Buf(f"bank{i}", BankView(self.pairs[i // 2], i % 2), excl=True) for i in range(8)]
        self.bank_rr = 0
        self.uid = 0

    def sb(self, name, shape, dt, stack=None):
        self.uid += 1
        t = (stack or self.gst).enter_context(self.nc.sbuf_tensor(f"{name}_{self.uid}", shape, dt))
        return Buf(name, t)

    def bank(self):
        b = self.banks[self.bank_rr]
        self.bank_rr = (self.bank_rr + 1) % 8
        return b

    @staticmethod
    def fsz(ap):
        n = 1
        for s in ap.shape[1:]:
            n *= int(s)
        return n

    def vcost(self, eng, ap):
        n = self.fsz(ap)
        if eng == "pool":
            return 0.3 + n / 480.0
        if eng == "act":
            return 0.22 + n / 1400.0
        return 0.08 + n / 960.0

    def mm(self, out, lhsT, rhs, start, stop, r, w):
        c = max(self.fsz(rhs), 64) / 1600.0 + 0.04
        self.S.op("pe", lambda e: e.matmul(out, lhsT=lhsT, rhs=rhs, start=start, stop=stop), reads=r, writes=w, cost=c)

    def tr(self, out, in_, ident, r, w):
        self.S.op("pe", lambda e: e.transpose(out, in_, ident), reads=r, writes=w, cost=0.11)

    def act(self, out, in_, func, r, w, **kw):
        self.S.op("act", lambda e: e.activation(out=out, in_=in_, func=func, **kw), reads=r, writes=w, cost=self.vcost("act", out))

    def tt(self, eng, out, in0, in1, op, r, w):
        self.S.op(eng, lambda e: e.tensor_tensor(out=out, in0=in0, in1=in1, op=op), reads=r, writes=w, cost=self.vcost(eng, out))

    def ts(self, eng, out, in0, s1, s2, op0, op1, r, w, **kw):
        c = self.vcost(eng, out)
        if op1 is None:
            self.S.op(eng, lambda e: e.tensor_scalar(out=out, in0=in0, scalar1=s1, scalar2=None, op0=op0, **kw), reads=r, writes=w, cost=c)
        else:
            self.S.op(eng, lambda e: e.tensor_scalar(out=out, in0=in0, scalar1=s1, scalar2=s2, op0=op0, op1=op1, **kw), reads=r, writes=w, cost=c)

    def stt(self, out, in0, scalar, in1, op0, op1, r, w, **kw):
        self.S.op("dve", lambda e: e.scalar_tensor_tensor(out=out, in0=in0, scalar=scalar, in1=in1, op0=op0, op1=op1, **kw), reads=r, writes=w,
                  cost=0.12 + self.fsz(out) / 960.0)

    def cp(self, eng, out, in_, r, w):
        c = self.vcost(eng, out)
        if eng == "act":
            self.S.op("act", lambda e: e.activation(out=out, in_=in_, func=AF.Copy), reads=r, writes=w, cost=c)
        else:
            self.S.op(eng, lambda e: e.tensor_copy(out=out, in_=in_), reads=r, writes=w, cost=c)

    def memset(self, eng, ap, val, w):
        self.S.op(eng, lambda e: e.memset(ap, val), writes=w, cost=self.vcost(eng, ap))

    def recip(self, out, in_, r, w):
        self.S.op("dve", lambda e: e.reciprocal(out=out, in_=in_), reads=r, writes=w, cost=self.vcost("dve", out))

    def dma(self, q, out, in_, r, w):
        nbytes = self.fsz(out) * int(out.shape[0]) * 4
        self.S.op(q, lambda e: e.dma_start(out=out, in_=in_), reads=r, writes=w, dma=True, cost=2.0 + nbytes / 150000.0)


def alibi_slopes(n):
    return np.exp2(-8.0 * np.arange(1, n + 1) / n).astype(np.float32)


def host_constants():
    c = {}
    c["identf"] = np.eye(128, dtype=np.float32)
    sl = alibi_slopes(12)
    ik = np.arange(128)[:, None].astype(np.float64)
    iq = np.arange(128)[None, :].astype(np.float64)
    E = np.zeros((128, 12, 256), np.float32)
    for g, (win, dil) in enumerate(DIL_GROUPS):
        for hs in range(4):
            hh = g * 4 + hs
            s = float(sl[hh]) * dil
            dist_prev = 128 + iq - ik
            ok_prev = (dist_prev <= 128)
            E[:, hh, 0:128] = np.where(ok_prev, np.exp(-s * dist_prev), 0.0)
            dist_cur = iq - ik
            ok_cur = dist_cur >= 0
            E[:, hh, 128:256] = np.where(ok_cur, np.exp(-s * dist_cur), 0.0)
    c["edil"] = E
    return c


def expand_gain(g):
    return np.ascontiguousarray(np.broadcast_to(g.reshape(8, 128).T[:, :, None], (128, 8, 128))).astype(np.float32)


def vec_fm(v):
    return np.ascontiguousarray(v.reshape(8, 128).T).astype(np.float32)


def block_diag(gw):
    out = np.zeros((128, 8, 128), np.float32)
    for c in range(8):
        out[0:64, c, 0:64] = gw[2 * c]
        out[64:128, c, 64:128] = gw[2 * c + 1]
    return out


NEGB = 8192.0


def _bf16_split3(a):
    import ml_dtypes
    a = a.astype(np.float32)
    hi = a.astype(ml_dtypes.bfloat16).astype(np.float32)
    r1 = (a - hi).astype(np.float32)
    mid = r1.astype(ml_dtypes.bfloat16).astype(np.float32)
    r2 = (r1 - mid).astype(np.float32)
    lo = r2.astype(ml_dtypes.bfloat16).astype(np.float32)
    return hi, mid, lo


def host_constants_nsa():
    c = {}
    sl = alibi_slopes(16)
    i = np.arange(128)[:, None].astype(np.float64)
    m = np.arange(247)[None, :].astype(np.float64)
    dist = i - 16.0 * (m - 120.0) - 31.0
    E = np.zeros((128, 16, 247), np.float32)
    for h in range(16):
        E[:, h, :] = np.where(dist >= 0, np.exp(-float(sl[h]) * np.maximum(dist, 0.0)), 0.0)
    c["ecmp"] = E
    ii = np.arange(128)[:, None]
    rel = np.arange(62)[None, :] - 30
    cur = (ii >= 64).astype(np.int64)
    forced = (rel == cur) | (rel == cur - 1)
    future = rel > cur
    m1 = np.where(forced | future, 0.0, 1.0).astype(np.float32)
    m2 = np.where(forced, 1e6, np.where(future, -1e6, 0.0)).astype(np.float32)
    c["m12"] = np.ascontiguousarray(np.stack([m1, m2], axis=1))
    ik = np.arange(128)[:, None]
    iq = np.arange(128)[None, :]
    diag = np.where(ik > iq, -NEGB, 0.0).astype(np.float32)
    far = np.where(ik <= iq, -NEGB, 0.0).astype(np.float32)
    c["tri"] = np.ascontiguousarray(np.stack([np.tile(diag, (1, 4)), np.tile(far, (1, 4))], axis=1))
    k = np.arange(2048)
    kp = k - 1024
    hi = (np.floor(kp / 128.0) * 128.0).astype(np.float32)
    lo = (kp - hi).astype(np.float32)
    ka = np.zeros((2, 41, 2048), np.float32)
    for j in range(32):
        ka[0, j, :] = (k // 64 == j).astype(np.float32)
    for v in range(2):
        ka[v, 32:35, :] = 1.0
        ka[v, 35:38, :] = lo[None, :]
        ka[v, 38:41, :] = hi[None, :]
    c["kaug"] = ka
    qa = np.zeros((16, 9, 2048), np.float32)
    qp = (np.arange(2048) - 1024).astype(np.float32)
    for h in range(16):
        s8 = np.float32(8.0) * np.float32(sl[h])
        a = (-s8 * qp).astype(np.float32)
        ah, am, al = _bf16_split3(a)
        sh, sm, sl_ = _bf16_split3(np.full((2048,), s8, np.float32))
        qa[h, 0], qa[h, 1], qa[h, 2] = ah, am, al
        qa[h, 3], qa[h, 4], qa[h, 5] = sh, sm, sl_
        qa[h, 6], qa[h, 7], qa[h, 8] = sh, sm, sl_
    c["qal"] = qa
    return c


def chain(*gens):
    for g_ in gens:
        yield from g_


L3_STEPS = 1


def run_lanes(lanes, weights=None):
    active = list(lanes)
    w = {id(l): 1 for l in active}
    if weights:
        for l, wt in zip(lanes, weights):
            w[id(l)] = wt
    while active:
        for l in list(active):
            for _ in range(w[id(l)]):
                try:
                    next(l)
                except StopIteration:
                    active.remove(l)
                    break


def build_program(stop_after=None):
    nc = bass.Bass("TRN2", target_bir_lowering=False)

    ckstate = {}

    def ck(name):
        if stop_after == name:
            ckstate["S"].dead = True

    def din(name, shape):
        return nc.dram_tensor(name, list(shape), F32, kind="ExternalInput").ap()

    x_d = din("x", [S_LEN, D])
    mem_d = din("mem", [256, D])
    hawk_w_in = din("hawk_w_in", [D, 7680])
    hawk_w_out = din("hawk_w_out", [1792, D])
    hawk_w_mem_kv = din("hawk_w_mem_kv", [D, 512])
    g_hawk = din("g_hawk", [128, 8, 128])
    g_hawk_mem = din("g_hawk_mem", [128, 8, 128])
    lru_vec = din("lru_vec", [128, 8, 8])
    bd_a = din("bd_a", [128, 8, 128])
    bd_x = din("bd_x", [128, 8, 128])
    identf_d = din("identf", [128, 128])
    edil_d = din("edil", [128, 12, 256])
    final_g = din("final_norm", [D])
    hawk_norm_v = din("hawk_norm_v", [D])
    nsa_norm_v = din("nsa_norm_v", [D])
    nsa_w_in = din("nsa_w_in", [D, 3376])
    nsa_w_out = din("nsa_w_out", [1280, D])
    nsa_w_mem_kv = din("nsa_w_mem_kv", [D, 512])
    g_nsa = din("g_nsa", [128, 8, 128])
    g_nsa_mem = din("g_nsa_mem", [128, 8, 128])
    w1k_d = din("w1k", [128, 32, 256])
    w1v_d = din("w1v", [128, 32, 256])
    w2k_d = din("w2k", [256, 64])
    w2v_d = din("w2v", [256, 64])
    peT_d = din("peT", [64, 2, 32])
    ecmp_d = din("ecmp", [128, 16, 247])
    m12_d = din("m12", [128, 2, 62])
    tri_d = din("tri", [128, 2, 512])
    kaug_d = din("kaug", [2, 41, 2048])
    qal_d = din("qal", [16, 9, 2048])
    out_d = nc.dram_tensor("out", [S_LEN, D], F32, kind="ExternalOutput").ap()
    x1_scr = nc.dram_tensor("x1_scr", [S_LEN, D], F32, kind="Internal").ap()

    with ExitStack() as gst:
        kb = KB(nc, gst)
        S = kb.S
        ckstate["S"] = S
        DX = Buf("x_dram", None)
        DX1 = Buf("x1_dram", None)
        DOUT = Buf("out_dram", None)

        xnT = kb.sb("xnT", [128, 8, S_LEN], BF16)
        memnT = kb.sb("memnT", [128, 8, 256], BF16)
        identf = kb.sb("identf", [128, 128], F32)
        identb = kb.sb("identb", [128, 128], BF16)
        onesb = kb.sb("onesb", [128, 128], BF16)
        wstage = [kb.sb(f"wstage{i}", [128, 1024], F32) for i in range(3)]
        ws_rr = [0]
        stat = kb.sb("stat", [128, 64], F32)
        stat_rr = [0]

        kb.dma("sp", identf[:], identf_d, [], [identf])
        kb.cp("dve", identb[:], identf[:], [identf], [identb])
        kb.memset("dve", onesb[:], 1.0, [onesb])

        def next_ws():
            b = wstage[ws_rr[0]]
            ws_rr[0] = (ws_rr[0] + 1) % len(wstage)
            return b

        def load_w(dst, dst_ap3, src_ap3, n, gain=None, key=None, q="sp", part=128, eng="pool"):
            dcs = dst_ap3.shape[1]
            if gain is None:
                kb.dma("pool", dst_ap3, src_ap3, [], [(dst, key)])
                return
            assert dcs * n <= 1024
            stg = next_ws()
            sv = stg[0:part, 0:dcs * n].rearrange("p (c n) -> p c n", c=dcs)
            kb.dma(q, sv, src_ap3, [], [stg])
            if gain is not None:
                kb.tt(eng, dst_ap3, sv, gain[0:part, 0:dcs, 0:n], ALU.mult, [stg, gain], [(dst, key)])
            else:
                kb.cp(eng, dst_ap3, sv, [stg], [(dst, key)])

        def win_cols(w_dram, c0, n):
            return w_dram.rearrange("(dc p) n -> p dc n", p=128)[:, :, c0:c0 + n]

        class NormCtx:
            def __init__(self, stack, nbuf=2):
                self.xstage = [kb.sb(f"xstage{i}", [128, 1024], F32, stack) for i in range(nbuf)]
                self.xnb = [kb.sb(f"xnb{i}", [128, 1024], BF16, stack) for i in range(nbuf)]
                self.junk = kb.sb("junk", [128, 1024], BF16, stack)

        def tile_rstd(ncx, xbuf, xap):
            i = stat_rr[0]
            stat_rr[0] = (stat_rr[0] + 1) % 32
            ss = stat[:, 2 * i:2 * i + 1]
            rs = stat[:, 2 * i + 1:2 * i + 2]
            kb.stt(ncx.junk[:], xap, 1.0, xap, ALU.mult, ALU.mult, [xbuf], [ncx.junk, (stat, i)], accum_out=ss)
            kb.ts("dve", ss, ss, 1.0 / D, EPS, ALU.mult, ALU.add, [(stat, i)], [(stat, i)])
            kb.act(ss, ss, AF.Sqrt, [(stat, i)], [(stat, i)])
            kb.recip(rs, ss, [(stat, i)], [(stat, i)])
            return rs, i

        def norm_to_T(ncx, xbuf, xap, dstT, t, ntok_off, gB=None):
            rs, i = tile_rstd(ncx, xbuf, xap)
            nb = ncx.xnb[t % 2]
            if gB is None:
                kb.ts("dve", nb[:], xap, rs, None, ALU.mult, None, [xbuf, (stat, i)], [nb])
            else:
                kb.stt(nb[:], xap, rs, gB[:], ALU.mult, ALU.mult, [xbuf, (stat, i), gB], [nb])
            bk = kb.bank()
            bv = bk[:].bitcast(BF16)
            for c in range(8):
                kb.tr(bv[:, c * 128:(c + 1) * 128], nb[:, c * 128:(c + 1) * 128], identb[:], [nb, identb], [bk])
            kb.cp("act", dstT[:, :, ntok_off:ntok_off + 128], bv.rearrange("p (c n) -> p c n", c=8), [bk], [(dstT, t)])

        def norm_to_T_gen(ncx, xbuf, xap, dstT, t, ntok_off, bk, bi, gB=None):
            rs, i = tile_rstd(ncx, xbuf, xap)
            yield
            nb = ncx.xnb[bi]
            if gB is None:
                kb.ts("dve", nb[:], xap, rs, None, ALU.mult, None, [xbuf, (stat, i)], [nb])
            else:
                kb.stt(nb[:], xap, rs, gB[:], ALU.mult, ALU.mult, [xbuf, (stat, i), gB], [nb])
            yield
            bv = bk[:].bitcast(BF16)
            for c in range(8):
                kb.tr(bv[:, c * 128:(c + 1) * 128], nb[:, c * 128:(c + 1) * 128], identb[:], [nb, identb], [bk])
            yield
            kb.cp("act", dstT[:, :, ntok_off:ntok_off + 128], bv.rearrange("p (c n) -> p c n", c=8), [bk], [(dstT, t)])
            yield

        with ExitStack() as pa:
            ncxs = [NormCtx(pa, 2), NormCtx(pa, 2)]
            gBa = kb.sb("gBa", [128, 1024], F32, pa)
            kb.dma("sp", gBa[:], hawk_norm_v.partition_broadcast(128), [], [gBa])

            def a_lane(L):
                ncx = ncxs[L]
                for t in range(L, NT_ + 2, 2):
                    xs = ncx.xstage[(t // 2) % 2]
                    if t < NT_:
                        kb.dma("sp", xs[:], x_d[t * 128:(t + 1) * 128, :], [DX], [xs])
                        yield from norm_to_T_gen(ncx, xs, xs[:], xnT, t, t * 128, kb.banks[2 * L + (t // 2) % 2], (t // 2) % 2, gB=gBa)
                    else:
                        tm = t - NT_
                        kb.dma("sp", xs[:], mem_d[tm * 128:(tm + 1) * 128, :], [], [xs])
                        yield from norm_to_T_gen(ncx, xs, xs[:], memnT, tm, tm * 128, kb.banks[2 * L + (t // 2) % 2], (t // 2) % 2)

            run_lanes([a_lane(0), a_lane(1)])
            S.barrier()
            ck("A")

        def make_loader(w_in_d, gain, nslots, stack):
            wslots = [kb.sb(f"wslot{i}", [128, 8, 128], BF16, stack) for i in range(nslots)]
            rr = [0]
            pre = {}

            def raw(c0, n, q, into, off):
                if into is None:
                    wsl = wslots[rr[0]]
                    rr[0] = (rr[0] + 1) % nslots
                else:
                    wsl = into
                load_w(wsl, wsl[:, :, off:off + n], win_cols(w_in_d, c0, n), n, gain=None, q=q, key=off)
                return wsl

            def load_win(c0, n=128, q="sp", into=None, off=0):
                if into is None and (c0, n) in pre:
                    return pre.pop((c0, n))
                return raw(c0, n, q, into, off)

            def prefetch(c0, n=128):
                pre[(c0, n)] = raw(c0, n, "sp", None, 0)
            load_win.prefetch = prefetch
            return load_win

        def proj_fm(wsl, n, evac, woff=0):
            for tc in range(4):
                bk = kb.bank()
                for dc in range(8):
                    kb.mm(bk[0:n, :], wsl[:, dc, woff:woff + n], xnT[:, dc, tc * 512:(tc + 1) * 512], dc == 0, dc == 7,
                          [wsl, xnT], [bk])
                evac(bk, bk[0:n, :], tc)

        def mem_kv(w_kv_d, gain, kmT, vm, stack):
            wkv = kb.sb("wkv", [128, 8, 512], BF16, stack)
            for j in range(4):
                load_w(wkv, wkv[:, :, j * 128:(j + 1) * 128], win_cols(w_kv_d, j * 128, 128), 128, gain=gain, key=j)
            for h in range(4):
                bk = kb.bank()
                for dc in range(8):
                    kb.mm(bk[0:64, 0:256], wkv[:, dc, h * 64:(h + 1) * 64], memnT[:, dc, :], dc == 0, dc == 7,
                          [wkv, memnT], [bk])
                kb.cp("act", kmT[0:64, h, :], bk[0:64, 0:256], [bk], [(kmT, h)])
            for mt in range(2):
                bk = kb.bank()
                for dc in range(8):
                    kb.mm(bk[:, 0:256], memnT[:, dc, mt * 128:(mt + 1) * 128], wkv[:, dc, 256:512], dc == 0, dc == 7,
                          [wkv, memnT], [bk])
                kb.cp("act", vm[:, mt, :], bk[:, 0:256], [bk], [(vm, mt)])

        def mem_attn(load_win, colq, colz, kmT, vm, ymT, stack):
            qmTs = [kb.sb(f"qmT{i}", [64, S_LEN], BF16, stack) for i in range(2)]
            szms = [kb.sb(f"szm{i}", [64, S_LEN], BF16, stack) for i in range(2)]
            PTm = [kb.sb(f"PTm{i}", [128, 512], BF16, stack) for i in range(4)]
            rdm = [kb.sb(f"rdm{i}", [64, 512], F32, stack) for i in range(2)]

            def lane(h, L):
                qmT, szm = qmTs[L], szms[L]
                b0, b1, b2, b3 = [kb.banks[4 * L + j] for j in range(4)]
                wq = load_win(colq + h * 64, 64)
                wz = load_win(colz + h * 64, 64)
                for (wsl, dst, func) in ((wq, qmT, None), (wz, szm, AF.Silu)):
                    for tc in range(4):
                        bk = b0 if tc % 2 == 0 else b1
                        for dc in range(8):
                            kb.mm(bk[0:64, :], wsl[:, dc, 0:64], xnT[:, dc, tc * 512:(tc + 1) * 512], dc == 0, dc == 7, [wsl, xnT], [bk])
                        yield
                        if func is None:
                            kb.cp("act", dst[0:64, tc * 512:(tc + 1) * 512], bk[0:64, :], [bk], [(dst, tc)])
                        else:
                            kb.act(dst[0:64, tc * 512:(tc + 1) * 512], bk[0:64, :], func, [bk], [(dst, tc)])
                        yield
                for tc in range(4):
                    pts = []
                    for mt in range(2):
                        bk = b0 if mt == 0 else b1
                        kb.mm(bk[:, :], kmT[0:64, h, mt * 128:(mt + 1) * 128], qmT[0:64, tc * 512:(tc + 1) * 512],
                              True, True, [kmT, (qmT, tc)], [bk])
                        pt = PTm[2 * L + mt]
                        kb.act(pt[:], bk[:], AF.Exp, [bk], [pt], scale=0.125)
                        pts.append(pt)
                        yield
                    for mt in range(2):
                        kb.mm(b2[0:64, :], vm[:, mt, h * 64:(h + 1) * 64], pts[mt][:], mt == 0, mt == 1, [vm, pts[mt]], [b2])
                    for mt in range(2):
                        kb.mm(b3[0:64, :], onesb[:, 0:64], pts[mt][:], mt == 0, mt == 1, [onesb, pts[mt]], [b3])
                    yield
                    rd = rdm[L]
                    kb.recip(rd[:], b3[0:64, :], [b3], [rd])
                    kb.tt("dve", rd[:], b2[0:64, :], rd[:], ALU.mult, [b2, rd], [rd])
                    yield
                    kb.tt("dve", ymT[0:64, h, tc * 512:(tc + 1) * 512], rd[:], szm[0:64, tc * 512:(tc + 1) * 512], ALU.mult,
                          [rd, (szm, tc)], [(ymT, (h, tc))])
                    yield

            run_lanes([lane(0, 0), lane(1, 1)])
            run_lanes([lane(2, 0), lane(3, 1)])

        def out_proj(w_out_d, nch, yTl, ymT, resid_d, resid_buf, final, stack, dbg=False):
            ysrc = []
            for (yb_, n_) in yTl:
                for ci in range(n_):
                    ysrc.append((yb_, ci))
            ncxs = [NormCtx(stack, 1), NormCtx(stack, 1)]
            WO = kb.sb("WO", [128, nch, 1024], BF16, stack)
            WOm = kb.sb("WOm", [64, 4, 1024], BF16, stack)
            wo_v = w_out_d[0:nch * 128, :].rearrange("(c p) n -> p c n", p=128)
            for c in range(0, nch, 4):
                load_w(WO, WO[:, c:c + 4, :], wo_v[:, c:c + 4, :], 1024, key=c)
            wom_v = w_out_d[nch * 128:nch * 128 + 256, :].rearrange("(h p) n -> p h n", p=64)
            load_w(WOm, WOm[0:64, :, :], wom_v, 1024, key=0, part=64)
            x1t = [kb.sb(f"x1t{i}", [128, 1024], F32, stack) for i in range(2)]
            if not final:
                gNx = kb.sb("gNx", [128, 1024], F32, stack)
                kb.dma("sp", gNx[:], nsa_norm_v.partition_broadcast(128), [], [gNx])
            if final:
                gF = kb.sb("gF", [128, 1024], F32, stack)
                kb.dma("sp", gF[:], final_g.partition_broadcast(128), [], [gF])
                ot = [kb.sb(f"ot{i}", [128, 1024], F32, stack) for i in range(2)]

            def lane(L):
                ncx = ncxs[L]
                bks = [kb.banks[4 * L + j] for j in range(4)]
                for t in range(L, NT_, 2):
                    xs = ncx.xstage[0]
                    kb.dma("sp", xs[:], resid_d[t * 128:(t + 1) * 128, :], [resid_buf], [xs])
                    x1 = x1t[L]
                    for half in range(2):
                        bk = bks[half]
                        for c in range(nch):
                            yb_, ci = ysrc[c]
                            kb.mm(bk[:, :], yb_[:, ci, t * 128:(t + 1) * 128], WO[:, c, half * 512:(half + 1) * 512], c == 0, False,
                                  [yb_, WO], [bk])
                            if c % 4 == 3:
                                yield
                        for h in range(4):
                            kb.mm(bk[:, :], ymT[0:64, h, t * 128:(t + 1) * 128], WOm[0:64, h, half * 512:(half + 1) * 512], False, h == 3,
                                  [ymT, WOm], [bk])
                        yield
                        kb.tt("dve", x1[:, half * 512:(half + 1) * 512], xs[:, half * 512:(half + 1) * 512], bk[:], ALU.add,
                              [xs, bk], [(x1, half)])
                        yield
                    if not final:
                        kb.dma("sp", x1_scr[t * 128:(t + 1) * 128, :], x1[:], [x1], [DX1])
                        yield from norm_to_T_gen(ncx, x1, x1[:], xnT, t, t * 128, bks[2], 0, gB=gNx)
                        if dbg:
                            kb.dma("sp", out_d[t * 128:(t + 1) * 128, :], x1[:], [x1], [DOUT])
                    else:
                        rs, i = tile_rstd(ncx, x1, x1[:])
                        yield
                        o = ot[L]
                        kb.stt(o[:], x1[:], rs, gF[:], ALU.mult, ALU.mult, [x1, (stat, i), gF], [o])
                        yield
                        kb.dma("sp", out_d[t * 128:(t + 1) * 128, :], o[:], [o], [DOUT])
                        yield

            run_lanes([lane(0), lane(1)])

        with ExitStack() as l0:
            yTa = kb.sb("yTa", [128, 8, S_LEN], BF16, l0)
            ymT = kb.sb("ymT", [64, 4, S_LEN], BF16, l0)
            gH = kb.sb("gH", [128, 8, 128], F32, l0)
            kb.dma("sp", gH[:], g_hawk, [], [gH])
            load_win = make_loader(hawk_w_in, gH, 7, l0)
            kmT = kb.sb("kmT", [64, 4, 256], BF16, l0)
            vm = kb.sb("vm", [128, 2, 256], BF16, l0)
            with ExitStack() as pm:
                gHm = kb.sb("gHm", [128, 8, 128], F32, pm)
                kb.dma("sp", gHm[:], g_hawk_mem, [], [gHm])
                mem_kv(hawk_w_mem_kv, gHm, kmT, vm, pm)
                for c0_ in (7168, 7424, 7168 + 64, 7424 + 64):
                    load_win.prefetch(c0_, 64)
                S.barrier()
                ck("memkv0")
            with ExitStack() as pd:
                mem_attn(load_win, 7168, 7424, kmT, vm, ymT, pd)
                for c0_ in (0, 1024, 128, 1024 + 128):
                    load_win.prefetch(c0_, 128)
                S.barrier()
                ck("mem0")

            with ExitStack() as pb:
                lv = kb.sb("lv", [128, 8, 8], F32, pb)
                cvec = kb.sb("cvec", [128, 8, 2], F32, pb)
                bda = kb.sb("bda", [128, 8, 128], BF16, pb)
                bdx = kb.sb("bdx", [128, 8, 128], BF16, pb)
                kb.dma("sp", lv[:], lru_vec, [], [lv])
                load_w(bda, bda[:], bd_a, 128)
                load_w(bdx, bdx[:], bd_x, 128)
                kb.act(cvec[:, :, 0], lv[:, :, 7], AF.Exp, [lv], [cvec], scale=-1.0)
                kb.act(cvec[:, :, 0], cvec[:, :, 0], AF.Ln, [cvec], [cvec], bias=1.0)
                kb.ts("dve", cvec[:, :, 1], cvec[:, :, 0], -16.0, None, ALU.mult, None, [cvec], [cvec])
                kb.ts("dve", cvec[:, :, 0], cvec[:, :, 0], -8.0, None, ALU.mult, None, [cvec], [cvec])
                sets = []
                for L in range(2):
                    sets.append(dict(
                        B1=kb.sb(f"B1_{L}", [128, S_LEN + 4], F32, pb), B2=kb.sb(f"B2_{L}", [128, S_LEN], F32, pb),
                        B3=kb.sb(f"B3_{L}", [128, S_LEN], F32, pb), B4=kb.sb(f"B4_{L}", [128, S_LEN], F32, pb),
                        xcb=kb.sb(f"xcb_{L}", [128, S_LEN], BF16, pb), sz=kb.sb(f"sz_{L}", [128, S_LEN], BF16, pb)))

                def lru_lane(L):
                    st_ = sets[L]
                    B1, B2, B3, B4, xcb, sz = st_["B1"], st_["B2"], st_["B3"], st_["B4"], st_["xcb"], st_["sz"]
                    bks = [kb.banks[4 * L + j] for j in range(4)]
                    for c in range(L, 8, 2):
                        wxa = load_win(c * 128)
                        wza = load_win(1024 + c * 128)
                        kb.memset("dve", B1[:, 0:3], 0.0, [(B1, "pad")])
                        for (wsl, which) in ((wxa, 0), (wza, 1)):
                            for tc in range(4):
                                bk = bks[tc % 4]
                                for dc in range(8):
                                    kb.mm(bk[:, :], wsl[:, dc, 0:128], xnT[:, dc, tc * 512:(tc + 1) * 512], dc == 0, dc == 7, [wsl, xnT], [bk])
                                yield
                                if which == 0:
                                    kb.cp("act", B1[:, 3 + tc * 512:3 + (tc + 1) * 512], bk[:], [bk], [(B1, tc)])
                                else:
                                    kb.act(sz[:, tc * 512:(tc + 1) * 512], bk[:], AF.Silu, [bk], [(sz, tc)])
                                yield
                        kb.ts("dve", B2[:], B1[:, 0:S_LEN], lv[:, c, 0:1], lv[:, c, 4:5], ALU.mult, ALU.add, [B1, lv], [B2])
                        yield
                        for k in range(1, 4):
                            kb.stt(B2[:], B1[:, k:k + S_LEN], lv[:, c, k:k + 1], B2[:], ALU.mult, ALU.add, [B1, lv, B2], [B2])
                            yield
                        kb.cp("dve", xcb[:], B2[:], [B2], [xcb])
                        yield
                        for (bd, dstb, col) in ((bda, B1, 5), (bdx, B4, 6)):
                            for tc in range(4):
                                bk = bks[tc % 4]
                                kb.mm(bk[:, :], bd[:, c, :], xcb[:, tc * 512:(tc + 1) * 512], True, True, [bd, xcb], [bk])
                                yield
                                kb.act(dstb[:, tc * 512:(tc + 1) * 512], bk[:], AF.Sigmoid, [bk, lv], [(dstb, tc)], bias=lv[:, c, col:col + 1])
                                yield
                        r_ap = B1[:, 0:S_LEN]
                        kb.act(B3[:], r_ap, AF.Exp, [B1, cvec], [B3], scale=cvec[:, c, 0:1])
                        yield
                        kb.act(r_ap, r_ap, AF.Exp, [B1, cvec], [B1], scale=cvec[:, c, 1:2])
                        yield
                        kb.tt("dve", B2[:], B2[:], B4[:], ALU.mult, [B2, B4], [B2])
                        yield
                        kb.ts("dve", r_ap, r_ap, -1.0, 1.0, ALU.mult, ALU.add, [B1], [B1])
                        yield
                        kb.ts("dve", r_ap, r_ap, 0.0, None, ALU.max, None, [B1], [B1])
                        yield
                        kb.act(r_ap, r_ap, AF.Sqrt, [B1], [B1])
                        kb.memset("dve", B1[:, 0:1], 1.0, [B1])
                        yield
                        kb.tt("dve", B2[:], B2[:], r_ap, ALU.mult, [B2, B1], [B2])
                        yield
                        S.op("dve", lambda e, B4=B4, B3=B3, B2=B2: e.tensor_tensor_scan(out=B4[:], data0=B3[:], data1=B2[:], initial=0.0,
                                                                                      op0=ALU.mult, op1=ALU.add), reads=[B3, B2], writes=[B4])
                        yield
                        kb.tt("dve", yTa[:, c, :], B4[:], sz[:], ALU.mult, [B4, sz], [(yTa, c)])
                        yield

                run_lanes([lru_lane(0), lru_lane(1)])
                for c0_ in (2048 + 4608, 2048, 2048 + 1536, 2048 + 3072):
                    load_win.prefetch(c0_, 128)
                S.barrier()
                ck("lru")
            yTb = kb.sb("yTb", [128, 4, S_LEN], BF16, l0)

            with ExitStack() as pc:
                edil = kb.sb("edil", [128, 12, 256], F32, pc)
                kb.dma("sp", edil[:], edil_d, [], [edil])
                qTs = [kb.sb(f"qT{i}", [128, S_LEN], BF16, pc) for i in range(2)]
                kTs = [kb.sb(f"kT{i}", [128, S_LEN], BF16, pc) for i in range(2)]
                vTs = [kb.sb(f"vT{i}", [128, S_LEN], BF16, pc) for i in range(2)]
                Vps = [kb.sb(f"Vp{i}", [128, 16, 128], BF16, pc) for i in range(2)]
                szbs = [kb.sb(f"szb{i}", [128, S_LEN], BF16, pc) for i in range(2)]
                NTa = kb.sb("NTa", [128, S_LEN], F32, pc)
                DBa = kb.sb("DBa", [128, S_LEN], F32, pc)
                Pf = [kb.sb(f"Pf{i}", [128, 256], F32, pc) for i in range(3)]
                PT = [kb.sb(f"PT{i}", [128, 256], BF16, pc) for i in range(3)]
                sc_d = 128.0 ** -0.5
                items = [(hs, g) for hs in range(4) for g in range(3)]
                prot = [0]

                def pbank():
                    b = kb.banks[6 + prot[0]]
                    prot[0] ^= 1
                    return b

                def dtoks(d, r, b):
                    t0 = r + d * 128 * b
                    return slice(t0, t0 + d * 127 + 1, d)

                def proj_fm_lane(wsl, dst, func):
                    for tc in range(4):
                        bk = pbank()
                        for dc in range(8):
                            kb.mm(bk[:, :], wsl[:, dc, 0:128], xnT[:, dc, tc * 512:(tc + 1) * 512], dc == 0, dc == 7, [wsl, xnT], [bk])
                        yield
                        if func is None:
                            kb.cp("act", dst[:, tc * 512:(tc + 1) * 512], bk[:], [bk], [(dst, tc)])
                        else:
                            kb.act(dst[:, tc * 512:(tc + 1) * 512], bk[:], func, [bk], [(dst, tc)])
                        yield

                def task_proj(i):
                    hs, g = items[i]
                    win, d = DIL_GROUPS[g]
                    hh = g * 4 + hs
                    nqb = (S_LEN // d) // 128
                    s = i % 2
                    if g == 0:
                        wz = load_win(2048 + 4608 + hs * 128)
                        yield from proj_fm_lane(wz, szbs[hs % 2], AF.Silu)
                    wq = load_win(2048 + hh * 128)
                    yield from proj_fm_lane(wq, qTs[s], None)
                    wk = load_win(2048 + 1536 + hh * 128)
                    yield from proj_fm_lane(wk, kTs[s], None)
                    wv = load_win(2048 + 3072 + hh * 128)
                    yield from proj_fm_lane(wv, vTs[s], None)
                    for j in range(4):
                        bk = pbank()
                        bv = bk[:].bitcast(BF16)
                        for k in range(4):
                            r, b = divmod(4 * j + k, nqb)
                            kb.tr(bv[:, k * 128:(k + 1) * 128], vTs[s][:, dtoks(d, r, b)], identb[:], [vTs[s], identb], [bk])
                        yield
                        kb.cp("act", Vps[s][:, 4 * j:4 * j + 4, :], bv[:, 0:512].rearrange("p (k n) -> p k n", k=4), [bk], [(Vps[s], j)])
                        yield

                def task_tile(i, ti, lane):
                    hs, g = items[i]
                    win, d = DIL_GROUPS[g]
                    hh = g * 4 + hs
                    nqb = (S_LEN // d) // 128
                    s = i % 2
                    qT, kT, Vp = qTs[s], kTs[s], Vps[s]
                    r, qb = divmod(ti, nqb)
                    qs = dtoks(d, r, qb)
                    kbs = [qb - 1, qb] if qb > 0 else [qb]
                    sbk = kb.banks[2 * lane]
                    ndb = kb.banks[2 * lane + 1]
                    for kbi in kbs:
                        typ = 0 if kbi < qb else 1
                        kb.mm(sbk[:, typ * 128:(typ + 1) * 128], kT[:, dtoks(d, r, kbi)], qT[:, qs], True, True, [kT, qT], [sbk])
                    yield
                    lo = 0 if qb > 0 else 128
                    pf = Pf[lane]
                    pt = PT[lane]
                    kb.act(pf[:, lo:256], sbk[:, lo:256], AF.Exp, [sbk], [pf], scale=sc_d)
                    yield
                    kb.tt("dve", pt[:, lo:256], pf[:, lo:256], edil[:, hh, lo:256], ALU.mult, [pf, edil], [pt])
                    yield
                    for j, kbi in enumerate(kbs):
                        typ = 0 if kbi < qb else 1
                        kb.mm(ndb[:, 0:128], Vp[:, r * nqb + kbi, :], pt[:, typ * 128:(typ + 1) * 128], j == 0, j == len(kbs) - 1,
                              [Vp, pt], [ndb])
                    for j, kbi in enumerate(kbs):
                        typ = 0 if kbi < qb else 1
                        kb.mm(ndb[:, 128:256], onesb[:], pt[:, typ * 128:(typ + 1) * 128], j == 0, j == len(kbs) - 1,
                              [onesb, pt], [ndb])
                    yield
                    if g == 0:
                        kb.cp("dve", NTa[:, qs], ndb[:, 0:128], [ndb], [NTa])
                        kb.cp("dve", DBa[:, qs], ndb[:, 128:256], [ndb], [DBa])
                    else:
                        kb.tt("dve", NTa[:, qs], NTa[:, qs], ndb[:, 0:128], ALU.add, [ndb, NTa], [NTa])
                        kb.tt("dve", DBa[:, qs], DBa[:, qs], ndb[:, 128:256], ALU.add, [ndb, DBa], [DBa])
                    yield

                def tile_lane(i, lane):
                    for ti in range(lane, 16, 3):
                        yield from task_tile(i, ti, lane)

                run_lanes([task_proj(0)])
                for i in range(len(items)):
                    hs, g = items[i]
                    lanes = [tile_lane(i, 0), tile_lane(i, 1), tile_lane(i, 2)]
                    if i + 1 < len(items):
                        lanes.append(task_proj(i + 1))
                    run_lanes(lanes)
                    if g == 2:
                        kb.recip(DBa[:], DBa[:], [DBa], [DBa])
                        kb.tt("dve", NTa[:], NTa[:], DBa[:], ALU.mult, [NTa, DBa], [NTa])
                        kb.tt("dve", yTb[:, hs, :], NTa[:], szbs[hs % 2][:], ALU.mult, [NTa, szbs[hs % 2]], [(yTb, hs)])
                S.barrier()
                ck("dil")

            with ExitStack() as pe_:
                out_proj(hawk_w_out, 12, [(yTa, 8), (yTb, 4)], ymT, x_d, DX, False, pe_, dbg=(stop_after == "l0"))
                S.barrier()
                ck("l0end")

        if stop_after != "l0":
          with ExitStack() as l1:
            yT1 = kb.sb("yT1", [128, 8, S_LEN], BF16, l1)
            ymT1 = kb.sb("ymT1", [64, 4, S_LEN], BF16, l1)
            gN = kb.sb("gN", [128, 8, 128], F32, l1)
            kb.dma("sp", gN[:], g_nsa, [], [gN])
            load_win = make_loader(nsa_w_in, gN, 4, l1)
            kcmpT = kb.sb("kcmpT", [64, 2, 128], BF16, l1)
            vcmp = kb.sb("vcmp", [128, 2, 64], BF16, l1)
            gates = kb.sb("gates", [128, 16, 48], F32, l1)
            with ExitStack() as pmm:
                kmT = kb.sb("kmT1", [64, 4, 256], BF16, pmm)
                vm = kb.sb("vm1", [128, 2, 256], BF16, pmm)
                with ExitStack() as pm:
                    gNm = kb.sb("gNm", [128, 8, 128], F32, pm)
                    kb.dma("sp", gNm[:], g_nsa_mem, [], [gNm])
                    mem_kv(nsa_w_mem_kv, gNm, kmT, vm, pm)
                    for c0_ in (2864, 3120, 2864 + 64, 3120 + 64):
                        load_win.prefetch(c0_, 64)
                    S.barrier()
                    ck("memkv1")
                with ExitStack() as pd:
                    mem_attn(load_win, 2864, 3120, kmT, vm, ymT1, pd)
                    load_win.prefetch(1792, 48)
                    load_win.prefetch(1024, 128)
                    load_win.prefetch(1024 + 128, 128)
                    S.barrier()
                    ck("mem1")

            with ExitStack() as pq:
                wg = load_win(1792, 48)
                for t in range(NT_):
                    bk = kb.bank()
                    for dc in range(8):
                        kb.mm(bk[:, 0:48], xnT[:, dc, t * 128:(t + 1) * 128], wg[:, dc, 0:48], dc == 0, dc == 7, [xnT, wg], [bk])
                    kb.act(gates[:, t, :], bk[:, 0:48], AF.Sigmoid, [bk], [(gates, t)])
                kcT = kb.sb("kcT", [128, S_LEN], BF16, pq)
                vcT = kb.sb("vcT", [128, S_LEN], BF16, pq)
                wkc = load_win(1024)
                proj_fm(wkc, 128, lambda bk, ap, tc: kb.cp("act", kcT[:, tc * 512:(tc + 1) * 512], ap, [bk], [(kcT, tc)]))
                wvc = load_win(1024 + 128)
                proj_fm(wvc, 128, lambda bk, ap, tc: kb.cp("act", vcT[:, tc * 512:(tc + 1) * 512], ap, [bk], [(vcT, tc)]))
                W1 = kb.sb("W1", [128, 32, 256], BF16, pq)
                w2 = kb.sb("w2", [128, 2, 64], BF16, pq)
                peS = kb.sb("peS", [64, 2, 32], F32, pq)
                peb = kb.sb("peb", [64, 2, 32], BF16, pq)
                hidT = kb.sb("hidT", [128, 2, 128], BF16, pq)
                cb = kb.sb("cb", [128, 2], F32, pq)
                kb.dma("sp", peS[:], peT_d, [], [peS])
                kb.cp("dve", peb[:], peS[:], [peS], [peb])
                for kv in range(2):
                    w1d = w1k_d if kv == 0 else w1v_d
                    w2d = w2k_d if kv == 0 else w2v_d
                    srcT = kcT if kv == 0 else vcT
                    for p4 in range(2):
                        load_w(W1, W1[:, p4 * 16:(p4 + 1) * 16, :], w1d[:, p4 * 16:(p4 + 1) * 16, :], 256, key=p4)
                    load_w(w2, w2[:, :, :], w2d.rearrange("(hc p) d -> p hc d", p=128), 64)
                    for hc in range(2):
                        bk = kb.bank()
                        for p in range(32):
                            kb.mm(bk[:, 0:1], W1[0:64, p, hc * 128:(hc + 1) * 128], peb[0:64, kv, p:p + 1], p == 0, p == 31,
                                  [W1, peb], [bk])
                        kb.cp("dve", cb[:, hc:hc + 1], bk[:, 0:1], [bk], [(cb, hc)])
                    for g in range(2):
                        for hc in range(2):
                            bk = kb.bank()
                            for p in range(32):
                                kb.mm(bk[:, 0:127], W1[g * 64:(g + 1) * 64, p, hc * 128:(hc + 1) * 128],
                                      srcT[g * 64:(g + 1) * 64, p:p + 16 * 126 + 1:16], p == 0, p == 31, [W1, srcT], [bk])
                            kb.act(hidT[:, hc, 0:127], bk[:, 0:127], AF.Silu, [bk, cb], [(hidT, hc)], bias=cb[:, hc:hc + 1])
                        bk = kb.bank()
                        if kv == 0:
                            for hc in range(2):
                                kb.mm(bk[0:64, 0:127], w2[:, hc, :], hidT[:, hc, 0:127], hc == 0, hc == 1, [w2, hidT], [bk])
                            kb.cp("dve", kcmpT[0:64, g, 0:127], bk[0:64, 0:127], [bk], [(kcmpT, g)])
                        else:
                            for hc in range(2):
                                kb.mm(bk[0:127, 0:64], hidT[:, hc, 0:127], w2[:, hc, :], hc == 0, hc == 1, [w2, hidT], [bk])
                            kb.cp("dve", vcmp[0:127, g, :], bk[0:127, 0:64], [bk], [(vcmp, g)])
                S.barrier()
                ck("cmpkv")

            with ExitStack() as pg:
                QAg = kb.sb("QAg", [105, 8, S_LEN], BF16, pg)
                KAs = kb.sb("KAs", [105, S_LEN], BF16, pg)
                KAw = kb.sb("KAw", [105, S_LEN], BF16, pg)
                VAs = kb.sb("VAs", [128, 16, 128], BF16, pg)
                VAw = kb.sb("VAw", [128, 16, 128], BF16, pg)
                ecmp = kb.sb("ecmp", [128, 8, 247], F32, pg)
                Wz = kb.sb("Wz", [128, 8, 512], BF16, pg)
                m12 = kb.sb("m12", [128, 2, 62], F32, pg)
                trib = kb.sb("trib", [128, 2, 512], BF16, pg)
                kb.dma("sp", m12[:], m12_d, [], [m12])
                kb.dma("pool", trib[:], tri_d, [], [trib])
                for v, KA in enumerate((KAs, KAw)):
                    kb.dma("pool", KA[64:105, :], kaug_d[v], [], [(KA, "aug")])
                kb.memset("pool", VAs[:, :, 64:128], 1.0, [(VAs, "ones")])
                kb.memset("pool", VAw[:, :, 64:128], 1.0, [(VAw, "ones")])
                Pu = [kb.sb(f"Pu{i}", [128, 4, 128], F32, pg) for i in range(2)]
                Pub = [kb.sb("Pub0", [128, 4, 128], BF16, pg)] * 2
                pT = [kb.sb("pT0", [128, 4, 128], BF16, pg)] * 2
                for i in range(2):
                    kb.memset("pool", Pu[i][:], 0.0, [Pu[i]])
                psg = kb.sb("psg", [128, 128], F32, pg)
                den8 = kb.sb("den8", [128, 8], F32, pg)
                cg8 = kb.sb("cg8", [128, 8], F32, pg)
                imp = kb.sb("imp", [128, 32], F32, pg)
                impm = kb.sb("impm", [128, 32], F32, pg)
                m8 = kb.sb("m8", [128, 8], F32, pg)
                negp = kb.sb("negp", [128, 96], F32, pg)
                negS = kb.sb("negS", [96, 128], BF16, pg)
                kb.memset("pool", negp[:], 0.0, [negp])
                PTs = [kb.sb(f"PTs{i}", [128, 512], BF16, pg) for i in range(10)]
                pts_rr = [0]
                zerob = kb.sb("zerob", [128, 260], BF16, pg)
                kb.memset("pool", zerob[:], 0.0, [zerob])
                acs = [kb.sb(f"acs{i}", [128, 260], F32, pg) for i in range(2)]
                rd4 = [kb.sb(f"rd4{i}", [128, 4], F32, pg) for i in range(2)]
                cg4 = [kb.sb(f"cg4{i}", [128, 4], F32, pg) for i in range(2)]
                Oa = [kb.sb(f"Oa{i}", [128, 512], F32, pg) for i in range(2)]
                szt = kb.sb("szt", [128, 512], F32, pg)
                Ob = kb.sb("Ob", [128, 512], BF16, pg)
                accbanks = [kb.banks[0], kb.banks[1]]
                rot = [2]

                def rbank():
                    b = kb.banks[rot[0]]
                    rot[0] = rot[0] + 1 if rot[0] < 7 else 2
                    return b

                strot = [0, 0]

                def stbank(lane):
                    b = kb.banks[2 + 2 * lane + strot[lane]]
                    strot[lane] ^= 1
                    return b

                def mbank():
                    return kb.banks[6]

                ptrot = [0, 0]

                def next_pt(lane):
                    p = PTs[5 * lane + ptrot[lane]]
                    ptrot[lane] = (ptrot[lane] + 1) % 5
                    return p

                for g in range(2):
                    kb.dma("sp", ecmp[:], ecmp_d[:, g * 8:(g + 1) * 8, :], [], [ecmp])
                    for j in range(4):
                        load_w(Wz, Wz[:, :, j * 128:(j + 1) * 128], win_cols(nsa_w_in, 1840 + g * 512 + j * 128, 128), 128,
                               gain=None, key=j)
                    for pr in range(4):
                        wq = load_win((g * 8 + 2 * pr) * 64, 128)
                        for tc in range(4):
                            bk = rbank()
                            for dc in range(8):
                                kb.mm(bk[:, :], wq[:, dc, 0:128], xnT[:, dc, tc * 512:(tc + 1) * 512], dc == 0, dc == 7, [wq, xnT], [bk])
                            kb.cp("act", QAg[0:64, 2 * pr, tc * 512:(tc + 1) * 512], bk[0:64, :], [bk], [QAg])
                            stq = PTs[5 * (pr % 2) + tc]
                            kb.cp("dve", stq[64:128, :], bk[64:128, :], [bk], [stq])
                            kb.dma("sp", QAg[0:64, 2 * pr + 1, tc * 512:(tc + 1) * 512], stq[64:128, :], [stq], [QAg])
                    kb.dma("pool", QAg[96:105, :, :], qal_d[g * 8:(g + 1) * 8].rearrange("h r n -> r h n"), [], [QAg])
                    for KA, col in ((KAs, 1024 + 2 * 128 + g * 64), (KAw, 1024 + 4 * 128 + g * 64)):
                        wk = load_win(col, 64)
                        for tc in range(4):
                            bk = rbank()
                            for dc in range(8):
                                kb.mm(bk[0:64, :], wk[:, dc, 0:64], xnT[:, dc, tc * 512:(tc + 1) * 512], dc == 0, dc == 7, [wk, xnT], [bk])
                            kb.cp("act", KA[0:64, tc * 512:(tc + 1) * 512], bk[0:64, :], [bk], [(KA, tc)])
                    wv = load_win(1024 + 3 * 128 + g * 64, 64)
                    load_win(1024 + 5 * 128 + g * 64, 64, into=wv, off=64)
                    for t in range(NT_):
                        bk = rbank()
                        for dc in range(8):
                            kb.mm(bk[:, 0:128], xnT[:, dc, t * 128:(t + 1) * 128], wv[:, dc, 0:128], dc == 0, dc == 7, [xnT, wv], [bk])
                        kb.cp("act", VAs[:, t, 0:64], bk[:, 0:64], [bk], [(VAs, t)])
                        kb.cp("dve", VAw[:, t, 0:64], bk[:, 64:128], [bk], [(VAw, t)])

                    def task_C(qt):
                        qc = slice(qt * 128, (qt + 1) * 128)
                        O = Oa[qt % 2]
                        gq = gates[:, qt, :]
                        eoff = 120 - 8 * qt
                        ocb = kb.banks[7]
                        for b4 in range(2):
                            sbk = mbank()
                            for hl in range(4):
                                r = b4 * 4 + hl
                                kb.mm(sbk[:, hl * 128:hl * 128 + 127], QAg[0:64, r, qc], kcmpT[0:64, g, 0:127], True, True,
                                      [(QAg, qt), kcmpT], [sbk])
                            yield
                            pu = Pu[b4]
                            s3 = sbk[:].rearrange("p (h n) -> p h n", h=4)
                            kb.act(pu[:, :, 0:127], s3[:, :, 0:127], AF.Exp, [sbk], [pu], scale=0.125)
                            yield
                            kb.tt("dve", pu[:, :, 0:127], pu[:, :, 0:127], ecmp[:, b4 * 4:(b4 + 1) * 4, eoff:eoff + 127], ALU.mult,
                                  [pu, ecmp], [pu])
                            S.op("dve", lambda e, pu=pu, b4=b4: e.tensor_reduce(out=den8[:, b4 * 4:(b4 + 1) * 4], in_=pu[:, :, 0:127],
                                                                                axis=AX.X, op=ALU.add),
                                 reads=[pu], writes=[(den8, b4)])
                            yield
                            kb.ts("dve", den8[:, b4 * 4:(b4 + 1) * 4], den8[:, b4 * 4:(b4 + 1) * 4], 1e-30, None, ALU.max, None,
                                  [(den8, b4)], [(den8, b4)])
                            kb.recip(den8[:, b4 * 4:(b4 + 1) * 4], den8[:, b4 * 4:(b4 + 1) * 4], [(den8, b4)], [(den8, b4)])
                            pub = Pub[b4]
                            kb.cp("dve", pub[:], pu[:], [pu], [pub])
                            yield
                            for hl in range(4):
                                r = b4 * 4 + hl
                                if r == 0:
                                    kb.ts("dve", psg[:, :], pu[:, hl, :], den8[:, r:r + 1], None, ALU.mult, None, [pu, (den8, b4)], [psg])
                                else:
                                    kb.stt(psg[:, :], pu[:, hl, :], den8[:, r:r + 1], psg[:, :], ALU.mult, ALU.add,
                                           [pu, (den8, b4), psg], [psg])
                                if hl % 2 == 1:
                                    yield
                            tbk = mbank()
                            tv = tbk[:].bitcast(BF16)
                            for hl in range(4):
                                kb.tr(tv[0:127, hl * 128:(hl + 1) * 128], pub[:, hl, 0:127], identb[:], [pub, identb], [tbk])
                            yield
                            ptt = pT[b4]
                            kb.cp("act", ptt[0:127, :, :], tv[0:127, 0:512].rearrange("p (h n) -> p h n", h=4), [tbk], [ptt])
                            yield
                            for hl in range(4):
                                r = b4 * 4 + hl
                                kb.mm(ocb[:, r * 64:(r + 1) * 64], ptt[0:127, hl, :], vcmp[0:127, g, :], True, True, [ptt, vcmp], [ocb])
                            yield
                        kb.tt("dve", cg8[:], den8[:], gq[:, g * 24:g * 24 + 24:3], ALU.mult, [den8, gates], [cg8])
                        kb.tt("dve", O[:].rearrange("p (h d) -> p h d", h=8), ocb[:].rearrange("p (h d) -> p h d", h=8),
                              cg8[:, 0:8].unsqueeze(2).to_broadcast([128, 8, 64]), ALU.mult, [ocb, cg8], [O])
                        yield
                        S.op("dve", lambda e: e.tensor_reduce(out=imp[:, :], in_=psg[:].rearrange("p (j a) -> p j a", a=4),
                                                              axis=AX.X, op=ALU.add), reads=[psg], writes=[imp])
                        kb.tt("dve", imp[:, 1:32], imp[:, 1:32], psg[:, 3:127:4], ALU.add, [imp, psg], [imp])
                        yield
                        moff = 30 - 2 * qt
                        kb.tt("dve", impm[:], imp[:], m12[:, 0, moff:moff + 32], ALU.mult, [imp, m12], [impm])
                        kb.tt("dve", impm[:], impm[:], m12[:, 1, moff:moff + 32], ALU.add, [impm, m12], [impm])
                        kb.memset("dve", impm[:, 0:1], 1e6, [impm])
                        yield
                        S.op("dve", lambda e: e.max(out=m8[:], in_=impm[:]), reads=[impm], writes=[m8])
                        kb.ts("dve", negp[:, 64:96], impm[:], m8[:, 7:8], 1.0, ALU.is_ge, ALU.subtract, [impm, m8], [negp])
                        kb.ts("dve", negp[:, 64:96], negp[:, 64:96], NEGB, None, ALU.mult, None, [negp], [negp])
                        yield
                        tbk = mbank()
                        kb.tr(tbk[0:96, 0:128], negp[:, 0:96], identf[:], [negp, identf], [tbk])
                        yield
                        kb.cp("dve", negS[64:96, :], tbk[64:96, 0:128], [tbk], [negS])
                        kb.cp("dve", QAg[64:96, :, qc], negS[64:96, :].unsqueeze(1).to_broadcast([32, 8, 128]), [negS], [(QAg, qt)])
                        yield

                    def task_branch(qt, br, b4):
                        qc = slice(qt * 128, (qt + 1) * 128)
                        O = Oa[qt % 2]
                        gq = gates[:, qt, :]
                        KA, VA = (KAw, VAw) if br == 2 else (KAs, VAs)
                        kbs = list(range(max(0, qt - 4), qt + 1)) if br == 2 else list(range(0, qt + 1))
                        accb = kb.banks[b4]
                        pend = []
                        nk = len(kbs)
                        kb.mm(accb[:, 0:260], zerob[:, 0:128], zerob[:, 0:260], True, False, [zerob], [accb])

                        def do_pv(item):
                            pi, pk, ppt = item
                            for hl in range(4):
                                kb.mm(accb[:, hl * 65:(hl + 1) * 65], ppt[:, hl * 128:(hl + 1) * 128], VA[:, pk, 0:65], False,
                                      (pi == nk - 1) and hl == 3, [VA, ppt], [accb])

                        for idx, kbi in enumerate(kbs):
                            sbk = stbank(b4)
                            masks = []
                            if kbi == qt:
                                masks.append(0)
                            if br == 2 and kbi == qt - 4:
                                masks.append(1)
                            kb.mm(sbk[:, :], KA[0:105, kbi * 128:(kbi + 1) * 128], QAg[0:105, b4 * 4:(b4 + 1) * 4, qc],
                                  True, len(masks) == 0, [KA, (QAg, qt)], [sbk])
                            for mi, mv in enumerate(masks):
                                kb.mm(sbk[:, :], identb[:], trib[:, mv, :], False, mi == len(masks) - 1, [identb, trib], [sbk])
                            pt = next_pt(b4)
                            kb.act(pt[:], sbk[:], AF.Exp, [sbk], [pt], scale=0.125)
                            yield
                            pend.append((idx, kbi, pt))
                            if len(pend) > 3:
                                do_pv(pend.pop(0))
                                yield
                        while pend:
                            do_pv(pend.pop(0))
                            yield
                        kb.cp("dve", acs[b4][:], accb[:, 0:260], [accb], [acs[b4]])
                        yield
                        a3 = acs[b4][:].rearrange("p (h n) -> p h n", h=4)
                        kb.recip(rd4[b4][:], a3[:, :, 64], [acs[b4]], [rd4[b4]])
                        h0 = (g * 8 + b4 * 4) * 3 + br
                        kb.tt("dve", cg4[b4][:], rd4[b4][:], gq[:, h0:h0 + 10:3], ALU.mult, [rd4[b4], gates], [cg4[b4]])
                        yield
                        kb.tt("dve", a3[:, :, 0:64], a3[:, :, 0:64], cg4[b4][:, 0:4].unsqueeze(2).to_broadcast([128, 4, 64]), ALU.mult,
                              [acs[b4], cg4[b4]], [acs[b4]])
                        yield
                        ov = O[:, b4 * 256:(b4 + 1) * 256].rearrange("p (h d) -> p h d", h=4)
                        kb.tt("dve", ov, ov, a3[:, :, 0:64], ALU.add, [(O, b4), acs[b4]], [(O, b4)])
                        yield

                    def task_Z(qt):
                        qc = slice(qt * 128, (qt + 1) * 128)
                        O = Oa[qt % 2]
                        zb = mbank()
                        for dc in range(8):
                            kb.mm(zb[:, :], xnT[:, dc, qc], Wz[:, dc, :], dc == 0, dc == 7, [xnT, Wz], [zb])
                            if dc % 4 == 3:
                                yield
                        kb.act(szt[:], zb[:], AF.Silu, [zb], [szt])
                        yield
                        kb.tt("dve", Ob[:], O[:], szt[:], ALU.mult, [O, szt], [Ob])
                        yield
                        tbk = mbank()
                        tv = tbk[:].bitcast(BF16)
                        for c4 in range(4):
                            kb.tr(tv[:, c4 * 128:(c4 + 1) * 128], Ob[:, c4 * 128:(c4 + 1) * 128], identb[:], [Ob, identb], [tbk])
                        yield
                        kb.cp("act", yT1[:, g * 4:(g + 1) * 4, qc], tv[:, 0:512].rearrange("p (c n) -> p c n", c=4), [tbk], [(yT1, (g, qt))])
                        yield

                    run_lanes([task_C(0)])
                    for qt in range(NT_):
                        l1_ = chain(task_branch(qt, 2, 0), task_branch(qt, 1, 0))
                        l2_ = chain(task_branch(qt, 2, 1), task_branch(qt, 1, 1))
                        third = []
                        if qt > 0:
                            third.append(task_Z(qt - 1))
                        if qt + 1 < NT_:
                            third.append(task_C(qt + 1))
                        run_lanes([l1_, l2_, chain(*third)], [1, 1, L3_STEPS])
                    run_lanes([task_Z(NT_ - 1)])
                S.barrier()
                ck("nsa")

            with ExitStack() as pe_:
                out_proj(nsa_w_out, 8, [(yT1, 8)], ymT1, x1_scr, DX1, True, pe_)
                S.barrier()
                ck("l1end")

        S.dead = False
        S.barrier()
        with nc.Block() as block:
            S.emit(block)
        print("program ops:", S.nops, "sems:", S.nsem)
    return nc


_CONST = None


def prep_inputs(inp):
    global _CONST
    if _CONST is None:
        _CONST = host_constants()
        _CONST.update(host_constants_nsa())
    f = lambda a: np.ascontiguousarray(np.asarray(a, dtype=np.float32))
    shared = {
        "hawk_w_in": f(inp["hawk_w_in"][0]),
        "hawk_w_out": f(inp["hawk_w_out"][0]),
        "hawk_w_mem_kv": f(inp["hawk_w_mem_kv"][0]),
        "g_hawk": expand_gain(f(inp["hawk_norm"][0])),
        "g_hawk_mem": expand_gain(f(inp["hawk_mem_norm"][0])),
        "bd_a": block_diag(f(inp["hawk_gate_a_w"][0])),
        "bd_x": block_diag(f(inp["hawk_gate_x_w"][0])),
        "final_norm": f(inp["final_norm"]),
        "hawk_norm_v": f(inp["hawk_norm"][0]),
        "nsa_norm_v": f(inp["nsa_norm"][0]),
        "nsa_w_in": f(inp["nsa_w_in"][0]),
        "nsa_w_out": f(inp["nsa_w_out"][0]),
        "nsa_w_mem_kv": f(inp["nsa_w_mem_kv"][0]),
        "g_nsa": expand_gain(f(inp["nsa_norm"][0])),
        "g_nsa_mem": expand_gain(f(inp["nsa_mem_norm"][0])),
        "w2k": f(inp["nsa_phi_k_w2"][0]),
        "w2v": f(inp["nsa_phi_v_w2"][0]),
    }
    for k in ("identf", "edil", "ecmp", "m12", "tri", "kaug", "qal"):
        shared[k] = _CONST[k]

    def w1_layout(w1):
        a = w1.reshape(32, 64, 256).transpose(1, 0, 2)
        return np.ascontiguousarray(np.concatenate([a, a], axis=0))
    shared["w1k"] = w1_layout(f(inp["nsa_phi_k_w1"][0]))
    shared["w1v"] = w1_layout(f(inp["nsa_phi_v_w1"][0]))
    shared["peT"] = np.ascontiguousarray(np.stack([f(inp["nsa_pe_k"][0]).T, f(inp["nsa_pe_v"][0]).T], axis=1))
    lv = np.zeros((128, 8, 8), np.float32)
    cw = f(inp["hawk_conv_w"][0])
    for k in range(4):
        lv[:, :, k] = vec_fm(cw[k])
    lv[:, :, 4] = vec_fm(f(inp["hawk_conv_b"][0]))
    lv[:, :, 5] = vec_fm(f(inp["hawk_gate_a_b"][0]).reshape(-1))
    lv[:, :, 6] = vec_fm(f(inp["hawk_gate_x_b"][0]).reshape(-1))
    lv[:, :, 7] = vec_fm(f(inp["hawk_lambda"][0]))
    shared["lru_vec"] = lv
    x = f(inp["x"])
    mem = f(inp["mem"])
    maps = []
    for b in range(x.shape[0]):
        m = dict(shared)
        m["x"] = x[b]
        m["mem"] = mem[b]
        maps.append(m)
    return maps


def kernel(**inputs):
    maps = prep_inputs(inputs)
    nc = build_program()
    res = run_bass_kernel_spmd(nc, maps, core_ids=list(range(len(maps))))
    out = np.stack([np.asarray(r["out"], dtype=np.float32) for r in res.results], axis=0)
    return out
```

```python
import math
from contextlib import ExitStack

import numpy as np
import concourse.bass as bass
import concourse.mybir as mybir
from concourse.bass_utils import run_bass_kernel_spmd

F32 = mybir.dt.float32
BF16 = mybir.dt.bfloat16
AF = mybir.ActivationFunctionType
ALU = mybir.AluOpType
AX = mybir.AxisListType

S_LEN = 2048
D = 1024
NT_ = 16
EPS = 1e-6
DIL_GROUPS = ((128, 1), (512, 4), (2048, 16))

SEM_LIMIT = 30000
N_DMA_SEMS = 24
SAME_ENGINE_SYNC = True


class Buf:
    def __init__(self, name, t, excl=False):
        self.name = name
        self.t = t
        self.excl = excl
        self.st = {}

    def __getitem__(self, idx):
        return self.t[idx]


class Sync:
    def __init__(self, nc, stack):
        self.nc = nc
        self.stack = stack
        self.engs = ["pe", "act", "dve", "pool", "sp"]
        self.ops = {e: [] for e in self.engs}
        self.cur_sem = {}
        self.cnt = {}
        self.nsem = 0
        for e in self.engs:
            self._new_sem(e)
        self.dma_sems = {}
        self.dma_val = {}
        self.dma_rr = {}
        for e in ["sp", "pool", "act"]:
            self.dma_sems[e] = [self._alloc_sem(f"d{e}{i}") for i in range(N_DMA_SEMS)]
            self.dma_val[e] = [0] * N_DMA_SEMS
            self.dma_rr[e] = 0
        self.seen = {e: {} for e in self.engs}
        self.all_ticks = {}
        self.nops = 0
        self.dead = False
        self.eng_free = {e: 0.0 for e in self.engs}
        self.lane = None
        self.tnow = 0.0

    def _alloc_sem(self, name):
        self.nsem += 1
        return self.stack.enter_context(self.nc.semaphore(f"s_{name}_{self.nsem}"))

    def _new_sem(self, e):
        self.cur_sem[e] = self._alloc_sem(e)
        self.cnt[e] = 0

    def _states(self, buf, key, create):
        if key is None:
            if create and None not in buf.st:
                buf.st[None] = [None, {}]
            return list(buf.st.values())
        out = []
        if None in buf.st:
            out.append(buf.st[None])
        if key not in buf.st and create:
            buf.st[key] = [None, {}]
        if key in buf.st:
            out.append(buf.st[key])
        return out

    @staticmethod
    def _norm(lst):
        out = []
        for r in lst or []:
            out.append(r if isinstance(r, tuple) else (r, None))
        return out

    def op(self, eng, fn, reads=None, writes=None, dma=False, cost=0.5):
        if self.dead:
            return None
        reads = self._norm(reads)
        writes = self._norm(writes)
        ex = [(b, None) for (b, k) in reads + writes if b.excl]
        if ex:
            reads = [(b, k) for (b, k) in reads if not b.excl]
            writes = [(b, k) for (b, k) in writes if not b.excl]
            for bk in ex:
                if bk not in writes:
                    writes.append(bk)
        need = []
        for buf, key in reads:
            for st in self._states(buf, key, False):
                if st[0] is not None:
                    need.append(st[0])
        for buf, key in writes:
            for st in self._states(buf, key, False):
                if st[0] is not None:
                    need.append(st[0])
                need.extend(st[1].values())
        if dma:
            i = self.dma_rr[eng]
            self.dma_rr[eng] = (i + 1) % N_DMA_SEMS
            sem = self.dma_sems[eng][i]
            prev = self.dma_val[eng][i]
            if prev > 0:
                need.append((sem, prev, "dma", 0.0))
            if prev + 16 > SEM_LIMIT:
                sem = self._alloc_sem(f"d{eng}{i}")
                self.dma_sems[eng][i] = sem
                prev = 0
            val = prev + 16
            self.dma_val[eng][i] = val
            inc = 16
            tick = [sem, val, "dma", 0.0]
        else:
            if self.cnt[eng] + 1 > SEM_LIMIT:
                self._new_sem(eng)
            self.cnt[eng] += 1
            sem = self.cur_sem[eng]
            val = self.cnt[eng]
            inc = 1
            tick = [sem, val, eng, 0.0]
        ready = 0.0
        for nd in need:
            if nd[3] > ready:
                ready = nd[3]
        start = max(self.eng_free[eng], ready + 0.06)
        if dma:
            self.eng_free[eng] = start + 0.06
        else:
            self.eng_free[eng] = start + cost
        tick[3] = start + cost
        tick = tuple(tick)
        if self.lane is not None and tick[3] > self.lane.clock:
            self.lane.clock = tick[3]
        if tick[3] > self.tnow:
            self.tnow = tick[3]
        waits = {}
        seen = self.seen[eng]
        for (s, v, src, _fin) in need:
            if src == eng and (eng == "pe" or not SAME_ENGINE_SYNC):
                continue
            sid = id(s)
            if seen.get(sid, 0) >= v:
                continue
            if sid not in waits or waits[sid][1] < v:
                waits[sid] = (s, v)
        for sid, (s, v) in waits.items():
            seen[sid] = v
        self.ops[eng].append((list(waits.values()), fn, sem, inc))
        self.all_ticks[id(sem)] = (sem, val)
        self.nops += 1
        wset = set((id(b), k) for b, k in writes)
        for buf, key in reads:
            if (id(buf), key) in wset:
                continue
            self._states(buf, key, True)
            buf.st[key][1][eng if not dma else ("dma", id(sem))] = tick
        for buf, key in writes:
            if key is None:
                buf.st = {None: [tick, {}]}
            else:
                buf.st[key] = [tick, {}]
        return tick

    def barrier(self):
        if self.dead:
            return
        ticks = list(self.all_ticks.values())
        for e in self.engs:
            wl = []
            for (s, v) in ticks:
                if self.seen[e].get(id(s), 0) < v:
                    wl.append((s, v))
                    self.seen[e][id(s)] = v
            if wl:
                self.ops[e].append((wl, None, None, 0))

    def emit(self, block):
        S = self

        def run(engname, e):
            for (wl, fn, sem, inc) in S.ops[engname]:
                for (s, v) in wl:
                    e.wait_ge(s, v)
                if fn is not None:
                    fn(e).then_inc(sem, inc)

        @block.sync
        def _(e):
            run("sp", e)

        @block.tensor
        def _(e):
            run("pe", e)

        @block.scalar
        def _(e):
            run("act", e)

        @block.vector
        def _(e):
            run("dve", e)

        @block.gpsimd
        def _(e):
            run("pool", e)


class BankView:
    def __init__(self, pair, half):
        self.pair = pair
        self.off = 512 * half

    def __getitem__(self, idx):
        if not isinstance(idx, tuple):
            idx = (idx, slice(None))
        pr, col = idx
        cs = (col.start or 0) + self.off
        ce = (col.stop if col.stop is not None else 512) + self.off
        return self.pair[pr, cs:ce:col.step] if col.step else self.pair[pr, cs:ce]


class KB:
    def __init__(self, nc, stack):
        self.nc = nc
        self.gst = stack
        self.S = Sync(nc, stack)
        self.pairs = [stack.enter_context(nc.psum_tensor(f"pair{i}", [128, 1024], F32)) for i in range(4)]
        self.banks = [Buf(f"bank{i}", BankView(self.pairs[i // 2], i % 2), excl=True) for i in range(8)]
        self.bank_rr = 0
        self.uid = 0

    def sb(self, name, shape, dt, stack=None):
        self.uid += 1
        t = (stack or self.gst).enter_context(self.nc.sbuf_tensor(f"{name}_{self.uid}", shape, dt))
        return Buf(name, t)

    def bank(self):
        b = self.banks[self.bank_rr]
        self.bank_rr = (self.bank_rr + 1) % 8
        return b

    @staticmethod
    def fsz(ap):
        n = 1
        for s in ap.shape[1:]:
            n *= int(s)
        return n

    def vcost(self, eng, ap):
        n = self.fsz(ap)
        if eng == "pool":
            return 0.3 + n / 480.0
        if eng == "act":
            return 0.22 + n / 1400.0
        return 0.08 + n / 960.0

    def mm(self, out, lhsT, rhs, start, stop, r, w):
        c = max(self.fsz(rhs), 64) / 1600.0 + 0.04
        self.S.op("pe", lambda e: e.matmul(out, lhsT=lhsT, rhs=rhs, start=start, stop=stop), reads=r, writes=w, cost=c)

    def tr(self, out, in_, ident, r, w):
        self.S.op("pe", lambda e: e.transpose(out, in_, ident), reads=r, writes=w, cost=0.11)

    def act(self, out, in_, func, r, w, **kw):
        self.S.op("act", lambda e: e.activation(out=out, in_=in_, func=func, **kw), reads=r, writes=w, cost=self.vcost("act", out))

    def tt(self, eng, out, in0, in1, op, r, w):
        self.S.op(eng, lambda e: e.tensor_tensor(out=out, in0=in0, in1=in1, op=op), reads=r, writes=w, cost=self.vcost(eng, out))

    def ts(self, eng, out, in0, s1, s2, op0, op1, r, w, **kw):
        c = self.vcost(eng, out)
        if op1 is None:
            self.S.op(eng, lambda e: e.tensor_scalar(out=out, in0=in0, scalar1=s1, scalar2=None, op0=op0, **kw), reads=r, writes=w, cost=c)
        else:
            self.S.op(eng, lambda e: e.tensor_scalar(out=out, in0=in0, scalar1=s1, scalar2=s2, op0=op0, op1=op1, **kw), reads=r, writes=w, cost=c)

    def stt(self, out, in0, scalar, in1, op0, op1, r, w, **kw):
        self.S.op("dve", lambda e: e.scalar_tensor_tensor(out=out, in0=in0, scalar=scalar, in1=in1, op0=op0, op1=op1, **kw), reads=r, writes=w,
                  cost=0.12 + self.fsz(out) / 960.0)

    def cp(self, eng, out, in_, r, w):
        c = self.vcost(eng, out)
        if eng == "act":
            self.S.op("act", lambda e: e.activation(out=out, in_=in_, func=AF.Copy), reads=r, writes=w, cost=c)
        else:
            self.S.op(eng, lambda e: e.tensor_copy(out=out, in_=in_), reads=r, writes=w, cost=c)

    def memset(self, eng, ap, val, w):
        self.S.op(eng, lambda e: e.memset(ap, val), writes=w, cost=self.vcost(eng, ap))

    def recip(self, out, in_, r, w):
        self.S.op("dve", lambda e: e.reciprocal(out=out, in_=in_), reads=r, writes=w, cost=self.vcost("dve", out))

    def dma(self, q, out, in_, r, w):
        nbytes = self.fsz(out) * int(out.shape[0]) * 4
        self.S.op(q, lambda e: e.dma_start(out=out, in_=in_), reads=r, writes=w, dma=True, cost=2.0 + nbytes / 150000.0)


def alibi_slopes(n):
    return np.exp2(-8.0 * np.arange(1, n + 1) / n).astype(np.float32)


def host_constants():
    c = {}
    c["identf"] = np.eye(128, dtype=np.float32)
    sl = alibi_slopes(12)
    ik = np.arange(128)[:, None].astype(np.float64)
    iq = np.arange(128)[None, :].astype(np.float64)
    E = np.zeros((128, 12, 256), np.float32)
    for g, (win, dil) in enumerate(DIL_GROUPS):
        for hs in range(4):
            hh = g * 4 + hs
            s = float(sl[hh]) * dil
            dist_prev = 128 + iq - ik
            ok_prev = (dist_prev <= 128)
            E[:, hh, 0:128] = np.where(ok_prev, np.exp(-s * dist_prev), 0.0)
            dist_cur = iq - ik
            ok_cur = dist_cur >= 0
            E[:, hh, 128:256] = np.where(ok_cur, np.exp(-s * dist_cur), 0.0)
    c["edil"] = E
    return c


def expand_gain(g):
    return np.ascontiguousarray(np.broadcast_to(g.reshape(8, 128).T[:, :, None], (128, 8, 128))).astype(np.float32)


def vec_fm(v):
    return np.ascontiguousarray(v.reshape(8, 128).T).astype(np.float32)


def block_diag(gw):
    out = np.zeros((128, 8, 128), np.float32)
    for c in range(8):
        out[0:64, c, 0:64] = gw[2 * c]
        out[64:128, c, 64:128] = gw[2 * c + 1]
    return out


NEGB = 8192.0


def _bf16_split3(a):
    import ml_dtypes
    a = a.astype(np.float32)
    hi = a.astype(ml_dtypes.bfloat16).astype(np.float32)
    r1 = (a - hi).astype(np.float32)
    mid = r1.astype(ml_dtypes.bfloat16).astype(np.float32)
    r2 = (r1 - mid).astype(np.float32)
    lo = r2.astype(ml_dtypes.bfloat16).astype(np.float32)
    return hi, mid, lo


def host_constants_nsa():
    c = {}
    sl = alibi_slopes(16)
    i = np.arange(128)[:, None].astype(np.float64)
    m = np.arange(247)[None, :].astype(np.float64)
    dist = i - 16.0 * (m - 120.0) - 31.0
    E = np.zeros((128, 16, 247), np.float32)
    for h in range(16):
        E[:, h, :] = np.where(dist >= 0, np.exp(-float(sl[h]) * np.maximum(dist, 0.0)), 0.0)
    c["ecmp"] = E
    ii = np.arange(128)[:, None]
    rel = np.arange(62)[None, :] - 30
    cur = (ii >= 64).astype(np.int64)
    forced = (rel == cur) | (rel == cur - 1)
    future = rel > cur
    m1 = np.where(forced | future, 0.0, 1.0).astype(np.float32)
    m2 = np.where(forced, 1e6, np.where(future, -1e6, 0.0)).astype(np.float32)
    c["m12"] = np.ascontiguousarray(np.stack([m1, m2], axis=1))
    ik = np.arange(128)[:, None]
    iq = np.arange(128)[None, :]
    diag = np.where(ik > iq, -NEGB, 0.0).astype(np.float32)
    far = np.where(ik <= iq, -NEGB, 0.0).astype(np.float32)
    c["tri"] = np.ascontiguousarray(np.stack([np.tile(diag, (1, 4)), np.tile(far, (1, 4))], axis=1))
    k = np.arange(2048)
    kp = k - 1024
    hi = (np.floor(kp / 128.0) * 128.0).astype(np.float32)
    lo = (kp - hi).astype(np.float32)
    ka = np.zeros((2, 41, 2048), np.float32)
    for j in range(32):
        ka[0, j, :] = (k // 64 == j).astype(np.float32)
    for v in range(2):
        ka[v, 32:35, :] = 1.0
        ka[v, 35:38, :] = lo[None, :]
        ka[v, 38:41, :] = hi[None, :]
    c["kaug"] = ka
    qa = np.zeros((16, 9, 2048), np.float32)
    qp = (np.arange(2048) - 1024).astype(np.float32)
    for h in range(16):
        s8 = np.float32(8.0) * np.float32(sl[h])
        a = (-s8 * qp).astype(np.float32)
        ah, am, al = _bf16_split3(a)
        sh, sm, sl_ = _bf16_split3(np.full((2048,), s8, np.float32))
        qa[h, 0], qa[h, 1], qa[h, 2] = ah, am, al
        qa[h, 3], qa[h, 4], qa[h, 5] = sh, sm, sl_
        qa[h, 6], qa[h, 7], qa[h, 8] = sh, sm, sl_
    c["qal"] = qa
    return c


def chain(*gens):
    for g_ in gens:
        yield from g_


L3_STEPS = 1


def run_lanes(lanes, weights=None):
    active = list(lanes)
    w = {id(l): 1 for l in active}
    if weights:
        for l, wt in zip(lanes, weights):
            w[id(l)] = wt
    while active:
        for l in list(active):
            for _ in range(w[id(l)]):
                try:
                    next(l)
                except StopIteration:
                    active.remove(l)
                    break


def build_program(stop_after=None):
    nc = bass.Bass("TRN2", target_bir_lowering=False)

    ckstate = {}

    def ck(name):
        if stop_after == name:
            ckstate["S"].dead = True

    def din(name, shape):
        return nc.dram_tensor(name, list(shape), F32, kind="ExternalInput").ap()

    x_d = din("x", [S_LEN, D])
    mem_d = din("mem", [256, D])
    hawk_w_in = din("hawk_w_in", [D, 7680])
    hawk_w_out = din("hawk_w_out", [1792, D])
    hawk_w_mem_kv = din("hawk_w_mem_kv", [D, 512])
    g_hawk = din("g_hawk", [128, 8, 128])
    g_hawk_mem = din("g_hawk_mem", [128, 8, 128])
    lru_vec = din("lru_vec", [128, 8, 8])
    bd_a = din("bd_a", [128, 8, 128])
    bd_x = din("bd_x", [128, 8, 128])
    identf_d = din("identf", [128, 128])
    edil_d = din("edil", [128, 12, 256])
    final_g = din("final_norm", [D])
    hawk_norm_v = din("hawk_norm_v", [D])
    nsa_norm_v = din("nsa_norm_v", [D])
    nsa_w_in = din("nsa_w_in", [D, 3376])
    nsa_w_out = din("nsa_w_out", [1280, D])
    nsa_w_mem_kv = din("nsa_w_mem_kv", [D, 512])
    g_nsa = din("g_nsa", [128, 8, 128])
    g_nsa_mem = din("g_nsa_mem", [128, 8, 128])
    w1k_d = din("w1k", [128, 32, 256])
    w1v_d = din("w1v", [128, 32, 256])
    w2k_d = din("w2k", [256, 64])
    w2v_d = din("w2v", [256, 64])
    peT_d = din("peT", [64, 2, 32])
    ecmp_d = din("ecmp", [128, 16, 247])
    m12_d = din("m12", [128, 2, 62])
    tri_d = din("tri", [128, 2, 512])
    kaug_d = din("kaug", [2, 41, 2048])
    qal_d = din("qal", [16, 9, 2048])
    out_d = nc.dram_tensor("out", [S_LEN, D], F32, kind="ExternalOutput").ap()
    x1_scr = nc.dram_tensor("x1_scr", [S_LEN, D], F32, kind="Internal").ap()

    with ExitStack() as gst:
        kb = KB(nc, gst)
        S = kb.S
        ckstate["S"] = S
        DX = Buf("x_dram", None)
        DX1 = Buf("x1_dram", None)
        DOUT = Buf("out_dram", None)

        xnT = kb.sb("xnT", [128, 8, S_LEN], BF16)
        memnT = kb.sb("memnT", [128, 8, 256], BF16)
        identf = kb.sb("identf", [128, 128], F32)
        identb = kb.sb("identb", [128, 128], BF16)
        onesb = kb.sb("onesb", [128, 128], BF16)
        wstage = [kb.sb(f"wstage{i}", [128, 1024], F32) for i in range(3)]
        ws_rr = [0]
        stat = kb.sb("stat", [128, 64], F32)
        stat_rr = [0]

        kb.dma("sp", identf[:], identf_d, [], [identf])
        kb.cp("dve", identb[:], identf[:], [identf], [identb])
        kb.memset("dve", onesb[:], 1.0, [onesb])

        def next_ws():
            b = wstage[ws_rr[0]]
            ws_rr[0] = (ws_rr[0] + 1) % len(wstage)
            return b

        def load_w(dst, dst_ap3, src_ap3, n, gain=None, key=None, q="sp", part=128, eng="pool"):
            dcs = dst_ap3.shape[1]
            if gain is None:
                kb.dma("pool", dst_ap3, src_ap3, [], [(dst, key)])
                return
            assert dcs * n <= 1024
            stg = next_ws()
            sv = stg[0:part, 0:dcs * n].rearrange("p (c n) -> p c n", c=dcs)
            kb.dma(q, sv, src_ap3, [], [stg])
            if gain is not None:
                kb.tt(eng, dst_ap3, sv, gain[0:part, 0:dcs, 0:n], ALU.mult, [stg, gain], [(dst, key)])
            else:
                kb.cp(eng, dst_ap3, sv, [stg], [(dst, key)])

        def win_cols(w_dram, c0, n):
            return w_dram.rearrange("(dc p) n -> p dc n", p=128)[:, :, c0:c0 + n]

        class NormCtx:
            def __init__(self, stack, nbuf=2):
                self.xstage = [kb.sb(f"xstage{i}", [128, 1024], F32, stack) for i in range(nbuf)]
                self.xnb = [kb.sb(f"xnb{i}", [128, 1024], BF16, stack) for i in range(nbuf)]
                self.junk = kb.sb("junk", [128, 1024], BF16, stack)

        def tile_rstd(ncx, xbuf, xap):
            i = stat_rr[0]
            stat_rr[0] = (stat_rr[0] + 1) % 32
            ss = stat[:, 2 * i:2 * i + 1]
            rs = stat[:, 2 * i + 1:2 * i + 2]
            kb.stt(ncx.junk[:], xap, 1.0, xap, ALU.mult, ALU.mult, [xbuf], [ncx.junk, (stat, i)], accum_out=ss)
            kb.ts("dve", ss, ss, 1.0 / D, EPS, ALU.mult, ALU.add, [(stat, i)], [(stat, i)])
            kb.act(ss, ss, AF.Sqrt, [(stat, i)], [(stat, i)])
            kb.recip(rs, ss, [(stat, i)], [(stat, i)])
            return rs, i

        def norm_to_T(ncx, xbuf, xap, dstT, t, ntok_off, gB=None):
            rs, i = tile_rstd(ncx, xbuf, xap)
            nb = ncx.xnb[t % 2]
            if gB is None:
                kb.ts("dve", nb[:], xap, rs, None, ALU.mult, None, [xbuf, (stat, i)], [nb])
            else:
                kb.stt(nb[:], xap, rs, gB[:], ALU.mult, ALU.mult, [xbuf, (stat, i), gB], [nb])
            bk = kb.bank()
            bv = bk[:].bitcast(BF16)
            for c in range(8):
                kb.tr(bv[:, c * 128:(c + 1) * 128], nb[:, c * 128:(c + 1) * 128], identb[:], [nb, identb], [bk])
            kb.cp("act", dstT[:, :, ntok_off:ntok_off + 128], bv.rearrange("p (c n) -> p c n", c=8), [bk], [(dstT, t)])

        def norm_to_T_gen(ncx, xbuf, xap, dstT, t, ntok_off, bk, bi, gB=None):
            rs, i = tile_rstd(ncx, xbuf, xap)
            yield
            nb = ncx.xnb[bi]
            if gB is None:
                kb.ts("dve", nb[:], xap, rs, None, ALU.mult, None, [xbuf, (stat, i)], [nb])
            else:
                kb.stt(nb[:], xap, rs, gB[:], ALU.mult, ALU.mult, [xbuf, (stat, i), gB], [nb])
            yield
            bv = bk[:].bitcast(BF16)
            for c in range(8):
                kb.tr(bv[:, c * 128:(c + 1) * 128], nb[:, c * 128:(c + 1) * 128], identb[:], [nb, identb], [bk])
            yield
            kb.cp("act", dstT[:, :, ntok_off:ntok_off + 128], bv.rearrange("p (c n) -> p c n", c=8), [bk], [(dstT, t)])
            yield

        with ExitStack() as pa:
            ncxs = [NormCtx(pa, 2), NormCtx(pa, 2)]
            gBa = kb.sb("gBa", [128, 1024], F32, pa)
            kb.dma("sp", gBa[:], hawk_norm_v.partition_broadcast(128), [], [gBa])

            def a_lane(L):
                ncx = ncxs[L]
                for t in range(L, NT_ + 2, 2):
                    xs = ncx.xstage[(t // 2) % 2]
                    if t < NT_:
                        kb.dma("sp", xs[:], x_d[t * 128:(t + 1) * 128, :], [DX], [xs])
                        yield from norm_to_T_gen(ncx, xs, xs[:], xnT, t, t * 128, kb.banks[2 * L + (t // 2) % 2], (t // 2) % 2, gB=gBa)
                    else:
                        tm = t - NT_
                        kb.dma("sp", xs[:], mem_d[tm * 128:(tm + 1) * 128, :], [], [xs])
                        yield from norm_to_T_gen(ncx, xs, xs[:], memnT, tm, tm * 128, kb.banks[2 * L + (t // 2) % 2], (t // 2) % 2)

            run_lanes([a_lane(0), a_lane(1)])
            S.barrier()
            ck("A")

        def make_loader(w_in_d, gain, nslots, stack):
            wslots = [kb.sb(f"wslot{i}", [128, 8, 128], BF16, stack) for i in range(nslots)]
            rr = [0]
            pre = {}

            def raw(c0, n, q, into, off):
                if into is None:
                    wsl = wslots[rr[0]]
                    rr[0] = (rr[0] + 1) % nslots
                else:
                    wsl = into
                load_w(wsl, wsl[:, :, off:off + n], win_cols(w_in_d, c0, n), n, gain=None, q=q, key=off)
                return wsl

            def load_win(c0, n=128, q="sp", into=None, off=0):
                if into is None and (c0, n) in pre:
                    return pre.pop((c0, n))
                return raw(c0, n, q, into, off)

            def prefetch(c0, n=128):
                pre[(c0, n)] = raw(c0, n, "sp", None, 0)
            load_win.prefetch = prefetch
            return load_win

        def proj_fm(wsl, n, evac, woff=0):
            for tc in range(4):
                bk = kb.bank()
                for dc in range(8):
                    kb.mm(bk[0:n, :], wsl[:, dc, woff:woff + n], xnT[:, dc, tc * 512:(tc + 1) * 512], dc == 0, dc == 7,
                          [wsl, xnT], [bk])
                evac(bk, bk[0:n, :], tc)

        def mem_kv(w_kv_d, gain, kmT, vm, stack):
            wkv = kb.sb("wkv", [128, 8, 512], BF16, stack)
            for j in range(4):
                load_w(wkv, wkv[:, :, j * 128:(j + 1) * 128], win_cols(w_kv_d, j * 128, 128), 128, gain=gain, key=j)
            for h in range(4):
                bk = kb.bank()
                for dc in range(8):
                    kb.mm(bk[0:64, 0:256], wkv[:, dc, h * 64:(h + 1) * 64], memnT[:, dc, :], dc == 0, dc == 7,
                          [wkv, memnT], [bk])
                kb.cp("act", kmT[0:64, h, :], bk[0:64, 0:256], [bk], [(kmT, h)])
            for mt in range(2):
                bk = kb.bank()
                for dc in range(8):
                    kb.mm(bk[:, 0:256], memnT[:, dc, mt * 128:(mt + 1) * 128], wkv[:, dc, 256:512], dc == 0, dc == 7,
                          [wkv, memnT], [bk])
                kb.cp("act", vm[:, mt, :], bk[:, 0:256], [bk], [(vm, mt)])

        def mem_attn(load_win, colq, colz, kmT, vm, ymT, stack):
            qmTs = [kb.sb(f"qmT{i}", [64, S_LEN], BF16, stack) for i in range(2)]
            szms = [kb.sb(f"szm{i}", [64, S_LEN], BF16, stack) for i in range(2)]
            PTm = [kb.sb(f"PTm{i}", [128, 512], BF16, stack) for i in range(4)]
            rdm = [kb.sb(f"rdm{i}", [64, 512], F32, stack) for i in range(2)]

            def lane(h, L):
                qmT, szm = qmTs[L], szms[L]
                b0, b1, b2, b3 = [kb.banks[4 * L + j] for j in range(4)]
                wq = load_win(colq + h * 64, 64)
                wz = load_win(colz + h * 64, 64)
                for (wsl, dst, func) in ((wq, qmT, None), (wz, szm, AF.Silu)):
                    for tc in range(4):
                        bk = b0 if tc % 2 == 0 else b1
                        for dc in range(8):
                            kb.mm(bk[0:64, :], wsl[:, dc, 0:64], xnT[:, dc, tc * 512:(tc + 1) * 512], dc == 0, dc == 7, [wsl, xnT], [bk])
                        yield
                        if func is None:
                            kb.cp("act", dst[0:64, tc * 512:(tc + 1) * 512], bk[0:64, :], [bk], [(dst, tc)])
                        else:
                            kb.act(dst[0:64, tc * 512:(tc + 1) * 512], bk[0:64, :], func, [bk], [(dst, tc)])
                        yield
                for tc in range(4):
                    pts = []
                    for mt in range(2):
                        bk = b0 if mt == 0 else b1
                        kb.mm(bk[:, :], kmT[0:64, h, mt * 128:(mt + 1) * 128], qmT[0:64, tc * 512:(tc + 1) * 512],
                              True, True, [kmT, (qmT, tc)], [bk])
                        pt = PTm[2 * L + mt]
                        kb.act(pt[:], bk[:], AF.Exp, [bk], [pt], scale=0.125)
                        pts.append(pt)
                        yield
                    for mt in range(2):
                        kb.mm(b2[0:64, :], vm[:, mt, h * 64:(h + 1) * 64], pts[mt][:], mt == 0, mt == 1, [vm, pts[mt]], [b2])
                    for mt in range(2):
                        kb.mm(b3[0:64, :], onesb[:, 0:64], pts[mt][:], mt == 0, mt == 1, [onesb, pts[mt]], [b3])
                    yield
                    rd = rdm[L]
                    kb.recip(rd[:], b3[0:64, :], [b3], [rd])
                    kb.tt("dve", rd[:], b2[0:64, :], rd[:], ALU.mult, [b2, rd], [rd])
                    yield
                    kb.tt("dve", ymT[0:64, h, tc * 512:(tc + 1) * 512], rd[:], szm[0:64, tc * 512:(tc + 1) * 512], ALU.mult,
                          [rd, (szm, tc)], [(ymT, (h, tc))])
                    yield

            run_lanes([lane(0, 0), lane(1, 1)])
            run_lanes([lane(2, 0), lane(3, 1)])

        def out_proj(w_out_d, nch, yTl, ymT, resid_d, resid_buf, final, stack, dbg=False):
            ysrc = []
            for (yb_, n_) in yTl:
                for ci in range(n_):
                    ysrc.append((yb_, ci))
            ncxs = [NormCtx(stack, 1), NormCtx(stack, 1)]
            WO = kb.sb("WO", [128, nch, 1024], BF16, stack)
            WOm = kb.sb("WOm", [64, 4, 1024], BF16, stack)
            wo_v = w_out_d[0:nch * 128, :].rearrange("(c p) n -> p c n", p=128)
            for c in range(0, nch, 4):
                load_w(WO, WO[:, c:c + 4, :], wo_v[:, c:c + 4, :], 1024, key=c)
            wom_v = w_out_d[nch * 128:nch * 128 + 256, :].rearrange("(h p) n -> p h n", p=64)
            load_w(WOm, WOm[0:64, :, :], wom_v, 1024, key=0, part=64)
            x1t = [kb.sb(f"x1t{i}", [128, 1024], F32, stack) for i in range(2)]
            if not final:
                gNx = kb.sb("gNx", [128, 1024], F32, stack)
                kb.dma("sp", gNx[:], nsa_norm_v.partition_broadcast(128), [], [gNx])
            if final:
                gF = kb.sb("gF", [128, 1024], F32, stack)
                kb.dma("sp", gF[:], final_g.partition_broadcast(128), [], [gF])
                ot = [kb.sb(f"ot{i}", [128, 1024], F32, stack) for i in range(2)]

            def lane(L):
                ncx = ncxs[L]
                bks = [kb.banks[4 * L + j] for j in range(4)]
                for t in range(L, NT_, 2):
                    xs = ncx.xstage[0]
                    kb.dma("sp", xs[:], resid_d[t * 128:(t + 1) * 128, :], [resid_buf], [xs])
                    x1 = x1t[L]
                    for half in range(2):
                        bk = bks[half]
                        for c in range(nch):
                            yb_, ci = ysrc[c]
                            kb.mm(bk[:, :], yb_[:, ci, t * 128:(t + 1) * 128], WO[:, c, half * 512:(half + 1) * 512], c == 0, False,
                                  [yb_, WO], [bk])
                            if c % 4 == 3:
                                yield
                        for h in range(4):
                            kb.mm(bk[:, :], ymT[0:64, h, t * 128:(t + 1) * 128], WOm[0:64, h, half * 512:(half + 1) * 512], False, h == 3,
                                  [ymT, WOm], [bk])
                        yield
                        kb.tt("dve", x1[:, half * 512:(half + 1) * 512], xs[:, half * 512:(half + 1) * 512], bk[:], ALU.add,
                              [xs, bk], [(x1, half)])
                        yield
                    if not final:
                        kb.dma("sp", x1_scr[t * 128:(t + 1) * 128, :], x1[:], [x1], [DX1])
                        yield from norm_to_T_gen(ncx, x1, x1[:], xnT, t, t * 128, bks[2], 0, gB=gNx)
                        if dbg:
                            kb.dma("sp", out_d[t * 128:(t + 1) * 128, :], x1[:], [x1], [DOUT])
                    else:
                        rs, i = tile_rstd(ncx, x1, x1[:])
                        yield
                        o = ot[L]
                        kb.stt(o[:], x1[:], rs, gF[:], ALU.mult, ALU.mult, [x1, (stat, i), gF], [o])
                        yield
                        kb.dma("sp", out_d[t * 128:(t + 1) * 128, :], o[:], [o], [DOUT])
                        yield

            run_lanes([lane(0), lane(1)])

        with ExitStack() as l0:
            yTa = kb.sb("yTa", [128, 8, S_LEN], BF16, l0)
            ymT = kb.sb("ymT", [64, 4, S_LEN], BF16, l0)
            load_win = make_loader(hawk_w_in, None, 9, l0)
            kmT = kb.sb("kmT", [64, 4, 256], BF16, l0)
            vm = kb.sb("vm", [128, 2, 256], BF16, l0)
            with ExitStack() as pm:
                gHm = kb.sb("gHm", [128, 8, 128], F32, pm)
                kb.dma("sp", gHm[:], g_hawk_mem, [], [gHm])
                mem_kv(hawk_w_mem_kv, gHm, kmT, vm, pm)
                for c0_ in (7168, 7424, 7168 + 64, 7424 + 64):
                    load_win.prefetch(c0_, 64)
                S.barrier()
                ck("memkv0")
            with ExitStack() as pd:
                mem_attn(load_win, 7168, 7424, kmT, vm, ymT, pd)
                for c0_ in (0, 1024, 128, 1024 + 128):
                    load_win.prefetch(c0_, 128)
                S.barrier()
                ck("mem0")

            with ExitStack() as pb:
                lv = kb.sb("lv", [128, 8, 8], F32, pb)
                cvec = kb.sb("cvec", [128, 8, 2], F32, pb)
                bda = kb.sb("bda", [128, 8, 128], BF16, pb)
                bdx = kb.sb("bdx", [128, 8, 128], BF16, pb)
                kb.dma("sp", lv[:], lru_vec, [], [lv])
                load_w(bda, bda[:], bd_a, 128)
                load_w(bdx, bdx[:], bd_x, 128)
                kb.act(cvec[:, :, 0], lv[:, :, 7], AF.Exp, [lv], [cvec], scale=-1.0)
                kb.act(cvec[:, :, 0], cvec[:, :, 0], AF.Ln, [cvec], [cvec], bias=1.0)
                kb.ts("dve", cvec[:, :, 1], cvec[:, :, 0], -16.0, None, ALU.mult, None, [cvec], [cvec])
                kb.ts("dve", cvec[:, :, 0], cvec[:, :, 0], -8.0, None, ALU.mult, None, [cvec], [cvec])
                sets = []
                for L in range(2):
                    sets.append(dict(
                        B1=kb.sb(f"B1_{L}", [128, S_LEN + 4], F32, pb), B2=kb.sb(f"B2_{L}", [128, S_LEN], F32, pb),
                        B3=kb.sb(f"B3_{L}", [128, S_LEN], F32, pb), B4=kb.sb(f"B4_{L}", [128, S_LEN], F32, pb),
                        xcb=kb.sb(f"xcb_{L}", [128, S_LEN], BF16, pb), sz=kb.sb(f"sz_{L}", [128, S_LEN], BF16, pb)))

                def lru_lane(L):
                    st_ = sets[L]
                    B1, B2, B3, B4, xcb, sz = st_["B1"], st_["B2"], st_["B3"], st_["B4"], st_["xcb"], st_["sz"]
                    bks = [kb.banks[4 * L + j] for j in range(4)]
                    for c in range(L, 8, 2):
                        wxa = load_win(c * 128)
                        wza = load_win(1024 + c * 128)
                        kb.memset("dve", B1[:, 0:3], 0.0, [(B1, "pad")])
                        for (wsl, which) in ((wxa, 0), (wza, 1)):
                            for tc in range(4):
                                bk = bks[tc % 4]
                                for dc in range(8):
                                    kb.mm(bk[:, :], wsl[:, dc, 0:128], xnT[:, dc, tc * 512:(tc + 1) * 512], dc == 0, dc == 7, [wsl, xnT], [bk])
                                yield
                                if which == 0:
                                    kb.cp("act", B1[:, 3 + tc * 512:3 + (tc + 1) * 512], bk[:], [bk], [(B1, tc)])
                                else:
                                    kb.act(sz[:, tc * 512:(tc + 1) * 512], bk[:], AF.Silu, [bk], [(sz, tc)])
                                yield
                        kb.ts("dve", B2[:], B1[:, 0:S_LEN], lv[:, c, 0:1], lv[:, c, 4:5], ALU.mult, ALU.add, [B1, lv], [B2])
                        yield
                        for k in range(1, 4):
                            kb.stt(B2[:], B1[:, k:k + S_LEN], lv[:, c, k:k + 1], B2[:], ALU.mult, ALU.add, [B1, lv, B2], [B2])
                            yield
                        kb.cp("dve", xcb[:], B2[:], [B2], [xcb])
                        yield
                        for (bd, dstb, col) in ((bda, B1, 5), (bdx, B4, 6)):
                            for tc in range(4):
                                bk = bks[tc % 4]
                                kb.mm(bk[:, :], bd[:, c, :], xcb[:, tc * 512:(tc + 1) * 512], True, True, [bd, xcb], [bk])
                                yield
                                kb.act(dstb[:, tc * 512:(tc + 1) * 512], bk[:], AF.Sigmoid, [bk, lv], [(dstb, tc)], bias=lv[:, c, col:col + 1])
                                yield
                        r_ap = B1[:, 0:S_LEN]
                        kb.act(B3[:], r_ap, AF.Exp, [B1, cvec], [B3], scale=cvec[:, c, 0:1])
                        yield
                        kb.act(r_ap, r_ap, AF.Exp, [B1, cvec], [B1], scale=cvec[:, c, 1:2])
                        yield
                        kb.tt("dve", B2[:], B2[:], B4[:], ALU.mult, [B2, B4], [B2])
                        yield
                        kb.ts("dve", r_ap, r_ap, -1.0, 1.0, ALU.mult, ALU.add, [B1], [B1])
                        yield
                        kb.ts("dve", r_ap, r_ap, 0.0, None, ALU.max, None, [B1], [B1])
                        yield
                        kb.act(r_ap, r_ap, AF.Sqrt, [B1], [B1])
                        kb.memset("dve", B1[:, 0:1], 1.0, [B1])
                        yield
                        kb.tt("dve", B2[:], B2[:], r_ap, ALU.mult, [B2, B1], [B2])
                        yield
                        S.op("dve", lambda e, B4=B4, B3=B3, B2=B2: e.tensor_tensor_scan(out=B4[:], data0=B3[:], data1=B2[:], initial=0.0,
                                                                                      op0=ALU.mult, op1=ALU.add), reads=[B3, B2], writes=[B4])
                        yield
                        kb.tt("dve", yTa[:, c, :], B4[:], sz[:], ALU.mult, [B4, sz], [(yTa, c)])
                        yield

                run_lanes([lru_lane(0), lru_lane(1)])
                for c0_ in (2048 + 4608, 2048, 2048 + 1536, 2048 + 3072):
                    load_win.prefetch(c0_, 128)
                S.barrier()
                ck("lru")
            yTb = kb.sb("yTb", [128, 4, S_LEN], BF16, l0)

            with ExitStack() as pc:
                edil = kb.sb("edil", [128, 12, 256], F32, pc)
                kb.dma("sp", edil[:], edil_d, [], [edil])
                qTs = [kb.sb(f"qT{i}", [128, S_LEN], BF16, pc) for i in range(2)]
                kTs = [kb.sb(f"kT{i}", [128, S_LEN], BF16, pc) for i in range(2)]
                vTs = [kb.sb(f"vT{i}", [128, S_LEN], BF16, pc) for i in range(2)]
                Vps = [kb.sb(f"Vp{i}", [128, 16, 128], BF16, pc) for i in range(2)]
                szbs = [kb.sb(f"szb{i}", [128, S_LEN], BF16, pc) for i in range(2)]
                NTa = kb.sb("NTa", [128, S_LEN], F32, pc)
                DBa = kb.sb("DBa", [128, S_LEN], F32, pc)
                Pf = [kb.sb(f"Pf{i}", [128, 256], F32, pc) for i in range(3)]
                PT = [kb.sb(f"PT{i}", [128, 256], BF16, pc) for i in range(3)]
                sc_d = 128.0 ** -0.5
                items = [(hs, g) for hs in range(4) for g in range(3)]
                prot = [0]

                def pbank():
                    b = kb.banks[6 + prot[0]]
                    prot[0] ^= 1
                    return b

                def dtoks(d, r, b):
                    t0 = r + d * 128 * b
                    return slice(t0, t0 + d * 127 + 1, d)

                def proj_fm_lane(wsl, dst, func):
                    for tc in range(4):
                        bk = pbank()
                        for dc in range(8):
                            kb.mm(bk[:, :], wsl[:, dc, 0:128], xnT[:, dc, tc * 512:(tc + 1) * 512], dc == 0, dc == 7, [wsl, xnT], [bk])
                        yield
                        if func is None:
                            kb.cp("act", dst[:, tc * 512:(tc + 1) * 512], bk[:], [bk], [(dst, tc)])
                        else:
                            kb.act(dst[:, tc * 512:(tc + 1) * 512], bk[:], func, [bk], [(dst, tc)])
                        yield

                def task_proj(i):
                    hs, g = items[i]
                    win, d = DIL_GROUPS[g]
                    hh = g * 4 + hs
                    nqb = (S_LEN // d) // 128
                    s = i % 2
                    if g == 0:
                        wz = load_win(2048 + 4608 + hs * 128)
                        yield from proj_fm_lane(wz, szbs[hs % 2], AF.Silu)
                    wq = load_win(2048 + hh * 128)
                    yield from proj_fm_lane(wq, qTs[s], None)
                    wk = load_win(2048 + 1536 + hh * 128)
                    yield from proj_fm_lane(wk, kTs[s], None)
                    wv = load_win(2048 + 3072 + hh * 128)
                    yield from proj_fm_lane(wv, vTs[s], None)
                    for j in range(4):
                        bk = pbank()
                        bv = bk[:].bitcast(BF16)
                        for k in range(4):
                            r, b = divmod(4 * j + k, nqb)
                            kb.tr(bv[:, k * 128:(k + 1) * 128], vTs[s][:, dtoks(d, r, b)], identb[:], [vTs[s], identb], [bk])
                        yield
                        kb.cp("act", Vps[s][:, 4 * j:4 * j + 4, :], bv[:, 0:512].rearrange("p (k n) -> p k n", k=4), [bk], [(Vps[s], j)])
                        yield

                def task_tile(i, ti, lane):
                    hs, g = items[i]
                    win, d = DIL_GROUPS[g]
                    hh = g * 4 + hs
                    nqb = (S_LEN // d) // 128
                    s = i % 2
                    qT, kT, Vp = qTs[s], kTs[s], Vps[s]
                    r, qb = divmod(ti, nqb)
                    qs = dtoks(d, r, qb)
                    kbs = [qb - 1, qb] if qb > 0 else [qb]
                    sbk = kb.banks[2 * lane]
                    ndb = kb.banks[2 * lane + 1]
                    for kbi in kbs:
                        typ = 0 if kbi < qb else 1
                        kb.mm(sbk[:, typ * 128:(typ + 1) * 128], kT[:, dtoks(d, r, kbi)], qT[:, qs], True, True, [kT, qT], [sbk])
                    yield
                    lo = 0 if qb > 0 else 128
                    pf = Pf[lane]
                    pt = PT[lane]
                    kb.act(pf[:, lo:256], sbk[:, lo:256], AF.Exp, [sbk], [pf], scale=sc_d)
                    yield
                    kb.tt("dve", pt[:, lo:256], pf[:, lo:256], edil[:, hh, lo:256], ALU.mult, [pf, edil], [pt])
                    yield
                    for j, kbi in enumerate(kbs):
                        typ = 0 if kbi < qb else 1
                        kb.mm(ndb[:, 0:128], Vp[:, r * nqb + kbi, :], pt[:, typ * 128:(typ + 1) * 128], j == 0, j == len(kbs) - 1,
                              [Vp, pt], [ndb])
                    for j, kbi in enumerate(kbs):
                        typ = 0 if kbi < qb else 1
                        kb.mm(ndb[:, 128:256], onesb[:], pt[:, typ * 128:(typ + 1) * 128], j == 0, j == len(kbs) - 1,
                              [onesb, pt], [ndb])
                    yield
                    if g == 0:
                        kb.cp("dve", NTa[:, qs], ndb[:, 0:128], [ndb], [NTa])
                        kb.cp("dve", DBa[:, qs], ndb[:, 128:256], [ndb], [DBa])
                    else:
                        kb.tt("dve", NTa[:, qs], NTa[:, qs], ndb[:, 0:128], ALU.add, [ndb, NTa], [NTa])
                        kb.tt("dve", DBa[:, qs], DBa[:, qs], ndb[:, 128:256], ALU.add, [ndb, DBa], [DBa])
                    yield

                def tile_lane(i, lane):
                    for ti in range(lane, 16, 3):
                        yield from task_tile(i, ti, lane)

                run_lanes([task_proj(0)])
                for i in range(len(items)):
                    hs, g = items[i]
                    lanes = [tile_lane(i, 0), tile_lane(i, 1), tile_lane(i, 2)]
                    if i + 1 < len(items):
                        lanes.append(task_proj(i + 1))
                    run_lanes(lanes)
                    if g == 2:
                        kb.recip(DBa[:], DBa[:], [DBa], [DBa])
                        kb.tt("dve", NTa[:], NTa[:], DBa[:], ALU.mult, [NTa, DBa], [NTa])
                        kb.tt("dve", yTb[:, hs, :], NTa[:], szbs[hs % 2][:], ALU.mult, [NTa, szbs[hs % 2]], [(yTb, hs)])
                S.barrier()
                ck("dil")

            with ExitStack() as pe_:
                out_proj(hawk_w_out, 12, [(yTa, 8), (yTb, 4)], ymT, x_d, DX, False, pe_, dbg=(stop_after == "l0"))
                S.barrier()
                ck("l0end")

        if stop_after != "l0":
          with ExitStack() as l1:
            yT1 = kb.sb("yT1", [128, 8, S_LEN], BF16, l1)
            ymT1 = kb.sb("ymT1", [64, 4, S_LEN], BF16, l1)
            load_win = make_loader(nsa_w_in, None, 6, l1)
            kcmpT = kb.sb("kcmpT", [64, 2, 128], BF16, l1)
            vcmp = kb.sb("vcmp", [128, 2, 64], BF16, l1)
            gates = kb.sb("gates", [128, 16, 48], F32, l1)
            with ExitStack() as pmm:
                kmT = kb.sb("kmT1", [64, 4, 256], BF16, pmm)
                vm = kb.sb("vm1", [128, 2, 256], BF16, pmm)
                with ExitStack() as pm:
                    gNm = kb.sb("gNm", [128, 8, 128], F32, pm)
                    kb.dma("sp", gNm[:], g_nsa_mem, [], [gNm])
                    mem_kv(nsa_w_mem_kv, gNm, kmT, vm, pm)
                    for c0_ in (2864, 3120, 2864 + 64, 3120 + 64):
                        load_win.prefetch(c0_, 64)
                    S.barrier()
                    ck("memkv1")
                with ExitStack() as pd:
                    mem_attn(load_win, 2864, 3120, kmT, vm, ymT1, pd)
                    load_win.prefetch(1792, 48)
                    load_win.prefetch(1024, 128)
                    load_win.prefetch(1024 + 128, 128)
                    S.barrier()
                    ck("mem1")

            with ExitStack() as pq:
                wg = load_win(1792, 48)
                for t in range(NT_):
                    bk = kb.bank()
                    for dc in range(8):
                        kb.mm(bk[:, 0:48], xnT[:, dc, t * 128:(t + 1) * 128], wg[:, dc, 0:48], dc == 0, dc == 7, [xnT, wg], [bk])
                    kb.act(gates[:, t, :], bk[:, 0:48], AF.Sigmoid, [bk], [(gates, t)])
                kcT = kb.sb("kcT", [128, S_LEN], BF16, pq)
                vcT = kb.sb("vcT", [128, S_LEN], BF16, pq)
                wkc = load_win(1024)
                proj_fm(wkc, 128, lambda bk, ap, tc: kb.cp("act", kcT[:, tc * 512:(tc + 1) * 512], ap, [bk], [(kcT, tc)]))
                wvc = load_win(1024 + 128)
                proj_fm(wvc, 128, lambda bk, ap, tc: kb.cp("act", vcT[:, tc * 512:(tc + 1) * 512], ap, [bk], [(vcT, tc)]))
                W1 = kb.sb("W1", [128, 32, 256], BF16, pq)
                w2 = kb.sb("w2", [128, 2, 64], BF16, pq)
                peS = kb.sb("peS", [64, 2, 32], F32, pq)
                peb = kb.sb("peb", [64, 2, 32], BF16, pq)
                hidT = kb.sb("hidT", [128, 2, 128], BF16, pq)
                cb = kb.sb("cb", [128, 2], F32, pq)
                kb.dma("sp", peS[:], peT_d, [], [peS])
                kb.cp("dve", peb[:], peS[:], [peS], [peb])
                for kv in range(2):
                    w1d = w1k_d if kv == 0 else w1v_d
                    w2d = w2k_d if kv == 0 else w2v_d
                    srcT = kcT if kv == 0 else vcT
                    for p4 in range(2):
                        load_w(W1, W1[:, p4 * 16:(p4 + 1) * 16, :], w1d[:, p4 * 16:(p4 + 1) * 16, :], 256, key=p4)
                    load_w(w2, w2[:, :, :], w2d.rearrange("(hc p) d -> p hc d", p=128), 64)
                    for hc in range(2):
                        bk = kb.bank()
                        for p in range(32):
                            kb.mm(bk[:, 0:1], W1[0:64, p, hc * 128:(hc + 1) * 128], peb[0:64, kv, p:p + 1], p == 0, p == 31,
                                  [W1, peb], [bk])
                        kb.cp("dve", cb[:, hc:hc + 1], bk[:, 0:1], [bk], [(cb, hc)])
                    for g in range(2):
                        for hc in range(2):
                            bk = kb.bank()
                            for p in range(32):
                                kb.mm(bk[:, 0:127], W1[g * 64:(g + 1) * 64, p, hc * 128:(hc + 1) * 128],
                                      srcT[g * 64:(g + 1) * 64, p:p + 16 * 126 + 1:16], p == 0, p == 31, [W1, srcT], [bk])
                            kb.act(hidT[:, hc, 0:127], bk[:, 0:127], AF.Silu, [bk, cb], [(hidT, hc)], bias=cb[:, hc:hc + 1])
                        bk = kb.bank()
                        if kv == 0:
                            for hc in range(2):
                                kb.mm(bk[0:64, 0:127], w2[:, hc, :], hidT[:, hc, 0:127], hc == 0, hc == 1, [w2, hidT], [bk])
                            kb.cp("dve", kcmpT[0:64, g, 0:127], bk[0:64, 0:127], [bk], [(kcmpT, g)])
                        else:
                            for hc in range(2):
                                kb.mm(bk[0:127, 0:64], hidT[:, hc, 0:127], w2[:, hc, :], hc == 0, hc == 1, [w2, hidT], [bk])
                            kb.cp("dve", vcmp[0:127, g, :], bk[0:127, 0:64], [bk], [(vcmp, g)])
                S.barrier()
                ck("cmpkv")

            with ExitStack() as pg:
                QAg = kb.sb("QAg", [105, 8, S_LEN], BF16, pg)
                KAs = kb.sb("KAs", [105, S_LEN], BF16, pg)
                KAw = kb.sb("KAw", [105, S_LEN], BF16, pg)
                VAs = kb.sb("VAs", [128, 16, 128], BF16, pg)
                VAw = kb.sb("VAw", [128, 16, 128], BF16, pg)
                ecmp = kb.sb("ecmp", [128, 8, 247], F32, pg)
                Wz = kb.sb("Wz", [128, 8, 512], BF16, pg)
                m12 = kb.sb("m12", [128, 2, 62], F32, pg)
                trib = kb.sb("trib", [128, 2, 512], BF16, pg)
                kb.dma("sp", m12[:], m12_d, [], [m12])
                kb.dma("pool", trib[:], tri_d, [], [trib])
                for v, KA in enumerate((KAs, KAw)):
                    kb.dma("pool", KA[64:105, :], kaug_d[v], [], [(KA, "aug")])
                kb.memset("pool", VAs[:, :, 64:128], 1.0, [(VAs, "ones")])
                kb.memset("pool", VAw[:, :, 64:128], 1.0, [(VAw, "ones")])
                Pu = [kb.sb(f"Pu{i}", [128, 4, 128], F32, pg) for i in range(2)]
                Pub = [kb.sb("Pub0", [128, 4, 128], BF16, pg)] * 2
                pT = [kb.sb("pT0", [128, 4, 128], BF16, pg)] * 2
                for i in range(2):
                    kb.memset("pool", Pu[i][:], 0.0, [Pu[i]])
                psg = kb.sb("psg", [128, 128], F32, pg)
                den8 = kb.sb("den8", [128, 8], F32, pg)
                cg8 = kb.sb("cg8", [128, 8], F32, pg)
                imp = kb.sb("imp", [128, 32], F32, pg)
                impm = kb.sb("impm", [128, 32], F32, pg)
                m8 = kb.sb("m8", [128, 8], F32, pg)
                negp = kb.sb("negp", [128, 96], F32, pg)
                negS = kb.sb("negS", [96, 128], BF16, pg)
                kb.memset("pool", negp[:], 0.0, [negp])
                PTs = [kb.sb(f"PTs{i}", [128, 512], BF16, pg) for i in range(10)]
                pts_rr = [0]
                zerob = kb.sb("zerob", [128, 260], BF16, pg)
                kb.memset("pool", zerob[:], 0.0, [zerob])
                acs = [kb.sb(f"acs{i}", [128, 260], F32, pg) for i in range(2)]
                rd4 = [kb.sb(f"rd4{i}", [128, 4], F32, pg) for i in range(2)]
                cg4 = [kb.sb(f"cg4{i}", [128, 4], F32, pg) for i in range(2)]
                Oa = [kb.sb(f"Oa{i}", [128, 512], F32, pg) for i in range(2)]
                szt = kb.sb("szt", [128, 512], F32, pg)
                Ob = kb.sb("Ob", [128, 512], BF16, pg)
                accbanks = [kb.banks[0], kb.banks[1]]
                rot = [2]

                def rbank():
                    b = kb.banks[rot[0]]
                    rot[0] = rot[0] + 1 if rot[0] < 7 else 2
                    return b

                strot = [0, 0]

                def stbank(lane):
                    b = kb.banks[2 + 2 * lane + strot[lane]]
                    strot[lane] ^= 1
                    return b

                def mbank():
                    return kb.banks[6]

                ptrot = [0, 0]

                def next_pt(lane):
                    p = PTs[5 * lane + ptrot[lane]]
                    ptrot[lane] = (ptrot[lane] + 1) % 5
                    return p

                for g in range(2):
                    kb.dma("sp", ecmp[:], ecmp_d[:, g * 8:(g + 1) * 8, :], [], [ecmp])
                    for j in range(4):
                        load_w(Wz, Wz[:, :, j * 128:(j + 1) * 128], win_cols(nsa_w_in, 1840 + g * 512 + j * 128, 128), 128,
                               gain=None, key=j)
                    for pr in range(4):
                        wq = load_win((g * 8 + 2 * pr) * 64, 128)
                        for tc in range(4):
                            bk = rbank()
                            for dc in range(8):
                                kb.mm(bk[:, :], wq[:, dc, 0:128], xnT[:, dc, tc * 512:(tc + 1) * 512], dc == 0, dc == 7, [wq, xnT], [bk])
                            kb.cp("act", QAg[0:64, 2 * pr, tc * 512:(tc + 1) * 512], bk[0:64, :], [bk], [QAg])
                            stq = PTs[5 * (pr % 2) + tc]
                            kb.cp("dve", stq[64:128, :], bk[64:128, :], [bk], [stq])
                            kb.dma("sp", QAg[0:64, 2 * pr + 1, tc * 512:(tc + 1) * 512], stq[64:128, :], [stq], [QAg])
                    kb.dma("pool", QAg[96:105, :, :], qal_d[g * 8:(g + 1) * 8].rearrange("h r n -> r h n"), [], [QAg])
                    for KA, col in ((KAs, 1024 + 2 * 128 + g * 64), (KAw, 1024 + 4 * 128 + g * 64)):
                        wk = load_win(col, 64)
                        for tc in range(4):
                            bk = rbank()
                            for dc in range(8):
                                kb.mm(bk[0:64, :], wk[:, dc, 0:64], xnT[:, dc, tc * 512:(tc + 1) * 512], dc == 0, dc == 7, [wk, xnT], [bk])
                            kb.cp("act", KA[0:64, tc * 512:(tc + 1) * 512], bk[0:64, :], [bk], [(KA, tc)])
                    wv = load_win(1024 + 3 * 128 + g * 64, 64)
                    load_win(1024 + 5 * 128 + g * 64, 64, into=wv, off=64)
                    for t in range(NT_):
                        bk = rbank()
                        for dc in range(8):
                            kb.mm(bk[:, 0:128], xnT[:, dc, t * 128:(t + 1) * 128], wv[:, dc, 0:128], dc == 0, dc == 7, [xnT, wv], [bk])
                        kb.cp("act", VAs[:, t, 0:64], bk[:, 0:64], [bk], [(VAs, t)])
                        kb.cp("dve", VAw[:, t, 0:64], bk[:, 64:128], [bk], [(VAw, t)])

                    def task_C(qt):
                        qc = slice(qt * 128, (qt + 1) * 128)
                        O = Oa[qt % 2]
                        gq = gates[:, qt, :]
                        eoff = 120 - 8 * qt
                        ocb = kb.banks[7]
                        for b4 in range(2):
                            sbk = mbank()
                            for hl in range(4):
                                r = b4 * 4 + hl
                                kb.mm(sbk[:, hl * 128:hl * 128 + 127], QAg[0:64, r, qc], kcmpT[0:64, g, 0:127], True, True,
                                      [(QAg, qt), kcmpT], [sbk])
                            yield
                            pu = Pu[b4]
                            s3 = sbk[:].rearrange("p (h n) -> p h n", h=4)
                            kb.act(pu[:, :, 0:127], s3[:, :, 0:127], AF.Exp, [sbk], [pu], scale=0.125)
                            yield
                            kb.tt("dve", pu[:, :, 0:127], pu[:, :, 0:127], ecmp[:, b4 * 4:(b4 + 1) * 4, eoff:eoff + 127], ALU.mult,
                                  [pu, ecmp], [pu])
                            S.op("dve", lambda e, pu=pu, b4=b4: e.tensor_reduce(out=den8[:, b4 * 4:(b4 + 1) * 4], in_=pu[:, :, 0:127],
                                                                                axis=AX.X, op=ALU.add),
                                 reads=[pu], writes=[(den8, b4)])
                            yield
                            kb.ts("dve", den8[:, b4 * 4:(b4 + 1) * 4], den8[:, b4 * 4:(b4 + 1) * 4], 1e-30, None, ALU.max, None,
                                  [(den8, b4)], [(den8, b4)])
                            kb.recip(den8[:, b4 * 4:(b4 + 1) * 4], den8[:, b4 * 4:(b4 + 1) * 4], [(den8, b4)], [(den8, b4)])
                            pub = Pub[b4]
                            kb.cp("dve", pub[:], pu[:], [pu], [pub])
                            yield
                            for hl in range(4):
                                r = b4 * 4 + hl
                                if r == 0:
                                    kb.ts("dve", psg[:, :], pu[:, hl, :], den8[:, r:r + 1], None, ALU.mult, None, [pu, (den8, b4)], [psg])
                                else:
                                    kb.stt(psg[:, :], pu[:, hl, :], den8[:, r:r + 1], psg[:, :], ALU.mult, ALU.add,
                                           [pu, (den8, b4), psg], [psg])
                                if hl % 2 == 1:
                                    yield
                            tbk = mbank()
                            tv = tbk[:].bitcast(BF16)
                            for hl in range(4):
                                kb.tr(tv[0:127, hl * 128:(hl + 1) * 128], pub[:, hl, 0:127], identb[:], [pub, identb], [tbk])
                            yield
                            ptt = pT[b4]
                            kb.cp("act", ptt[0:127, :, :], tv[0:127, 0:512].rearrange("p (h n) -> p h n", h=4), [tbk], [ptt])
                            yield
                            for hl in range(4):
                                r = b4 * 4 + hl
                                kb.mm(ocb[:, r * 64:(r + 1) * 64], ptt[0:127, hl, :], vcmp[0:127, g, :], True, True, [ptt, vcmp], [ocb])
                            yield
                        kb.tt("dve", cg8[:], den8[:], gq[:, g * 24:g * 24 + 24:3], ALU.mult, [den8, gates], [cg8])
                        kb.tt("dve", O[:].rearrange("p (h d) -> p h d", h=8), ocb[:].rearrange("p (h d) -> p h d", h=8),
                              cg8[:, 0:8].unsqueeze(2).to_broadcast([128, 8, 64]), ALU.mult, [ocb, cg8], [O])
                        yield
                        S.op("dve", lambda e: e.tensor_reduce(out=imp[:, :], in_=psg[:].rearrange("p (j a) -> p j a", a=4),
                                                              axis=AX.X, op=ALU.add), reads=[psg], writes=[imp])
                        kb.tt("dve", imp[:, 1:32], imp[:, 1:32], psg[:, 3:127:4], ALU.add, [imp, psg], [imp])
                        yield
                        moff = 30 - 2 * qt
                        kb.tt("dve", impm[:], imp[:], m12[:, 0, moff:moff + 32], ALU.mult, [imp, m12], [impm])
                        kb.tt("dve", impm[:], impm[:], m12[:, 1, moff:moff + 32], ALU.add, [impm, m12], [impm])
                        kb.memset("dve", impm[:, 0:1], 1e6, [impm])
                        yield
                        S.op("dve", lambda e: e.max(out=m8[:], in_=impm[:]), reads=[impm], writes=[m8])
                        kb.ts("dve", negp[:, 64:96], impm[:], m8[:, 7:8], 1.0, ALU.is_ge, ALU.subtract, [impm, m8], [negp])
                        kb.ts("dve", negp[:, 64:96], negp[:, 64:96], NEGB, None, ALU.mult, None, [negp], [negp])
                        yield
                        tbk = mbank()
                        kb.tr(tbk[0:96, 0:128], negp[:, 0:96], identf[:], [negp, identf], [tbk])
                        yield
                        kb.cp("dve", negS[64:96, :], tbk[64:96, 0:128], [tbk], [negS])
                        kb.cp("dve", QAg[64:96, :, qc], negS[64:96, :].unsqueeze(1).to_broadcast([32, 8, 128]), [negS], [(QAg, qt)])
                        yield

                    def task_branch(qt, br, b4):
                        qc = slice(qt * 128, (qt + 1) * 128)
                        O = Oa[qt % 2]
                        gq = gates[:, qt, :]
                        KA, VA = (KAw, VAw) if br == 2 else (KAs, VAs)
                        kbs = list(range(max(0, qt - 4), qt + 1)) if br == 2 else list(range(0, qt + 1))
                        accb = kb.banks[b4]
                        pend = []
                        nk = len(kbs)
                        kb.mm(accb[:, 0:260], zerob[:, 0:128], zerob[:, 0:260], True, False, [zerob], [accb])

                        def do_pv(item):
                            pi, pk, ppt = item
                            for hl in range(4):
                                kb.mm(accb[:, hl * 65:(hl + 1) * 65], ppt[:, hl * 128:(hl + 1) * 128], VA[:, pk, 0:65], False,
                                      (pi == nk - 1) and hl == 3, [VA, ppt], [accb])

                        for idx, kbi in enumerate(kbs):
                            sbk = stbank(b4)
                            masks = []
                            if kbi == qt:
                                masks.append(0)
                            if br == 2 and kbi == qt - 4:
                                masks.append(1)
                            kb.mm(sbk[:, :], KA[0:105, kbi * 128:(kbi + 1) * 128], QAg[0:105, b4 * 4:(b4 + 1) * 4, qc],
                                  True, len(masks) == 0, [KA, (QAg, qt)], [sbk])
                            for mi, mv in enumerate(masks):
                                kb.mm(sbk[:, :], identb[:], trib[:, mv, :], False, mi == len(masks) - 1, [identb, trib], [sbk])
                            pt = next_pt(b4)
                            kb.act(pt[:], sbk[:], AF.Exp, [sbk], [pt], scale=0.125)
                            yield
                            pend.append((idx, kbi, pt))
                            if len(pend) > 3:
                                do_pv(pend.pop(0))
                                yield
                        while pend:
                            do_pv(pend.pop(0))
                            yield
                        kb.cp("dve", acs[b4][:], accb[:, 0:260], [accb], [acs[b4]])
                        yield
                        a3 = acs[b4][:].rearrange("p (h n) -> p h n", h=4)
                        kb.recip(rd4[b4][:], a3[:, :, 64], [acs[b4]], [rd4[b4]])
                        h0 = (g * 8 + b4 * 4) * 3 + br
                        kb.tt("dve", cg4[b4][:], rd4[b4][:], gq[:, h0:h0 + 10:3], ALU.mult, [rd4[b4], gates], [cg4[b4]])
                        yield
                        kb.tt("dve", a3[:, :, 0:64], a3[:, :, 0:64], cg4[b4][:, 0:4].unsqueeze(2).to_broadcast([128, 4, 64]), ALU.mult,
                              [acs[b4], cg4[b4]], [acs[b4]])
                        yield
                        ov = O[:, b4 * 256:(b4 + 1) * 256].rearrange("p (h d) -> p h d", h=4)
                        kb.tt("dve", ov, ov, a3[:, :, 0:64], ALU.add, [(O, b4), acs[b4]], [(O, b4)])
                        yield

                    def task_Z(qt):
                        qc = slice(qt * 128, (qt + 1) * 128)
                        O = Oa[qt % 2]
                        zb = mbank()
                        for dc in range(8):
                            kb.mm(zb[:, :], xnT[:, dc, qc], Wz[:, dc, :], dc == 0, dc == 7, [xnT, Wz], [zb])
                            if dc % 4 == 3:
                                yield
                        kb.act(szt[:], zb[:], AF.Silu, [zb], [szt])
                        yield
                        kb.tt("dve", Ob[:], O[:], szt[:], ALU.mult, [O, szt], [Ob])
                        yield
                        tbk = mbank()
                        tv = tbk[:].bitcast(BF16)
                        for c4 in range(4):
                            kb.tr(tv[:, c4 * 128:(c4 + 1) * 128], Ob[:, c4 * 128:(c4 + 1) * 128], identb[:], [Ob, identb], [tbk])
                        yield
                        kb.cp("act", yT1[:, g * 4:(g + 1) * 4, qc], tv[:, 0:512].rearrange("p (c n) -> p c n", c=4), [tbk], [(yT1, (g, qt))])
                        yield

                    run_lanes([task_C(0)])
                    for qt in range(NT_):
                        l1_ = chain(task_branch(qt, 2, 0), task_branch(qt, 1, 0))
                        l2_ = chain(task_branch(qt, 2, 1), task_branch(qt, 1, 1))
                        third = []
                        if qt > 0:
                            third.append(task_Z(qt - 1))
                        if qt + 1 < NT_:
                            third.append(task_C(qt + 1))
                        run_lanes([l1_, l2_, chain(*third)], [1, 1, L3_STEPS])
                    run_lanes([task_Z(NT_ - 1)])
                S.barrier()
                ck("nsa")

            with ExitStack() as pe_:
                out_proj(nsa_w_out, 8, [(yT1, 8)], ymT1, x1_scr, DX1, True, pe_)
                S.barrier()
                ck("l1end")

        S.dead = False
        S.barrier()
        with nc.Block() as block:
            S.emit(block)
        print("program ops:", S.nops, "sems:", S.nsem)
    return nc


_CONST = None


def prep_inputs(inp):
    global _CONST
    if _CONST is None:
        _CONST = host_constants()
        _CONST.update(host_constants_nsa())
    f = lambda a: np.ascontiguousarray(np.asarray(a, dtype=np.float32))
    shared = {
        "hawk_w_in": f(inp["hawk_w_in"][0]),
        "hawk_w_out": f(inp["hawk_w_out"][0]),
        "hawk_w_mem_kv": f(inp["hawk_w_mem_kv"][0]),
        "g_hawk": expand_gain(f(inp["hawk_norm"][0])),
        "g_hawk_mem": expand_gain(f(inp["hawk_mem_norm"][0])),
        "bd_a": block_diag(f(inp["hawk_gate_a_w"][0])),
        "bd_x": block_diag(f(inp["hawk_gate_x_w"][0])),
        "final_norm": f(inp["final_norm"]),
        "hawk_norm_v": f(inp["hawk_norm"][0]),
        "nsa_norm_v": f(inp["nsa_norm"][0]),
        "nsa_w_in": f(inp["nsa_w_in"][0]),
        "nsa_w_out": f(inp["nsa_w_out"][0]),
        "nsa_w_mem_kv": f(inp["nsa_w_mem_kv"][0]),
        "g_nsa": expand_gain(f(inp["nsa_norm"][0])),
        "g_nsa_mem": expand_gain(f(inp["nsa_mem_norm"][0])),
        "w2k": f(inp["nsa_phi_k_w2"][0]),
        "w2v": f(inp["nsa_phi_v_w2"][0]),
    }
    for k in ("identf", "edil", "ecmp", "m12", "tri", "kaug", "qal"):
        shared[k] = _CONST[k]

    def w1_layout(w1):
        a = w1.reshape(32, 64, 256).transpose(1, 0, 2)
        return np.ascontiguousarray(np.concatenate([a, a], axis=0))
    shared["w1k"] = w1_layout(f(inp["nsa_phi_k_w1"][0]))
    shared["w1v"] = w1_layout(f(inp["nsa_phi_v_w1"][0]))
    shared["peT"] = np.ascontiguousarray(np.stack([f(inp["nsa_pe_k"][0]).T, f(inp["nsa_pe_v"][0]).T], axis=1))
    lv = np.zeros((128, 8, 8), np.float32)
    cw = f(inp["hawk_conv_w"][0])
    for k in range(4):
        lv[:, :, k] = vec_fm(cw[k])
    lv[:, :, 4] = vec_fm(f(inp["hawk_conv_b"][0]))
    lv[:, :, 5] = vec_fm(f(inp["hawk_gate_a_b"][0]).reshape(-1))
    lv[:, :, 6] = vec_fm(f(inp["hawk_gate_x_b"][0]).reshape(-1))
    lv[:, :, 7] = vec_fm(f(inp["hawk_lambda"][0]))
    shared["lru_vec"] = lv
    x = f(inp["x"])
    mem = f(inp["mem"])
    maps = []
    for b in range(x.shape[0]):
        m = dict(shared)
        m["x"] = x[b]
        m["mem"] = mem[b]
        maps.append(m)
    return maps


def kernel(**inputs):
    maps = prep_inputs(inputs)
    nc = build_program()
    res = run_bass_kernel_spmd(nc, maps, core_ids=list(range(len(maps))))
    out = np.stack([np.asarray(r["out"], dtype=np.float32) for r in res.results], axis=0)
    return out
```

```python
import math
from contextlib import ExitStack

import numpy as np
import concourse.bass as bass
import concourse.mybir as mybir
from concourse.bass_utils import run_bass_kernel_spmd

F32 = mybir.dt.float32
BF16 = mybir.dt.bfloat16
AF = mybir.ActivationFunctionType
ALU = mybir.AluOpType
AX = mybir.AxisListType

S_LEN = 2048
D = 1024
NT_ = 16
EPS = 1e-6
DIL_GROUPS = ((128, 1), (512, 4), (2048, 16))

SEM_LIMIT = 30000
N_DMA_SEMS = 24
SAME_ENGINE_SYNC = True


class Buf:
    def __init__(self, name, t, excl=False):
        self.name = name
        self.t = t
        self.excl = excl
        self.st = {}

    def __getitem__(self, idx):
        return self.t[idx]


class Sync:
    def __init__(self, nc, stack):
        self.nc = nc
        self.stack = stack
        self.engs = ["pe", "act", "dve", "pool", "sp"]
        self.ops = {e: [] for e in self.engs}
        self.cur_sem = {}
        self.cnt = {}
        self.nsem = 0
        for e in self.engs:
            self._new_sem(e)
        self.dma_sems = {}
        self.dma_val = {}
        self.dma_rr = {}
        for e in ["sp", "pool", "act"]:
            self.dma_sems[e] = [self._alloc_sem(f"d{e}{i}") for i in range(N_DMA_SEMS)]
            self.dma_val[e] = [0] * N_DMA_SEMS
            self.dma_rr[e] = 0
        self.seen = {e: {} for e in self.engs}
        self.all_ticks = {}
        self.nops = 0
        self.dead = False
        self.eng_free = {e: 0.0 for e in self.engs}
        self.lane = None
        self.tnow = 0.0

    def _alloc_sem(self, name):
        self.nsem += 1
        return self.stack.enter_context(self.nc.semaphore(f"s_{name}_{self.nsem}"))

    def _new_sem(self, e):
        self.cur_sem[e] = self._alloc_sem(e)
        self.cnt[e] = 0

    def _states(self, buf, key, create):
        if key is None:
            if create and None not in buf.st:
                buf.st[None] = [None, {}]
            return list(buf.st.values())
        out = []
        if None in buf.st:
            out.append(buf.st[None])
        if key not in buf.st and create:
            buf.st[key] = [None, {}]
        if key in buf.st:
            out.append(buf.st[key])
        return out

    @staticmethod
    def _norm(lst):
        out = []
        for r in lst or []:
            out.append(r if isinstance(r, tuple) else (r, None))
        return out

    def op(self, eng, fn, reads=None, writes=None, dma=False, cost=0.5):
        if self.dead:
            return None
        reads = self._norm(reads)
        writes = self._norm(writes)
        ex = [(b, None) for (b, k) in reads + writes if b.excl]
        if ex:
            reads = [(b, k) for (b, k) in reads if not b.excl]
            writes = [(b, k) for (b, k) in writes if not b.excl]
            for bk in ex:
                if bk not in writes:
                    writes.append(bk)
        need = []
        for buf, key in reads:
            for st in self._states(buf, key, False):
                if st[0] is not None:
                    need.append(st[0])
        for buf, key in writes:
            for st in self._states(buf, key, False):
                if st[0] is not None:
                    need.append(st[0])
                need.extend(st[1].values())
        if dma:
            i = self.dma_rr[eng]
            self.dma_rr[eng] = (i + 1) % N_DMA_SEMS
            sem = self.dma_sems[eng][i]
            prev = self.dma_val[eng][i]
            if prev > 0:
                need.append((sem, prev, "dma", 0.0))
            if prev + 16 > SEM_LIMIT:
                sem = self._alloc_sem(f"d{eng}{i}")
                self.dma_sems[eng][i] = sem
                prev = 0
            val = prev + 16
            self.dma_val[eng][i] = val
            inc = 16
            tick = [sem, val, "dma", 0.0]
        else:
            if self.cnt[eng] + 1 > SEM_LIMIT:
                self._new_sem(eng)
            self.cnt[eng] += 1
            sem = self.cur_sem[eng]
            val = self.cnt[eng]
            inc = 1
            tick = [sem, val, eng, 0.0]
        ready = 0.0
        for nd in need:
            if nd[3] > ready:
                ready = nd[3]
        start = max(self.eng_free[eng], ready + 0.06)
        if dma:
            self.eng_free[eng] = start + 0.06
        else:
            self.eng_free[eng] = start + cost
        tick[3] = start + cost
        tick = tuple(tick)
        if self.lane is not None and tick[3] > self.lane.clock:
            self.lane.clock = tick[3]
        if tick[3] > self.tnow:
            self.tnow = tick[3]
        waits = {}
        seen = self.seen[eng]
        for (s, v, src, _fin) in need:
            if src == eng and (eng == "pe" or not SAME_ENGINE_SYNC):
                continue
            sid = id(s)
            if seen.get(sid, 0) >= v:
                continue
            if sid not in waits or waits[sid][1] < v:
                waits[sid] = (s, v)
        for sid, (s, v) in waits.items():
            seen[sid] = v
        self.ops[eng].append((list(waits.values()), fn, sem, inc))
        self.all_ticks[id(sem)] = (sem, val)
        self.nops += 1
        wset = set((id(b), k) for b, k in writes)
        for buf, key in reads:
            if (id(buf), key) in wset:
                continue
            self._states(buf, key, True)
            buf.st[key][1][eng if not dma else ("dma", id(sem))] = tick
        for buf, key in writes:
            if key is None:
                buf.st = {None: [tick, {}]}
            else:
                buf.st[key] = [tick, {}]
        return tick

    def barrier(self):
        if self.dead:
            return
        ticks = list(self.all_ticks.values())
        for e in self.engs:
            wl = []
            for (s, v) in ticks:
                if self.seen[e].get(id(s), 0) < v:
                    wl.append((s, v))
                    self.seen[e][id(s)] = v
            if wl:
                self.ops[e].append((wl, None, None, 0))

    def emit(self, block):
        S = self

        def run(engname, e):
            for (wl, fn, sem, inc) in S.ops[engname]:
                for (s, v) in wl:
                    e.wait_ge(s, v)
                if fn is not None:
                    fn(e).then_inc(sem, inc)

        @block.sync
        def _(e):
            run("sp", e)

        @block.tensor
        def _(e):
            run("pe", e)

        @block.scalar
        def _(e):
            run("act", e)

        @block.vector
        def _(e):
            run("dve", e)

        @block.gpsimd
        def _(e):
            run("pool", e)


class BankView:
    def __init__(self, pair, half):
        self.pair = pair
        self.off = 512 * half

    def __getitem__(self, idx):
        if not isinstance(idx, tuple):
            idx = (idx, slice(None))
        pr, col = idx
        cs = (col.start or 0) + self.off
        ce = (col.stop if col.stop is not None else 512) + self.off
        return self.pair[pr, cs:ce:col.step] if col.step else self.pair[pr, cs:ce]


class KB:
    def __init__(self, nc, stack):
        self.nc = nc
        self.gst = stack
        self.S = Sync(nc, stack)
        self.pairs = [stack.enter_context(nc.psum_tensor(f"pair{i}", [128, 1024], F32)) for i in range(4)]
        self.banks = [Buf(f"bank{i}", BankView(self.pairs[i // 2], i % 2), excl=True) for i in range(8)]
        self.bank_rr = 0
        self.uid = 0

    def sb(self, name, shape, dt, stack=None):
        self.uid += 1
        t = (stack or self.gst).enter_context(self.nc.sbuf_tensor(f"{name}_{self.uid}", shape, dt))
        return Buf(name, t)

    def bank(self):
        b = self.banks[self.bank_rr]
        self.bank_rr = (self.bank_rr + 1) % 8
        return b

    @staticmethod
    def fsz(ap):
        n = 1
        for s in ap.shape[1:]:
            n *= int(s)
        return n

    def vcost(self, eng, ap):
        n = self.fsz(ap)
        if eng == "pool":
            return 0.3 + n / 480.0
        if eng == "act":
            return 0.22 + n / 1400.0
        return 0.08 + n / 960.0

    def mm(self, out, lhsT, rhs, start, stop, r, w):
        c = max(self.fsz(rhs), 64) / 1600.0 + 0.04
        self.S.op("pe", lambda e: e.matmul(out, lhsT=lhsT, rhs=rhs, start=start, stop=stop), reads=r, writes=w, cost=c)

    def tr(self, out, in_, ident, r, w):
        self.S.op("pe", lambda e: e.transpose(out, in_, ident), reads=r, writes=w, cost=0.11)

    def act(self, out, in_, func, r, w, **kw):
        self.S.op("act", lambda e: e.activation(out=out, in_=in_, func=func, **kw), reads=r, writes=w, cost=self.vcost("act", out))

    def tt(self, eng, out, in0, in1, op, r, w):
        self.S.op(eng, lambda e: e.tensor_tensor(out=out, in0=in0, in1=in1, op=op), reads=r, writes=w, cost=self.vcost(eng, out))

    def ts(self, eng, out, in0, s1, s2, op0, op1, r, w, **kw):
        c = self.vcost(eng, out)
        if op1 is None:
            self.S.op(eng, lambda e: e.tensor_scalar(out=out, in0=in0, scalar1=s1, scalar2=None, op0=op0, **kw), reads=r, writes=w, cost=c)
        else:
            self.S.op(eng, lambda e: e.tensor_scalar(out=out, in0=in0, scalar1=s1, scalar2=s2, op0=op0, op1=op1, **kw), reads=r, writes=w, cost=c)

    def stt(self, out, in0, scalar, in1, op0, op1, r, w, **kw):
        self.S.op("dve", lambda e: e.scalar_tensor_tensor(out=out, in0=in0, scalar=scalar, in1=in1, op0=op0, op1=op1, **kw), reads=r, writes=w,
                  cost=0.12 + self.fsz(out) / 960.0)

    def cp(self, eng, out, in_, r, w):
        c = self.vcost(eng, out)
        if eng == "act":
            self.S.op("act", lambda e: e.activation(out=out, in_=in_, func=AF.Copy), reads=r, writes=w, cost=c)
        else:
            self.S.op(eng, lambda e: e.tensor_copy(out=out, in_=in_), reads=r, writes=w, cost=c)

    def memset(self, eng, ap, val, w):
        self.S.op(eng, lambda e: e.memset(ap, val), writes=w, cost=self.vcost(eng, ap))

    def recip(self, out, in_, r, w):
        self.S.op("dve", lambda e: e.reciprocal(out=out, in_=in_), reads=r, writes=w, cost=self.vcost("dve", out))

    def dma(self, q, out, in_, r, w):
        nbytes = self.fsz(out) * int(out.shape[0]) * 4
        self.S.op(q, lambda e: e.dma_start(out=out, in_=in_), reads=r, writes=w, dma=True, cost=2.0 + nbytes / 150000.0)


def alibi_slopes(n):
    return np.exp2(-8.0 * np.arange(1, n + 1) / n).astype(np.float32)


def host_constants():
    c = {}
    c["identf"] = np.eye(128, dtype=np.float32)
    sl = alibi_slopes(12)
    ik = np.arange(128)[:, None].astype(np.float64)
    iq = np.arange(128)[None, :].astype(np.float64)
    E = np.zeros((128, 12, 256), np.float32)
    for g, (win, dil) in enumerate(DIL_GROUPS):
        for hs in range(4):
            hh = g * 4 + hs
            s = float(sl[hh]) * dil
            dist_prev = 128 + iq - ik
            ok_prev = (dist_prev <= 128)
            E[:, hh, 0:128] = np.where(ok_prev, np.exp(-s * dist_prev), 0.0)
            dist_cur = iq - ik
            ok_cur = dist_cur >= 0
            E[:, hh, 128:256] = np.where(ok_cur, np.exp(-s * dist_cur), 0.0)
    c["edil"] = E
    return c


def expand_gain(g):
    return np.ascontiguousarray(np.broadcast_to(g.reshape(8, 128).T[:, :, None], (128, 8, 128))).astype(np.float32)


def vec_fm(v):
    return np.ascontiguousarray(v.reshape(8, 128).T).astype(np.float32)


def block_diag(gw):
    out = np.zeros((128, 8, 128), np.float32)
    for c in range(8):
        out[0:64, c, 0:64] = gw[2 * c]
        out[64:128, c, 64:128] = gw[2 * c + 1]
    return out


NEGB = 8192.0


def _bf16_split3(a):
    import ml_dtypes
    a = a.astype(np.float32)
    hi = a.astype(ml_dtypes.bfloat16).astype(np.float32)
    r1 = (a - hi).astype(np.float32)
    mid = r1.astype(ml_dtypes.bfloat16).astype(np.float32)
    r2 = (r1 - mid).astype(np.float32)
    lo = r2.astype(ml_dtypes.bfloat16).astype(np.float32)
    return hi, mid, lo


def host_constants_nsa():
    c = {}
    sl = alibi_slopes(16)
    i = np.arange(128)[:, None].astype(np.float64)
    m = np.arange(247)[None, :].astype(np.float64)
    dist = i - 16.0 * (m - 120.0) - 31.0
    E = np.zeros((128, 16, 247), np.float32)
    for h in range(16):
        E[:, h, :] = np.where(dist >= 0, np.exp(-float(sl[h]) * np.maximum(dist, 0.0)), 0.0)
    c["ecmp"] = E
    ii = np.arange(128)[:, None]
    rel = np.arange(62)[None, :] - 30
    cur = (ii >= 64).astype(np.int64)
    forced = (rel == cur) | (rel == cur - 1)
    future = rel > cur
    m1 = np.where(forced | future, 0.0, 1.0).astype(np.float32)
    m2 = np.where(forced, 1e6, np.where(future, -1e6, 0.0)).astype(np.float32)
    c["m12"] = np.ascontiguousarray(np.stack([m1, m2], axis=1))
    ik = np.arange(128)[:, None]
    iq = np.arange(128)[None, :]
    diag = np.where(ik > iq, -NEGB, 0.0).astype(np.float32)
    far = np.where(ik <= iq, -NEGB, 0.0).astype(np.float32)
    c["tri"] = np.ascontiguousarray(np.stack([np.tile(diag, (1, 4)), np.tile(far, (1, 4))], axis=1))
    k = np.arange(2048)
    kp = k - 1024
    hi = (np.floor(kp / 128.0) * 128.0).astype(np.float32)
    lo = (kp - hi).astype(np.float32)
    ka = np.zeros((2, 41, 2048), np.float32)
    for j in range(32):
        ka[0, j, :] = (k // 64 == j).astype(np.float32)
    for v in range(2):
        ka[v, 32:35, :] = 1.0
        ka[v, 35:38, :] = lo[None, :]
        ka[v, 38:41, :] = hi[None, :]
    c["kaug"] = ka
    qa = np.zeros((16, 9, 2048), np.float32)
    qp = (np.arange(2048) - 1024).astype(np.float32)
    for h in range(16):
        s8 = np.float32(8.0) * np.float32(sl[h])
        a = (-s8 * qp).astype(np.float32)
        ah, am, al = _bf16_split3(a)
        sh, sm, sl_ = _bf16_split3(np.full((2048,), s8, np.float32))
        qa[h, 0], qa[h, 1], qa[h, 2] = ah, am, al
        qa[h, 3], qa[h, 4], qa[h, 5] = sh, sm, sl_
        qa[h, 6], qa[h, 7], qa[h, 8] = sh, sm, sl_
    c["qal"] = qa
    return c


def chain(*gens):
    for g_ in gens:
        yield from g_


L3_STEPS = 1


def run_lanes(lanes, weights=None):
    active = list(lanes)
    w = {id(l): 1 for l in active}
    if weights:
        for l, wt in zip(lanes, weights):
            w[id(l)] = wt
    while active:
        for l in list(active):
            for _ in range(w[id(l)]):
                try:
                    next(l)
                except StopIteration:
                    active.remove(l)
                    break


def build_program(stop_after=None):
    nc = bass.Bass("TRN2", target_bir_lowering=False)

    ckstate = {}

    def ck(name):
        if stop_after == name:
            ckstate["S"].dead = True

    def din(name, shape):
        return nc.dram_tensor(name, list(shape), F32, kind="ExternalInput").ap()

    x_d = din("x", [S_LEN, D])
    mem_d = din("mem", [256, D])
    hawk_w_in = din("hawk_w_in", [D, 7680])
    hawk_w_out = din("hawk_w_out", [1792, D])
    hawk_w_mem_kv = din("hawk_w_mem_kv", [D, 512])
    g_hawk = din("g_hawk", [128, 8, 128])
    g_hawk_mem = din("g_hawk_mem", [128, 8, 128])
    lru_vec = din("lru_vec", [128, 8, 8])
    bd_a = din("bd_a", [128, 8, 128])
    bd_x = din("bd_x", [128, 8, 128])
    identf_d = din("identf", [128, 128])
    edil_d = din("edil", [128, 12, 256])
    final_g = din("final_norm", [D])
    hawk_norm_v = din("hawk_norm_v", [D])
    nsa_norm_v = din("nsa_norm_v", [D])
    nsa_w_in = din("nsa_w_in", [D, 3376])
    nsa_w_out = din("nsa_w_out", [1280, D])
    nsa_w_mem_kv = din("nsa_w_mem_kv", [D, 512])
    g_nsa = din("g_nsa", [128, 8, 128])
    g_nsa_mem = din("g_nsa_mem", [128, 8, 128])
    w1k_d = din("w1k", [128, 32, 256])
    w1v_d = din("w1v", [128, 32, 256])
    w2k_d = din("w2k", [256, 64])
    w2v_d = din("w2v", [256, 64])
    peT_d = din("peT", [64, 2, 32])
    ecmp_d = din("ecmp", [128, 16, 247])
    m12_d = din("m12", [128, 2, 62])
    tri_d = din("tri", [128, 2, 512])
    kaug_d = din("kaug", [2, 41, 2048])
    qal_d = din("qal", [16, 9, 2048])
    out_d = nc.dram_tensor("out", [S_LEN, D], F32, kind="ExternalOutput").ap()
    x1_scr = nc.dram_tensor("x1_scr", [S_LEN, D], F32, kind="Internal").ap()

    with ExitStack() as gst:
        kb = KB(nc, gst)
        S = kb.S
        ckstate["S"] = S
        DX = Buf("x_dram", None)
        DX1 = Buf("x1_dram", None)
        DOUT = Buf("out_dram", None)

        xnT = kb.sb("xnT", [128, 8, S_LEN], BF16)
        memnT = kb.sb("memnT", [128, 8, 256], BF16)
        identf = kb.sb("identf", [128, 128], F32)
        identb = kb.sb("identb", [128, 128], BF16)
        onesb = kb.sb("onesb", [128, 128], BF16)
        wstage = [kb.sb(f"wstage{i}", [128, 1024], F32) for i in range(3)]
        ws_rr = [0]
        stat = kb.sb("stat", [128, 64], F32)
        stat_rr = [0]

        kb.dma("sp", identf[:], identf_d, [], [identf])
        kb.cp("dve", identb[:], identf[:], [identf], [identb])
        kb.memset("dve", onesb[:], 1.0, [onesb])

        def next_ws():
            b = wstage[ws_rr[0]]
            ws_rr[0] = (ws_rr[0] + 1) % len(wstage)
            return b

        def load_w(dst, dst_ap3, src_ap3, n, gain=None, key=None, q="sp", part=128, eng="pool"):
            dcs = dst_ap3.shape[1]
            if gain is None:
                kb.dma("pool", dst_ap3, src_ap3, [], [(dst, key)])
                return
            assert dcs * n <= 1024
            stg = next_ws()
            sv = stg[0:part, 0:dcs * n].rearrange("p (c n) -> p c n", c=dcs)
            kb.dma(q, sv, src_ap3, [], [stg])
            if gain is not None:
                kb.tt(eng, dst_ap3, sv, gain[0:part, 0:dcs, 0:n], ALU.mult, [stg, gain], [(dst, key)])
            else:
                kb.cp(eng, dst_ap3, sv, [stg], [(dst, key)])

        def win_cols(w_dram, c0, n):
            return w_dram.rearrange("(dc p) n -> p dc n", p=128)[:, :, c0:c0 + n]

        class NormCtx:
            def __init__(self, stack, nbuf=2):
                self.xstage = [kb.sb(f"xstage{i}", [128, 1024], F32, stack) for i in range(nbuf)]
                self.xnb = [kb.sb(f"xnb{i}", [128, 1024], BF16, stack) for i in range(nbuf)]
                self.junk = kb.sb("junk", [128, 1024], BF16, stack)

        def tile_rstd(ncx, xbuf, xap):
            i = stat_rr[0]
            stat_rr[0] = (stat_rr[0] + 1) % 32
            ss = stat[:, 2 * i:2 * i + 1]
            rs = stat[:, 2 * i + 1:2 * i + 2]
            kb.stt(ncx.junk[:], xap, 1.0, xap, ALU.mult, ALU.mult, [xbuf], [ncx.junk, (stat, i)], accum_out=ss)
            kb.ts("dve", ss, ss, 1.0 / D, EPS, ALU.mult, ALU.add, [(stat, i)], [(stat, i)])
            kb.act(ss, ss, AF.Sqrt, [(stat, i)], [(stat, i)])
            kb.recip(rs, ss, [(stat, i)], [(stat, i)])
            return rs, i

        def norm_to_T(ncx, xbuf, xap, dstT, t, ntok_off, gB=None):
            rs, i = tile_rstd(ncx, xbuf, xap)
            nb = ncx.xnb[t % 2]
            if gB is None:
                kb.ts("dve", nb[:], xap, rs, None, ALU.mult, None, [xbuf, (stat, i)], [nb])
            else:
                kb.stt(nb[:], xap, rs, gB[:], ALU.mult, ALU.mult, [xbuf, (stat, i), gB], [nb])
            bk = kb.bank()
            bv = bk[:].bitcast(BF16)
            for c in range(8):
                kb.tr(bv[:, c * 128:(c + 1) * 128], nb[:, c * 128:(c + 1) * 128], identb[:], [nb, identb], [bk])
            kb.cp("act", dstT[:, :, ntok_off:ntok_off + 128], bv.rearrange("p (c n) -> p c n", c=8), [bk], [(dstT, t)])

        def norm_to_T_gen(ncx, xbuf, xap, dstT, t, ntok_off, bk, bi, gB=None):
            rs, i = tile_rstd(ncx, xbuf, xap)
            yield
            nb = ncx.xnb[bi]
            if gB is None:
                kb.ts("dve", nb[:], xap, rs, None, ALU.mult, None, [xbuf, (stat, i)], [nb])
            else:
                kb.stt(nb[:], xap, rs, gB[:], ALU.mult, ALU.mult, [xbuf, (stat, i), gB], [nb])
            yield
            bv = bk[:].bitcast(BF16)
            for c in range(8):
                kb.tr(bv[:, c * 128:(c + 1) * 128], nb[:, c * 128:(c + 1) * 128], identb[:], [nb, identb], [bk])
            yield
            kb.cp("act", dstT[:, :, ntok_off:ntok_off + 128], bv.rearrange("p (c n) -> p c n", c=8), [bk], [(dstT, t)])
            yield

        with ExitStack() as pa:
            ncxs = [NormCtx(pa, 2), NormCtx(pa, 2)]
            gBa = kb.sb("gBa", [128, 1024], F32, pa)
            kb.dma("sp", gBa[:], hawk_norm_v.partition_broadcast(128), [], [gBa])

            def a_lane(L):
                ncx = ncxs[L]
                for t in range(L, NT_ + 2, 2):
                    xs = ncx.xstage[(t // 2) % 2]
                    if t < NT_:
                        kb.dma("sp", xs[:], x_d[t * 128:(t + 1) * 128, :], [DX], [xs])
                        yield from norm_to_T_gen(ncx, xs, xs[:], xnT, t, t * 128, kb.banks[2 * L + (t // 2) % 2], (t // 2) % 2, gB=gBa)
                    else:
                        tm = t - NT_
                        kb.dma("sp", xs[:], mem_d[tm * 128:(tm + 1) * 128, :], [], [xs])
                        yield from norm_to_T_gen(ncx, xs, xs[:], memnT, tm, tm * 128, kb.banks[2 * L + (t // 2) % 2], (t // 2) % 2)

            run_lanes([a_lane(0), a_lane(1)])
            S.barrier()
            ck("A")

        def make_loader(w_in_d, gain, nslots, stack):
            wslots = [kb.sb(f"wslot{i}", [128, 8, 128], BF16, stack) for i in range(nslots)]
            rr = [0]
            pre = {}

            def raw(c0, n, q, into, off):
                if into is None:
                    wsl = wslots[rr[0]]
                    rr[0] = (rr[0] + 1) % nslots
                else:
                    wsl = into
                load_w(wsl, wsl[:, :, off:off + n], win_cols(w_in_d, c0, n), n, gain=None, q=q, key=off)
                return wsl

            def load_win(c0, n=128, q="sp", into=None, off=0):
                if into is None and (c0, n) in pre:
                    return pre.pop((c0, n))
                return raw(c0, n, q, into, off)

            def prefetch(c0, n=128):
                pre[(c0, n)] = raw(c0, n, "sp", None, 0)
            load_win.prefetch = prefetch
            return load_win

        def proj_fm(wsl, n, evac, woff=0):
            for tc in range(4):
                bk = kb.bank()
                for dc in range(8):
                    kb.mm(bk[0:n, :], wsl[:, dc, woff:woff + n], xnT[:, dc, tc * 512:(tc + 1) * 512], dc == 0, dc == 7,
                          [wsl, xnT], [bk])
                evac(bk, bk[0:n, :], tc)

        def mem_kv(w_kv_d, gain, kmT, vm, stack):
            wkv = kb.sb("wkv", [128, 8, 512], BF16, stack)
            for j in range(4):
                load_w(wkv, wkv[:, :, j * 128:(j + 1) * 128], win_cols(w_kv_d, j * 128, 128), 128, gain=gain, key=j)
            for h in range(4):
                bk = kb.bank()
                for dc in range(8):
                    kb.mm(bk[0:64, 0:256], wkv[:, dc, h * 64:(h + 1) * 64], memnT[:, dc, :], dc == 0, dc == 7,
                          [wkv, memnT], [bk])
                kb.cp("act", kmT[0:64, h, :], bk[0:64, 0:256], [bk], [(kmT, h)])
            for mt in range(2):
                bk = kb.bank()
                for dc in range(8):
                    kb.mm(bk[:, 0:256], memnT[:, dc, mt * 128:(mt + 1) * 128], wkv[:, dc, 256:512], dc == 0, dc == 7,
                          [wkv, memnT], [bk])
                kb.cp("act", vm[:, mt, :], bk[:, 0:256], [bk], [(vm, mt)])

        def mem_attn(load_win, colq, colz, kmT, vm, ymT, stack):
            qmTs = [kb.sb(f"qmT{i}", [64, S_LEN], BF16, stack) for i in range(2)]
            szms = [kb.sb(f"szm{i}", [64, S_LEN], BF16, stack) for i in range(2)]
            PTm = [kb.sb(f"PTm{i}", [128, 512], BF16, stack) for i in range(4)]
            rdm = [kb.sb(f"rdm{i}", [64, 512], F32, stack) for i in range(2)]

            def lane(h, L):
                qmT, szm = qmTs[L], szms[L]
                b0, b1, b2, b3 = [kb.banks[4 * L + j] for j in range(4)]
                wq = load_win(colq + h * 64, 64)
                wz = load_win(colz + h * 64, 64)
                for (wsl, dst, func) in ((wq, qmT, None), (wz, szm, AF.Silu)):
                    for tc in range(4):
                        bk = b0 if tc % 2 == 0 else b1
                        for dc in range(8):
                            kb.mm(bk[0:64, :], wsl[:, dc, 0:64], xnT[:, dc, tc * 512:(tc + 1) * 512], dc == 0, dc == 7, [wsl, xnT], [bk])
                        yield
                        if func is None:
                            kb.cp("act", dst[0:64, tc * 512:(tc + 1) * 512], bk[0:64, :], [bk], [(dst, tc)])
                        else:
                            kb.act(dst[0:64, tc * 512:(tc + 1) * 512], bk[0:64, :], func, [bk], [(dst, tc)])
                        yield
                for tc in range(4):
                    pts = []
                    for mt in range(2):
                        bk = b0 if mt == 0 else b1
                        kb.mm(bk[:, :], kmT[0:64, h, mt * 128:(mt + 1) * 128], qmT[0:64, tc * 512:(tc + 1) * 512],
                              True, True, [kmT, (qmT, tc)], [bk])
                        pt = PTm[2 * L + mt]
                        kb.act(pt[:], bk[:], AF.Exp, [bk], [pt], scale=0.125)
                        pts.append(pt)
                        yield
                    for mt in range(2):
                        kb.mm(b2[0:64, :], vm[:, mt, h * 64:(h + 1) * 64], pts[mt][:], mt == 0, mt == 1, [vm, pts[mt]], [b2])
                    for mt in range(2):
                        kb.mm(b3[0:64, :], onesb[:, 0:64], pts[mt][:], mt == 0, mt == 1, [onesb, pts[mt]], [b3])
                    yield
                    rd = rdm[L]
                    kb.recip(rd[:], b3[0:64, :], [b3], [rd])
                    kb.tt("dve", rd[:], b2[0:64, :], rd[:], ALU.mult, [b2, rd], [rd])
                    yield
                    kb.tt("dve", ymT[0:64, h, tc * 512:(tc + 1) * 512], rd[:], szm[0:64, tc * 512:(tc + 1) * 512], ALU.mult,
                          [rd, (szm, tc)], [(ymT, (h, tc))])
                    yield

            run_lanes([lane(0, 0), lane(1, 1)])
            run_lanes([lane(2, 0), lane(3, 1)])

        def out_proj(w_out_d, nch, yTl, ymT, resid_d, resid_buf, final, stack, dbg=False):
            ysrc = []
            for (yb_, n_) in yTl:
                for ci in range(n_):
                    ysrc.append((yb_, ci))
            ncxs = [NormCtx(stack, 2), NormCtx(stack, 2)]
            WO = kb.sb("WO", [128, nch, 1024], BF16, stack)
            WOm = kb.sb("WOm", [64, 4, 1024], BF16, stack)
            wo_v = w_out_d[0:nch * 128, :].rearrange("(c p) n -> p c n", p=128)
            for c in range(0, nch, 4):
                load_w(WO, WO[:, c:c + 4, :], wo_v[:, c:c + 4, :], 1024, key=c)
            wom_v = w_out_d[nch * 128:nch * 128 + 256, :].rearrange("(h p) n -> p h n", p=64)
            load_w(WOm, WOm[0:64, :, :], wom_v, 1024, key=0, part=64)
            x1t = [kb.sb(f"x1t{i}", [128, 1024], F32, stack) for i in range(2)]
            if not final:
                gNx = kb.sb("gNx", [128, 1024], F32, stack)
                kb.dma("sp", gNx[:], nsa_norm_v.partition_broadcast(128), [], [gNx])
            if final:
                gF = kb.sb("gF", [128, 1024], F32, stack)
                kb.dma("sp", gF[:], final_g.partition_broadcast(128), [], [gF])
                ot = [kb.sb(f"ot{i}", [128, 1024], F32, stack) for i in range(2)]

            def lane(L):
                ncx = ncxs[L]
                bks = [kb.banks[4 * L + j] for j in range(4)]
                for t in range(L, NT_, 2):
                    xs = ncx.xstage[(t // 2) % 2]
                    kb.dma("sp", xs[:], resid_d[t * 128:(t + 1) * 128, :], [resid_buf], [xs])
                    x1 = x1t[L]
                    for half in range(2):
                        bk = bks[half]
                        for c in range(nch):
                            yb_, ci = ysrc[c]
                            kb.mm(bk[:, :], yb_[:, ci, t * 128:(t + 1) * 128], WO[:, c, half * 512:(half + 1) * 512], c == 0, False,
                                  [yb_, WO], [bk])
                            if c % 4 == 3:
                                yield
                        for h in range(4):
                            kb.mm(bk[:, :], ymT[0:64, h, t * 128:(t + 1) * 128], WOm[0:64, h, half * 512:(half + 1) * 512], False, h == 3,
                                  [ymT, WOm], [bk])
                        yield
                        kb.tt("dve", x1[:, half * 512:(half + 1) * 512], xs[:, half * 512:(half + 1) * 512], bk[:], ALU.add,
                              [xs, bk], [(x1, half)])
                        yield
                    if not final:
                        kb.dma("sp", x1_scr[t * 128:(t + 1) * 128, :], x1[:], [x1], [DX1])
                        yield from norm_to_T_gen(ncx, x1, x1[:], xnT, t, t * 128, bks[2], (t // 2) % 2, gB=gNx)
                        if dbg:
                            kb.dma("sp", out_d[t * 128:(t + 1) * 128, :], x1[:], [x1], [DOUT])
                    else:
                        rs, i = tile_rstd(ncx, x1, x1[:])
                        yield
                        o = ot[L]
                        kb.stt(o[:], x1[:], rs, gF[:], ALU.mult, ALU.mult, [x1, (stat, i), gF], [o])
                        yield
                        kb.dma("sp", out_d[t * 128:(t + 1) * 128, :], o[:], [o], [DOUT])
                        yield

            run_lanes([lane(0), lane(1)])

        with ExitStack() as l0:
            yTa = kb.sb("yTa", [128, 8, S_LEN], BF16, l0)
            ymT = kb.sb("ymT", [64, 4, S_LEN], BF16, l0)
            load_win = make_loader(hawk_w_in, None, 9, l0)
            kmT = kb.sb("kmT", [64, 4, 256], BF16, l0)
            vm = kb.sb("vm", [128, 2, 256], BF16, l0)
            with ExitStack() as pm:
                gHm = kb.sb("gHm", [128, 8, 128], F32, pm)
                kb.dma("sp", gHm[:], g_hawk_mem, [], [gHm])
                mem_kv(hawk_w_mem_kv, gHm, kmT, vm, pm)
                for c0_ in (7168, 7424, 7168 + 64, 7424 + 64):
                    load_win.prefetch(c0_, 64)
                S.barrier()
                ck("memkv0")
            with ExitStack() as pd:
                mem_attn(load_win, 7168, 7424, kmT, vm, ymT, pd)
                for c0_ in (0, 1024, 128, 1024 + 128):
                    load_win.prefetch(c0_, 128)
                S.barrier()
                ck("mem0")

            with ExitStack() as pb:
                lv = kb.sb("lv", [128, 8, 8], F32, pb)
                cvec = kb.sb("cvec", [128, 8, 2], F32, pb)
                bda = kb.sb("bda", [128, 8, 128], BF16, pb)
                bdx = kb.sb("bdx", [128, 8, 128], BF16, pb)
                kb.dma("sp", lv[:], lru_vec, [], [lv])
                load_w(bda, bda[:], bd_a, 128)
                load_w(bdx, bdx[:], bd_x, 128)
                kb.act(cvec[:, :, 0], lv[:, :, 7], AF.Exp, [lv], [cvec], scale=-1.0)
                kb.act(cvec[:, :, 0], cvec[:, :, 0], AF.Ln, [cvec], [cvec], bias=1.0)
                kb.ts("dve", cvec[:, :, 1], cvec[:, :, 0], -16.0, None, ALU.mult, None, [cvec], [cvec])
                kb.ts("dve", cvec[:, :, 0], cvec[:, :, 0], -8.0, None, ALU.mult, None, [cvec], [cvec])
                sets = []
                for L in range(2):
                    sets.append(dict(
                        B1=kb.sb(f"B1_{L}", [128, S_LEN + 4], F32, pb), B2=kb.sb(f"B2_{L}", [128, S_LEN], F32, pb),
                        B3=kb.sb(f"B3_{L}", [128, S_LEN], F32, pb), B4=kb.sb(f"B4_{L}", [128, S_LEN], F32, pb),
                        xcb=kb.sb(f"xcb_{L}", [128, S_LEN], BF16, pb), sz=kb.sb(f"sz_{L}", [128, S_LEN], BF16, pb)))

                def lru_lane(L):
                    st_ = sets[L]
                    B1, B2, B3, B4, xcb, sz = st_["B1"], st_["B2"], st_["B3"], st_["B4"], st_["xcb"], st_["sz"]
                    bks = [kb.banks[4 * L + j] for j in range(4)]
                    for c in range(L, 8, 2):
                        wxa = load_win(c * 128)
                        wza = load_win(1024 + c * 128)
                        kb.memset("dve", B1[:, 0:3], 0.0, [(B1, "pad")])
                        for (wsl, which) in ((wxa, 0), (wza, 1)):
                            for tc in range(4):
                                bk = bks[tc % 4]
                                for dc in range(8):
                                    kb.mm(bk[:, :], wsl[:, dc, 0:128], xnT[:, dc, tc * 512:(tc + 1) * 512], dc == 0, dc == 7, [wsl, xnT], [bk])
                                yield
                                if which == 0:
                                    kb.cp("act", B1[:, 3 + tc * 512:3 + (tc + 1) * 512], bk[:], [bk], [(B1, tc)])
                                else:
                                    kb.act(sz[:, tc * 512:(tc + 1) * 512], bk[:], AF.Silu, [bk], [(sz, tc)])
                                yield
                        kb.ts("dve", B2[:], B1[:, 0:S_LEN], lv[:, c, 0:1], lv[:, c, 4:5], ALU.mult, ALU.add, [B1, lv], [B2])
                        yield
                        for k in range(1, 4):
                            kb.stt(B2[:], B1[:, k:k + S_LEN], lv[:, c, k:k + 1], B2[:], ALU.mult, ALU.add, [B1, lv, B2], [B2])
                            yield
                        kb.cp("dve", xcb[:], B2[:], [B2], [xcb])
                        yield
                        for (bd, dstb, col) in ((bda, B1, 5), (bdx, B4, 6)):
                            for tc in range(4):
                                bk = bks[tc % 4]
                                kb.mm(bk[:, :], bd[:, c, :], xcb[:, tc * 512:(tc + 1) * 512], True, True, [bd, xcb], [bk])
                                yield
                                kb.act(dstb[:, tc * 512:(tc + 1) * 512], bk[:], AF.Sigmoid, [bk, lv], [(dstb, tc)], bias=lv[:, c, col:col + 1])
                                yield
                        r_ap = B1[:, 0:S_LEN]
                        kb.act(B3[:], r_ap, AF.Exp, [B1, cvec], [B3], scale=cvec[:, c, 0:1])
                        yield
                        kb.act(r_ap, r_ap, AF.Exp, [B1, cvec], [B1], scale=cvec[:, c, 1:2])
                        yield
                        kb.tt("dve", B2[:], B2[:], B4[:], ALU.mult, [B2, B4], [B2])
                        yield
                        kb.ts("dve", r_ap, r_ap, -1.0, 1.0, ALU.mult, ALU.add, [B1], [B1])
                        yield
                        kb.ts("dve", r_ap, r_ap, 0.0, None, ALU.max, None, [B1], [B1])
                        yield
                        kb.act(r_ap, r_ap, AF.Sqrt, [B1], [B1])
                        kb.memset("dve", B1[:, 0:1], 1.0, [B1])
                        yield
                        kb.tt("dve", B2[:], B2[:], r_ap, ALU.mult, [B2, B1], [B2])
                        yield
                        S.op("dve", lambda e, B4=B4, B3=B3, B2=B2: e.tensor_tensor_scan(out=B4[:], data0=B3[:], data1=B2[:], initial=0.0,
                                                                                      op0=ALU.mult, op1=ALU.add), reads=[B3, B2], writes=[B4])
                        yield
                        kb.tt("dve", yTa[:, c, :], B4[:], sz[:], ALU.mult, [B4, sz], [(yTa, c)])
                        yield

                run_lanes([lru_lane(0), lru_lane(1)])
                for c0_ in (2048 + 4608, 2048, 2048 + 1536, 2048 + 3072):
                    load_win.prefetch(c0_, 128)
                S.barrier()
                ck("lru")
            yTb = kb.sb("yTb", [128, 4, S_LEN], BF16, l0)

            with ExitStack() as pc:
                edil = kb.sb("edil", [128, 12, 256], F32, pc)
                kb.dma("sp", edil[:], edil_d, [], [edil])
                qTs = [kb.sb(f"qT{i}", [128, S_LEN], BF16, pc) for i in range(2)]
                kTs = [kb.sb(f"kT{i}", [128, S_LEN], BF16, pc) for i in range(2)]
                vTs = [kb.sb(f"vT{i}", [128, S_LEN], BF16, pc) for i in range(2)]
                Vps = [kb.sb(f"Vp{i}", [128, 16, 128], BF16, pc) for i in range(2)]
                szbs = [kb.sb(f"szb{i}", [128, S_LEN], BF16, pc) for i in range(2)]
                NTa = kb.sb("NTa", [128, S_LEN], F32, pc)
                DBa = kb.sb("DBa", [128, S_LEN], F32, pc)
                Pf = [kb.sb(f"Pf{i}", [128, 256], F32, pc) for i in range(3)]
                PT = [kb.sb(f"PT{i}", [128, 256], BF16, pc) for i in range(3)]
                sc_d = 128.0 ** -0.5
                items = [(hs, g) for hs in range(4) for g in range(3)]
                prot = [0]

                def pbank():
                    b = kb.banks[6 + prot[0]]
                    prot[0] ^= 1
                    return b

                def dtoks(d, r, b):
                    t0 = r + d * 128 * b
                    return slice(t0, t0 + d * 127 + 1, d)

                def proj_fm_lane(wsl, dst, func):
                    for tc in range(4):
                        bk = pbank()
                        for dc in range(8):
                            kb.mm(bk[:, :], wsl[:, dc, 0:128], xnT[:, dc, tc * 512:(tc + 1) * 512], dc == 0, dc == 7, [wsl, xnT], [bk])
                        yield
                        if func is None:
                            kb.cp("act", dst[:, tc * 512:(tc + 1) * 512], bk[:], [bk], [(dst, tc)])
                        else:
                            kb.act(dst[:, tc * 512:(tc + 1) * 512], bk[:], func, [bk], [(dst, tc)])
                        yield

                def task_proj(i):
                    hs, g = items[i]
                    win, d = DIL_GROUPS[g]
                    hh = g * 4 + hs
                    nqb = (S_LEN // d) // 128
                    s = i % 2
                    if g == 0:
                        wz = load_win(2048 + 4608 + hs * 128)
                        yield from proj_fm_lane(wz, szbs[hs % 2], AF.Silu)
                    wq = load_win(2048 + hh * 128)
                    yield from proj_fm_lane(wq, qTs[s], None)
                    wk = load_win(2048 + 1536 + hh * 128)
                    yield from proj_fm_lane(wk, kTs[s], None)
                    wv = load_win(2048 + 3072 + hh * 128)
                    yield from proj_fm_lane(wv, vTs[s], None)
                    for j in range(4):
                        bk = pbank()
                        bv = bk[:].bitcast(BF16)
                        for k in range(4):
                            r, b = divmod(4 * j + k, nqb)
                            kb.tr(bv[:, k * 128:(k + 1) * 128], vTs[s][:, dtoks(d, r, b)], identb[:], [vTs[s], identb], [bk])
                        yield
                        kb.cp("act", Vps[s][:, 4 * j:4 * j + 4, :], bv[:, 0:512].rearrange("p (k n) -> p k n", k=4), [bk], [(Vps[s], j)])
                        yield

                def task_tile(i, ti, lane):
                    hs, g = items[i]
                    win, d = DIL_GROUPS[g]
                    hh = g * 4 + hs
                    nqb = (S_LEN // d) // 128
                    s = i % 2
                    qT, kT, Vp = qTs[s], kTs[s], Vps[s]
                    r, qb = divmod(ti, nqb)
                    qs = dtoks(d, r, qb)
                    kbs = [qb - 1, qb] if qb > 0 else [qb]
                    sbk = kb.banks[2 * lane]
                    ndb = kb.banks[2 * lane + 1]
                    for kbi in kbs:
                        typ = 0 if kbi < qb else 1
                        kb.mm(sbk[:, typ * 128:(typ + 1) * 128], kT[:, dtoks(d, r, kbi)], qT[:, qs], True, True, [kT, qT], [sbk])
                    yield
                    lo = 0 if qb > 0 else 128
                    pf = Pf[lane]
                    pt = PT[lane]
                    kb.act(pf[:, lo:256], sbk[:, lo:256], AF.Exp, [sbk], [pf], scale=sc_d)
                    yield
                    kb.tt("dve", pt[:, lo:256], pf[:, lo:256], edil[:, hh, lo:256], ALU.mult, [pf, edil], [pt])
                    yield
                    for j, kbi in enumerate(kbs):
                        typ = 0 if kbi < qb else 1
                        kb.mm(ndb[:, 0:128], Vp[:, r * nqb + kbi, :], pt[:, typ * 128:(typ + 1) * 128], j == 0, j == len(kbs) - 1,
                              [Vp, pt], [ndb])
                    for j, kbi in enumerate(kbs):
                        typ = 0 if kbi < qb else 1
                        kb.mm(ndb[:, 128:256], onesb[:], pt[:, typ * 128:(typ + 1) * 128], j == 0, j == len(kbs) - 1,
                              [onesb, pt], [ndb])
                    yield
                    if g == 0:
                        kb.cp("dve", NTa[:, qs], ndb[:, 0:128], [ndb], [NTa])
                        kb.cp("dve", DBa[:, qs], ndb[:, 128:256], [ndb], [DBa])
                    else:
                        kb.tt("dve", NTa[:, qs], NTa[:, qs], ndb[:, 0:128], ALU.add, [ndb, NTa], [NTa])
                        kb.tt("dve", DBa[:, qs], DBa[:, qs], ndb[:, 128:256], ALU.add, [ndb, DBa], [DBa])
                    yield

                def tile_lane(i, lane):
                    for ti in range(lane, 16, 3):
                        yield from task_tile(i, ti, lane)

                run_lanes([task_proj(0)])
                for i in range(len(items)):
                    hs, g = items[i]
                    lanes = [tile_lane(i, 0), tile_lane(i, 1), tile_lane(i, 2)]
                    if i + 1 < len(items):
                        lanes.append(task_proj(i + 1))
                    run_lanes(lanes)
                    if g == 2:
                        kb.recip(DBa[:], DBa[:], [DBa], [DBa])
                        kb.tt("dve", NTa[:], NTa[:], DBa[:], ALU.mult, [NTa, DBa], [NTa])
                        kb.tt("dve", yTb[:, hs, :], NTa[:], szbs[hs % 2][:], ALU.mult, [NTa, szbs[hs % 2]], [(yTb, hs)])
                S.barrier()
                ck("dil")

            with ExitStack() as pe_:
                out_proj(hawk_w_out, 12, [(yTa, 8), (yTb, 4)], ymT, x_d, DX, False, pe_, dbg=(stop_after == "l0"))
                S.barrier()
                ck("l0end")

        if stop_after != "l0":
          with ExitStack() as l1:
            yT1 = kb.sb("yT1", [128, 8, S_LEN], BF16, l1)
            ymT1 = kb.sb("ymT1", [64, 4, S_LEN], BF16, l1)
            load_win = make_loader(nsa_w_in, None, 5, l1)
            kcmpT = kb.sb("kcmpT", [64, 2, 128], BF16, l1)
            vcmp = kb.sb("vcmp", [128, 2, 64], BF16, l1)
            gates = kb.sb("gates", [128, 16, 48], F32, l1)
            with ExitStack() as pmm:
                kmT = kb.sb("kmT1", [64, 4, 256], BF16, pmm)
                vm = kb.sb("vm1", [128, 2, 256], BF16, pmm)
                with ExitStack() as pm:
                    gNm = kb.sb("gNm", [128, 8, 128], F32, pm)
                    kb.dma("sp", gNm[:], g_nsa_mem, [], [gNm])
                    mem_kv(nsa_w_mem_kv, gNm, kmT, vm, pm)
                    for c0_ in (2864, 3120, 2864 + 64, 3120 + 64):
                        load_win.prefetch(c0_, 64)
                    S.barrier()
                    ck("memkv1")
                with ExitStack() as pd:
                    mem_attn(load_win, 2864, 3120, kmT, vm, ymT1, pd)
                    load_win.prefetch(1792, 48)
                    load_win.prefetch(1024, 128)
                    load_win.prefetch(1024 + 128, 128)
                    S.barrier()
                    ck("mem1")

            with ExitStack() as pq:
                wg = load_win(1792, 48)
                for t in range(NT_):
                    bk = kb.bank()
                    for dc in range(8):
                        kb.mm(bk[:, 0:48], xnT[:, dc, t * 128:(t + 1) * 128], wg[:, dc, 0:48], dc == 0, dc == 7, [xnT, wg], [bk])
                    kb.act(gates[:, t, :], bk[:, 0:48], AF.Sigmoid, [bk], [(gates, t)])
                kcT = kb.sb("kcT", [128, S_LEN], BF16, pq)
                vcT = kb.sb("vcT", [128, S_LEN], BF16, pq)
                wkc = load_win(1024)
                proj_fm(wkc, 128, lambda bk, ap, tc: kb.cp("act", kcT[:, tc * 512:(tc + 1) * 512], ap, [bk], [(kcT, tc)]))
                wvc = load_win(1024 + 128)
                proj_fm(wvc, 128, lambda bk, ap, tc: kb.cp("act", vcT[:, tc * 512:(tc + 1) * 512], ap, [bk], [(vcT, tc)]))
                W1 = kb.sb("W1", [128, 32, 256], BF16, pq)
                w2 = kb.sb("w2", [128, 2, 64], BF16, pq)
                peS = kb.sb("peS", [64, 2, 32], F32, pq)
                peb = kb.sb("peb", [64, 2, 32], BF16, pq)
                hidT = kb.sb("hidT", [128, 2, 128], BF16, pq)
                cb = kb.sb("cb", [128, 2], F32, pq)
                kb.dma("sp", peS[:], peT_d, [], [peS])
                kb.cp("dve", peb[:], peS[:], [peS], [peb])
                for kv in range(2):
                    w1d = w1k_d if kv == 0 else w1v_d
                    w2d = w2k_d if kv == 0 else w2v_d
                    srcT = kcT if kv == 0 else vcT
                    for p4 in range(2):
                        load_w(W1, W1[:, p4 * 16:(p4 + 1) * 16, :], w1d[:, p4 * 16:(p4 + 1) * 16, :], 256, key=p4)
                    load_w(w2, w2[:, :, :], w2d.rearrange("(hc p) d -> p hc d", p=128), 64)
                    for hc in range(2):
                        bk = kb.bank()
                        for p in range(32):
                            kb.mm(bk[:, 0:1], W1[0:64, p, hc * 128:(hc + 1) * 128], peb[0:64, kv, p:p + 1], p == 0, p == 31,
                                  [W1, peb], [bk])
                        kb.cp("dve", cb[:, hc:hc + 1], bk[:, 0:1], [bk], [(cb, hc)])
                    for g in range(2):
                        for hc in range(2):
                            bk = kb.bank()
                            for p in range(32):
                                kb.mm(bk[:, 0:127], W1[g * 64:(g + 1) * 64, p, hc * 128:(hc + 1) * 128],
                                      srcT[g * 64:(g + 1) * 64, p:p + 16 * 126 + 1:16], p == 0, p == 31, [W1, srcT], [bk])
                            kb.act(hidT[:, hc, 0:127], bk[:, 0:127], AF.Silu, [bk, cb], [(hidT, hc)], bias=cb[:, hc:hc + 1])
                        bk = kb.bank()
                        if kv == 0:
                            for hc in range(2):
                                kb.mm(bk[0:64, 0:127], w2[:, hc, :], hidT[:, hc, 0:127], hc == 0, hc == 1, [w2, hidT], [bk])
                            kb.cp("dve", kcmpT[0:64, g, 0:127], bk[0:64, 0:127], [bk], [(kcmpT, g)])
                        else:
                            for hc in range(2):
                                kb.mm(bk[0:127, 0:64], hidT[:, hc, 0:127], w2[:, hc, :], hc == 0, hc == 1, [w2, hidT], [bk])
                            kb.cp("dve", vcmp[0:127, g, :], bk[0:127, 0:64], [bk], [(vcmp, g)])
                S.barrier()
                ck("cmpkv")

            with ExitStack() as pg:
                QAg = kb.sb("QAg", [105, 8, S_LEN], BF16, pg)
                KAs = kb.sb("KAs", [105, S_LEN], BF16, pg)
                KAw = kb.sb("KAw", [105, S_LEN], BF16, pg)
                VAs = kb.sb("VAs", [128, 16, 128], BF16, pg)
                VAw = kb.sb("VAw", [128, 16, 128], BF16, pg)
                ecmp = kb.sb("ecmp", [128, 8, 247], F32, pg)
                Wz = kb.sb("Wz", [128, 8, 512], BF16, pg)
                m12 = kb.sb("m12", [128, 2, 62], F32, pg)
                trib = kb.sb("trib", [128, 2, 512], BF16, pg)
                kb.dma("sp", m12[:], m12_d, [], [m12])
                kb.dma("pool", trib[:], tri_d, [], [trib])
                for v, KA in enumerate((KAs, KAw)):
                    kb.dma("pool", KA[64:105, :], kaug_d[v], [], [(KA, "aug")])
                kb.memset("pool", VAs[:, :, 64:128], 1.0, [(VAs, "ones")])
                kb.memset("pool", VAw[:, :, 64:128], 1.0, [(VAw, "ones")])
                Pu = [kb.sb(f"Pu{i}", [128, 4, 128], F32, pg) for i in range(2)]
                Pub = [kb.sb("Pub0", [128, 4, 128], BF16, pg)] * 2
                pT = [kb.sb("pT0", [128, 4, 128], BF16, pg)] * 2
                for i in range(2):
                    kb.memset("pool", Pu[i][:], 0.0, [Pu[i]])
                psg = kb.sb("psg", [128, 128], F32, pg)
                den8 = kb.sb("den8", [128, 8], F32, pg)
                cg8 = kb.sb("cg8", [128, 8], F32, pg)
                imp = kb.sb("imp", [128, 32], F32, pg)
                impm = kb.sb("impm", [128, 32], F32, pg)
                m8 = kb.sb("m8", [128, 8], F32, pg)
                negp = kb.sb("negp", [128, 96], F32, pg)
                negS = kb.sb("negS", [96, 128], BF16, pg)
                kb.memset("pool", negp[:], 0.0, [negp])
                PTs = [kb.sb(f"PTs{i}", [128, 512], BF16, pg) for i in range(10)]
                pts_rr = [0]
                zerob = kb.sb("zerob", [128, 260], BF16, pg)
                kb.memset("pool", zerob[:], 0.0, [zerob])
                acs = [kb.sb(f"acs{i}", [128, 260], F32, pg) for i in range(2)]
                rd4 = [kb.sb(f"rd4{i}", [128, 4], F32, pg) for i in range(2)]
                cg4 = [kb.sb(f"cg4{i}", [128, 4], F32, pg) for i in range(2)]
                Oa = [kb.sb(f"Oa{i}", [128, 512], F32, pg) for i in range(3)]
                szt = kb.sb("szt", [128, 512], F32, pg)
                Ob = kb.sb("Ob", [128, 512], BF16, pg)
                accbanks = [kb.banks[0], kb.banks[1]]
                rot = [2]

                def rbank():
                    b = kb.banks[rot[0]]
                    rot[0] = rot[0] + 1 if rot[0] < 7 else 2
                    return b

                strot = [0, 0]

                def stbank(lane):
                    b = kb.banks[2 + 2 * lane + strot[lane]]
                    strot[lane] ^= 1
                    return b

                def mbank():
                    return kb.banks[6]

                ptrot = [0, 0]

                def next_pt(lane):
                    p = PTs[5 * lane + ptrot[lane]]
                    ptrot[lane] = (ptrot[lane] + 1) % 5
                    return p

                for g in range(2):
                    kb.dma("sp", ecmp[:], ecmp_d[:, g * 8:(g + 1) * 8, :], [], [ecmp])
                    for j in range(4):
                        load_w(Wz, Wz[:, :, j * 128:(j + 1) * 128], win_cols(nsa_w_in, 1840 + g * 512 + j * 128, 128), 128,
                               gain=None, key=j)
                    for pr in range(4):
                        wq = load_win((g * 8 + 2 * pr) * 64, 128)
                        for tc in range(4):
                            bk = rbank()
                            for dc in range(8):
                                kb.mm(bk[:, :], wq[:, dc, 0:128], xnT[:, dc, tc * 512:(tc + 1) * 512], dc == 0, dc == 7, [wq, xnT], [bk])
                            kb.cp("act", QAg[0:64, 2 * pr, tc * 512:(tc + 1) * 512], bk[0:64, :], [bk], [QAg])
                            stq = PTs[5 * (pr % 2) + tc]
                            kb.cp("dve", stq[64:128, :], bk[64:128, :], [bk], [stq])
                            kb.dma("sp", QAg[0:64, 2 * pr + 1, tc * 512:(tc + 1) * 512], stq[64:128, :], [stq], [QAg])
                    kb.dma("pool", QAg[96:105, :, :], qal_d[g * 8:(g + 1) * 8].rearrange("h r n -> r h n"), [], [QAg])
                    for KA, col in ((KAs, 1024 + 2 * 128 + g * 64), (KAw, 1024 + 4 * 128 + g * 64)):
                        wk = load_win(col, 64)
                        for tc in range(4):
                            bk = rbank()
                            for dc in range(8):
                                kb.mm(bk[0:64, :], wk[:, dc, 0:64], xnT[:, dc, tc * 512:(tc + 1) * 512], dc == 0, dc == 7, [wk, xnT], [bk])
                            kb.cp("act", KA[0:64, tc * 512:(tc + 1) * 512], bk[0:64, :], [bk], [(KA, tc)])
                    wv = load_win(1024 + 3 * 128 + g * 64, 64)
                    load_win(1024 + 5 * 128 + g * 64, 64, into=wv, off=64)
                    for t in range(NT_):
                        bk = rbank()
                        for dc in range(8):
                            kb.mm(bk[:, 0:128], xnT[:, dc, t * 128:(t + 1) * 128], wv[:, dc, 0:128], dc == 0, dc == 7, [xnT, wv], [bk])
                        kb.cp("act", VAs[:, t, 0:64], bk[:, 0:64], [bk], [(VAs, t)])
                        kb.cp("dve", VAw[:, t, 0:64], bk[:, 64:128], [bk], [(VAw, t)])

                    def task_C(qt):
                        qc = slice(qt * 128, (qt + 1) * 128)
                        O = Oa[qt % 3]
                        gq = gates[:, qt, :]
                        eoff = 120 - 8 * qt
                        ocb = kb.banks[7]
                        for b4 in range(2):
                            sbk = mbank()
                            for hl in range(4):
                                r = b4 * 4 + hl
                                kb.mm(sbk[:, hl * 128:hl * 128 + 127], QAg[0:64, r, qc], kcmpT[0:64, g, 0:127], True, True,
                                      [(QAg, qt), kcmpT], [sbk])
                            yield
                            pu = Pu[b4]
                            s3 = sbk[:].rearrange("p (h n) -> p h n", h=4)
                            kb.act(pu[:, :, 0:127], s3[:, :, 0:127], AF.Exp, [sbk], [pu], scale=0.125)
                            yield
                            kb.tt("dve", pu[:, :, 0:127], pu[:, :, 0:127], ecmp[:, b4 * 4:(b4 + 1) * 4, eoff:eoff + 127], ALU.mult,
                                  [pu, ecmp], [pu])
                            S.op("dve", lambda e, pu=pu, b4=b4: e.tensor_reduce(out=den8[:, b4 * 4:(b4 + 1) * 4], in_=pu[:, :, 0:127],
                                                                                axis=AX.X, op=ALU.add),
                                 reads=[pu], writes=[(den8, b4)])
                            yield
                            kb.ts("dve", den8[:, b4 * 4:(b4 + 1) * 4], den8[:, b4 * 4:(b4 + 1) * 4], 1e-30, None, ALU.max, None,
                                  [(den8, b4)], [(den8, b4)])
                            kb.recip(den8[:, b4 * 4:(b4 + 1) * 4], den8[:, b4 * 4:(b4 + 1) * 4], [(den8, b4)], [(den8, b4)])
                            pub = Pub[b4]
                            kb.cp("dve", pub[:], pu[:], [pu], [pub])
                            yield
                            for hl in range(4):
                                r = b4 * 4 + hl
                                if r == 0:
                                    kb.ts("dve", psg[:, :], pu[:, hl, :], den8[:, r:r + 1], None, ALU.mult, None, [pu, (den8, b4)], [psg])
                                else:
                                    kb.stt(psg[:, :], pu[:, hl, :], den8[:, r:r + 1], psg[:, :], ALU.mult, ALU.add,
                                           [pu, (den8, b4), psg], [psg])
                                if hl % 2 == 1:
                                    yield
                            tbk = mbank()
                            tv = tbk[:].bitcast(BF16)
                            for hl in range(4):
                                kb.tr(tv[0:127, hl * 128:(hl + 1) * 128], pub[:, hl, 0:127], identb[:], [pub, identb], [tbk])
                            yield
                            ptt = pT[b4]
                            kb.cp("act", ptt[0:127, :, :], tv[0:127, 0:512].rearrange("p (h n) -> p h n", h=4), [tbk], [ptt])
                            yield
                            for hl in range(4):
                                r = b4 * 4 + hl
                                kb.mm(ocb[:, r * 64:(r + 1) * 64], ptt[0:127, hl, :], vcmp[0:127, g, :], True, True, [ptt, vcmp], [ocb])
                            yield
                        kb.tt("dve", cg8[:], den8[:], gq[:, g * 24:g * 24 + 24:3], ALU.mult, [den8, gates], [cg8])
                        kb.tt("dve", O[:].rearrange("p (h d) -> p h d", h=8), ocb[:].rearrange("p (h d) -> p h d", h=8),
                              cg8[:, 0:8].unsqueeze(2).to_broadcast([128, 8, 64]), ALU.mult, [ocb, cg8], [O])
                        yield
                        S.op("dve", lambda e: e.tensor_reduce(out=imp[:, :], in_=psg[:].rearrange("p (j a) -> p j a", a=4),
                                                              axis=AX.X, op=ALU.add), reads=[psg], writes=[imp])
                        kb.tt("dve", imp[:, 1:32], imp[:, 1:32], psg[:, 3:127:4], ALU.add, [imp, psg], [imp])
                        yield
                        moff = 30 - 2 * qt
                        kb.tt("dve", impm[:], imp[:], m12[:, 0, moff:moff + 32], ALU.mult, [imp, m12], [impm])
                        kb.tt("dve", impm[:], impm[:], m12[:, 1, moff:moff + 32], ALU.add, [impm, m12], [impm])
                        kb.memset("dve", impm[:, 0:1], 1e6, [impm])
                        yield
                        S.op("dve", lambda e: e.max(out=m8[:], in_=impm[:]), reads=[impm], writes=[m8])
                        kb.ts("dve", negp[:, 64:96], impm[:], m8[:, 7:8], 1.0, ALU.is_ge, ALU.subtract, [impm, m8], [negp])
                        kb.ts("dve", negp[:, 64:96], negp[:, 64:96], NEGB, None, ALU.mult, None, [negp], [negp])
                        yield
                        tbk = mbank()
                        kb.tr(tbk[0:96, 0:128], negp[:, 0:96], identf[:], [negp, identf], [tbk])
                        yield
                        kb.cp("dve", negS[64:96, :], tbk[64:96, 0:128], [tbk], [negS])
                        kb.cp("dve", QAg[64:96, :, qc], negS[64:96, :].unsqueeze(1).to_broadcast([32, 8, 128]), [negS], [(QAg, qt)])
                        yield

                    def task_branch(qt, br, b4):
                        qc = slice(qt * 128, (qt + 1) * 128)
                        O = Oa[qt % 3]
                        gq = gates[:, qt, :]
                        KA, VA = (KAw, VAw) if br == 2 else (KAs, VAs)
                        kbs = list(range(max(0, qt - 4), qt + 1)) if br == 2 else list(range(0, qt + 1))
                        accb = kb.banks[b4]
                        pend = []
                        nk = len(kbs)
                        kb.mm(accb[:, 0:260], zerob[:, 0:128], zerob[:, 0:260], True, False, [zerob], [accb])

                        def do_pv(item):
                            pi, pk, ppt = item
                            for hl in range(4):
                                kb.mm(accb[:, hl * 65:(hl + 1) * 65], ppt[:, hl * 128:(hl + 1) * 128], VA[:, pk, 0:65], False,
                                      (pi == nk - 1) and hl == 3, [VA, ppt], [accb])

                        for idx, kbi in enumerate(kbs):
                            sbk = stbank(b4)
                            masks = []
                            if kbi == qt:
                                masks.append(0)
                            if br == 2 and kbi == qt - 4:
                                masks.append(1)
                            kb.mm(sbk[:, :], KA[0:105, kbi * 128:(kbi + 1) * 128], QAg[0:105, b4 * 4:(b4 + 1) * 4, qc],
                                  True, len(masks) == 0, [KA, (QAg, qt)], [sbk])
                            for mi, mv in enumerate(masks):
                                kb.mm(sbk[:, :], identb[:], trib[:, mv, :], False, mi == len(masks) - 1, [identb, trib], [sbk])
                            pt = next_pt(b4)
                            kb.act(pt[:], sbk[:], AF.Exp, [sbk], [pt], scale=0.125)
                            yield
                            pend.append((idx, kbi, pt))
                            if len(pend) > 3:
                                do_pv(pend.pop(0))
                                yield
                        while pend:
                            do_pv(pend.pop(0))
                            yield
                        kb.cp("dve", acs[b4][:], accb[:, 0:260], [accb], [acs[b4]])
                        yield
                        a3 = acs[b4][:].rearrange("p (h n) -> p h n", h=4)
                        kb.recip(rd4[b4][:], a3[:, :, 64], [acs[b4]], [rd4[b4]])
                        h0 = (g * 8 + b4 * 4) * 3 + br
                        kb.tt("dve", cg4[b4][:], rd4[b4][:], gq[:, h0:h0 + 10:3], ALU.mult, [rd4[b4], gates], [cg4[b4]])
                        yield
                        kb.tt("dve", a3[:, :, 0:64], a3[:, :, 0:64], cg4[b4][:, 0:4].unsqueeze(2).to_broadcast([128, 4, 64]), ALU.mult,
                              [acs[b4], cg4[b4]], [acs[b4]])
                        yield
                        ov = O[:, b4 * 256:(b4 + 1) * 256].rearrange("p (h d) -> p h d", h=4)
                        kb.tt("dve", ov, ov, a3[:, :, 0:64], ALU.add, [(O, b4), acs[b4]], [(O, b4)])
                        yield

                    def task_Z(qt):
                        qc = slice(qt * 128, (qt + 1) * 128)
                        O = Oa[qt % 3]
                        zb = mbank()
                        for dc in range(8):
                            kb.mm(zb[:, :], xnT[:, dc, qc], Wz[:, dc, :], dc == 0, dc == 7, [xnT, Wz], [zb])
                            if dc % 4 == 3:
                                yield
                        kb.act(szt[:], zb[:], AF.Silu, [zb], [szt])
                        yield
                        kb.tt("dve", Ob[:], O[:], szt[:], ALU.mult, [O, szt], [Ob])
                        yield
                        tbk = mbank()
                        tv = tbk[:].bitcast(BF16)
                        for c4 in range(4):
                            kb.tr(tv[:, c4 * 128:(c4 + 1) * 128], Ob[:, c4 * 128:(c4 + 1) * 128], identb[:], [Ob, identb], [tbk])
                        yield
                        kb.cp("act", yT1[:, g * 4:(g + 1) * 4, qc], tv[:, 0:512].rearrange("p (c n) -> p c n", c=4), [tbk], [(yT1, (g, qt))])
                        yield

                    run_lanes([task_C(0)])
                    for qt in range(NT_):
                        l1_ = chain(task_branch(qt, 2, 0), task_branch(qt, 1, 0))
                        l2_ = chain(task_branch(qt, 2, 1), task_branch(qt, 1, 1))
                        third = []
                        if qt + 1 < NT_:
                            third.append(task_C(qt + 1))
                        if qt > 0:
                            third.append(task_Z(qt - 1))
                        run_lanes([l1_, l2_, chain(*third)], [1, 1, L3_STEPS])
                    run_lanes([task_Z(NT_ - 1)])
                S.barrier()
                ck("nsa")

            with ExitStack() as pe_:
                out_proj(nsa_w_out, 8, [(yT1, 8)], ymT1, x1_scr, DX1, True, pe_)
                S.barrier()
                ck("l1end")

        S.dead = False
        S.barrier()
        with nc.Block() as block:
            S.emit(block)
        print("program ops:", S.nops, "sems:", S.nsem)
    return nc


_CONST = None


def prep_inputs(inp):
    global _CONST
    if _CONST is None:
        _CONST = host_constants()
        _CONST.update(host_constants_nsa())
    f = lambda a: np.ascontiguousarray(np.asarray(a, dtype=np.float32))
    shared = {
        "hawk_w_in": f(inp["hawk_w_in"][0]),
        "hawk_w_out": f(inp["hawk_w_out"][0]),
        "hawk_w_mem_kv": f(inp["hawk_w_mem_kv"][0]),
        "g_hawk": expand_gain(f(inp["hawk_norm"][0])),
        "g_hawk_mem": expand_gain(f(inp["hawk_mem_norm"][0])),
        "bd_a": block_diag(f(inp["hawk_gate_a_w"][0])),
        "bd_x": block_diag(f(inp["hawk_gate_x_w"][0])),
        "final_norm": f(inp["final_norm"]),
        "hawk_norm_v": f(inp["hawk_norm"][0]),
        "nsa_norm_v": f(inp["nsa_norm"][0]),
        "nsa_w_in": f(inp["nsa_w_in"][0]),
        "nsa_w_out": f(inp["nsa_w_out"][0]),
        "nsa_w_mem_kv": f(inp["nsa_w_mem_kv"][0]),
        "g_nsa": expand_gain(f(inp["nsa_norm"][0])),
        "g_nsa_mem": expand_gain(f(inp["nsa_mem_norm"][0])),
        "w2k": f(inp["nsa_phi_k_w2"][0]),
        "w2v": f(inp["nsa_phi_v_w2"][0]),
    }
    for k in ("identf", "edil", "ecmp", "m12", "tri", "kaug", "qal"):
        shared[k] = _CONST[k]

    def w1_layout(w1):
        a = w1.reshape(32, 64, 256).transpose(1, 0, 2)
        return np.ascontiguousarray(np.concatenate([a, a], axis=0))
    shared["w1k"] = w1_layout(f(inp["nsa_phi_k_w1"][0]))
    shared["w1v"] = w1_layout(f(inp["nsa_phi_v_w1"][0]))
    shared["peT"] = np.ascontiguousarray(np.stack([f(inp["nsa_pe_k"][0]).T, f(inp["nsa_pe_v"][0]).T], axis=1))
    lv = np.zeros((128, 8, 8), np.float32)
    cw = f(inp["hawk_conv_w"][0])
    for k in range(4):
        lv[:, :, k] = vec_fm(cw[k])
    lv[:, :, 4] = vec_fm(f(inp["hawk_conv_b"][0]))
    lv[:, :, 5] = vec_fm(f(inp["hawk_gate_a_b"][0]).reshape(-1))
    lv[:, :, 6] = vec_fm(f(inp["hawk_gate_x_b"][0]).reshape(-1))
    lv[:, :, 7] = vec_fm(f(inp["hawk_lambda"][0]))
    shared["lru_vec"] = lv
    x = f(inp["x"])
    mem = f(inp["mem"])
    maps = []
    for b in range(x.shape[0]):
        m = dict(shared)
        m["x"] = x[b]
        m["mem"] = mem[b]
        maps.append(m)
    return maps


def kernel(**inputs):
    maps = prep_inputs(inputs)
    nc = build_program()
    res = run_bass_kernel_spmd(nc, maps, core_ids=list(range(len(maps))))
    out = np.stack([np.asarray(r["out"], dtype=np.float32) for r in res.results], axis=0)
    return out
```

```python
import math
from contextlib import ExitStack

import numpy as np
import concourse.bass as bass
import concourse.mybir as mybir
from concourse.bass_utils import run_bass_kernel_spmd

F32 = mybir.dt.float32
BF16 = mybir.dt.bfloat16
AF = mybir.ActivationFunctionType
ALU = mybir.AluOpType
AX = mybir.AxisListType

S_LEN = 2048
D = 1024
NT_ = 16
EPS = 1e-6
DIL_GROUPS = ((128, 1), (512, 4), (2048, 16))

SEM_LIMIT = 30000
N_DMA_SEMS = 24
SAME_ENGINE_SYNC = True


class Buf:
    def __init__(self, name, t, excl=False):
        self.name = name
        self.t = t
        self.excl = excl
        self.st = {}

    def __getitem__(self, idx):
        return self.t[idx]


class Sync:
    def __init__(self, nc, stack):
        self.nc = nc
        self.stack = stack
        self.engs = ["pe", "act", "dve", "pool", "sp"]
        self.ops = {e: [] for e in self.engs}
        self.cur_sem = {}
        self.cnt = {}
        self.nsem = 0
        for e in self.engs:
            self._new_sem(e)
        self.dma_sems = {}
        self.dma_val = {}
        self.dma_rr = {}
        for e in ["sp", "pool", "act"]:
            self.dma_sems[e] = [self._alloc_sem(f"d{e}{i}") for i in range(N_DMA_SEMS)]
            self.dma_val[e] = [0] * N_DMA_SEMS
            self.dma_rr[e] = 0
        self.seen = {e: {} for e in self.engs}
        self.all_ticks = {}
        self.nops = 0
        self.dead = False
        self.eng_free = {e: 0.0 for e in self.engs}
        self.lane = None
        self.tnow = 0.0

    def _alloc_sem(self, name):
        self.nsem += 1
        return self.stack.enter_context(self.nc.semaphore(f"s_{name}_{self.nsem}"))

    def _new_sem(self, e):
        self.cur_sem[e] = self._alloc_sem(e)
        self.cnt[e] = 0

    def _states(self, buf, key, create):
        if key is None:
            if create and None not in buf.st:
                buf.st[None] = [None, {}]
            return list(buf.st.values())
        out = []
        if None in buf.st:
            out.append(buf.st[None])
        if key not in buf.st and create:
            buf.st[key] = [None, {}]
        if key in buf.st:
            out.append(buf.st[key])
        return out

    @staticmethod
    def _norm(lst):
        out = []
        for r in lst or []:
            out.append(r if isinstance(r, tuple) else (r, None))
        return out

    def op(self, eng, fn, reads=None, writes=None, dma=False, cost=0.5):
        if self.dead:
            return None
        reads = self._norm(reads)
        writes = self._norm(writes)
        ex = [(b, None) for (b, k) in reads + writes if b.excl]
        if ex:
            reads = [(b, k) for (b, k) in reads if not b.excl]
            writes = [(b, k) for (b, k) in writes if not b.excl]
            for bk in ex:
                if bk not in writes:
                    writes.append(bk)
        need = []
        for buf, key in reads:
            for st in self._states(buf, key, False):
                if st[0] is not None:
                    need.append(st[0])
        for buf, key in writes:
            for st in self._states(buf, key, False):
                if st[0] is not None:
                    need.append(st[0])
                need.extend(st[1].values())
        if dma:
            i = self.dma_rr[eng]
            self.dma_rr[eng] = (i + 1) % N_DMA_SEMS
            sem = self.dma_sems[eng][i]
            prev = self.dma_val[eng][i]
            if prev > 0:
                need.append((sem, prev, "dma", 0.0))
            if prev + 16 > SEM_LIMIT:
                sem = self._alloc_sem(f"d{eng}{i}")
                self.dma_sems[eng][i] = sem
                prev = 0
            val = prev + 16
            self.dma_val[eng][i] = val
            inc = 16
            tick = [sem, val, "dma", 0.0]
        else:
            if self.cnt[eng] + 1 > SEM_LIMIT:
                self._new_sem(eng)
            self.cnt[eng] += 1
            sem = self.cur_sem[eng]
            val = self.cnt[eng]
            inc = 1
            tick = [sem, val, eng, 0.0]
        ready = 0.0
        for nd in need:
            if nd[3] > ready:
                ready = nd[3]
        start = max(self.eng_free[eng], ready + 0.06)
        if dma:
            self.eng_free[eng] = start + 0.06
        else:
            self.eng_free[eng] = start + cost
        tick[3] = start + cost
        tick = tuple(tick)
        if self.lane is not None and tick[3] > self.lane.clock:
            self.lane.clock = tick[3]
        if tick[3] > self.tnow:
            self.tnow = tick[3]
        waits = {}
        seen = self.seen[eng]
        for (s, v, src, _fin) in need:
            if src == eng and (eng == "pe" or not SAME_ENGINE_SYNC):
                continue
            sid = id(s)
            if seen.get(sid, 0) >= v:
                continue
            if sid not in waits or waits[sid][1] < v:
                waits[sid] = (s, v)
        for sid, (s, v) in waits.items():
            seen[sid] = v
        self.ops[eng].append((list(waits.values()), fn, sem, inc))
        self.all_ticks[id(sem)] = (sem, val)
        self.nops += 1
        wset = set((id(b), k) for b, k in writes)
        for buf, key in reads:
            if (id(buf), key) in wset:
                continue
            self._states(buf, key, True)
            buf.st[key][1][eng if not dma else ("dma", id(sem))] = tick
        for buf, key in writes:
            if key is None:
                buf.st = {None: [tick, {}]}
            else:
                buf.st[key] = [tick, {}]
        return tick

    def barrier(self):
        if self.dead:
            return
        ticks = list(self.all_ticks.values())
        for e in self.engs:
            wl = []
            for (s, v) in ticks:
                if self.seen[e].get(id(s), 0) < v:
                    wl.append((s, v))
                    self.seen[e][id(s)] = v
            if wl:
                self.ops[e].append((wl, None, None, 0))

    def emit(self, block):
        S = self

        def run(engname, e):
            for (wl, fn, sem, inc) in S.ops[engname]:
                for (s, v) in wl:
                    e.wait_ge(s, v)
                if fn is not None:
                    fn(e).then_inc(sem, inc)

        @block.sync
        def _(e):
            run("sp", e)

        @block.tensor
        def _(e):
            run("pe", e)

        @block.scalar
        def _(e):
            run("act", e)

        @block.vector
        def _(e):
            run("dve", e)

        @block.gpsimd
        def _(e):
            run("pool", e)


class BankView:
    def __init__(self, pair, half):
        self.pair = pair
        self.off = 512 * half

    def __getitem__(self, idx):
        if not isinstance(idx, tuple):
            idx = (idx, slice(None))
        pr, col = idx
        cs = (col.start or 0) + self.off
        ce = (col.stop if col.stop is not None else 512) + self.off
        return self.pair[pr, cs:ce:col.step] if col.step else self.pair[pr, cs:ce]


class KB:
    def __init__(self, nc, stack):
        self.nc = nc
        self.gst = stack
        self.S = Sync(nc, stack)
        self.pairs = [stack.enter_context(nc.psum_tensor(f"pair{i}", [128, 1024], F32)) for i in range(4)]
        self.banks = [Buf(f"bank{i}", BankView(self.pairs[i // 2], i % 2), excl=True) for i in range(8)]
        self.bank_rr = 0
        self.uid = 0

    def sb(self, name, shape, dt, stack=None):
        self.uid += 1
        t = (stack or self.gst).enter_context(self.nc.sbuf_tensor(f"{name}_{self.uid}", shape, dt))
        return Buf(name, t)

    def bank(self):
        b = self.banks[self.bank_rr]
        self.bank_rr = (self.bank_rr + 1) % 8
        return b

    @staticmethod
    def fsz(ap):
        n = 1
        for s in ap.shape[1:]:
            n *= int(s)
        return n

    def vcost(self, eng, ap):
        n = self.fsz(ap)
        if eng == "pool":
            return 0.3 + n / 480.0
        if eng == "act":
            return 0.22 + n / 1400.0
        return 0.08 + n / 960.0

    def mm(self, out, lhsT, rhs, start, stop, r, w):
        c = max(self.fsz(rhs), 64) / 1600.0 + 0.04
        self.S.op("pe", lambda e: e.matmul(out, lhsT=lhsT, rhs=rhs, start=start, stop=stop), reads=r, writes=w, cost=c)

    def tr(self, out, in_, ident, r, w):
        self.S.op("pe", lambda e: e.transpose(out, in_, ident), reads=r, writes=w, cost=0.11)

    def act(self, out, in_, func, r, w, **kw):
        self.S.op("act", lambda e: e.activation(out=out, in_=in_, func=func, **kw), reads=r, writes=w, cost=self.vcost("act", out))

    def tt(self, eng, out, in0, in1, op, r, w):
        self.S.op(eng, lambda e: e.tensor_tensor(out=out, in0=in0, in1=in1, op=op), reads=r, writes=w, cost=self.vcost(eng, out))

    def ts(self, eng, out, in0, s1, s2, op0, op1, r, w, **kw):
        c = self.vcost(eng, out)
        if op1 is None:
            self.S.op(eng, lambda e: e.tensor_scalar(out=out, in0=in0, scalar1=s1, scalar2=None, op0=op0, **kw), reads=r, writes=w, cost=c)
        else:
            self.S.op(eng, lambda e: e.tensor_scalar(out=out, in0=in0, scalar1=s1, scalar2=s2, op0=op0, op1=op1, **kw), reads=r, writes=w, cost=c)

    def stt(self, out, in0, scalar, in1, op0, op1, r, w, **kw):
        self.S.op("dve", lambda e: e.scalar_tensor_tensor(out=out, in0=in0, scalar=scalar, in1=in1, op0=op0, op1=op1, **kw), reads=r, writes=w,
                  cost=0.12 + self.fsz(out) / 960.0)

    def cp(self, eng, out, in_, r, w):
        c = self.vcost(eng, out)
        if eng == "act":
            self.S.op("act", lambda e: e.activation(out=out, in_=in_, func=AF.Copy), reads=r, writes=w, cost=c)
        else:
            self.S.op(eng, lambda e: e.tensor_copy(out=out, in_=in_), reads=r, writes=w, cost=c)

    def memset(self, eng, ap, val, w):
        self.S.op(eng, lambda e: e.memset(ap, val), writes=w, cost=self.vcost(eng, ap))

    def recip(self, out, in_, r, w):
        self.S.op("dve", lambda e: e.reciprocal(out=out, in_=in_), reads=r, writes=w, cost=self.vcost("dve", out))

    def dma(self, q, out, in_, r, w):
        nbytes = self.fsz(out) * int(out.shape[0]) * 4
        self.S.op(q, lambda e: e.dma_start(out=out, in_=in_), reads=r, writes=w, dma=True, cost=2.0 + nbytes / 150000.0)


def alibi_slopes(n):
    return np.exp2(-8.0 * np.arange(1, n + 1) / n).astype(np.float32)


def host_constants():
    c = {}
    c["identf"] = np.eye(128, dtype=np.float32)
    sl = alibi_slopes(12)
    ik = np.arange(128)[:, None].astype(np.float64)
    iq = np.arange(128)[None, :].astype(np.float64)
    E = np.zeros((128, 12, 256), np.float32)
    for g, (win, dil) in enumerate(DIL_GROUPS):
        for hs in range(4):
            hh = g * 4 + hs
            s = float(sl[hh]) * dil
            dist_prev = 128 + iq - ik
            ok_prev = (dist_prev <= 128)
            E[:, hh, 0:128] = np.where(ok_prev, np.exp(-s * dist_prev), 0.0)
            dist_cur = iq - ik
            ok_cur = dist_cur >= 0
            E[:, hh, 128:256] = np.where(ok_cur, np.exp(-s * dist_cur), 0.0)
    c["edil"] = E
    return c


def expand_gain(g):
    return np.ascontiguousarray(np.broadcast_to(g.reshape(8, 128).T[:, :, None], (128, 8, 128))).astype(np.float32)


def vec_fm(v):
    return np.ascontiguousarray(v.reshape(8, 128).T).astype(np.float32)


def block_diag(gw):
    out = np.zeros((128, 8, 128), np.float32)
    for c in range(8):
        out[0:64, c, 0:64] = gw[2 * c]
        out[64:128, c, 64:128] = gw[2 * c + 1]
    return out


NEGB = 8192.0


def _bf16_split3(a):
    import ml_dtypes
    a = a.astype(np.float32)
    hi = a.astype(ml_dtypes.bfloat16).astype(np.float32)
    r1 = (a - hi).astype(np.float32)
    mid = r1.astype(ml_dtypes.bfloat16).astype(np.float32)
    r2 = (r1 - mid).astype(np.float32)
    lo = r2.astype(ml_dtypes.bfloat16).astype(np.float32)
    return hi, mid, lo


def host_constants_nsa():
    c = {}
    sl = alibi_slopes(16)
    i = np.arange(128)[:, None].astype(np.float64)
    m = np.arange(247)[None, :].astype(np.float64)
    dist = i - 16.0 * (m - 120.0) - 31.0
    E = np.zeros((128, 16, 247), np.float32)
    for h in range(16):
        E[:, h, :] = np.where(dist >= 0, np.exp(-float(sl[h]) * np.maximum(dist, 0.0)), 0.0)
    c["ecmp"] = E
    ii = np.arange(128)[:, None]
    rel = np.arange(62)[None, :] - 30
    cur = (ii >= 64).astype(np.int64)
    forced = (rel == cur) | (rel == cur - 1)
    future = rel > cur
    m1 = np.where(forced | future, 0.0, 1.0).astype(np.float32)
    m2 = np.where(forced, 1e6, np.where(future, -1e6, 0.0)).astype(np.float32)
    c["m12"] = np.ascontiguousarray(np.stack([m1, m2], axis=1))
    ik = np.arange(128)[:, None]
    iq = np.arange(128)[None, :]
    diag = np.where(ik > iq, -NEGB, 0.0).astype(np.float32)
    far = np.where(ik <= iq, -NEGB, 0.0).astype(np.float32)
    c["tri"] = np.ascontiguousarray(np.stack([np.tile(diag, (1, 4)), np.tile(far, (1, 4))], axis=1))
    k = np.arange(2048)
    kp = k - 1024
    hi = (np.floor(kp / 128.0) * 128.0).astype(np.float32)
    lo = (kp - hi).astype(np.float32)
    ka = np.zeros((2, 41, 2048), np.float32)
    for j in range(32):
        ka[0, j, :] = (k // 64 == j).astype(np.float32)
    for v in range(2):
        ka[v, 32:35, :] = 1.0
        ka[v, 35:38, :] = lo[None, :]
        ka[v, 38:41, :] = hi[None, :]
    c["kaug"] = ka
    qa = np.zeros((16, 9, 2048), np.float32)
    qp = (np.arange(2048) - 1024).astype(np.float32)
    for h in range(16):
        s8 = np.float32(8.0) * np.float32(sl[h])
        a = (-s8 * qp).astype(np.float32)
        ah, am, al = _bf16_split3(a)
        sh, sm, sl_ = _bf16_split3(np.full((2048,), s8, np.float32))
        qa[h, 0], qa[h, 1], qa[h, 2] = ah, am, al
        qa[h, 3], qa[h, 4], qa[h, 5] = sh, sm, sl_
        qa[h, 6], qa[h, 7], qa[h, 8] = sh, sm, sl_
    c["qal"] = qa
    return c


def chain(*gens):
    for g_ in gens:
        yield from g_


L3_STEPS = 1


def run_lanes(lanes, weights=None):
    active = list(lanes)
    w = {id(l): 1 for l in active}
    if weights:
        for l, wt in zip(lanes, weights):
            w[id(l)] = wt
    while active:
        for l in list(active):
            for _ in range(w[id(l)]):
                try:
                    next(l)
                except StopIteration:
                    active.remove(l)
                    break


def build_program(stop_after=None):
    nc = bass.Bass("TRN2", target_bir_lowering=False)

    ckstate = {}

    def ck(name):
        if stop_after == name:
            ckstate["S"].dead = True

    def din(name, shape):
        return nc.dram_tensor(name, list(shape), F32, kind="ExternalInput").ap()

    x_d = din("x", [S_LEN, D])
    mem_d = din("mem", [256, D])
    hawk_w_in = din("hawk_w_in", [D, 7680])
    hawk_w_out = din("hawk_w_out", [1792, D])
    hawk_w_mem_kv = din("hawk_w_mem_kv", [D, 512])
    g_hawk = din("g_hawk", [128, 8, 128])
    g_hawk_mem = din("g_hawk_mem", [128, 8, 128])
    lru_vec = din("lru_vec", [128, 8, 8])
    bd_a = din("bd_a", [128, 8, 128])
    bd_x = din("bd_x", [128, 8, 128])
    identf_d = din("identf", [128, 128])
    edil_d = din("edil", [128, 12, 256])
    final_g = din("final_norm", [D])
    hawk_norm_v = din("hawk_norm_v", [D])
    nsa_norm_v = din("nsa_norm_v", [D])
    nsa_w_in = din("nsa_w_in", [D, 3376])
    nsa_w_out = din("nsa_w_out", [1280, D])
    nsa_w_mem_kv = din("nsa_w_mem_kv", [D, 512])
    g_nsa = din("g_nsa", [128, 8, 128])
    g_nsa_mem = din("g_nsa_mem", [128, 8, 128])
    w1k_d = din("w1k", [128, 32, 256])
    w1v_d = din("w1v", [128, 32, 256])
    w2k_d = din("w2k", [256, 64])
    w2v_d = din("w2v", [256, 64])
    peT_d = din("peT", [64, 2, 32])
    ecmp_d = din("ecmp", [128, 16, 247])
    m12_d = din("m12", [128, 2, 62])
    tri_d = din("tri", [128, 2, 512])
    kaug_d = din("kaug", [2, 41, 2048])
    qal_d = din("qal", [16, 9, 2048])
    out_d = nc.dram_tensor("out", [S_LEN, D], F32, kind="ExternalOutput").ap()
    x1_scr = nc.dram_tensor("x1_scr", [S_LEN, D], F32, kind="Internal").ap()

    with ExitStack() as gst:
        kb = KB(nc, gst)
        S = kb.S
        ckstate["S"] = S
        DX = Buf("x_dram", None)
        DX1 = Buf("x1_dram", None)
        DOUT = Buf("out_dram", None)

        xnT = kb.sb("xnT", [128, 8, S_LEN], BF16)
        memnT = kb.sb("memnT", [128, 8, 256], BF16)
        identf = kb.sb("identf", [128, 128], F32)
        identb = kb.sb("identb", [128, 128], BF16)
        onesb = kb.sb("onesb", [128, 128], BF16)
        wstage = [kb.sb(f"wstage{i}", [128, 1024], F32) for i in range(3)]
        ws_rr = [0]
        stat = kb.sb("stat", [128, 64], F32)
        stat_rr = [0]

        kb.dma("sp", identf[:], identf_d, [], [identf])
        kb.cp("dve", identb[:], identf[:], [identf], [identb])
        kb.memset("dve", onesb[:], 1.0, [onesb])

        def next_ws():
            b = wstage[ws_rr[0]]
            ws_rr[0] = (ws_rr[0] + 1) % len(wstage)
            return b

        def load_w(dst, dst_ap3, src_ap3, n, gain=None, key=None, q="sp", part=128, eng="pool"):
            dcs = dst_ap3.shape[1]
            if gain is None:
                kb.dma("pool", dst_ap3, src_ap3, [], [(dst, key)])
                return
            assert dcs * n <= 1024
            stg = next_ws()
            sv = stg[0:part, 0:dcs * n].rearrange("p (c n) -> p c n", c=dcs)
            kb.dma(q, sv, src_ap3, [], [stg])
            if gain is not None:
                kb.tt(eng, dst_ap3, sv, gain[0:part, 0:dcs, 0:n], ALU.mult, [stg, gain], [(dst, key)])
            else:
                kb.cp(eng, dst_ap3, sv, [stg], [(dst, key)])

        def win_cols(w_dram, c0, n):
            return w_dram.rearrange("(dc p) n -> p dc n", p=128)[:, :, c0:c0 + n]

        class NormCtx:
            def __init__(self, stack, nbuf=2):
                self.xstage = [kb.sb(f"xstage{i}", [128, 1024], F32, stack) for i in range(nbuf)]
                self.xnb = [kb.sb(f"xnb{i}", [128, 1024], BF16, stack) for i in range(nbuf)]
                self.junk = kb.sb("junk", [128, 1024], BF16, stack)

        def tile_rstd(ncx, xbuf, xap):
            i = stat_rr[0]
            stat_rr[0] = (stat_rr[0] + 1) % 32
            ss = stat[:, 2 * i:2 * i + 1]
            rs = stat[:, 2 * i + 1:2 * i + 2]
            kb.stt(ncx.junk[:], xap, 1.0, xap, ALU.mult, ALU.mult, [xbuf], [ncx.junk, (stat, i)], accum_out=ss)
            kb.ts("dve", ss, ss, 1.0 / D, EPS, ALU.mult, ALU.add, [(stat, i)], [(stat, i)])
            kb.act(ss, ss, AF.Sqrt, [(stat, i)], [(stat, i)])
            kb.recip(rs, ss, [(stat, i)], [(stat, i)])
            return rs, i

        def norm_to_T(ncx, xbuf, xap, dstT, t, ntok_off, gB=None):
            rs, i = tile_rstd(ncx, xbuf, xap)
            nb = ncx.xnb[t % 2]
            if gB is None:
                kb.ts("dve", nb[:], xap, rs, None, ALU.mult, None, [xbuf, (stat, i)], [nb])
            else:
                kb.stt(nb[:], xap, rs, gB[:], ALU.mult, ALU.mult, [xbuf, (stat, i), gB], [nb])
            bk = kb.bank()
            bv = bk[:].bitcast(BF16)
            for c in range(8):
                kb.tr(bv[:, c * 128:(c + 1) * 128], nb[:, c * 128:(c + 1) * 128], identb[:], [nb, identb], [bk])
            kb.cp("act", dstT[:, :, ntok_off:ntok_off + 128], bv.rearrange("p (c n) -> p c n", c=8), [bk], [(dstT, t)])

        def norm_to_T_gen(ncx, xbuf, xap, dstT, t, ntok_off, bk, bi, gB=None):
            rs, i = tile_rstd(ncx, xbuf, xap)
            yield
            nb = ncx.xnb[bi]
            if gB is None:
                kb.ts("dve", nb[:], xap, rs, None, ALU.mult, None, [xbuf, (stat, i)], [nb])
            else:
                kb.stt(nb[:], xap, rs, gB[:], ALU.mult, ALU.mult, [xbuf, (stat, i), gB], [nb])
            yield
            bv = bk[:].bitcast(BF16)
            for c in range(8):
                kb.tr(bv[:, c * 128:(c + 1) * 128], nb[:, c * 128:(c + 1) * 128], identb[:], [nb, identb], [bk])
            yield
            kb.cp("act", dstT[:, :, ntok_off:ntok_off + 128], bv.rearrange("p (c n) -> p c n", c=8), [bk], [(dstT, t)])
            yield

        with ExitStack() as pa:
            ncxs = [NormCtx(pa, 2), NormCtx(pa, 2)]
            gBa = kb.sb("gBa", [128, 1024], F32, pa)
            kb.dma("sp", gBa[:], hawk_norm_v.partition_broadcast(128), [], [gBa])

            def a_lane(L):
                ncx = ncxs[L]
                for t in range(L, NT_ + 2, 2):
                    xs = ncx.xstage[(t // 2) % 2]
                    if t < NT_:
                        kb.dma("sp", xs[:], x_d[t * 128:(t + 1) * 128, :], [DX], [xs])
                        yield from norm_to_T_gen(ncx, xs, xs[:], xnT, t, t * 128, kb.banks[2 * L + (t // 2) % 2], (t // 2) % 2, gB=gBa)
                    else:
                        tm = t - NT_
                        kb.dma("sp", xs[:], mem_d[tm * 128:(tm + 1) * 128, :], [], [xs])
                        yield from norm_to_T_gen(ncx, xs, xs[:], memnT, tm, tm * 128, kb.banks[2 * L + (t // 2) % 2], (t // 2) % 2)

            run_lanes([a_lane(0), a_lane(1)])
            S.barrier()
            ck("A")

        def make_loader(w_in_d, gain, nslots, stack):
            wslots = [kb.sb(f"wslot{i}", [128, 8, 128], BF16, stack) for i in range(nslots)]
            rr = [0]
            pre = {}

            def raw(c0, n, q, into, off):
                if into is None:
                    wsl = wslots[rr[0]]
                    rr[0] = (rr[0] + 1) % nslots
                else:
                    wsl = into
                load_w(wsl, wsl[:, :, off:off + n], win_cols(w_in_d, c0, n), n, gain=None, q=q, key=off)
                return wsl

            def load_win(c0, n=128, q="sp", into=None, off=0):
                if into is None and (c0, n) in pre:
                    return pre.pop((c0, n))
                return raw(c0, n, q, into, off)

            def prefetch(c0, n=128):
                pre[(c0, n)] = raw(c0, n, "sp", None, 0)
            load_win.prefetch = prefetch
            return load_win

        def proj_fm(wsl, n, evac, woff=0):
            for tc in range(4):
                bk = kb.bank()
                for dc in range(8):
                    kb.mm(bk[0:n, :], wsl[:, dc, woff:woff + n], xnT[:, dc, tc * 512:(tc + 1) * 512], dc == 0, dc == 7,
                          [wsl, xnT], [bk])
                evac(bk, bk[0:n, :], tc)

        def mem_kv(w_kv_d, gain, kmT, vm, stack):
            wkv = kb.sb("wkv", [128, 8, 512], BF16, stack)
            for j in range(4):
                load_w(wkv, wkv[:, :, j * 128:(j + 1) * 128], win_cols(w_kv_d, j * 128, 128), 128, gain=gain, key=j)
            for h in range(4):
                bk = kb.bank()
                for dc in range(8):
                    kb.mm(bk[0:64, 0:256], wkv[:, dc, h * 64:(h + 1) * 64], memnT[:, dc, :], dc == 0, dc == 7,
                          [wkv, memnT], [bk])
                kb.cp("act", kmT[0:64, h, :], bk[0:64, 0:256], [bk], [(kmT, h)])
            for mt in range(2):
                bk = kb.bank()
                for dc in range(8):
                    kb.mm(bk[:, 0:256], memnT[:, dc, mt * 128:(mt + 1) * 128], wkv[:, dc, 256:512], dc == 0, dc == 7,
                          [wkv, memnT], [bk])
                kb.cp("act", vm[:, mt, :], bk[:, 0:256], [bk], [(vm, mt)])

        def mem_attn(load_win, colq, colz, kmT, vm, ymT, stack):
            qmTs = [kb.sb(f"qmT{i}", [64, S_LEN], BF16, stack) for i in range(2)]
            szms = [kb.sb(f"szm{i}", [64, S_LEN], BF16, stack) for i in range(2)]
            PTm = [kb.sb(f"PTm{i}", [128, 512], BF16, stack) for i in range(4)]
            rdm = [kb.sb(f"rdm{i}", [64, 512], F32, stack) for i in range(2)]

            def lane(h, L):
                qmT, szm = qmTs[L], szms[L]
                b0, b1, b2, b3 = [kb.banks[4 * L + j] for j in range(4)]
                wq = load_win(colq + h * 64, 64)
                wz = load_win(colz + h * 64, 64)
                for (wsl, dst, func) in ((wq, qmT, None), (wz, szm, AF.Silu)):
                    for tc in range(4):
                        bk = b0 if tc % 2 == 0 else b1
                        for dc in range(8):
                            kb.mm(bk[0:64, :], wsl[:, dc, 0:64], xnT[:, dc, tc * 512:(tc + 1) * 512], dc == 0, dc == 7, [wsl, xnT], [bk])
                        yield
                        if func is None:
                            kb.cp("act", dst[0:64, tc * 512:(tc + 1) * 512], bk[0:64, :], [bk], [(dst, tc)])
                        else:
                            kb.act(dst[0:64, tc * 512:(tc + 1) * 512], bk[0:64, :], func, [bk], [(dst, tc)])
                        yield
                for tc in range(4):
                    pts = []
                    for mt in range(2):
                        bk = b0 if mt == 0 else b1
                        kb.mm(bk[:, :], kmT[0:64, h, mt * 128:(mt + 1) * 128], qmT[0:64, tc * 512:(tc + 1) * 512],
                              True, True, [kmT, (qmT, tc)], [bk])
                        pt = PTm[2 * L + mt]
                        kb.act(pt[:], bk[:], AF.Exp, [bk], [pt], scale=0.125)
                        pts.append(pt)
                        yield
                    for mt in range(2):
                        kb.mm(b2[0:64, :], vm[:, mt, h * 64:(h + 1) * 64], pts[mt][:], mt == 0, mt == 1, [vm, pts[mt]], [b2])
                    for mt in range(2):
                        kb.mm(b3[0:64, :], onesb[:, 0:64], pts[mt][:], mt == 0, mt == 1, [onesb, pts[mt]], [b3])
                    yield
                    rd = rdm[L]
                    kb.recip(rd[:], b3[0:64, :], [b3], [rd])
                    kb.tt("dve", rd[:], b2[0:64, :], rd[:], ALU.mult, [b2, rd], [rd])
                    yield
                    kb.tt("dve", ymT[0:64, h, tc * 512:(tc + 1) * 512], rd[:], szm[0:64, tc * 512:(tc + 1) * 512], ALU.mult,
                          [rd, (szm, tc)], [(ymT, (h, tc))])
                    yield

            run_lanes([lane(0, 0), lane(1, 1)])
            run_lanes([lane(2, 0), lane(3, 1)])

        def out_proj(w_out_d, nch, yTl, ymT, resid_d, resid_buf, final, stack, dbg=False):
            ysrc = []
            for (yb_, n_) in yTl:
                for ci in range(n_):
                    ysrc.append((yb_, ci))
            ncxs = [NormCtx(stack, 2), NormCtx(stack, 2)]
            WO = kb.sb("WO", [128, nch, 1024], BF16, stack)
            WOm = kb.sb("WOm", [64, 4, 1024], BF16, stack)
            wo_v = w_out_d[0:nch * 128, :].rearrange("(c p) n -> p c n", p=128)
            for c in range(0, nch, 4):
                load_w(WO, WO[:, c:c + 4, :], wo_v[:, c:c + 4, :], 1024, key=c)
            wom_v = w_out_d[nch * 128:nch * 128 + 256, :].rearrange("(h p) n -> p h n", p=64)
            load_w(WOm, WOm[0:64, :, :], wom_v, 1024, key=0, part=64)
            x1t = [kb.sb(f"x1t{i}", [128, 1024], F32, stack) for i in range(2)]
            if not final:
                gNx = kb.sb("gNx", [128, 1024], F32, stack)
                kb.dma("sp", gNx[:], nsa_norm_v.partition_broadcast(128), [], [gNx])
            if final:
                gF = kb.sb("gF", [128, 1024], F32, stack)
                kb.dma("sp", gF[:], final_g.partition_broadcast(128), [], [gF])
                ot = [kb.sb(f"ot{i}", [128, 1024], F32, stack) for i in range(2)]

            def lane(L):
                ncx = ncxs[L]
                bks = [kb.banks[4 * L + j] for j in range(4)]
                for t in range(L, NT_, 2):
                    xs = ncx.xstage[(t // 2) % 2]
                    kb.dma("sp", xs[:], resid_d[t * 128:(t + 1) * 128, :], [resid_buf], [xs])
                    x1 = x1t[L]
                    for half in range(2):
                        bk = bks[half]
                        for c in range(nch):
                            yb_, ci = ysrc[c]
                            kb.mm(bk[:, :], yb_[:, ci, t * 128:(t + 1) * 128], WO[:, c, half * 512:(half + 1) * 512], c == 0, False,
                                  [yb_, WO], [bk])
                            if c % 4 == 3:
                                yield
                        for h in range(4):
                            kb.mm(bk[:, :], ymT[0:64, h, t * 128:(t + 1) * 128], WOm[0:64, h, half * 512:(half + 1) * 512], False, h == 3,
                                  [ymT, WOm], [bk])
                        yield
                        kb.tt("dve", x1[:, half * 512:(half + 1) * 512], xs[:, half * 512:(half + 1) * 512], bk[:], ALU.add,
                              [xs, bk], [(x1, half)])
                        yield
                    if not final:
                        kb.dma("sp", x1_scr[t * 128:(t + 1) * 128, :], x1[:], [x1], [DX1])
                        yield from norm_to_T_gen(ncx, x1, x1[:], xnT, t, t * 128, bks[2], (t // 2) % 2, gB=gNx)
                        if dbg:
                            kb.dma("sp", out_d[t * 128:(t + 1) * 128, :], x1[:], [x1], [DOUT])
                    else:
                        rs, i = tile_rstd(ncx, x1, x1[:])
                        yield
                        o = ot[L]
                        kb.stt(o[:], x1[:], rs, gF[:], ALU.mult, ALU.mult, [x1, (stat, i), gF], [o])
                        yield
                        kb.dma("sp", out_d[t * 128:(t + 1) * 128, :], o[:], [o], [DOUT])
                        yield

            run_lanes([lane(0), lane(1)])

        with ExitStack() as l0:
            yTa = kb.sb("yTa", [128, 8, S_LEN], BF16, l0)
            ymT = kb.sb("ymT", [64, 4, S_LEN], BF16, l0)
            load_win = make_loader(hawk_w_in, None, 9, l0)
            kmT = kb.sb("kmT", [64, 4, 256], BF16, l0)
            vm = kb.sb("vm", [128, 2, 256], BF16, l0)
            with ExitStack() as pm:
                gHm = kb.sb("gHm", [128, 8, 128], F32, pm)
                kb.dma("sp", gHm[:], g_hawk_mem, [], [gHm])
                mem_kv(hawk_w_mem_kv, gHm, kmT, vm, pm)
                for c0_ in (7168, 7424, 7168 + 64, 7424 + 64):
                    load_win.prefetch(c0_, 64)
                S.barrier()
                ck("memkv0")
            with ExitStack() as pd:
                mem_attn(load_win, 7168, 7424, kmT, vm, ymT, pd)
                for c0_ in (0, 1024, 128, 1024 + 128):
                    load_win.prefetch(c0_, 128)
                S.barrier()
                ck("mem0")

            with ExitStack() as pb:
                lv = kb.sb("lv", [128, 8, 8], F32, pb)
                cvec = kb.sb("cvec", [128, 8, 2], F32, pb)
                bda = kb.sb("bda", [128, 8, 128], BF16, pb)
                bdx = kb.sb("bdx", [128, 8, 128], BF16, pb)
                kb.dma("sp", lv[:], lru_vec, [], [lv])
                load_w(bda, bda[:], bd_a, 128)
                load_w(bdx, bdx[:], bd_x, 128)
                kb.act(cvec[:, :, 0], lv[:, :, 7], AF.Exp, [lv], [cvec], scale=-1.0)
                kb.act(cvec[:, :, 0], cvec[:, :, 0], AF.Ln, [cvec], [cvec], bias=1.0)
                kb.ts("dve", cvec[:, :, 1], cvec[:, :, 0], -16.0, None, ALU.mult, None, [cvec], [cvec])
                kb.ts("dve", cvec[:, :, 0], cvec[:, :, 0], -8.0, None, ALU.mult, None, [cvec], [cvec])
                sets = []
                for L in range(2):
                    sets.append(dict(
                        B1=kb.sb(f"B1_{L}", [128, S_LEN + 4], F32, pb), B2=kb.sb(f"B2_{L}", [128, S_LEN], F32, pb),
                        B3=kb.sb(f"B3_{L}", [128, S_LEN], F32, pb), B4=kb.sb(f"B4_{L}", [128, S_LEN], F32, pb),
                        xcb=kb.sb(f"xcb_{L}", [128, S_LEN], BF16, pb), sz=kb.sb(f"sz_{L}", [128, S_LEN], BF16, pb)))

                def lru_lane(L):
                    st_ = sets[L]
                    B1, B2, B3, B4, xcb, sz = st_["B1"], st_["B2"], st_["B3"], st_["B4"], st_["xcb"], st_["sz"]
                    bks = [kb.banks[4 * L + j] for j in range(4)]
                    for c in range(L, 8, 2):
                        wxa = load_win(c * 128)
                        wza = load_win(1024 + c * 128)
                        kb.memset("dve", B1[:, 0:3], 0.0, [(B1, "pad")])
                        for (wsl, which) in ((wxa, 0), (wza, 1)):
                            for tc in range(4):
                                bk = bks[tc % 4]
                                for dc in range(8):
                                    kb.mm(bk[:, :], wsl[:, dc, 0:128], xnT[:, dc, tc * 512:(tc + 1) * 512], dc == 0, dc == 7, [wsl, xnT], [bk])
                                yield
                                if which == 0:
                                    kb.cp("act", B1[:, 3 + tc * 512:3 + (tc + 1) * 512], bk[:], [bk], [(B1, tc)])
                                else:
                                    kb.act(sz[:, tc * 512:(tc + 1) * 512], bk[:], AF.Silu, [bk], [(sz, tc)])
                                yield
                        kb.ts("dve", B2[:], B1[:, 0:S_LEN], lv[:, c, 0:1], lv[:, c, 4:5], ALU.mult, ALU.add, [B1, lv], [B2])
                        yield
                        for k in range(1, 4):
                            kb.stt(B2[:], B1[:, k:k + S_LEN], lv[:, c, k:k + 1], B2[:], ALU.mult, ALU.add, [B1, lv, B2], [B2])
                            yield
                        kb.cp("dve", xcb[:], B2[:], [B2], [xcb])
                        yield
                        for (bd, dstb, col) in ((bda, B1, 5), (bdx, B4, 6)):
                            for tc in range(4):
                                bk = bks[tc % 4]
                                kb.mm(bk[:, :], bd[:, c, :], xcb[:, tc * 512:(tc + 1) * 512], True, True, [bd, xcb], [bk])
                                yield
                                kb.act(dstb[:, tc * 512:(tc + 1) * 512], bk[:], AF.Sigmoid, [bk, lv], [(dstb, tc)], bias=lv[:, c, col:col + 1])
                                yield
                        r_ap = B1[:, 0:S_LEN]
                        kb.act(B3[:], r_ap, AF.Exp, [B1, cvec], [B3], scale=cvec[:, c, 0:1])
                        yield
                        kb.act(r_ap, r_ap, AF.Exp, [B1, cvec], [B1], scale=cvec[:, c, 1:2])
                        yield
                        kb.tt("dve", B2[:], B2[:], B4[:], ALU.mult, [B2, B4], [B2])
                        yield
                        kb.ts("dve", r_ap, r_ap, -1.0, 1.0, ALU.mult, ALU.add, [B1], [B1])
                        yield
                        kb.ts("dve", r_ap, r_ap, 0.0, None, ALU.max, None, [B1], [B1])
                        yield
                        kb.act(r_ap, r_ap, AF.Sqrt, [B1], [B1])
                        kb.memset("dve", B1[:, 0:1], 1.0, [B1])
                        yield
                        kb.tt("dve", B2[:], B2[:], r_ap, ALU.mult, [B2, B1], [B2])
                        yield
                        S.op("dve", lambda e, B4=B4, B3=B3, B2=B2: e.tensor_tensor_scan(out=B4[:], data0=B3[:], data1=B2[:], initial=0.0,
                                                                                      op0=ALU.mult, op1=ALU.add), reads=[B3, B2], writes=[B4])
                        yield
                        kb.tt("dve", yTa[:, c, :], B4[:], sz[:], ALU.mult, [B4, sz], [(yTa, c)])
                        yield

                run_lanes([lru_lane(0), lru_lane(1)])
                for c0_ in (2048 + 4608, 2048, 2048 + 1536, 2048 + 3072):
                    load_win.prefetch(c0_, 128)
                S.barrier()
                ck("lru")
            yTb = kb.sb("yTb", [128, 4, S_LEN], BF16, l0)

            with ExitStack() as pc:
                edil = kb.sb("edil", [128, 12, 256], F32, pc)
                kb.dma("sp", edil[:], edil_d, [], [edil])
                qTs = [kb.sb(f"qT{i}", [128, S_LEN], BF16, pc) for i in range(2)]
                kTs = [kb.sb(f"kT{i}", [128, S_LEN], BF16, pc) for i in range(2)]
                vTs = [kb.sb(f"vT{i}", [128, S_LEN], BF16, pc) for i in range(2)]
                Vps = [kb.sb(f"Vp{i}", [128, 16, 128], BF16, pc) for i in range(2)]
                szbs = [kb.sb(f"szb{i}", [128, S_LEN], BF16, pc) for i in range(2)]
                NTa = kb.sb("NTa", [128, S_LEN], F32, pc)
                DBa = kb.sb("DBa", [128, S_LEN], F32, pc)
                Pf = [kb.sb(f"Pf{i}", [128, 256], F32, pc) for i in range(3)]
                PT = [kb.sb(f"PT{i}", [128, 256], BF16, pc) for i in range(3)]
                sc_d = 128.0 ** -0.5
                items = [(hs, g) for hs in range(4) for g in range(3)]
                prot = [0]

                def pbank():
                    b = kb.banks[6 + prot[0]]
                    prot[0] ^= 1
                    return b

                def dtoks(d, r, b):
                    t0 = r + d * 128 * b
                    return slice(t0, t0 + d * 127 + 1, d)

                def proj_fm_lane(wsl, dst, func):
                    for tc in range(4):
                        bk = pbank()
                        for dc in range(8):
                            kb.mm(bk[:, :], wsl[:, dc, 0:128], xnT[:, dc, tc * 512:(tc + 1) * 512], dc == 0, dc == 7, [wsl, xnT], [bk])
                        yield
                        if func is None:
                            kb.cp("act", dst[:, tc * 512:(tc + 1) * 512], bk[:], [bk], [(dst, tc)])
                        else:
                            kb.act(dst[:, tc * 512:(tc + 1) * 512], bk[:], func, [bk], [(dst, tc)])
                        yield

                def task_proj(i):
                    hs, g = items[i]
                    win, d = DIL_GROUPS[g]
                    hh = g * 4 + hs
                    nqb = (S_LEN // d) // 128
                    s = i % 2
                    if g == 0:
                        wz = load_win(2048 + 4608 + hs * 128)
                        yield from proj_fm_lane(wz, szbs[hs % 2], AF.Silu)
                    wq = load_win(2048 + hh * 128)
                    yield from proj_fm_lane(wq, qTs[s], None)
                    wk = load_win(2048 + 1536 + hh * 128)
                    yield from proj_fm_lane(wk, kTs[s], None)
                    wv = load_win(2048 + 3072 + hh * 128)
                    yield from proj_fm_lane(wv, vTs[s], None)
                    for j in range(4):
                        bk = pbank()
                        bv = bk[:].bitcast(BF16)
                        for k in range(4):
                            r, b = divmod(4 * j + k, nqb)
                            kb.tr(bv[:, k * 128:(k + 1) * 128], vTs[s][:, dtoks(d, r, b)], identb[:], [vTs[s], identb], [bk])
                        yield
                        kb.cp("act", Vps[s][:, 4 * j:4 * j + 4, :], bv[:, 0:512].rearrange("p (k n) -> p k n", k=4), [bk], [(Vps[s], j)])
                        yield

                def task_tile(i, ti, lane):
                    hs, g = items[i]
                    win, d = DIL_GROUPS[g]
                    hh = g * 4 + hs
                    nqb = (S_LEN // d) // 128
                    s = i % 2
                    qT, kT, Vp = qTs[s], kTs[s], Vps[s]
                    r, qb = divmod(ti, nqb)
                    qs = dtoks(d, r, qb)
                    kbs = [qb - 1, qb] if qb > 0 else [qb]
                    sbk = kb.banks[2 * lane]
                    ndb = kb.banks[2 * lane + 1]
                    for kbi in kbs:
                        typ = 0 if kbi < qb else 1
                        kb.mm(sbk[:, typ * 128:(typ + 1) * 128], kT[:, dtoks(d, r, kbi)], qT[:, qs], True, True, [kT, qT], [sbk])
                    yield
                    lo = 0 if qb > 0 else 128
                    pf = Pf[lane]
                    pt = PT[lane]
                    kb.act(pf[:, lo:256], sbk[:, lo:256], AF.Exp, [sbk], [pf], scale=sc_d)
                    yield
                    kb.tt("dve", pt[:, lo:256], pf[:, lo:256], edil[:, hh, lo:256], ALU.mult, [pf, edil], [pt])
                    yield
                    for j, kbi in enumerate(kbs):
                        typ = 0 if kbi < qb else 1
                        kb.mm(ndb[:, 0:128], Vp[:, r * nqb + kbi, :], pt[:, typ * 128:(typ + 1) * 128], j == 0, j == len(kbs) - 1,
                              [Vp, pt], [ndb])
                    for j, kbi in enumerate(kbs):
                        typ = 0 if kbi < qb else 1
                        kb.mm(ndb[:, 128:256], onesb[:], pt[:, typ * 128:(typ + 1) * 128], j == 0, j == len(kbs) - 1,
                              [onesb, pt], [ndb])
                    yield
                    if g == 0:
                        kb.cp("dve", NTa[:, qs], ndb[:, 0:128], [ndb], [NTa])
                        kb.cp("dve", DBa[:, qs], ndb[:, 128:256], [ndb], [DBa])
                    else:
                        kb.tt("dve", NTa[:, qs], NTa[:, qs], ndb[:, 0:128], ALU.add, [ndb, NTa], [NTa])
                        kb.tt("dve", DBa[:, qs], DBa[:, qs], ndb[:, 128:256], ALU.add, [ndb, DBa], [DBa])
                    yield

                def tile_lane(i, lane):
                    for ti in range(lane, 16, 3):
                        yield from task_tile(i, ti, lane)

                run_lanes([task_proj(0)])
                for i in range(len(items)):
                    hs, g = items[i]
                    lanes = [tile_lane(i, 0), tile_lane(i, 1), tile_lane(i, 2)]
                    if i + 1 < len(items):
                        lanes.append(task_proj(i + 1))
                    run_lanes(lanes)
                    if g == 2:
                        kb.recip(DBa[:], DBa[:], [DBa], [DBa])
                        kb.tt("dve", NTa[:], NTa[:], DBa[:], ALU.mult, [NTa, DBa], [NTa])
                        kb.tt("dve", yTb[:, hs, :], NTa[:], szbs[hs % 2][:], ALU.mult, [NTa, szbs[hs % 2]], [(yTb, hs)])
                S.barrier()
                ck("dil")

            with ExitStack() as pe_:
                out_proj(hawk_w_out, 12, [(yTa, 8), (yTb, 4)], ymT, x_d, DX, False, pe_, dbg=(stop_after == "l0"))
                S.barrier()
                ck("l0end")

        if stop_after != "l0":
          with ExitStack() as l1:
            yT1 = kb.sb("yT1", [128, 8, S_LEN], BF16, l1)
            ymT1 = kb.sb("ymT1", [64, 4, S_LEN], BF16, l1)
            load_win = make_loader(nsa_w_in, None, 5, l1)
            kcmpT = kb.sb("kcmpT", [64, 2, 128], BF16, l1)
            vcmp = kb.sb("vcmp", [128, 2, 64], BF16, l1)
            gates = kb.sb("gates", [128, 16, 48], F32, l1)
            with ExitStack() as pmm:
                kmT = kb.sb("kmT1", [64, 4, 256], BF16, pmm)
                vm = kb.sb("vm1", [128, 2, 256], BF16, pmm)
                with ExitStack() as pm:
                    gNm = kb.sb("gNm", [128, 8, 128], F32, pm)
                    kb.dma("sp", gNm[:], g_nsa_mem, [], [gNm])
                    mem_kv(nsa_w_mem_kv, gNm, kmT, vm, pm)
                    for c0_ in (2864, 3120, 2864 + 64, 3120 + 64):
                        load_win.prefetch(c0_, 64)
                    S.barrier()
                    ck("memkv1")
                with ExitStack() as pd:
                    mem_attn(load_win, 2864, 3120, kmT, vm, ymT1, pd)
                    load_win.prefetch(1792, 48)
                    load_win.prefetch(1024, 128)
                    load_win.prefetch(1024 + 128, 128)
                    S.barrier()
                    ck("mem1")

            with ExitStack() as pq:
                wg = load_win(1792, 48)
                for t in range(NT_):
                    bk = kb.bank()
                    for dc in range(8):
                        kb.mm(bk[:, 0:48], xnT[:, dc, t * 128:(t + 1) * 128], wg[:, dc, 0:48], dc == 0, dc == 7, [xnT, wg], [bk])
                    kb.act(gates[:, t, :], bk[:, 0:48], AF.Sigmoid, [bk], [(gates, t)])
                kcT = kb.sb("kcT", [128, S_LEN], BF16, pq)
                vcT = kb.sb("vcT", [128, S_LEN], BF16, pq)
                wkc = load_win(1024)
                proj_fm(wkc, 128, lambda bk, ap, tc: kb.cp("act", kcT[:, tc * 512:(tc + 1) * 512], ap, [bk], [(kcT, tc)]))
                wvc = load_win(1024 + 128)
                proj_fm(wvc, 128, lambda bk, ap, tc: kb.cp("act", vcT[:, tc * 512:(tc + 1) * 512], ap, [bk], [(vcT, tc)]))
                W1 = kb.sb("W1", [128, 32, 256], BF16, pq)
                w2 = kb.sb("w2", [128, 2, 64], BF16, pq)
                peS = kb.sb("peS", [64, 2, 32], F32, pq)
                peb = kb.sb("peb", [64, 2, 32], BF16, pq)
                hidT = kb.sb("hidT", [128, 2, 128], BF16, pq)
                cb = kb.sb("cb", [128, 2], F32, pq)
                kb.dma("sp", peS[:], peT_d, [], [peS])
                kb.cp("dve", peb[:], peS[:], [peS], [peb])
                for kv in range(2):
                    w1d = w1k_d if kv == 0 else w1v_d
                    w2d = w2k_d if kv == 0 else w2v_d
                    srcT = kcT if kv == 0 else vcT
                    for p4 in range(2):
                        load_w(W1, W1[:, p4 * 16:(p4 + 1) * 16, :], w1d[:, p4 * 16:(p4 + 1) * 16, :], 256, key=p4)
                    load_w(w2, w2[:, :, :], w2d.rearrange("(hc p) d -> p hc d", p=128), 64)
                    for hc in range(2):
                        bk = kb.bank()
                        for p in range(32):
                            kb.mm(bk[:, 0:1], W1[0:64, p, hc * 128:(hc + 1) * 128], peb[0:64, kv, p:p + 1], p == 0, p == 31,
                                  [W1, peb], [bk])
                        kb.cp("dve", cb[:, hc:hc + 1], bk[:, 0:1], [bk], [(cb, hc)])
                    for g in range(2):
                        for hc in range(2):
                            bk = kb.bank()
                            for p in range(32):
                                kb.mm(bk[:, 0:127], W1[g * 64:(g + 1) * 64, p, hc * 128:(hc + 1) * 128],
                                      srcT[g * 64:(g + 1) * 64, p:p + 16 * 126 + 1:16], p == 0, p == 31, [W1, srcT], [bk])
                            kb.act(hidT[:, hc, 0:127], bk[:, 0:127], AF.Silu, [bk, cb], [(hidT, hc)], bias=cb[:, hc:hc + 1])
                        bk = kb.bank()
                        if kv == 0:
                            for hc in range(2):
                                kb.mm(bk[0:64, 0:127], w2[:, hc, :], hidT[:, hc, 0:127], hc == 0, hc == 1, [w2, hidT], [bk])
                            kb.cp("dve", kcmpT[0:64, g, 0:127], bk[0:64, 0:127], [bk], [(kcmpT, g)])
                        else:
                            for hc in range(2):
                                kb.mm(bk[0:127, 0:64], hidT[:, hc, 0:127], w2[:, hc, :], hc == 0, hc == 1, [w2, hidT], [bk])
                            kb.cp("dve", vcmp[0:127, g, :], bk[0:127, 0:64], [bk], [(vcmp, g)])
                S.barrier()
                ck("cmpkv")

            with ExitStack() as pg:
                QAg = kb.sb("QAg", [105, 8, S_LEN], BF16, pg)
                KAs = kb.sb("KAs", [105, S_LEN], BF16, pg)
                KAw = kb.sb("KAw", [105, S_LEN], BF16, pg)
                VAs = kb.sb("VAs", [128, 16, 128], BF16, pg)
                VAw = kb.sb("VAw", [128, 16, 128], BF16, pg)
                ecmp = kb.sb("ecmp", [128, 8, 247], F32, pg)
                Wz = kb.sb("Wz", [128, 8, 512], BF16, pg)
                m12 = kb.sb("m12", [128, 2, 62], F32, pg)
                trib = kb.sb("trib", [128, 2, 512], BF16, pg)
                kb.dma("sp", m12[:], m12_d, [], [m12])
                kb.dma("pool", trib[:], tri_d, [], [trib])
                for v, KA in enumerate((KAs, KAw)):
                    kb.dma("pool", KA[64:105, :], kaug_d[v], [], [(KA, "aug")])
                kb.memset("pool", VAs[:, :, 64:128], 1.0, [(VAs, "ones")])
                kb.memset("pool", VAw[:, :, 64:128], 1.0, [(VAw, "ones")])
                Pu = [kb.sb(f"Pu{i}", [128, 4, 128], F32, pg) for i in range(2)]
                Pub = [kb.sb("Pub0", [128, 4, 128], BF16, pg)] * 2
                pT = [kb.sb("pT0", [128, 4, 128], BF16, pg)] * 2
                for i in range(2):
                    kb.memset("pool", Pu[i][:], 0.0, [Pu[i]])
                psg = kb.sb("psg", [128, 128], F32, pg)
                den8 = kb.sb("den8", [128, 8], F32, pg)
                cg8 = kb.sb("cg8", [128, 8], F32, pg)
                imp = kb.sb("imp", [128, 32], F32, pg)
                impm = kb.sb("impm", [128, 32], F32, pg)
                m8 = kb.sb("m8", [128, 8], F32, pg)
                negp = kb.sb("negp", [128, 96], F32, pg)
                negS = kb.sb("negS", [96, 128], BF16, pg)
                kb.memset("pool", negp[:], 0.0, [negp])
                PTs = [kb.sb(f"PTs{i}", [128, 512], BF16, pg) for i in range(10)]
                pts_rr = [0]
                zerob = kb.sb("zerob", [128, 260], BF16, pg)
                kb.memset("pool", zerob[:], 0.0, [zerob])
                acs = [kb.sb(f"acs{i}", [128, 260], F32, pg) for i in range(2)]
                rd4 = [kb.sb(f"rd4{i}", [128, 4], F32, pg) for i in range(2)]
                cg4 = [kb.sb(f"cg4{i}", [128, 4], F32, pg) for i in range(2)]
                Oa = [kb.sb(f"Oa{i}", [128, 512], F32, pg) for i in range(3)]
                szt = kb.sb("szt", [128, 512], F32, pg)
                Ob = kb.sb("Ob", [128, 512], BF16, pg)
                accbanks = [kb.banks[0], kb.banks[1]]
                rot = [2]

                def rbank():
                    b = kb.banks[rot[0]]
                    rot[0] = rot[0] + 1 if rot[0] < 7 else 2
                    return b

                strot = [0, 0]

                def stbank(lane):
                    b = kb.banks[2 + 2 * lane + strot[lane]]
                    strot[lane] ^= 1
                    return b

                def mbank():
                    return kb.banks[6]

                ptrot = [0, 0]

                def next_pt(lane):
                    p = PTs[5 * lane + ptrot[lane]]
                    ptrot[lane] = (ptrot[lane] + 1) % 5
                    return p

                for g in range(2):
                    kb.dma("sp", ecmp[:], ecmp_d[:, g * 8:(g + 1) * 8, :], [], [ecmp])
                    for j in range(4):
                        load_w(Wz, Wz[:, :, j * 128:(j + 1) * 128], win_cols(nsa_w_in, 1840 + g * 512 + j * 128, 128), 128,
                               gain=None, key=j)
                    for pr in range(4):
                        wq = load_win((g * 8 + 2 * pr) * 64, 128)
                        for tc in range(4):
                            bk = rbank()
                            for dc in range(8):
                                kb.mm(bk[:, :], wq[:, dc, 0:128], xnT[:, dc, tc * 512:(tc + 1) * 512], dc == 0, dc == 7, [wq, xnT], [bk])
                            kb.cp("act", QAg[0:64, 2 * pr, tc * 512:(tc + 1) * 512], bk[0:64, :], [bk], [QAg])
                            stq = PTs[5 * (pr % 2) + tc]
                            kb.cp("dve", stq[64:128, :], bk[64:128, :], [bk], [stq])
                            kb.dma("sp", QAg[0:64, 2 * pr + 1, tc * 512:(tc + 1) * 512], stq[64:128, :], [stq], [QAg])
                    kb.dma("pool", QAg[96:105, :, :], qal_d[g * 8:(g + 1) * 8].rearrange("h r n -> r h n"), [], [QAg])
                    for KA, col in ((KAs, 1024 + 2 * 128 + g * 64), (KAw, 1024 + 4 * 128 + g * 64)):
                        wk = load_win(col, 64)
                        for tc in range(4):
                            bk = rbank()
                            for dc in range(8):
                                kb.mm(bk[0:64, :], wk[:, dc, 0:64], xnT[:, dc, tc * 512:(tc + 1) * 512], dc == 0, dc == 7, [wk, xnT], [bk])
                            kb.cp("act", KA[0:64, tc * 512:(tc + 1) * 512], bk[0:64, :], [bk], [(KA, tc)])
                    wv = load_win(1024 + 3 * 128 + g * 64, 64)
                    load_win(1024 + 5 * 128 + g * 64, 64, into=wv, off=64)
                    for t in range(NT_):
                        bk = rbank()
                        for dc in range(8):
                            kb.mm(bk[:, 0:128], xnT[:, dc, t * 128:(t + 1) * 128], wv[:, dc, 0:128], dc == 0, dc == 7, [xnT, wv], [bk])
                        kb.cp("act", VAs[:, t, 0:64], bk[:, 0:64], [bk], [(VAs, t)])
                        kb.cp("dve", VAw[:, t, 0:64], bk[:, 64:128], [bk], [(VAw, t)])

                    def task_C(qt):
                        qc = slice(qt * 128, (qt + 1) * 128)
                        O = Oa[qt % 3]
                        gq = gates[:, qt, :]
                        eoff = 120 - 8 * qt
                        ocb = kb.banks[7]
                        for b4 in range(2):
                            sbk = mbank()
                            for hl in range(4):
                                r = b4 * 4 + hl
                                kb.mm(sbk[:, hl * 128:hl * 128 + 127], QAg[0:64, r, qc], kcmpT[0:64, g, 0:127], True, True,
                                      [(QAg, qt), kcmpT], [sbk])
                            yield
                            pu = Pu[b4]
                            s3 = sbk[:].rearrange("p (h n) -> p h n", h=4)
                            kb.act(pu[:, :, 0:127], s3[:, :, 0:127], AF.Exp, [sbk], [pu], scale=0.125)
                            yield
                            kb.tt("dve", pu[:, :, 0:127], pu[:, :, 0:127], ecmp[:, b4 * 4:(b4 + 1) * 4, eoff:eoff + 127], ALU.mult,
                                  [pu, ecmp], [pu])
                            S.op("dve", lambda e, pu=pu, b4=b4: e.tensor_reduce(out=den8[:, b4 * 4:(b4 + 1) * 4], in_=pu[:, :, 0:127],
                                                                                axis=AX.X, op=ALU.add),
                                 reads=[pu], writes=[(den8, b4)])
                            yield
                            kb.ts("dve", den8[:, b4 * 4:(b4 + 1) * 4], den8[:, b4 * 4:(b4 + 1) * 4], 1e-30, None, ALU.max, None,
                                  [(den8, b4)], [(den8, b4)])
                            kb.recip(den8[:, b4 * 4:(b4 + 1) * 4], den8[:, b4 * 4:(b4 + 1) * 4], [(den8, b4)], [(den8, b4)])
                            pub = Pub[b4]
                            kb.cp("dve", pub[:], pu[:], [pu], [pub])
                            yield
                            for hl in range(4):
                                r = b4 * 4 + hl
                                if r == 0:
                                    kb.ts("dve", psg[:, :], pu[:, hl, :], den8[:, r:r + 1], None, ALU.mult, None, [pu, (den8, b4)], [psg])
                                else:
                                    kb.stt(psg[:, :], pu[:, hl, :], den8[:, r:r + 1], psg[:, :], ALU.mult, ALU.add,
                                           [pu, (den8, b4), psg], [psg])
                                if hl % 2 == 1:
                                    yield
                            tbk = mbank()
                            tv = tbk[:].bitcast(BF16)
                            for hl in range(4):
                                kb.tr(tv[0:127, hl * 128:(hl + 1) * 128], pub[:, hl, 0:127], identb[:], [pub, identb], [tbk])
                            yield
                            ptt = pT[b4]
                            kb.cp("act", ptt[0:127, :, :], tv[0:127, 0:512].rearrange("p (h n) -> p h n", h=4), [tbk], [ptt])
                            yield
                            for hl in range(4):
                                r = b4 * 4 + hl
                                kb.mm(ocb[:, r * 64:(r + 1) * 64], ptt[0:127, hl, :], vcmp[0:127, g, :], True, True, [ptt, vcmp], [ocb])
                            yield
                        kb.tt("dve", cg8[:], den8[:], gq[:, g * 24:g * 24 + 24:3], ALU.mult, [den8, gates], [cg8])
                        kb.tt("dve", O[:].rearrange("p (h d) -> p h d", h=8), ocb[:].rearrange("p (h d) -> p h d", h=8),
                              cg8[:, 0:8].unsqueeze(2).to_broadcast([128, 8, 64]), ALU.mult, [ocb, cg8], [O])
                        yield
                        S.op("dve", lambda e: e.tensor_reduce(out=imp[:, :], in_=psg[:].rearrange("p (j a) -> p j a", a=4),
                                                              axis=AX.X, op=ALU.add), reads=[psg], writes=[imp])
                        kb.tt("dve", imp[:, 1:32], imp[:, 1:32], psg[:, 3:127:4], ALU.add, [imp, psg], [imp])
                        yield
                        moff = 30 - 2 * qt
                        kb.tt("dve", impm[:], imp[:], m12[:, 0, moff:moff + 32], ALU.mult, [imp, m12], [impm])
                        kb.tt("dve", impm[:], impm[:], m12[:, 1, moff:moff + 32], ALU.add, [impm, m12], [impm])
                        kb.memset("dve", impm[:, 0:1], 1e6, [impm])
                        yield
                        S.op("dve", lambda e: e.max(out=m8[:], in_=impm[:]), reads=[impm], writes=[m8])
                        kb.ts("dve", negp[:, 64:96], impm[:], m8[:, 7:8], 1.0, ALU.is_ge, ALU.subtract, [impm, m8], [negp])
                        kb.ts("dve", negp[:, 64:96], negp[:, 64:96], NEGB, None, ALU.mult, None, [negp], [negp])
                        yield
                        tbk = mbank()
                        kb.tr(tbk[0:96, 0:128], negp[:, 0:96], identf[:], [negp, identf], [tbk])
                        yield
                        kb.cp("dve", negS[64:96, :], tbk[64:96, 0:128], [tbk], [negS])
                        kb.cp("dve", QAg[64:96, :, qc], negS[64:96, :].unsqueeze(1).to_broadcast([32, 8, 128]), [negS], [(QAg, qt)])
                        yield

                    def task_branch(qt, br, b4):
                        qc = slice(qt * 128, (qt + 1) * 128)
                        O = Oa[qt % 3]
                        gq = gates[:, qt, :]
                        KA, VA = (KAw, VAw) if br == 2 else (KAs, VAs)
                        kbs = list(range(max(0, qt - 4), qt + 1)) if br == 2 else list(range(0, qt + 1))
                        accb = kb.banks[b4]
                        pend = []
                        nk = len(kbs)
                        kb.mm(accb[:, 0:260], zerob[:, 0:128], zerob[:, 0:260], True, False, [zerob], [accb])

                        def do_pv(item):
                            pi, pk, ppt = item
                            for hl in range(4):
                                kb.mm(accb[:, hl * 65:(hl + 1) * 65], ppt[:, hl * 128:(hl + 1) * 128], VA[:, pk, 0:65], False,
                                      (pi == nk - 1) and hl == 3, [VA, ppt], [accb])

                        for idx, kbi in enumerate(kbs):
                            sbk = stbank(b4)
                            masks = []
                            if kbi == qt:
                                masks.append(0)
                            if br == 2 and kbi == qt - 4:
                                masks.append(1)
                            kb.mm(sbk[:, :], KA[0:105, kbi * 128:(kbi + 1) * 128], QAg[0:105, b4 * 4:(b4 + 1) * 4, qc],
                                  True, len(masks) == 0, [KA, (QAg, qt)], [sbk])
                            for mi, mv in enumerate(masks):
                                kb.mm(sbk[:, :], identb[:], trib[:, mv, :], False, mi == len(masks) - 1, [identb, trib], [sbk])
                            pt = next_pt(b4)
                            kb.act(pt[:], sbk[:], AF.Exp, [sbk], [pt], scale=0.125)
                            yield
                            pend.append((idx, kbi, pt))
                            if len(pend) > 3:
                                do_pv(pend.pop(0))
                                yield
                        while pend:
                            do_pv(pend.pop(0))
                            yield
                        kb.cp("dve", acs[b4][:], accb[:, 0:260], [accb], [acs[b4]])
                        yield
                        a3 = acs[b4][:].rearrange("p (h n) -> p h n", h=4)
                        kb.recip(rd4[b4][:], a3[:, :, 64], [acs[b4]], [rd4[b4]])
                        h0 = (g * 8 + b4 * 4) * 3 + br
                        kb.tt("dve", cg4[b4][:], rd4[b4][:], gq[:, h0:h0 + 10:3], ALU.mult, [rd4[b4], gates], [cg4[b4]])
                        yield
                        kb.tt("dve", a3[:, :, 0:64], a3[:, :, 0:64], cg4[b4][:, 0:4].unsqueeze(2).to_broadcast([128, 4, 64]), ALU.mult,
                              [acs[b4], cg4[b4]], [acs[b4]])
                        yield
                        ov = O[:, b4 * 256:(b4 + 1) * 256].rearrange("p (h d) -> p h d", h=4)
                        kb.tt("dve", ov, ov, a3[:, :, 0:64], ALU.add, [(O, b4), acs[b4]], [(O, b4)])
                        yield

                    def task_Z(qt):
                        qc = slice(qt * 128, (qt + 1) * 128)
                        O = Oa[qt % 3]
                        zb = mbank()
                        for dc in range(8):
                            kb.mm(zb[:, :], xnT[:, dc, qc], Wz[:, dc, :], dc == 0, dc == 7, [xnT, Wz], [zb])
                            if dc % 4 == 3:
                                yield
                        kb.act(szt[:], zb[:], AF.Silu, [zb], [szt])
                        yield
                        kb.tt("dve", Ob[:], O[:], szt[:], ALU.mult, [O, szt], [Ob])
                        yield
                        tbk = mbank()
                        tv = tbk[:].bitcast(BF16)
                        for c4 in range(4):
                            kb.tr(tv[:, c4 * 128:(c4 + 1) * 128], Ob[:, c4 * 128:(c4 + 1) * 128], identb[:], [Ob, identb], [tbk])
                        yield
                        kb.cp("act", yT1[:, g * 4:(g + 1) * 4, qc], tv[:, 0:512].rearrange("p (c n) -> p c n", c=4), [tbk], [(yT1, (g, qt))])
                        yield

                    run_lanes([task_C(0)])
                    for qt in range(NT_):
                        l1_ = chain(task_branch(qt, 2, 0), task_branch(qt, 1, 0))
                        l2_ = chain(task_branch(qt, 2, 1), task_branch(qt, 1, 1))
                        third = []
                        if qt + 1 < NT_:
                            third.append(task_C(qt + 1))
                        if qt > 0:
                            third.append(task_Z(qt - 1))
                        run_lanes([l1_, chain(*third), l2_], [1, L3_STEPS, 1])
                    run_lanes([task_Z(NT_ - 1)])
                S.barrier()
                ck("nsa")

            with ExitStack() as pe_:
                out_proj(nsa_w_out, 8, [(yT1, 8)], ymT1, x1_scr, DX1, True, pe_)
                S.barrier()
                ck("l1end")

        S.dead = False
        S.barrier()
        with nc.Block() as block:
            S.emit(block)
        print("program ops:", S.nops, "sems:", S.nsem)
    return nc


_CONST = None


def prep_inputs(inp):
    global _CONST
    if _CONST is None:
        _CONST = host_constants()
        _CONST.update(host_constants_nsa())
    f = lambda a: np.ascontiguousarray(np.asarray(a, dtype=np.float32))
    shared = {
        "hawk_w_in": f(inp["hawk_w_in"][0]),
        "hawk_w_out": f(inp["hawk_w_out"][0]),
        "hawk_w_mem_kv": f(inp["hawk_w_mem_kv"][0]),
        "g_hawk": expand_gain(f(inp["hawk_norm"][0])),
        "g_hawk_mem": expand_gain(f(inp["hawk_mem_norm"][0])),
        "bd_a": block_diag(f(inp["hawk_gate_a_w"][0])),
        "bd_x": block_diag(f(inp["hawk_gate_x_w"][0])),
        "final_norm": f(inp["final_norm"]),
        "hawk_norm_v": f(inp["hawk_norm"][0]),
        "nsa_norm_v": f(inp["nsa_norm"][0]),
        "nsa_w_in": f(inp["nsa_w_in"][0]),
        "nsa_w_out": f(inp["nsa_w_out"][0]),
        "nsa_w_mem_kv": f(inp["nsa_w_mem_kv"][0]),
        "g_nsa": expand_gain(f(inp["nsa_norm"][0])),
        "g_nsa_mem": expand_gain(f(inp["nsa_mem_norm"][0])),
        "w2k": f(inp["nsa_phi_k_w2"][0]),
        "w2v": f(inp["nsa_phi_v_w2"][0]),
    }
    for k in ("identf", "edil", "ecmp", "m12", "tri", "kaug", "qal"):
        shared[k] = _CONST[k]

    def w1_layout(w1):
        a = w1.reshape(32, 64, 256).transpose(1, 0, 2)
        return np.ascontiguousarray(np.concatenate([a, a], axis=0))
    shared["w1k"] = w1_layout(f(inp["nsa_phi_k_w1"][0]))
    shared["w1v"] = w1_layout(f(inp["nsa_phi_v_w1"][0]))
    shared["peT"] = np.ascontiguousarray(np.stack([f(inp["nsa_pe_k"][0]).T, f(inp["nsa_pe_v"][0]).T], axis=1))
    lv = np.zeros((128, 8, 8), np.float32)
    cw = f(inp["hawk_conv_w"][0])
    for k in range(4):
        lv[:, :, k] = vec_fm(cw[k])
    lv[:, :, 4] = vec_fm(f(inp["hawk_conv_b"][0]))
    lv[:, :, 5] = vec_fm(f(inp["hawk_gate_a_b"][0]).reshape(-1))
    lv[:, :, 6] = vec_fm(f(inp["hawk_gate_x_b"][0]).reshape(-1))
    lv[:, :, 7] = vec_fm(f(inp["hawk_lambda"][0]))
    shared["lru_vec"] = lv
    x = f(inp["x"])
    mem = f(inp["mem"])
    maps = []
    for b in range(x.shape[0]):
        m = dict(shared)
        m["x"] = x[b]
        m["mem"] = mem[b]
        maps.append(m)
    return maps


def kernel(**inputs):
    maps = prep_inputs(inputs)
    nc = build_program()
    res = run_bass_kernel_spmd(nc, maps, core_ids=list(range(len(maps))))
    out = np.stack([np.asarray(r["out"], dtype=np.float32) for r in res.results], axis=0)
    return out
```
